# Optimizing a Trainium2 kernel written in Bass

```python
import math
import jax
import jax.numpy as jnp
from jax import lax
import numpy as np

D_MODEL = 2048
BATCH = 8
SEQ = 2048
DEPTH = 2

FOX_HEADS = 8
FOX_HEAD_DIM = 128
FOX_WIDTH = FOX_HEADS * FOX_HEAD_DIM
SSM_WIDTH = D_MODEL - FOX_WIDTH
SSM_GROUP = 16
SSM_GROUPS = SSM_WIDTH // SSM_GROUP
SSM_STATE = 64
EVEN_IN = 3 * FOX_WIDTH + FOX_HEADS + SSM_WIDTH
FORGET_BIAS_INIT = 2.0
DT_MIN = 1e-3
DT_MAX = 1e-1
SWA_HEADS = 32
SWA_KV_HEADS = 4
SWA_HEAD_DIM = 64
SWA_GROUPS = SWA_HEADS // SWA_KV_HEADS
SWA_WINDOW = 128
ODD_IN = (SWA_HEADS + 2 * SWA_KV_HEADS) * SWA_HEAD_DIM
ROPE_DIM = SWA_HEAD_DIM // 4
ROPE_THETA = 500000.0
Q_BLOCK = 128
D_FF = 5504
CONV_WIDTH = 3
LN_EPS = 1e-5
DEEPNORM_ALPHA = (2.0 * DEPTH) ** 0.25
DEEPNORM_BETA = (8.0 * DEPTH) ** -0.25
N_EVEN = (DEPTH + 1) // 2
N_ODD = DEPTH // 2

kernel_name = 'hybrid_fox_s5_swa_deepnorm'


def _layer_norm(x, g, b):
    x32 = x.astype(jnp.float32)
    mu = jnp.mean(x32, axis=-1, keepdims=True)
    var = jnp.mean(jnp.square(x32 - mu), axis=-1, keepdims=True)
    y = (x32 - mu) * lax.rsqrt(var + LN_EPS)
    return (y * g.astype(jnp.float32) + b.astype(jnp.float32)).astype(x.dtype)


def _forgetting_attention(q, k, v, f_logit):
    s_len = q.shape[1]
    dh = q.shape[-1]
    scale = 1.0 / math.sqrt(dh)
    log_f = jax.nn.log_sigmoid(f_logit.astype(jnp.float32))
    c = jnp.cumsum(log_f, axis=1).transpose(0, 2, 1)
    outs = []
    for start in range(0, s_len, Q_BLOCK):
        end = start + Q_BLOCK
        s = jnp.einsum('bqhd,bkhd->bhqk', q[:, start:end], k[:, :end]).astype(jnp.float32) * scale
        s = s + c[:, :, start:end, None] - c[:, :, None, :end]
        causal = jnp.arange(start, end)[:, None] >= jnp.arange(end)[None, :]
        s = jnp.where(causal, s, -jnp.inf)
        p = jax.nn.softmax(s, axis=-1).astype(v.dtype)
        outs.append(jnp.einsum('bhqk,bkhd->bqhd', p, v[:, :end]))
    return jnp.concatenate(outs, axis=1)


def _s5_scan(u, lam_re, lam_im, log_step, b_re, b_im, c_re, c_im, d_skip):
    u32 = u.astype(jnp.float32)
    lr = lam_re.astype(jnp.float32)
    li = lam_im.astype(jnp.float32)
    dt = jnp.exp(log_step.astype(jnp.float32))[:, None]
    mag = jnp.exp(lr * dt)
    a_re = mag * jnp.cos(li * dt)
    a_im = mag * jnp.sin(li * dt)
    den = lr * lr + li * li
    xr = a_re - 1.0
    xi = a_im
    g_re = (xr * lr + xi * li) / den
    g_im = (xi * lr - xr * li) / den
    br = b_re.astype(jnp.float32)
    bi = b_im.astype(jnp.float32)
    bb_re = g_re[..., None] * br - g_im[..., None] * bi
    bb_im = g_re[..., None] * bi + g_im[..., None] * br
    bu_re = jnp.einsum('gpc,bsgc->bsgp', bb_re, u32)
    bu_im = jnp.einsum('gpc,bsgc->bsgp', bb_im, u32)

    def combine(e1, e2):
        a1r, a1i, b1r, b1i = e1
        a2r, a2i, b2r, b2i = e2
        return (a2r * a1r - a2i * a1i,
                a2r * a1i + a2i * a1r,
                a2r * b1r - a2i * b1i + b2r,
                a2r * b1i + a2i * b1r + b2i)

    s_len = u.shape[1]
    a_re_t = jnp.broadcast_to(a_re[None, None], (1, s_len) + a_re.shape)
    a_im_t = jnp.broadcast_to(a_im[None, None], (1, s_len) + a_im.shape)
    _, _, h_re, h_im = lax.associative_scan(combine, (a_re_t, a_im_t, bu_re, bu_im), axis=1)
    y = (jnp.einsum('gcp,bsgp->bsgc', c_re.astype(jnp.float32), h_re)
         - jnp.einsum('gcp,bsgp->bsgc', c_im.astype(jnp.float32), h_im)
         + d_skip.astype(jnp.float32) * u32)
    return y.astype(u.dtype)


def _partial_rope(x, positions):
    half = ROPE_DIM // 2
    inv_freq = ROPE_THETA ** (-jnp.arange(half, dtype=jnp.float32) / half)
    ang = positions.astype(jnp.float32)[..., None] * inv_freq
    cos = jnp.cos(ang)[:, :, None, :]
    sin = jnp.sin(ang)[:, :, None, :]
    xr = x[..., :ROPE_DIM].astype(jnp.float32)
    x1 = xr[..., :half]
    x2 = xr[..., half:]
    rot = jnp.concatenate([x1 * cos - x2 * sin, x2 * cos + x1 * sin], axis=-1).astype(x.dtype)
    return jnp.concatenate([rot, x[..., ROPE_DIM:]], axis=-1)


def _sliding_window_attention(q, k, v, sinks):
    bsz, s_len, _, dh = q.shape
    nb = s_len // Q_BLOCK
    scale = 1.0 / math.sqrt(dh)
    qb = q.reshape(bsz, nb, Q_BLOCK, SWA_KV_HEADS, SWA_GROUPS, dh)
    kb = k.reshape(bsz, nb, Q_BLOCK, SWA_KV_HEADS, dh)
    vb = v.reshape(bsz, nb, Q_BLOCK, SWA_KV_HEADS, dh)
    pad = ((0, 0), (1, 0), (0, 0), (0, 0), (0, 0))
    kk = jnp.concatenate([jnp.pad(kb, pad)[:, :-1], kb], axis=2)
    vv = jnp.concatenate([jnp.pad(vb, pad)[:, :-1], vb], axis=2)
    s = jnp.einsum('bnqhgd,bnkhd->bnhgqk', qb, kk).astype(jnp.float32) * scale
    qi = jnp.arange(Q_BLOCK)[:, None]
    kj = jnp.arange(2 * Q_BLOCK)[None, :]
    rel = Q_BLOCK + qi - kj
    band = (rel >= 0) & (rel < SWA_WINDOW)
    exists = (jnp.arange(nb)[:, None, None] > 0) | (kj[None] >= Q_BLOCK)
    valid = band[None] & exists
    s = jnp.where(valid[None, :, None, None], s, -jnp.inf)
    sink = jnp.broadcast_to(
        sinks.astype(jnp.float32).reshape(SWA_KV_HEADS, SWA_GROUPS)[None, None, :, :, None, None],
        s.shape[:-1] + (1,))
    p = jax.nn.softmax(jnp.concatenate([s, sink], axis=-1), axis=-1)[..., :-1]
    o = jnp.einsum('bnhgqk,bnkhd->bnqhgd', p.astype(v.dtype), vv)
    return o.reshape(bsz, s_len, SWA_HEADS * dh)


def _even_mixer(x, w_in, b_f, lam_re, lam_im, log_step, b_re, b_im, c_re, c_im, d_skip, w_glu, w_out):
    bsz, s_len, _ = x.shape
    proj = jnp.einsum('bsd,de->bse', x, w_in)
    q, k, v, f_logit, u = jnp.split(
        proj, [FOX_WIDTH, 2 * FOX_WIDTH, 3 * FOX_WIDTH, 3 * FOX_WIDTH + FOX_HEADS], axis=-1)
    hs = (bsz, s_len, FOX_HEADS, FOX_HEAD_DIM)
    fox = _forgetting_attention(q.reshape(hs), k.reshape(hs), v.reshape(hs), f_logit + b_f)
    fox = fox.reshape(bsz, s_len, FOX_WIDTH)
    y = _s5_scan(u.reshape(bsz, s_len, SSM_GROUPS, SSM_GROUP),
                 lam_re, lam_im, log_step, b_re, b_im, c_re, c_im, d_skip)
    z = jnp.einsum('bsc,ce->bse', jax.nn.gelu(y.reshape(bsz, s_len, SSM_WIDTH)), w_glu)
    ssm = z[..., :SSM_WIDTH] * jax.nn.sigmoid(z[..., SSM_WIDTH:])
    return jnp.einsum('bsc,cd->bsd', jnp.concatenate([fox, ssm], axis=-1), w_out)


def _odd_mixer(x, positions, w_in, sinks, w_out):
    bsz, s_len, _ = x.shape
    proj = jnp.einsum('bsd,de->bse', x, w_in)
    qw = SWA_HEADS * SWA_HEAD_DIM
    kw = SWA_KV_HEADS * SWA_HEAD_DIM
    q, k, v = jnp.split(proj, [qw, qw + kw], axis=-1)
    q = _partial_rope(q.reshape(bsz, s_len, SWA_HEADS, SWA_HEAD_DIM), positions)
    k = _partial_rope(k.reshape(bsz, s_len, SWA_KV_HEADS, SWA_HEAD_DIM), positions)
    v = v.reshape(bsz, s_len, SWA_KV_HEADS, SWA_HEAD_DIM)
    o = _sliding_window_attention(q, k, v, sinks)
    return jnp.einsum('bsc,cd->bsd', o, w_out)


def _conv_ffn(x, w_up, conv_w, conv_b, w_down):
    s_len = x.shape[1]
    h = jnp.einsum('bsd,df->bsf', x, w_up)
    hp = jnp.pad(h, ((0, 0), (CONV_WIDTH - 1, 0), (0, 0)))
    h = conv_b + sum(conv_w[t] * hp[:, t:t + s_len] for t in range(CONV_WIDTH))
    gate = h[..., :D_FF]
    val = h[..., D_FF:]
    return jnp.einsum('bsf,fd->bsd', jax.nn.silu(gate) * val, w_down)


def setup_inputs(seed: int = 0) -> dict:
    key = jax.random.key(seed)
    ks = jax.random.split(key, 26)
    f32 = jnp.float32

    def nrm(k, shape, scale):
        return jax.random.normal(k, shape, f32) * scale

    x = nrm(ks[0], (BATCH, SEQ, D_MODEL), 1.0)
    offs = jax.random.randint(ks[1], (BATCH, 1), 0, 1024, dtype=jnp.int32)
    positions = (offs + jnp.arange(SEQ, dtype=jnp.int32)[None, :]).astype(jnp.int32)

    ev_w_in = nrm(ks[2], (N_EVEN, D_MODEL, EVEN_IN), D_MODEL ** -0.5)
    ev_w_in = ev_w_in.at[:, :, 2 * FOX_WIDTH:3 * FOX_WIDTH].multiply(DEEPNORM_BETA)
    ev_b_f = FORGET_BIAS_INIT + nrm(ks[3], (N_EVEN, FOX_HEADS), 0.1)
    ev_lambda_re = -0.5 + nrm(ks[4], (N_EVEN, SSM_GROUPS, SSM_STATE), 0.01)
    ev_lambda_im = (math.pi * jnp.arange(SSM_STATE, dtype=f32))[None, None, :] + nrm(
        ks[5], (N_EVEN, SSM_GROUPS, SSM_STATE), 0.01)
    ev_log_step = jax.random.uniform(ks[6], (N_EVEN, SSM_GROUPS), f32,
                                     minval=math.log(DT_MIN), maxval=math.log(DT_MAX))
    ev_ssm_b_re = nrm(ks[7], (N_EVEN, SSM_GROUPS, SSM_STATE, SSM_GROUP), (2 * SSM_GROUP) ** -0.5)
    ev_ssm_b_im = nrm(ks[8], (N_EVEN, SSM_GROUPS, SSM_STATE, SSM_GROUP), (2 * SSM_GROUP) ** -0.5)
    ev_ssm_c_re = nrm(ks[9], (N_EVEN, SSM_GROUPS, SSM_GROUP, SSM_STATE), (2 * SSM_STATE) ** -0.5)
    ev_ssm_c_im = nrm(ks[10], (N_EVEN, SSM_GROUPS, SSM_GROUP, SSM_STATE), (2 * SSM_STATE) ** -0.5)
    ev_ssm_d = nrm(ks[11], (N_EVEN, SSM_GROUPS, SSM_GROUP), 1.0)
    ev_w_glu = nrm(ks[12], (N_EVEN, SSM_WIDTH, 2 * SSM_WIDTH), SSM_WIDTH ** -0.5)
    ev_w_out = nrm(ks[13], (N_EVEN, D_MODEL, D_MODEL), D_MODEL ** -0.5 * DEEPNORM_BETA)

    od_w_in = nrm(ks[14], (N_ODD, D_MODEL, ODD_IN), D_MODEL ** -0.5)
    v_start = (SWA_HEADS + SWA_KV_HEADS) * SWA_HEAD_DIM
    od_w_in = od_w_in.at[:, :, v_start:].multiply(DEEPNORM_BETA)
    od_sinks = nrm(ks[15], (N_ODD, SWA_HEADS), 0.1)
    od_w_out = nrm(ks[16], (N_ODD, SWA_HEADS * SWA_HEAD_DIM, D_MODEL),
                   (SWA_HEADS * SWA_HEAD_DIM) ** -0.5 * DEEPNORM_BETA)

    ln_mix_g = 1.0 + nrm(ks[17], (DEPTH, D_MODEL), 0.02)
    ln_mix_b = nrm(ks[18], (DEPTH, D_MODEL), 0.02)
    ffn_w_up = nrm(ks[19], (DEPTH, D_MODEL, 2 * D_FF), D_MODEL ** -0.5)
    ffn_conv_w = nrm(ks[20], (DEPTH, CONV_WIDTH, 2 * D_FF), CONV_WIDTH ** -0.5)
    ffn_conv_b = nrm(ks[21], (DEPTH, 2 * D_FF), 0.02)
    ffn_w_down = nrm(ks[22], (DEPTH, D_FF, D_MODEL), D_FF ** -0.5 * DEEPNORM_BETA)
    ln_ffn_g = 1.0 + nrm(ks[23], (DEPTH, D_MODEL), 0.02)
    ln_ffn_b = nrm(ks[24], (DEPTH, D_MODEL), 0.02)

    return {'x': x, 'positions': positions,
            'ev_w_in': ev_w_in, 'ev_b_f': ev_b_f,
            'ev_lambda_re': ev_lambda_re, 'ev_lambda_im': ev_lambda_im, 'ev_log_step': ev_log_step,
            'ev_ssm_b_re': ev_ssm_b_re, 'ev_ssm_b_im': ev_ssm_b_im,
            'ev_ssm_c_re': ev_ssm_c_re, 'ev_ssm_c_im': ev_ssm_c_im, 'ev_ssm_d': ev_ssm_d,
            'ev_w_glu': ev_w_glu, 'ev_w_out': ev_w_out,
            'od_w_in': od_w_in, 'od_sinks': od_sinks, 'od_w_out': od_w_out,
            'ln_mix_g': ln_mix_g, 'ln_mix_b': ln_mix_b,
            'ffn_w_up': ffn_w_up, 'ffn_conv_w': ffn_conv_w, 'ffn_conv_b': ffn_conv_b,
            'ffn_w_down': ffn_w_down, 'ln_ffn_g': ln_ffn_g, 'ln_ffn_b': ln_ffn_b}


def reference(x, positions, ev_w_in, ev_b_f, ev_lambda_re, ev_lambda_im, ev_log_step,
              ev_ssm_b_re, ev_ssm_b_im, ev_ssm_c_re, ev_ssm_c_im, ev_ssm_d, ev_w_glu, ev_w_out,
              od_w_in, od_sinks, od_w_out, ln_mix_g, ln_mix_b,
              ffn_w_up, ffn_conv_w, ffn_conv_b, ffn_w_down, ln_ffn_g, ln_ffn_b):
    for i in range(DEPTH):
        j = i // 2
        if i % 2 == 0:
            mix = _even_mixer(x, ev_w_in[j], ev_b_f[j], ev_lambda_re[j], ev_lambda_im[j],
                              ev_log_step[j], ev_ssm_b_re[j], ev_ssm_b_im[j], ev_ssm_c_re[j],
                              ev_ssm_c_im[j], ev_ssm_d[j], ev_w_glu[j], ev_w_out[j])
        else:
            mix = _odd_mixer(x, positions, od_w_in[j], od_sinks[j], od_w_out[j])
        x = _layer_norm(DEEPNORM_ALPHA * x + mix, ln_mix_g[i], ln_mix_b[i])
        ffn = _conv_ffn(x, ffn_w_up[i], ffn_conv_w[i], ffn_conv_b[i], ffn_w_down[i])
        x = _layer_norm(DEEPNORM_ALPHA * x + ffn, ln_ffn_g[i], ln_ffn_b[i])
    return x
```

```python
import math
import os
_SKIP = set(os.environ.get('K_SKIP', '').split(','))
from contextlib import ExitStack

import numpy as np
import concourse.bass as bass
import concourse.mybir as mybir
from concourse.bass_utils import run_bass_kernel_spmd

F32 = mybir.dt.float32
BF16 = mybir.dt.bfloat16
I32 = mybir.dt.int32
AF = mybir.ActivationFunctionType
ALU = mybir.AluOpType

S = 2048
D = 2048
DC = 16
T = 512
NT = 4
DFF = 5504
NJ = 43
ALPHA = (2.0 * 2) ** 0.25
LN_EPS = 1e-5
SBUF_BASE = 16384
SBUF_LIMIT = SBUF_BASE + 212800

ENGS = ("pe", "act", "dve", "pool", "sp")
SAME_SYNC = True


class Sched:
    def __init__(self, nc, es):
        self.nc = nc
        self.es = es
        self.streams = {e: [] for e in ENGS}
        self.cnt = {e: 0 for e in ENGS}
        self.waited = {e: {} for e in ENGS}
        self.lastw = {}
        self.readers = {}
        self.pend_r = {e: [] for e in ENGS}
        self.pend_w = {e: [] for e in ENGS}
        self.sems = {}
        self.dcount = {}
        self.ranges = {}
        self.alias = {}
        self.tcache = {}
        self.persist = set()
        for e in ENGS:
            self.sems["E_" + e] = es.enter_context(nc.semaphore("E_" + e))

    def sb(self, key, shape, dtype, off, persistent=False):
        nbytes = int(np.prod(shape[1:])) * (2 if dtype == BF16 else 4)
        assert off % 4 == 0 and off + nbytes <= SBUF_LIMIT, (key, off, nbytes)
        ck = (key, off, tuple(shape), str(dtype))
        if ck in self.tcache and self.ranges.get(key) == (off, off + nbytes):
            return self.tcache[ck]
        t = self.nc.alloc_sbuf_tensor_at(key, list(shape), dtype, offset=off)
        self.tcache[ck] = t
        assert key not in self.ranges, ("key re-registered at a different place", key)
        self.reg(key, off, off + nbytes)
        if persistent:
            self.persist.add(key)
        return t

    def phase_reset(self):
        self.barrier()
        self.lastw = {}
        self.readers = {}
        for k in list(self.ranges):
            if k not in self.persist:
                del self.ranges[k]
                del self.alias[k]
        for k in self.alias:
            self.alias[k] = [a for a in self.alias[k] if a in self.ranges]
        self.tcache = {ck: t for ck, t in self.tcache.items() if ck[0] in self.persist}

    def reg(self, key, lo, hi):
        self.ranges[key] = (lo, hi)
        al = []
        for k, (a, b) in self.ranges.items():
            if k != key and a < hi and lo < b:
                al.append(k)
                self.alias[k].append(key)
        self.alias[key] = al

    def _keys(self, k):
        return [k] + self.alias.get(k, [])

    def _wait(self, eng, tok):
        sem, val = tok
        if sem == "E_" + eng:
            if eng in ("pe", "sp") or not SAME_SYNC:
                return
        if self.waited[eng].get(sem, 0) >= val:
            return
        self.waited[eng][sem] = val
        self.streams[eng].append(("w", sem, val))

    def _deps(self, eng, reads, writes):
        toks = []
        for k0 in reads:
            for k in self._keys(k0):
                t = self.lastw.get(k)
                if t:
                    toks.append(t)
                if isinstance(k, str) and k.startswith("ps") and k[2:].isdigit():
                    toks.extend(r for r in self.readers.get(k, ()) if r[0] != "E_" + eng)
                for e2 in ENGS:
                    assert k not in self.pend_w[e2] or e2 == eng, ("pending write", k, e2, eng)
        for k0 in writes:
            for k in self._keys(k0):
                t = self.lastw.get(k)
                if t:
                    toks.append(t)
                toks.extend(self.readers.get(k, ()))
                for e2 in ENGS:
                    if e2 != eng:
                        assert k not in self.pend_r[e2] and k not in self.pend_w[e2], ("pending", k, e2, eng)
        for t in toks:
            self._wait(eng, t)

    def _commit(self, tok, reads, writes):
        for k in reads:
            self.readers.setdefault(k, []).append(tok)
        for k in writes:
            self.lastw[k] = tok
            self.readers[k] = []

    def op(self, eng, fn, reads=(), writes=(), inc=True):
        reads = list(reads)
        writes = list(writes)
        self._deps(eng, reads, writes)
        if inc:
            self.cnt[eng] += 1
            tok = ("E_" + eng, self.cnt[eng])
            self.streams[eng].append(("o", fn, "E_" + eng, 1))
            self._commit(tok, reads + self.pend_r[eng], writes + self.pend_w[eng])
            self.pend_r[eng] = []
            self.pend_w[eng] = []
        else:
            self.streams[eng].append(("o", fn, None, 0))
            self.pend_r[eng] += reads
            self.pend_w[eng] += writes

    def dma(self, q, sem, out, in_, reads=(), writes=(), **kw):
        reads = list(reads)
        writes = list(writes)
        if sem not in self.sems:
            self.sems[sem] = self.es.enter_context(self.nc.semaphore(sem))
            self.dcount[sem] = 0
        if self.dcount[sem]:
            self._wait(q, (sem, self.dcount[sem]))
        self._deps(q, reads, writes)
        self.dcount[sem] += 16
        tok = (sem, self.dcount[sem])
        self.streams[q].append(("o", lambda e, o=out, i=in_, kw=kw: e.dma_start(out=o, in_=i, **kw), sem, 16))
        self._commit(tok, reads, writes)

    def barrier(self):
        for e in ENGS:
            assert not self.pend_r[e] and not self.pend_w[e], ("barrier with pending ops", e)
        for e in ENGS:
            for e2 in ENGS:
                if self.cnt[e2] and (e2 != e or e in ("act", "dve", "pool")):
                    if self.waited[e].get("E_" + e2, 0) < self.cnt[e2]:
                        self.waited[e]["E_" + e2] = self.cnt[e2]
                        self.streams[e].append(("w", "E_" + e2, self.cnt[e2]))
            for s, v in self.dcount.items():
                if v:
                    self._wait(e, (s, v))

    def emit(self):
        nc = self.nc
        with nc.Block() as block:
            def mk(name):
                def body(e):
                    for it in self.streams[name]:
                        if it[0] == "w":
                            e.wait_ge(self.sems[it[1]], it[2])
                        else:
                            ins = it[1](e)
                            if it[2] is not None:
                                ins.then_inc(self.sems[it[2]], it[3])
                return body
            block.tensor(mk("pe"))
            block.scalar(mk("act"))
            block.vector(mk("dve"))
            block.gpsimd(mk("pool"))
            block.sync(mk("sp"))


def build(mode="full"):
    nc = bass.Bass("TRN2", target_bir_lowering=False)
    es = ExitStack()
    sc = Sched(nc, es)

    def dram_in(name, shape, dt=F32):
        return nc.dram_tensor(name, list(shape), dt, kind="ExternalInput").ap()

    x_in = dram_in("x", [S, D])
    pos_in = dram_in("pos", [S // 128, 128], I32)
    ev_w_in = dram_in("ev_w_in", [D, 4104])
    ev_b_f = dram_in("ev_b_f", [8, 1])
    ev_lre = dram_in("ev_lre", [64, 64])
    ev_lim = dram_in("ev_lim", [64, 64])
    ev_lstep = dram_in("ev_lstep", [1, 64])
    ev_bre = dram_in("ev_bre", [64, 64, 16])
    ev_bim = dram_in("ev_bim", [64, 64, 16])
    ev_cre = dram_in("ev_cre", [64, 16, 64])
    ev_cim = dram_in("ev_cim", [64, 16, 64])
    ev_d = dram_in("ev_d", [8, 128])
    ev_w_glu = dram_in("ev_w_glu", [1024, 2048])
    ev_w_out = dram_in("ev_w_out", [D, D])
    od_w_in = dram_in("od_w_in", [D, 2560])
    od_sinks = dram_in("od_sinks", [1, 32])
    od_w_out = dram_in("od_w_out", [D, D])
    ln_mix_g = dram_in("ln_mix_g", [2, 16, 128])
    ln_mix_b = dram_in("ln_mix_b", [2, 16, 128])
    ffn_w_up = dram_in("ffn_w_up", [2, D, 2 * DFF])
    ffn_conv_w = dram_in("ffn_conv_w", [2, 3, 86, 128])
    ffn_conv_b = dram_in("ffn_conv_b", [2, 86, 128])
    ffn_w_down = dram_in("ffn_w_down", [2, DFF, D])
    ln_ffn_g = dram_in("ln_ffn_g", [2, 16, 128])
    ln_ffn_b = dram_in("ln_ffn_b", [2, 16, 128])
    out_d = nc.dram_tensor("out", [S, D], F32, kind="ExternalOutput").ap()
    RA = nc.dram_tensor("resA", [DC, 128, S], F32, kind="Internal").ap()
    RB = nc.dram_tensor("resB", [DC, 128, S], F32, kind="Internal").ap()

    PS = [es.enter_context(nc.psum_tensor("ps%d" % i, [128, 512], F32)) for i in range(8)]

    def psk(i):
        return "ps%d" % i

    off = [SBUF_BASE]

    def alloc(key, shape, dt):
        nbytes = int(np.prod(shape[1:])) * (2 if dt == BF16 else 4)
        nbytes = (nbytes + 31) // 32 * 32
        t = sc.sb(key, shape, dt, off[0], persistent=True)
        off[0] += nbytes
        return t

    ident_f = alloc("ident_f", [128, 128], F32)
    ident_b = alloc("ident_b", [128, 128], BF16)
    ones_f = alloc("ones_f", [128, 128], F32)
    ones_b = alloc("ones_b", [128, 128], BF16)
    lnp = alloc("lnp", [128, 8, 16], F32)
    convp = alloc("convp", [128, 2, 4, 86], F32)
    CONST_END = off[0]
    XTB_OFF = CONST_END
    xTb = sc.sb("xTb_all", [128, DC, S], BF16, XTB_OFF, persistent=True)
    for tt in range(NT):
        sc.reg("xTb%d" % tt, SBUF_LIMIT + 1000 + tt, SBUF_LIMIT + 1000 + tt + 1)
        sc.persist.add("xTb%d" % tt)
    PH_OFF = XTB_OFF + DC * S * 2

    class Arena:
        def __init__(self, start):
            self.o = (start + 31) // 32 * 32

        def take(self, key, shape, dt, n=None):
            nbytes = int(np.prod(shape[1:])) * (2 if dt == BF16 else 4)
            nbytes = (nbytes + 31) // 32 * 32
            if n is None:
                t = sc.sb(key, shape, dt, self.o)
                self.o += nbytes
                return t
            ts = []
            for i in range(n):
                ts.append(sc.sb("%s%d" % (key, i), shape, dt, self.o))
                self.o += nbytes
            return ts

    def setup_consts():
        sc.op("pool", lambda e: e.memset(ones_f[:], 1.0), writes=["ones_f"])
        sc.op("pool", lambda e: e.memset(ident_f[:], 1.0), writes=["ident_f"])
        sc.op("pool", lambda e: e.affine_select(out=ident_f[:], in_=ident_f[:], pattern=[[-1, 128]],
                                                 compare_op=ALU.is_equal, fill=0.0, base=0,
                                                 channel_multiplier=1),
              reads=["ident_f"], writes=["ident_f"])
        sc.op("dve", lambda e: e.tensor_copy(out=ident_b[:], in_=ident_f[:]), reads=["ident_f"], writes=["ident_b"])
        sc.op("dve", lambda e: e.tensor_copy(out=ones_b[:], in_=ones_f[:]), reads=["ones_f"], writes=["ones_b"])
        stg = sc.sb("c_stg", [128, 128], F32, PH_OFF)
        jobs = []
        for l in range(2):
            for k, src in enumerate((ln_mix_g, ln_mix_b, ln_ffn_g, ln_ffn_b)):
                jobs.append((src[l], 16, lnp[:, l * 4 + k, :]))
            for k in range(3):
                jobs.append((ffn_conv_w[l, k], 86, convp[:, l, k, :]))
            jobs.append((ffn_conv_b[l], 86, convp[:, l, 3, :]))
        for n, (src, rows, dst) in enumerate(jobs):
            sc.dma("sp", "c_stg", stg[0:rows, :], src, writes=["c_stg"])
            sc.op("pe", lambda e, r=rows: e.transpose(PS[0][:, 0:r], stg[0:r, :], ident_f[0:r, 0:r]),
                  reads=["c_stg", "ident_f"], writes=[psk(0)])
            sc.op("dve", lambda e, r=rows, d=dst: e.tensor_copy(out=d, in_=PS[0][:, 0:r]),
                  reads=[psk(0)], writes=["consts"])

    def phase_input(x_src, Rout):
        A = Arena(PH_OFF)
        xin = A.take("xin", [128, D], F32, 2)
        st32 = A.take("st32", [128, DC, T], F32)
        for tt in range(NT):
            for tb in range(4):
                g = tt * 4 + tb
                xi = xin[g % 2]
                xk = "xin%d" % (g % 2)
                sc.dma("sp", xk, xi[:], x_src[g * 128:(g + 1) * 128, :], writes=[xk])
                for q in range(4):
                    bank = q % 2
                    for c4 in range(4):
                        dc = q * 4 + c4
                        sc.op("pe", lambda e, b=bank, c4=c4, dc=dc, xi=xi: e.transpose(
                            PS[b][:, c4 * 128:(c4 + 1) * 128], xi[:, dc * 128:(dc + 1) * 128], ident_f[:]),
                            reads=[xk, "ident_f"], writes=[psk(bank)], inc=(c4 == 3))
                    pv = PS[bank][:].rearrange("p (c t) -> p c t", c=4)
                    if 'act' not in _SKIP:
                      sc.op("act", lambda e, pv=pv, q=q, tb=tb: e.copy(
                        out=st32[:, q * 4:(q + 1) * 4, tb * 128:(tb + 1) * 128], in_=pv),
                        reads=[psk(bank)], writes=["st32"])
                    if 'dve' not in _SKIP:
                      sc.op("dve", lambda e, pv=pv, q=q, g=g: e.tensor_copy(
                        out=xTb[:, q * 4:(q + 1) * 4, g * 128:(g + 1) * 128], in_=pv),
                        reads=[psk(bank)], writes=["xTb%d" % tt])
            if 'st' not in _SKIP:
              sc.dma("sp", "st32", Rout[:, :, tt * T:(tt + 1) * T].rearrange("c p t -> p c t"), st32[:],
                   reads=["st32"], writes=[("R", id(Rout), tt)])

    def layer_norm(r32, rkey, gi, bi, tt, lo, Rout=None, final=False, ost_lo=None):
        A = Arena(lo)
        sq = A.take("ln_sq", [128, T], F32, 2)
        mean = A.take("ln_mean", [128, T], F32)
        rstd = A.take("ln_rstd", [128, T], F32)
        if final:
            ost = Arena(ost_lo).take("ln_ost", [128, D], F32, 2)
        for c in range(DC):
            s = sq[c % 2]
            sk = "ln_sq%d" % (c % 2)
            sc.op("act", lambda e, s=s, c=c: e.activation(out=s[:], in_=r32[:, c, :], func=AF.Square),
                  reads=[rkey], writes=[sk])
            sc.op("pe", lambda e, c=c: e.matmul(PS[6][:], lhsT=ones_f[:], rhs=r32[:, c, :], start=(c == 0), stop=(c == DC - 1)),
                  reads=[rkey, "ones_f"], writes=[psk(6)], inc=(c == DC - 1))
            sc.op("pe", lambda e, s=s, c=c: e.matmul(PS[7][:], lhsT=ones_f[:], rhs=s[:], start=(c == 0), stop=(c == DC - 1)),
                  reads=[sk, "ones_f"], writes=[psk(7)], inc=True)
        sc.op("act", lambda e: e.mul(out=mean[:], in_=PS[6][:], mul=1.0 / D), reads=[psk(6)], writes=["ln_mean"])
        sc.op("dve", lambda e: e.tensor_tensor(out=rstd[:], in0=mean[:], in1=mean[:], op=ALU.mult),
              reads=["ln_mean"], writes=["ln_rstd"])
        sc.op("dve", lambda e: e.scalar_tensor_tensor(out=rstd[:], in0=PS[7][:], scalar=1.0 / D, in1=rstd[:],
                                                      op0=ALU.mult, op1=ALU.subtract),
              reads=[psk(7), "ln_rstd"], writes=["ln_rstd"])
        sc.op("dve", lambda e: e.tensor_scalar(out=rstd[:], in0=rstd[:], scalar1=LN_EPS, scalar2=None, op0=ALU.add),
              reads=["ln_rstd"], writes=["ln_rstd"])
        sc.op("act", lambda e: e.activation(out=rstd[:], in_=rstd[:], func=AF.Sqrt), reads=["ln_rstd"], writes=["ln_rstd"])
        sc.op("dve", lambda e: e.reciprocal(out=rstd[:], in_=rstd[:]), reads=["ln_rstd"], writes=["ln_rstd"])
        for c in range(DC):
            sc.op("dve", lambda e, c=c: e.tensor_tensor(out=r32[:, c, :], in0=r32[:, c, :], in1=mean[:], op=ALU.subtract),
                  reads=[rkey, "ln_mean"], writes=[rkey])
            sc.op("dve", lambda e, c=c: e.tensor_tensor(out=r32[:, c, :], in0=r32[:, c, :], in1=rstd[:], op=ALU.mult),
                  reads=[rkey, "ln_rstd"], writes=[rkey])
            sc.op("act", lambda e, c=c: e.activation(out=r32[:, c, :], in_=r32[:, c, :], func=AF.Identity,
                                                     bias=lnp[:, bi, c:c + 1], scale=lnp[:, gi, c:c + 1]),
                  reads=[rkey, "consts"], writes=[rkey])
            if not final:
                sc.op("pool", lambda e, c=c: e.tensor_copy(out=xTb[:, c, tt * T:(tt + 1) * T], in_=r32[:, c, :]),
                      reads=[rkey], writes=["xTb%d" % tt])
        if not final:
            sc.dma("sp", "ln_out", Rout[:, :, tt * T:(tt + 1) * T].rearrange("c p t -> p c t"), r32[:],
                   reads=[rkey], writes=[("R", id(Rout), tt)])
        else:
            for tb in range(4):
                g = tt * 4 + tb
                os_ = ost[g % 2]
                ok = "ln_ost%d" % (g % 2)
                for q in range(4):
                    bank = q % 2
                    for c4 in range(4):
                        dc = q * 4 + c4
                        sc.op("pe", lambda e, b=bank, c4=c4, dc=dc, tb=tb: e.transpose(
                            PS[b][:, c4 * 128:(c4 + 1) * 128], r32[:, dc, tb * 128:(tb + 1) * 128], ident_f[:]),
                            reads=[rkey, "ident_f"], writes=[psk(bank)], inc=(c4 == 3))
                    if q % 2 == 0:
                        sc.op("act", lambda e, b=bank, q=q, os_=os_: e.copy(out=os_[:, q * 512:(q + 1) * 512], in_=PS[b][:]),
                              reads=[psk(bank)], writes=[ok])
                    else:
                        sc.op("dve", lambda e, b=bank, q=q, os_=os_: e.tensor_copy(out=os_[:, q * 512:(q + 1) * 512], in_=PS[b][:]),
                              reads=[psk(bank)], writes=[ok])
                sc.dma("sp", ok, out_d[g * 128:(g + 1) * 128, :], os_[:], reads=[ok], writes=[("out", g)])

    def phase_ffn(l, Rin, Rout, final):
        A = Arena(PH_OFF)
        aT = A.take("aT", [128, NJ, T], BF16)
        r32 = A.take("r32", [128, DC, T], F32)
        NW = 3
        wg = A.take("wg", [128, DC, 128], BF16, NW)
        wv = A.take("wv", [128, DC, 128], BF16, NW)
        ND = 2
        wd = A.take("wd", [128, NJ, 128], BF16, ND)
        carry = A.take("carry", [128, 2, NJ, 2], F32, 2)
        lo = A.o
        gb = A.take("gb", [128, T], F32, 2)
        vb = A.take("vb", [128, T], F32, 2)
        wup = ffn_w_up[l].rearrange("(c p) f -> p c f", p=128)
        wdn = ffn_w_down[l].rearrange("(j p) d -> p j d", p=128)

        def cw(k, j):
            return convp[:, l, k, j:j + 1]

        up_jobs = [(tt, j) for tt in range(NT) for j in range(NJ)]
        dn_jobs = [(tt, dc) for tt in range(NT) for dc in range(DC)]
        up_issued = [0]
        dn_issued = [0]

        def issue_up(n):
            while up_issued[0] < min(n, len(up_jobs)):
                i = up_issued[0]
                _, j = up_jobs[i]
                ws = i % NW
                sc.dma("pool", "wg%d" % ws, wg[ws][:], wup[:, :, j * 128:(j + 1) * 128], writes=["wg%d" % ws])
                sc.dma("pool", "wv%d" % ws, wv[ws][:], wup[:, :, DFF + j * 128:DFF + (j + 1) * 128], writes=["wv%d" % ws])
                up_issued[0] += 1

        def issue_dn(n):
            while dn_issued[0] < min(n, len(dn_jobs)):
                i = dn_issued[0]
                _, dc = dn_jobs[i]
                ds = i % ND
                sc.dma("pool", "wd%d" % ds, wd[ds][:], wdn[:, :, dc * 128:(dc + 1) * 128], writes=["wd%d" % ds])
                dn_issued[0] += 1

        for tt in range(NT):
            tsl = slice(tt * T, (tt + 1) * T)
            xk = "xTb%d" % tt
            sc.dma("sp", "r32", r32[:], Rin[:, :, tsl].rearrange("c p t -> p c t"),
                   reads=[("R", id(Rin), tt)], writes=["r32"])
            cin = carry[(tt + 1) % 2]
            cout = carry[tt % 2]
            cink = "carry%d" % ((tt + 1) % 2)
            coutk = "carry%d" % (tt % 2)
            for j in range(NJ):
                ui = tt * NJ + j
                issue_up(ui + NW - 0 if ui == 0 else ui + NW)
                if j == 8:
                    issue_dn(tt * DC + 1)
                if j == 16:
                    issue_dn(tt * DC + 2)
                ws = ui % NW
                pg = (j % 2) * 2
                for half, (w_, wk, gv, cofs) in enumerate(((wg[ws], "wg%d" % ws, gb[j % 2], 0),
                                                           (wv[ws], "wv%d" % ws, vb[j % 2], NJ))):
                    bank = pg + half
                    P = PS[bank]
                    for c in range(DC):
                        rhs = xTb[:, c, tsl]
                        sc.op("pe", lambda e, P=P, w_=w_, c=c, rhs=rhs: e.matmul(P[:], lhsT=w_[:, c, :], rhs=rhs,
                                                                                  start=(c == 0), stop=(c == DC - 1)),
                              reads=[wk, xk], writes=[psk(bank)], inc=(c == DC - 1))
                    gk = ("gb%d" if half == 0 else "vb%d") % (j % 2)
                    jj = j + cofs
                    sc.op("act", lambda e, P=P, gv=gv, jj=jj: e.activation(out=gv[:], in_=P[:], func=AF.Identity,
                                                                           bias=cw(3, jj), scale=cw(2, jj)),
                          reads=[psk(bank), "consts"], writes=[gk])
                    sc.op("dve", lambda e, P=P, gv=gv, jj=jj: e.scalar_tensor_tensor(
                        out=gv[:, 1:T], in0=P[:, 0:T - 1], scalar=cw(1, jj), in1=gv[:, 1:T], op0=ALU.mult, op1=ALU.add),
                        reads=[psk(bank), gk, "consts"], writes=[gk])
                    sc.op("dve", lambda e, P=P, gv=gv, jj=jj: e.scalar_tensor_tensor(
                        out=gv[:, 2:T], in0=P[:, 0:T - 2], scalar=cw(0, jj), in1=gv[:, 2:T], op0=ALU.mult, op1=ALU.add),
                        reads=[psk(bank), gk, "consts"], writes=[gk])
                    if tt > 0:
                        sc.op("dve", lambda e, gv=gv, jj=jj, half=half, j=j, cin=cin: e.scalar_tensor_tensor(
                            out=gv[:, 0:1], in0=cin[:, half, j, 1:2], scalar=cw(1, jj), in1=gv[:, 0:1],
                            op0=ALU.mult, op1=ALU.add), reads=[cink, gk, "consts"], writes=[gk])
                        sc.op("dve", lambda e, gv=gv, jj=jj, half=half, j=j, cin=cin: e.scalar_tensor_tensor(
                            out=gv[:, 0:2], in0=cin[:, half, j, 0:2], scalar=cw(0, jj), in1=gv[:, 0:2],
                            op0=ALU.mult, op1=ALU.add), reads=[cink, gk, "consts"], writes=[gk])
                    if tt < NT - 1:
                        sc.op("act", lambda e, P=P, half=half, j=j, cout=cout: e.copy(out=cout[:, half, j, :], in_=P[:, T - 2:T]),
                              reads=[psk(bank)], writes=[coutk])
                g_ = gb[j % 2]
                v_ = vb[j % 2]
                sc.op("act", lambda e, g_=g_: e.activation(out=g_[:], in_=g_[:], func=AF.Silu),
                      reads=["gb%d" % (j % 2)], writes=["gb%d" % (j % 2)])
                sc.op("pool", lambda e, g_=g_, v_=v_, j=j: e.tensor_tensor(out=aT[:, j, :], in0=g_[:], in1=v_[:], op=ALU.mult),
                      reads=["gb%d" % (j % 2), "vb%d" % (j % 2)], writes=["aT"])
            for dc in range(DC):
                di = tt * DC + dc
                issue_dn(di + ND)
                ds = di % ND
                bank = 4 + dc % 2
                P = PS[bank]
                for j in range(NJ):
                    sc.op("pe", lambda e, P=P, j=j, ds=ds: e.matmul(P[:], lhsT=wd[ds][:, j, :], rhs=aT[:, j, :],
                                                                    start=(j == 0), stop=(j == NJ - 1)),
                          reads=["wd%d" % ds, "aT"], writes=[psk(bank)], inc=(j == NJ - 1))
                sc.op("dve", lambda e, P=P, dc=dc: e.scalar_tensor_tensor(
                    out=r32[:, dc, :], in0=r32[:, dc, :], scalar=ALPHA, in1=P[:], op0=ALU.mult, op1=ALU.add),
                    reads=[psk(bank), "r32"], writes=["r32"])
            layer_norm(r32, "r32", l * 4 + 2, l * 4 + 3, tt, lo, Rout=Rout, final=final, ost_lo=PH_OFF)

    class Prefetch:
        def __init__(self, n, nslots, issue):
            self.n, self.nslots, self.issue, self.issued = n, nslots, issue, 0

        def ensure(self, i):
            while self.issued < min(self.n, i + self.nslots):
                self.issue(self.issued, self.issued % self.nslots)
                self.issued += 1

    MIX_OFF = PH_OFF
    PH2_OFF = MIX_OFF + DC * S * 2

    _mix = []

    def get_mixT():
        if not _mix:
            _mix.append(nc.alloc_sbuf_tensor_at("mixT_all", [128, DC, S], BF16, offset=MIX_OFF))
        t = _mix[0]
        if "mixlo" not in sc.ranges:
            sc.reg("mixlo", MIX_OFF, MIX_OFF + 8 * S * 2)
            sc.reg("mixhi", MIX_OFF + 8 * S * 2, MIX_OFF + 16 * S * 2)
        return t

    def phase_outproj(l, w_out, Rin, Rout):
        mixT = get_mixT()
        A = Arena(PH2_OFF)
        r32 = A.take("r32", [128, DC, T], F32)
        NW = 3
        wo = A.take("wo", [128, DC, 128], BF16, NW)
        lo = A.o
        wsrc = w_out.rearrange("(c p) f -> p c f", p=128)
        jobs = [(tt, dc) for tt in range(NT) for dc in range(DC)]
        pf = Prefetch(len(jobs), NW, lambda i, s_: sc.dma(
            "pool", "wo%d" % s_, wo[s_][:], wsrc[:, :, jobs[i][1] * 128:(jobs[i][1] + 1) * 128], writes=["wo%d" % s_]))
        for tt in range(NT):
            tsl = slice(tt * T, (tt + 1) * T)
            sc.dma("sp", "r32", r32[:], Rin[:, :, tsl].rearrange("c p t -> p c t"),
                   reads=[("R", id(Rin), tt)], writes=["r32"])
            for dc in range(DC):
                i = tt * DC + dc
                pf.ensure(i)
                ws = i % NW
                bank = 4 + dc % 2
                P = PS[bank]
                for c in range(DC):
                    rhs = mixT[:, c, tsl]
                    sc.op("pe", lambda e, P=P, ws=ws, c=c, rhs=rhs: e.matmul(P[:], lhsT=wo[ws][:, c, :], rhs=rhs,
                                                                          start=(c == 0), stop=(c == DC - 1)),
                          reads=["wo%d" % ws, "mixlo" if c < 8 else "mixhi"], writes=[psk(bank)], inc=(c == DC - 1))
                sc.op("dve", lambda e, P=P, dc=dc: e.scalar_tensor_tensor(
                    out=r32[:, dc, :], in0=r32[:, dc, :], scalar=ALPHA, in1=P[:], op0=ALU.mult, op1=ALU.add),
                    reads=[psk(bank), "r32"], writes=["r32"])
            layer_norm(r32, "r32", l * 4 + 0, l * 4 + 1, tt, lo, Rout=Rout)

    def phase_fox():
        mixT = get_mixT()
        A = Arena(PH2_OFF)
        qT = A.take("qT", [128, S], BF16, 2)
        kT = A.take("kT", [128, S], BF16, 2)
        Vt = A.take("Vt", [128, 16, 128], BF16, 2)
        wq = A.take("wq", [128, DC, 128], BF16, 2)
        wk = A.take("wk", [128, DC, 128], BF16, 2)
        wv = A.take("wv", [128, DC, 128], BF16, 2)
        PT = A.take("PT", [128, T], BF16, 2)
        negc = A.take("negc", [128, S], F32)
        cneg = A.take("cneg", [128, S], F32)
        negcT = A.take("negcT", [128, 16, 8], F32)
        recip = A.take("recip", [128, T], F32)
        sel = A.take("sel", [128, 8, 128], F32)
        maskb = A.take("maskb", [128, 128], BF16)
        wf = A.take("wf", [128, DC, 8], BF16)
        bfc = A.take("bfc", [128, 2], F32)
        ones_row = sc.sb("ones_row", [128, S], BF16, sc.ranges["qT0"][0])
        win = ev_w_in.rearrange("(c p) f -> p c f", p=128)
        SC = 1.0 / math.sqrt(128.0)

        sc.op("pool", lambda e: e.memset(sel[:], 1.0), writes=["sel"])
        sc.op("pool", lambda e: e.affine_select(out=sel[:], in_=sel[:], pattern=[[-1, 8], [0, 128]],
                                                 compare_op=ALU.is_equal, fill=0.0, base=0, channel_multiplier=1),
              reads=["sel"], writes=["sel"])
        sc.op("pool", lambda e: e.memset(maskb[:], 0.0), writes=["maskb"])
        sc.op("pool", lambda e: e.affine_select(out=maskb[:], in_=maskb[:], pattern=[[1, 128]],
                                                 compare_op=ALU.is_ge, fill=-30000.0, base=0, channel_multiplier=-1),
              reads=["maskb"], writes=["maskb"])
        sc.op("pool", lambda e: e.memset(cneg[:], 0.0), writes=["cneg"])
        sc.op("pool", lambda e: e.memset(ones_row[0:8, :], 1.0), writes=["ones_row"])
        sc.dma("pool", "wf", wf[:], win[:, :, 3072:3080], writes=["wf"])
        sc.dma("sp", "bfc", bfc[0:8, 0:1], ev_b_f, writes=["bfc"])
        sc.op("dve", lambda e: e.tensor_scalar(out=bfc[0:8, 1:2], in0=bfc[0:8, 0:1], scalar1=-1.0, scalar2=None, op0=ALU.mult),
              reads=["bfc"], writes=["bfc"])
        for tt in range(NT):
            tsl = slice(tt * T, (tt + 1) * T)
            for c in range(DC):
                rhs = xTb[:, c, tsl]
                sc.op("pe", lambda e, c=c, rhs=rhs: e.matmul(PS[6][0:8, :], lhsT=wf[:, c, :], rhs=rhs,
                                                              start=(c == 0), stop=(c == DC - 1)),
                      reads=["wf", "xTb%d" % tt], writes=[psk(6)], inc=(c == DC - 1))
            sc.op("act", lambda e, tsl=tsl: e.activation(out=negc[0:8, tsl], in_=PS[6][0:8, :], func=AF.Exp,
                                                          bias=bfc[0:8, 1:2], scale=-1.0),
                  reads=[psk(6), "bfc"], writes=["negc"])
            sc.op("act", lambda e, tsl=tsl: e.activation(out=negc[0:8, tsl], in_=negc[0:8, tsl], func=AF.Ln, bias=1.0, scale=1.0),
                  reads=["negc"], writes=["negc"])
        sc.op("dve", lambda e: e.tensor_tensor_scan(out=negc[0:8, :], data0=ones_row[0:8, :], data1=negc[0:8, :],
                                                    initial=0.0, op0=ALU.mult, op1=ALU.add),
              reads=["negc", "ones_row"], writes=["negc"])
        sc.op("act", lambda e: e.mul(out=cneg[0:8, :], in_=negc[0:8, :], mul=-1.0), reads=["negc"], writes=["cneg"])
        for tb in range(16):
            sc.op("pe", lambda e, tb=tb: e.transpose(PS[7][:, tb * 8:(tb + 1) * 8], negc[0:8, tb * 128:(tb + 1) * 128],
                                                     ident_f[0:8, 0:8]),
                  reads=["negc", "ident_f"], writes=[psk(7)], inc=(tb == 15))
        sc.op("dve", lambda e: e.tensor_copy(out=negcT[:].rearrange("p a b -> p (a b)"), in_=PS[7][:, 0:128]),
              reads=[psk(7)], writes=["negcT"])

        def load_head(h):
            s_ = h % 2
            sc.dma("pool", "wq%d" % s_, wq[s_][:], win[:, :, h * 128:(h + 1) * 128], writes=["wq%d" % s_])
            sc.dma("pool", "wk%d" % s_, wk[s_][:], win[:, :, 1024 + h * 128:1024 + (h + 1) * 128], writes=["wk%d" % s_])
            sc.dma("pool", "wv%d" % s_, wv[s_][:], win[:, :, 2048 + h * 128:2048 + (h + 1) * 128], writes=["wv%d" % s_])

        load_head(0)
        pcnt = [0]
        for h in range(8):
            s_ = h % 2
            if h + 1 < 8:
                load_head(h + 1)
            qk, kk, vk = "qT%d" % s_, "kT%d" % s_, "Vt%d" % s_
            for tt in range(NT):
                tsl = slice(tt * T, (tt + 1) * T)
                for which, (w_, wkey) in enumerate(((wq[s_], "wq%d" % s_), (wk[s_], "wk%d" % s_))):
                    bank = which
                    for c in range(DC):
                        rhs = xTb[:, c, tsl]
                        sc.op("pe", lambda e, bank=bank, w_=w_, c=c, rhs=rhs: e.matmul(
                            PS[bank][:], lhsT=w_[:, c, :], rhs=rhs, start=(c == 0), stop=(c == DC - 1)),
                            reads=[wkey, "xTb%d" % tt], writes=[psk(bank)], inc=(c == DC - 1))
                    if which == 0:
                        sc.op("act", lambda e, tsl=tsl, s_=s_: e.activation(out=qT[s_][:, tsl], in_=PS[0][:], func=AF.Copy, scale=SC),
                              reads=[psk(0)], writes=[qk])
                    else:
                        sc.op("dve", lambda e, tsl=tsl, s_=s_: e.tensor_copy(out=kT[s_][:, tsl], in_=PS[1][:]),
                              reads=[psk(1)], writes=[kk])
            for t4 in range(4):
                bank = t4 % 2
                for q4 in range(4):
                    tb = t4 * 4 + q4
                    for c in range(DC):
                        lhsT = xTb[:, c, tb * 128:(tb + 1) * 128]
                        sc.op("pe", lambda e, bank=bank, q4=q4, c=c, lhsT=lhsT, s_=s_: e.matmul(
                            PS[bank][:, q4 * 128:(q4 + 1) * 128], lhsT=lhsT, rhs=wv[s_][:, c, :],
                            start=(c == 0), stop=(c == DC - 1)),
                            reads=["wv%d" % s_, "xTb%d" % (tb // 4)], writes=[psk(bank)], inc=(c == DC - 1))
                pv = PS[bank][:].rearrange("p (a b) -> p a b", a=4)
                if t4 % 2 == 0:
                    sc.op("act", lambda e, pv=pv, t4=t4, s_=s_: e.copy(out=Vt[s_][:, t4 * 4:(t4 + 1) * 4, :], in_=pv),
                          reads=[psk(bank)], writes=[vk])
                else:
                    sc.op("dve", lambda e, pv=pv, t4=t4, s_=s_: e.tensor_copy(out=Vt[s_][:, t4 * 4:(t4 + 1) * 4, :], in_=pv),
                          reads=[psk(bank)], writes=[vk])
            for Qi in range(4):
                nblk = 4 * Qi + 4
                for j in range(nblk):
                    n0 = max(0, j * 128 - Qi * T)
                    diag = j * 128 >= Qi * T
                    q0 = Qi * T + n0
                    q1 = (Qi + 1) * T
                    bank = 2 + pcnt[0] % 2
                    ps_ = pcnt[0] % 2
                    pcnt[0] += 1
                    P = PS[bank]
                    sc.op("pe", lambda e, P=P, n0=n0, j=j, q0=q0, q1=q1, s_=s_: e.matmul(
                        P[:, n0:T], lhsT=kT[s_][:, j * 128:(j + 1) * 128], rhs=qT[s_][:, q0:q1], start=True, stop=False),
                        reads=[kk, qk], writes=[psk(bank)], inc=False)
                    sc.op("pe", lambda e, P=P, n0=n0, q0=q0, q1=q1, h=h, diag=diag: e.matmul(
                        P[:, n0:T], lhsT=sel[:, h, :], rhs=cneg[:, q0:q1], start=False, stop=(not diag)),
                        reads=["sel", "cneg"], writes=[psk(bank)], inc=(not diag))
                    if diag:
                        sc.op("pe", lambda e, P=P, n0=n0: e.matmul(
                            P[:, n0:n0 + 128], lhsT=ident_b[:], rhs=maskb[:], start=False, stop=True),
                            reads=["ident_b", "maskb"], writes=[psk(bank)], inc=True)
                    ptk = "PT%d" % ps_
                    sc.op("act", lambda e, P=P, n0=n0, j=j, h=h, ps_=ps_: e.activation(
                        out=PT[ps_][:, n0:T], in_=P[:, n0:T], func=AF.Exp, bias=negcT[:, j, h:h + 1], scale=1.0),
                        reads=[psk(bank), "negcT"], writes=[ptk])
                    last = (j == nblk - 1)
                    sc.op("pe", lambda e, n0=n0, j=j, ps_=ps_, s_=s_, last=last: e.matmul(
                        PS[4][:, n0:T], lhsT=Vt[s_][:, j, :], rhs=PT[ps_][:, n0:T], start=(j == 0), stop=last),
                        reads=[vk, ptk], writes=[psk(4)], inc=False)
                    sc.op("pe", lambda e, n0=n0, j=j, ps_=ps_, last=last: e.matmul(
                        PS[5][:, n0:T], lhsT=ones_b[:], rhs=PT[ps_][:, n0:T], start=(j == 0), stop=last),
                        reads=["ones_b", ptk], writes=[psk(5)], inc=True)
                sc.op("dve", lambda e: e.reciprocal(out=recip[:], in_=PS[5][:]), reads=[psk(5)], writes=["recip"])
                sc.op("dve", lambda e, h=h, Qi=Qi: e.tensor_tensor(out=mixT[:, h, Qi * T:(Qi + 1) * T], in0=PS[4][:], in1=recip[:],
                                                                  op=ALU.mult),
                      reads=[psk(4), "recip"], writes=["mixlo"])
        wu = wq
        pf = Prefetch(8, 2, lambda i, s_: sc.dma("pool", "wq%d" % s_, wu[s_][:], win[:, :, 3080 + i * 128:3080 + (i + 1) * 128],
                                                 writes=["wq%d" % s_]))
        for ch in range(8):
            pf.ensure(ch)
            s_ = ch % 2
            for tt in range(NT):
                tsl = slice(tt * T, (tt + 1) * T)
                bank = tt % 2
                for c in range(DC):
                    rhs = xTb[:, c, tsl]
                    sc.op("pe", lambda e, bank=bank, c=c, rhs=rhs, s_=s_: e.matmul(
                        PS[bank][:], lhsT=wu[s_][:, c, :], rhs=rhs, start=(c == 0), stop=(c == DC - 1)),
                        reads=["wq%d" % s_, "xTb%d" % tt], writes=[psk(bank)], inc=(c == DC - 1))
                if tt % 2 == 0:
                    sc.op("act", lambda e, bank=bank, ch=ch, tsl=tsl: e.copy(out=mixT[:, 8 + ch, tsl], in_=PS[bank][:]),
                          reads=[psk(bank)], writes=["mixhi"])
                else:
                    sc.op("dve", lambda e, bank=bank, ch=ch, tsl=tsl: e.tensor_copy(out=mixT[:, 8 + ch, tsl], in_=PS[bank][:]),
                          reads=[psk(bank)], writes=["mixhi"])

    Wd = nc.dram_tensor("s5w", [8, 128, 16, 128], BF16, kind="Internal").ap()
    TWO_PI = 2.0 * math.pi

    def bk_level(k, l, first, xr_, xi_, xrk, xik, i, AQ_re, AQ_im, AQ_in):
        if first >= S:
            return
        src = slice(first - k, S - k, 2 * k)
        dst = slice(first, S, 2 * k)
        ar = AQ_re[:, i, l:l + 1]
        ai = AQ_im[:, i, l:l + 1]
        an = AQ_in[:, i, l:l + 1]
        sc.op("dve", lambda e: e.scalar_tensor_tensor(out=xr_[:, dst], in0=xr_[:, src], scalar=ar, in1=xr_[:, dst],
                                                      op0=ALU.mult, op1=ALU.add),
              reads=[xrk, "AQ_re"], writes=[xrk])
        sc.op("dve", lambda e: e.scalar_tensor_tensor(out=xr_[:, dst], in0=xi_[:, src], scalar=an, in1=xr_[:, dst],
                                                      op0=ALU.mult, op1=ALU.add),
              reads=[xrk, xik, "AQ_in"], writes=[xrk])
        sc.op("dve", lambda e: e.scalar_tensor_tensor(out=xi_[:, dst], in0=xi_[:, src], scalar=ar, in1=xi_[:, dst],
                                                      op0=ALU.mult, op1=ALU.add),
              reads=[xik, "AQ_re"], writes=[xik])
        sc.op("dve", lambda e: e.scalar_tensor_tensor(out=xi_[:, dst], in0=xr_[:, src], scalar=ai, in1=xi_[:, dst],
                                                      op0=ALU.mult, op1=ALU.add),
              reads=[xik, xrk, "AQ_im"], writes=[xik])

    def phase_s5():
        mixT = get_mixT()
        AP_ = Arena(PH2_OFF)
        yg = AP_.take("yg", [128, 8, S], BF16)
        wslot = AP_.take("wslot", [128, 16, 128], BF16, 2)
        AQ_re = AP_.take("AQ_re", [128, 32, 11], F32)
        AQ_im = AP_.take("AQ_im", [128, 32, 11], F32)
        AQ_in = AP_.take("AQ_in", [128, 32, 11], F32)
        Dcol = AP_.take("Dcol", [128, 8], F32)
        y32 = AP_.take("y32", [128, T], F32, 2)
        gt = AP_.take("gt", [128, T], F32, 2)
        wz1 = AP_.take("wz1", [128, 8, 128], BF16, 2)
        wz2 = AP_.take("wz2", [128, 8, 128], BF16, 2)
        sig = AP_.take("sig", [128, T], F32, 2)
        AX = Arena(XTB_OFF)
        lamraw = AX.take("lamraw", [128, 2, 128], F32)
        names = ["lr", "li", "dt", "lrd", "lid", "mag", "kf", "rr", "rc", "m1", "sn", "cs", "are", "aim",
                 "den", "xr", "gre", "gim", "t1", "t2"]
        sm = {n: AX.take("s5_" + n, [128, 64], F32) for n in names}
        ki = AX.take("s5_ki", [128, 64], I32)
        P_re = AX.take("P_re", [128, 64, 11], F32)
        P_im = AX.take("P_im", [128, 64, 11], F32)
        b_re = AX.take("b_re", [128, 64, 16], F32)
        b_im = AX.take("b_im", [128, 64, 16], F32)
        bb_re = AX.take("bb_re", [128, 64, 16], F32)
        bb_im = AX.take("bb_im", [128, 64, 16], F32)
        btmp = AX.take("btmp", [128, 64, 16], F32)
        CT_re = AX.take("CT_re", [128, 8, 128], F32)
        CT_im = AX.take("CT_im", [128, 8, 128], F32)
        MC = AX.take("MC", [128, 4, 128], F32)
        MB = AX.take("MB", [128, 4, 128], F32)
        dstg = AX.take("dstg", [128, 128], F32)
        stage = AX.take("wstage", [128, 16, 128], BF16, 2)

        def dve(fn, reads, writes):
            sc.op("dve", fn, reads=reads, writes=writes)

        def tt_(out, a, b, op, reads, writes):
            dve(lambda e: e.tensor_tensor(out=out, in0=a, in1=b, op=op), reads, writes)

        K = lambda n: "s5_" + n
        for half in range(2):
            sc.dma("sp", "s5ld", lamraw[0:64, 0, half * 64:(half + 1) * 64], ev_lre, writes=["lamraw"])
            sc.dma("sp", "s5ld", lamraw[0:64, 1, half * 64:(half + 1) * 64], ev_lim, writes=["lamraw"])
            sc.dma("sp", "s5ld", b_re[half * 64:(half + 1) * 64], ev_bre.rearrange("g p c -> p g c"), writes=["b_re"])
            sc.dma("sp", "s5ld", b_im[half * 64:(half + 1) * 64], ev_bim.rearrange("g p c -> p g c"), writes=["b_im"])
            sc.dma("sp", "s5ld", CT_re[:, :, half * 64:(half + 1) * 64], ev_cre.rearrange("(j a) c p -> (a c) j p", a=8),
                   writes=["CT_re"])
            sc.dma("sp", "s5ld", CT_im[:, :, half * 64:(half + 1) * 64], ev_cim.rearrange("(j a) c p -> (a c) j p", a=8),
                   writes=["CT_im"])
        sc.dma("sp", "s5ld", sm["dt"][:], ev_lstep.broadcast_to([128, 64]), writes=[K("dt")])
        sc.dma("sp", "s5ld", dstg[0:8, :], ev_d, writes=["dstg"])
        for w, n in ((0, "lr"), (1, "li")):
            sc.op("pe", lambda e, w=w: e.transpose(PS[0][:, 0:64], lamraw[0:64, w, :], ident_f[0:64, 0:64]),
                  reads=["lamraw", "ident_f"], writes=[psk(0)])
            dve(lambda e, n=n: e.tensor_copy(out=sm[n][:], in_=PS[0][:, 0:64]), [psk(0)], [K(n)])
        sc.op("pe", lambda e: e.transpose(PS[0][:, 0:8], dstg[0:8, :], ident_f[0:8, 0:8]),
              reads=["dstg", "ident_f"], writes=[psk(0)])
        dve(lambda e: e.tensor_copy(out=Dcol[:], in_=PS[0][:, 0:8]), [psk(0)], ["Dcol"])
        sc.op("act", lambda e: e.activation(out=sm["dt"][:], in_=sm["dt"][:], func=AF.Exp), reads=[K("dt")], writes=[K("dt")])
        tt_(sm["lrd"][:], sm["lr"][:], sm["dt"][:], ALU.mult, [K("lr"), K("dt")], [K("lrd")])
        tt_(sm["lid"][:], sm["li"][:], sm["dt"][:], ALU.mult, [K("li"), K("dt")], [K("lid")])
        sc.op("act", lambda e: e.activation(out=sm["mag"][:], in_=sm["lrd"][:], func=AF.Exp), reads=[K("lrd")], writes=[K("mag")])
        dve(lambda e: e.tensor_scalar(out=sm["kf"][:], in0=sm["lid"][:], scalar1=1.0 / TWO_PI, scalar2=0.5,
                                      op0=ALU.mult, op1=ALU.add), [K("lid")], [K("kf")])
        dve(lambda e: e.tensor_copy(out=ki[:], in_=sm["kf"][:]), [K("kf")], ["s5_ki"])
        dve(lambda e: e.tensor_copy(out=sm["kf"][:], in_=ki[:]), ["s5_ki"], [K("kf")])
        dve(lambda e: e.scalar_tensor_tensor(out=sm["rr"][:], in0=sm["kf"][:], scalar=-TWO_PI, in1=sm["lid"][:],
                                             op0=ALU.mult, op1=ALU.add), [K("kf"), K("lid")], [K("rr")])

        def wrap(n):
            dve(lambda e: e.tensor_scalar(out=sm["m1"][:], in0=sm[n][:], scalar1=math.pi, scalar2=None, op0=ALU.is_gt),
                [K(n)], [K("m1")])
            dve(lambda e: e.scalar_tensor_tensor(out=sm[n][:], in0=sm["m1"][:], scalar=-TWO_PI, in1=sm[n][:],
                                                 op0=ALU.mult, op1=ALU.add), [K("m1"), K(n)], [K(n)])
            dve(lambda e: e.tensor_scalar(out=sm["m1"][:], in0=sm[n][:], scalar1=-math.pi, scalar2=None, op0=ALU.is_lt),
                [K(n)], [K("m1")])
            dve(lambda e: e.scalar_tensor_tensor(out=sm[n][:], in0=sm["m1"][:], scalar=TWO_PI, in1=sm[n][:],
                                                 op0=ALU.mult, op1=ALU.add), [K("m1"), K(n)], [K(n)])

        wrap("rr")
        dve(lambda e: e.tensor_scalar(out=sm["rc"][:], in0=sm["rr"][:], scalar1=0.5 * math.pi, scalar2=None, op0=ALU.add),
            [K("rr")], [K("rc")])
        wrap("rc")
        sc.op("act", lambda e: e.activation(out=sm["sn"][:], in_=sm["rr"][:], func=AF.Sin), reads=[K("rr")], writes=[K("sn")])
        sc.op("act", lambda e: e.activation(out=sm["cs"][:], in_=sm["rc"][:], func=AF.Sin), reads=[K("rc")], writes=[K("cs")])
        tt_(sm["are"][:], sm["mag"][:], sm["cs"][:], ALU.mult, [K("mag"), K("cs")], [K("are")])
        tt_(sm["aim"][:], sm["mag"][:], sm["sn"][:], ALU.mult, [K("mag"), K("sn")], [K("aim")])
        tt_(sm["t1"][:], sm["lr"][:], sm["lr"][:], ALU.mult, [K("lr")], [K("t1")])
        tt_(sm["den"][:], sm["li"][:], sm["li"][:], ALU.mult, [K("li")], [K("den")])
        tt_(sm["den"][:], sm["den"][:], sm["t1"][:], ALU.add, [K("den"), K("t1")], [K("den")])
        dve(lambda e: e.reciprocal(out=sm["den"][:], in_=sm["den"][:]), [K("den")], [K("den")])
        dve(lambda e: e.tensor_scalar(out=sm["xr"][:], in0=sm["are"][:], scalar1=-1.0, scalar2=None, op0=ALU.add),
            [K("are")], [K("xr")])
        tt_(sm["t1"][:], sm["xr"][:], sm["lr"][:], ALU.mult, [K("xr"), K("lr")], [K("t1")])
        tt_(sm["t2"][:], sm["aim"][:], sm["li"][:], ALU.mult, [K("aim"), K("li")], [K("t2")])
        tt_(sm["t1"][:], sm["t1"][:], sm["t2"][:], ALU.add, [K("t1"), K("t2")], [K("t1")])
        tt_(sm["gre"][:], sm["t1"][:], sm["den"][:], ALU.mult, [K("t1"), K("den")], [K("gre")])
        tt_(sm["t1"][:], sm["aim"][:], sm["lr"][:], ALU.mult, [K("aim"), K("lr")], [K("t1")])
        tt_(sm["t2"][:], sm["xr"][:], sm["li"][:], ALU.mult, [K("xr"), K("li")], [K("t2")])
        tt_(sm["t1"][:], sm["t1"][:], sm["t2"][:], ALU.subtract, [K("t1"), K("t2")], [K("t1")])
        tt_(sm["gim"][:], sm["t1"][:], sm["den"][:], ALU.mult, [K("t1"), K("den")], [K("gim")])
        gre_b = sm["gre"][:].unsqueeze(2).broadcast_to([128, 64, 16])
        gim_b = sm["gim"][:].unsqueeze(2).broadcast_to([128, 64, 16])
        tt_(bb_re[:], b_re[:], gre_b, ALU.mult, ["b_re", K("gre")], ["bb_re"])
        tt_(btmp[:], b_im[:], gim_b, ALU.mult, ["b_im", K("gim")], ["btmp"])
        tt_(bb_re[:], bb_re[:], btmp[:], ALU.subtract, ["bb_re", "btmp"], ["bb_re"])
        tt_(bb_im[:], b_im[:], gre_b, ALU.mult, ["b_im", K("gre")], ["bb_im"])
        tt_(btmp[:], b_re[:], gim_b, ALU.mult, ["b_re", K("gim")], ["btmp"])
        tt_(bb_im[:], bb_im[:], btmp[:], ALU.add, ["bb_im", "btmp"], ["bb_im"])
        dve(lambda e: e.tensor_copy(out=P_re[:, :, 0], in_=sm["are"][:]), [K("are")], ["P_re"])
        dve(lambda e: e.tensor_copy(out=P_im[:, :, 0], in_=sm["aim"][:]), [K("aim")], ["P_im"])
        for l in range(1, 11):
            tt_(sm["t1"][:], P_re[:, :, l - 1], P_re[:, :, l - 1], ALU.mult, ["P_re"], [K("t1")])
            tt_(sm["t2"][:], P_im[:, :, l - 1], P_im[:, :, l - 1], ALU.mult, ["P_im"], [K("t2")])
            tt_(P_re[:, :, l], sm["t1"][:], sm["t2"][:], ALU.subtract, [K("t1"), K("t2")], ["P_re"])
            tt_(sm["t1"][:], P_re[:, :, l - 1], P_im[:, :, l - 1], ALU.mult, ["P_re", "P_im"], [K("t1")])
            dve(lambda e, l=l: e.tensor_scalar(out=P_im[:, :, l], in0=sm["t1"][:], scalar1=2.0, scalar2=None, op0=ALU.mult),
                [K("t1")], ["P_im"])
        for (src, dst, dk) in ((P_re, AQ_re, "AQ_re"), (P_im, AQ_im, "AQ_im")):
            sv = src[:].rearrange("p (i two) l -> p i two l", two=2)
            sk = "P_re" if src is P_re else "P_im"
            dve(lambda e, sv=sv, dst=dst: e.tensor_copy(out=dst[0:64], in_=sv[0:64, :, 0, :]), [sk], [dk])
            dve(lambda e, sv=sv, dst=dst: e.tensor_copy(out=dst[64:128], in_=sv[64:128, :, 1, :]), [sk], [dk])
        dve(lambda e: e.tensor_scalar(out=AQ_in[:], in0=AQ_im[:], scalar1=-1.0, scalar2=None, op0=ALU.mult), ["AQ_im"], ["AQ_in"])
        sc.op("pool", lambda e: e.memset(MC[:], 0.0), writes=["MC"])
        for m in range(4):
            sc.op("pool", lambda e, m=m: e.memset(MC[0:64, m, 32 * m:32 * m + 16], 1.0), reads=["MC"], writes=["MC"])
            sc.op("pool", lambda e, m=m: e.memset(MC[64:128, m, 32 * m + 16:32 * m + 32], 1.0), reads=["MC"], writes=["MC"])
        for m in range(4):
            sc.op("pe", lambda e, m=m: e.transpose(PS[1][:, m * 128:(m + 1) * 128], MC[:, m, :], ident_f[:]),
                  reads=["MC", "ident_f"], writes=[psk(1)], inc=(m == 3))
        dve(lambda e: e.tensor_copy(out=MB[:].rearrange("p a b -> p (a b)"), in_=PS[1][:]), [psk(1)], ["MB"])
        for j in range(8):
            st_ = stage[j % 2]
            stk = "wstage%d" % (j % 2)
            srcs = ((bb_re[:, 8 * j:8 * j + 8, :].rearrange("p a b -> p (a b)"), "bb_re", 2, 0),
                    (bb_im[:, 8 * j:8 * j + 8, :].rearrange("p a b -> p (a b)"), "bb_im", 2, 1),
                    (CT_re[:, j, :], "CT_re", 3, 0), (CT_im[:, j, :], "CT_im", 3, 1))
            for (src, sk, bank, half) in srcs:
                sc.op("pe", lambda e, src=src, bank=bank, half=half: e.transpose(
                    PS[bank][:, half * 128:(half + 1) * 128], src, ident_f[:]),
                    reads=[sk, "ident_f"], writes=[psk(bank)], inc=True)
            for m in range(4):
                dve(lambda e, st_=st_, m=m: e.tensor_tensor(out=st_[:, 0 + m, :], in0=PS[2][:, 0:128], in1=MB[:, m, :], op=ALU.mult),
                    [psk(2), "MB"], [stk])
                dve(lambda e, st_=st_, m=m: e.tensor_tensor(out=st_[:, 4 + m, :], in0=PS[2][:, 128:256], in1=MB[:, m, :], op=ALU.mult),
                    [psk(2), "MB"], [stk])
                dve(lambda e, st_=st_, m=m: e.tensor_tensor(out=st_[:, 8 + m, :], in0=PS[3][:, 0:128], in1=MC[:, m, :], op=ALU.mult),
                    [psk(3), "MC"], [stk])
                dve(lambda e, st_=st_, m=m: e.scalar_tensor_tensor(out=st_[:, 12 + m, :], in0=PS[3][:, 128:256], scalar=-1.0,
                                                                   in1=MC[:, m, :], op0=ALU.mult, op1=ALU.mult),
                    [psk(3), "MC"], [stk])
            sc.dma("sp", stk, Wd[j], st_[:], reads=[stk], writes=[("Wd", j)])
        mixT2 = mixT
        AX2 = Arena(XTB_OFF)
        X_re = AX2.take("X_re", [128, S], F32, 2)
        X_im = AX2.take("X_im", [128, S], F32, 2)
        H_re = AX2.take("H_re", [128, S], BF16, 4)
        H_im = AX2.take("H_im", [128, S], BF16, 4)
        C1 = 2.0 * math.sqrt(2.0 / math.pi)
        C2 = C1 * 0.044715
        bcnt = [0]
        for j in range(8):
            wsl = wslot[j % 2]
            wk_ = "wslot%d" % (j % 2)
            sc.dma("sp", wk_, wsl[:], Wd[j], reads=[("Wd", j)], writes=[wk_])
            for m in range(4):
                i = 4 * j + m
                xs = i % 2
                xr_, xi_ = X_re[xs], X_im[xs]
                xrk, xik = "X_re%d" % xs, "X_im%d" % xs
                for tt in range(NT):
                    tsl = slice(tt * T, (tt + 1) * T)
                    for half, (dst, dk) in enumerate(((xr_, xrk), (xi_, xik))):
                        bank = bcnt[0] % 4
                        bcnt[0] += 1
                        sc.op("pe", lambda e, bank=bank, half=half, m=m, wsl=wsl, j=j, tsl=tsl: e.matmul(
                            PS[bank][:], lhsT=wsl[:, 4 * half + m, :], rhs=mixT2[:, 8 + j, tsl], start=True, stop=True),
                            reads=[wk_, "mixhi"], writes=[psk(bank)], inc=True)
                        sc.op("act", lambda e, bank=bank, dst=dst, tsl=tsl: e.copy(out=dst[:, tsl], in_=PS[bank][:]),
                              reads=[psk(bank)], writes=[dk])
                for l in range(11):
                    bk_level(1 << l, l, 2 * (1 << l) - 1, xr_, xi_, xrk, xik, i, AQ_re, AQ_im, AQ_in)
                for l in range(9, -1, -1):
                    bk_level(1 << l, l, 3 * (1 << l) - 1, xr_, xi_, xrk, xik, i, AQ_re, AQ_im, AQ_in)
                sc.op("act", lambda e, m=m, xr_=xr_: e.copy(out=H_re[m][:], in_=xr_[:]), reads=[xrk], writes=["H_re%d" % m])
                sc.op("act", lambda e, m=m, xi_=xi_: e.copy(out=H_im[m][:], in_=xi_[:]), reads=[xik], writes=["H_im%d" % m])
            for tt in range(NT):
                tsl = slice(tt * T, (tt + 1) * T)
                bank = 4 + tt % 2
                for m in range(4):
                    sc.op("pe", lambda e, bank=bank, m=m, wsl=wsl, tsl=tsl: e.matmul(
                        PS[bank][:], lhsT=wsl[:, 8 + m, :], rhs=H_re[m][:, tsl], start=(m == 0), stop=False),
                        reads=[wk_, "H_re%d" % m], writes=[psk(bank)], inc=False)
                    sc.op("pe", lambda e, bank=bank, m=m, wsl=wsl, tsl=tsl: e.matmul(
                        PS[bank][:], lhsT=wsl[:, 12 + m, :], rhs=H_im[m][:, tsl], start=False, stop=(m == 3)),
                        reads=[wk_, "H_im%d" % m], writes=[psk(bank)], inc=(m == 3))
                y_ = y32[tt % 2]
                g_ = gt[tt % 2]
                yk, gk = "y32%d" % (tt % 2), "gt%d" % (tt % 2)
                sc.op("dve", lambda e, bank=bank, y_=y_, j=j, tsl=tsl: e.scalar_tensor_tensor(
                    out=y_[:], in0=mixT2[:, 8 + j, tsl], scalar=Dcol[:, j:j + 1], in1=PS[bank][:], op0=ALU.mult, op1=ALU.add),
                    reads=[psk(bank), "mixhi", "Dcol"], writes=[yk])
                sc.op("pool", lambda e, y_=y_, g_=g_: e.tensor_tensor(out=g_[:], in0=y_[:], in1=y_[:], op=ALU.mult),
                      reads=[yk], writes=[gk])
                sc.op("pool", lambda e, g_=g_: e.tensor_scalar(out=g_[:], in0=g_[:], scalar1=C2, scalar2=C1, op0=ALU.mult, op1=ALU.add),
                      reads=[gk], writes=[gk])
                sc.op("pool", lambda e, y_=y_, g_=g_: e.tensor_tensor(out=g_[:], in0=g_[:], in1=y_[:], op=ALU.mult),
                      reads=[gk, yk], writes=[gk])
                sc.op("act", lambda e, g_=g_: e.activation(out=g_[:], in_=g_[:], func=AF.Sigmoid), reads=[gk], writes=[gk])
                sc.op("pool", lambda e, y_=y_, g_=g_, j=j, tsl=tsl: e.tensor_tensor(out=yg[:, j, tsl], in0=g_[:], in1=y_[:], op=ALU.mult),
                      reads=[gk, yk], writes=["yg"])
        wgl = ev_w_glu.rearrange("(c p) f -> p c f", p=128)

        def issue_glu(i, s_):
            sc.dma("pool", "wz1%d" % s_, wz1[s_][:], wgl[:, :, i * 128:(i + 1) * 128], writes=["wz1%d" % s_])
            sc.dma("pool", "wz2%d" % s_, wz2[s_][:], wgl[:, :, 1024 + i * 128:1024 + (i + 1) * 128], writes=["wz2%d" % s_])

        pf = Prefetch(8, 2, issue_glu)
        for e_ in range(8):
            pf.ensure(e_)
            s_ = e_ % 2
            for tt in range(NT):
                tsl = slice(tt * T, (tt + 1) * T)
                ba, bb_ = (tt % 2) * 2, (tt % 2) * 2 + 1
                for (bank, w_, wkey) in ((ba, wz1[s_], "wz1%d" % s_), (bb_, wz2[s_], "wz2%d" % s_)):
                    for c in range(8):
                        sc.op("pe", lambda e, bank=bank, w_=w_, c=c, tsl=tsl: e.matmul(
                            PS[bank][:], lhsT=w_[:, c, :], rhs=yg[:, c, tsl], start=(c == 0), stop=(c == 7)),
                            reads=[wkey, "yg"], writes=[psk(bank)], inc=(c == 7))
                sg_ = sig[tt % 2]
                sgk = "sig%d" % (tt % 2)
                sc.op("act", lambda e, bb_=bb_, sg_=sg_: e.activation(out=sg_[:], in_=PS[bb_][:], func=AF.Sigmoid),
                      reads=[psk(bb_)], writes=[sgk])
                sc.op("dve", lambda e, ba=ba, sg_=sg_, e_=e_, tsl=tsl: e.tensor_tensor(
                    out=mixT2[:, 8 + e_, tsl], in0=PS[ba][:], in1=sg_[:], op=ALU.mult),
                    reads=[psk(ba), sgk], writes=["mixhi"])

    INVF = [float(v) for v in (np.float32(500000.0) ** (-(np.arange(8, dtype=np.float32) / np.float32(8.0))))]

    def psb(i):
        return PS[i][:].bitcast(BF16)

    def phase_swa():
        mixT = get_mixT()
        A = Arena(PH2_OFF)
        wcol = A.take("wcol", [128, DC, 256], BF16, 2)
        qb = A.take("qb", [128, 4, 64], BF16, 2)
        kd = A.take("kd", [128, 4, 2, 64], BF16, 2)
        qTp = A.take("qTp", [128, S], BF16, 4)
        kTd = A.take("kTd", [128, S], BF16, 4)
        Vs = A.take("Vs", [128, 16, 256], BF16)
        PTa = A.take("PTa", [128, T], BF16, 2)
        PTb = A.take("PTb", [128, T], BF16, 2)
        maskA = A.take("maskA", [128, 4, 128], BF16)
        maskB = A.take("maskB", [128, 4, 128], BF16)
        cosk = A.take("cosk", [128, 16, 8], F32)
        sink = A.take("sink", [128, 16, 8], F32)
        cosq = A.take("cosq", [128, 16, 8], F32)
        sinq = A.take("sinq", [128, 16, 8], F32)
        ang = A.take("ang", [128, 16, 8], F32)
        angc = A.take("angc", [128, 16, 8], F32)
        rtmp = A.take("rtmp", [128, 16, 8], F32)
        rki = A.take("rki", [128, 16, 8], I32)
        posi = A.take("posi", [128, 128], I32)
        posf = A.take("posf", [128, 128], F32)
        posT = A.take("posT", [128, 16], F32)
        invf = A.take("invf", [128, 8], F32)
        es = A.take("es", [128, 32], F32)
        esP = A.take("esP", [128, 16], F32)
        rt = A.take("rt", [128, 6, 4, 8], F32, 2)
        rot = A.take("rot", [128, 4, 16], F32, 2)
        rc = A.take("rc", [128, 256], F32, 2)
        win = od_w_in.rearrange("(c p) f -> p c f", p=128)

        def dve(fn, reads, writes):
            sc.op("dve", fn, reads=reads, writes=writes)

        sc.op("pool", lambda e: e.memset(maskA[:], 0.0), writes=["maskA"])
        sc.op("pool", lambda e: e.affine_select(out=maskA[:], in_=maskA[:], pattern=[[0, 4], [-1, 128]],
                                                 compare_op=ALU.is_ge, fill=-30000.0, base=-1, channel_multiplier=1),
              reads=["maskA"], writes=["maskA"])
        sc.op("pool", lambda e: e.memset(maskB[:], 0.0), writes=["maskB"])
        sc.op("pool", lambda e: e.affine_select(out=maskB[:], in_=maskB[:], pattern=[[0, 4], [1, 128]],
                                                 compare_op=ALU.is_ge, fill=-30000.0, base=0, channel_multiplier=-1),
              reads=["maskB"], writes=["maskB"])
        sc.dma("sp", "swld", es[:], od_sinks.broadcast_to([128, 32]), writes=["es"])
        sc.op("act", lambda e: e.activation(out=es[:], in_=es[:], func=AF.Exp), reads=["es"], writes=["es"])
        esv = es[:].rearrange("p (a two) -> p a two", two=2)
        dve(lambda e: e.tensor_copy(out=esP[0:64, :], in_=esv[0:64, :, 0]), ["es"], ["esP"])
        dve(lambda e: e.tensor_copy(out=esP[64:128, :], in_=esv[64:128, :, 1]), ["es"], ["esP"])
        sc.dma("sp", "swld", posi[0:16, :], pos_in, writes=["posi"])
        dve(lambda e: e.tensor_copy(out=posf[0:16, :], in_=posi[0:16, :]), ["posi"], ["posf"])
        sc.op("pe", lambda e: e.transpose(PS[7][:, 0:16], posf[0:16, :], ident_f[0:16, 0:16]),
              reads=["posf", "ident_f"], writes=[psk(7)])
        dve(lambda e: e.tensor_copy(out=posT[:], in_=PS[7][:, 0:16]), [psk(7)], ["posT"])
        for f in range(8):
            sc.op("pool", lambda e, f=f: e.memset(invf[:, f:f + 1], INVF[f]), writes=["invf"])
        dve(lambda e: e.tensor_tensor(out=ang[:], in0=posT[:].unsqueeze(2).broadcast_to([128, 16, 8]),
                                      in1=invf[:].unsqueeze(1).broadcast_to([128, 16, 8]), op=ALU.mult),
            ["posT", "invf"], ["ang"])

        def reduce_(x, xk):
            dve(lambda e: e.tensor_scalar(out=rtmp[:], in0=x[:], scalar1=1.0 / TWO_PI, scalar2=0.5, op0=ALU.mult, op1=ALU.add),
                [xk], ["rtmp"])
            dve(lambda e: e.tensor_copy(out=rki[:], in_=rtmp[:]), ["rtmp"], ["rki"])
            dve(lambda e: e.tensor_copy(out=rtmp[:], in_=rki[:]), ["rki"], ["rtmp"])
            dve(lambda e: e.scalar_tensor_tensor(out=x[:], in0=rtmp[:], scalar=-TWO_PI, in1=x[:], op0=ALU.mult, op1=ALU.add),
                ["rtmp", xk], [xk])
            for (thr, op_, add) in ((math.pi, ALU.is_gt, -TWO_PI), (-math.pi, ALU.is_lt, TWO_PI)):
                dve(lambda e, thr=thr, op_=op_: e.tensor_scalar(out=rtmp[:], in0=x[:], scalar1=thr, scalar2=None, op0=op_),
                    [xk], ["rtmp"])
                dve(lambda e, add=add: e.scalar_tensor_tensor(out=x[:], in0=rtmp[:], scalar=add, in1=x[:], op0=ALU.mult, op1=ALU.add),
                    ["rtmp", xk], [xk])

        dve(lambda e: e.tensor_scalar(out=angc[:], in0=ang[:], scalar1=0.5 * math.pi, scalar2=None, op0=ALU.add), ["ang"], ["angc"])
        reduce_(ang, "ang")
        reduce_(angc, "angc")
        sc.op("act", lambda e: e.activation(out=sink[:], in_=ang[:], func=AF.Sin), reads=["ang"], writes=["sink"])
        sc.op("act", lambda e: e.activation(out=cosk[:], in_=angc[:], func=AF.Sin), reads=["angc"], writes=["cosk"])
        sc.op("act", lambda e: e.mul(out=sinq[:], in_=sink[:], mul=0.125), reads=["sink"], writes=["sinq"])
        sc.op("act", lambda e: e.mul(out=cosq[:], in_=cosk[:], mul=0.125), reads=["cosk"], writes=["cosq"])
        for hq in range(4):
            sc.op("pool", lambda e, hq=hq: e.memset(qTp[hq][:], 0.0), writes=["qTp%d" % hq])

        units = [("k", 2048), ("v", 2304)] + [("q", 256 * u) for u in range(8)]
        pf = Prefetch(len(units), 2, lambda i, s_: sc.dma("pool", "wcol%d" % s_, wcol[s_][:],
                                                          win[:, :, units[i][1]:units[i][1] + 256], writes=["wcol%d" % s_]))
        ecnt = [0]

        def rope(P4, cs, sn, tb, o1, o2, okeys, slot):
            r_ = rt[slot]
            rk = "rt%d" % slot
            cb = cs[:, tb, :].unsqueeze(1).broadcast_to([128, 4, 8])
            sb_ = sn[:, tb, :].unsqueeze(1).broadcast_to([128, 4, 8])
            x1 = P4[:, :, 0:8]
            x2 = P4[:, :, 8:16]
            ck = ["cosk", "sink", "cosq", "sinq"]
            dve(lambda e: e.tensor_tensor(out=r_[:, 0], in0=x1, in1=cb, op=ALU.mult), okeys["ps"] + ck, [rk])
            dve(lambda e: e.tensor_tensor(out=r_[:, 1], in0=x2, in1=sb_, op=ALU.mult), okeys["ps"] + ck, [rk])
            dve(lambda e: e.tensor_tensor(out=r_[:, 2], in0=x2, in1=cb, op=ALU.mult), okeys["ps"] + ck, [rk])
            dve(lambda e: e.tensor_tensor(out=r_[:, 3], in0=x1, in1=sb_, op=ALU.mult), okeys["ps"] + ck, [rk])
            dve(lambda e: e.tensor_tensor(out=o1, in0=r_[:, 0], in1=r_[:, 1], op=ALU.subtract), [rk], okeys["out"])
            dve(lambda e: e.tensor_tensor(out=o2, in0=r_[:, 2], in1=r_[:, 3], op=ALU.add), [rk], okeys["out"])

        for ui, (kind, col0) in enumerate(units):
            pf.ensure(ui)
            ws = ui % 2
            wkey = "wcol%d" % ws
            quad = ui - 2
            for tb in range(16):
                bank = (tb // 2) % 2
                half = tb % 2
                Pfull = PS[bank][:, half * 256:(half + 1) * 256]
                for c in range(DC):
                    lhsT = xTb[:, c, tb * 128:(tb + 1) * 128]
                    sc.op("pe", lambda e, Pfull=Pfull, lhsT=lhsT, ws=ws, c=c: e.matmul(
                        Pfull, lhsT=lhsT, rhs=wcol[ws][:, c, :], start=(c == 0), stop=(c == DC - 1)),
                        reads=[wkey, "xTb%d" % (tb // 4)], writes=[psk(bank)], inc=(c == DC - 1))
                P4 = Pfull.rearrange("p (h d) -> p h d", h=4)
                slot = ecnt[0] % 2
                ecnt[0] += 1
                if kind == "v":
                    sc.op("act", lambda e, Pfull=Pfull, tb=tb: e.copy(out=Vs[:, tb, :], in_=Pfull), reads=[psk(bank)], writes=["Vs"])
                elif kind == "k":
                    kd_ = kd[slot]
                    kk = "kd%d" % slot
                    sc.op("act", lambda e, P4=P4, kd_=kd_: e.copy(out=kd_[:, :, 0, :], in_=P4), reads=[psk(bank)], writes=[kk])
                    ro = rot[slot]
                    rok = "rot%d" % slot
                    rope(P4, cosk, sink, tb, ro[:, :, 0:8], ro[:, :, 8:16], {"ps": [psk(bank)], "out": [rok]}, slot)
                    dve(lambda e, kd_=kd_, ro=ro: e.tensor_copy(out=kd_[:, :, 0, 0:16], in_=ro[:]), [rok, kk], [kk])
                    dve(lambda e, kd_=kd_: e.tensor_copy(out=kd_[:, :, 1, :], in_=kd_[:, :, 0, :]), [kk], [kk])
                    for kv in range(4):
                        sc.op("pe", lambda e, kv=kv, kd_=kd_: e.transpose(
                            psb(2)[:, kv * 128:(kv + 1) * 128], kd_[:, kv, :, :].rearrange("p a b -> p (a b)"), ident_b[:]),
                            reads=[kk, "ident_b"], writes=[psk(2)], inc=(kv == 3))
                    for kv in range(4):
                        eng = "act" if kv % 2 == 0 else "dve"
                        if eng == "act":
                            sc.op("act", lambda e, kv=kv, tb=tb: e.copy(out=kTd[kv][:, tb * 128:(tb + 1) * 128],
                                                                        in_=psb(2)[:, kv * 128:(kv + 1) * 128]),
                                  reads=[psk(2)], writes=["kTd%d" % kv])
                        else:
                            dve(lambda e, kv=kv, tb=tb: e.tensor_copy(out=kTd[kv][:, tb * 128:(tb + 1) * 128],
                                                                       in_=psb(2)[:, kv * 128:(kv + 1) * 128]),
                                [psk(2)], ["kTd%d" % kv])
                else:
                    qb_ = qb[slot]
                    qk_ = "qb%d" % slot
                    sc.op("act", lambda e, P4=P4, qb_=qb_: e.activation(out=qb_[:], in_=P4, func=AF.Copy, scale=0.125),
                          reads=[psk(bank)], writes=[qk_])
                    rope(P4, cosq, sinq, tb, qb_[:, :, 0:8], qb_[:, :, 8:16], {"ps": [psk(bank), qk_], "out": [qk_]}, slot)
                    for mc in range(2):
                        sc.op("pe", lambda e, mc=mc, qb_=qb_: e.transpose(
                            psb(2)[:, mc * 128:(mc + 1) * 128], qb_[:, 2 * mc:2 * mc + 2, :].rearrange("p a b -> p (a b)"),
                            ident_b[:]), reads=[qk_, "ident_b"], writes=[psk(2)], inc=(mc == 1))
                    for hq in range(4):
                        e_, mc = hq % 2, hq // 2
                        src = psb(2)[e_ * 64:(e_ + 1) * 64, mc * 128:(mc + 1) * 128]
                        dst = qTp[hq][e_ * 64:(e_ + 1) * 64, tb * 128:(tb + 1) * 128]
                        if hq % 2 == 0:
                            sc.op("act", lambda e, src=src, dst=dst: e.copy(out=dst, in_=src), reads=[psk(2)], writes=["qTp%d" % hq])
                        else:
                            dve(lambda e, src=src, dst=dst: e.tensor_copy(out=dst, in_=src), [psk(2)], ["qTp%d" % hq])
            if kind != "q":
                continue
            hk = quad // 2
            for i in range(16):
                par = i % 2
                pa, pb_ = PTa[par], PTb[par]
                pak, pbk = "PTa%d" % par, "PTb%d" % par
                blocks = ([(3, i - 1, maskA, "maskA", pa, pak)] if i > 0 else []) + [(4, i, maskB, "maskB", pb_, pbk)]
                for (bank, kb, mk, mkk, pt, ptk) in blocks:
                    for hq in range(4):
                        sc.op("pe", lambda e, bank=bank, kb=kb, hq=hq, i=i, hk=hk: e.matmul(
                            PS[bank][:, hq * 128:(hq + 1) * 128], lhsT=kTd[hk][:, kb * 128:(kb + 1) * 128],
                            rhs=qTp[hq][:, i * 128:(i + 1) * 128], start=(hq == 0), stop=False),
                            reads=["kTd%d" % hk, "qTp%d" % hq], writes=[psk(bank)], inc=False)
                    sc.op("pe", lambda e, bank=bank, mk=mk: e.matmul(
                        PS[bank][:], lhsT=ident_b[:], rhs=mk[:].rearrange("p a b -> p (a b)"), start=False, stop=True),
                        reads=["ident_b", mkk], writes=[psk(bank)], inc=True)
                    sc.op("act", lambda e, bank=bank, pt=pt: e.activation(out=pt[:], in_=PS[bank][:], func=AF.Exp),
                          reads=[psk(bank)], writes=[ptk])
                nb = len(blocks)
                for (obank, use_v) in ((5, True), (6, False)):
                    for hq in range(4):
                        e_, mc = hq % 2, hq // 2
                        out = PS[obank][e_ * 64:(e_ + 1) * 64, mc * 128:(mc + 1) * 128]
                        for bi, (bank, kb, mk, mkk, pt, ptk) in enumerate(blocks):
                            lhsT = Vs[:, kb, hk * 64:(hk + 1) * 64] if use_v else ones_b[:, 0:64]
                            rhs = pt[:, hq * 128:(hq + 1) * 128]
                            first = (bi == 0 and hq < 2)
                            lastmm = (hq == 3 and bi == nb - 1)
                            stopf = (hq >= 2 and bi == nb - 1)
                            sc.op("pe", lambda e, out=out, lhsT=lhsT, rhs=rhs, first=first, stopf=stopf, e_=e_: e.matmul(
                                out, lhsT=lhsT, rhs=rhs, start=first, stop=stopf, tile_position=(0, e_ * 64)),
                                reads=[ptk, "Vs" if use_v else "ones_b"], writes=[psk(obank)], inc=lastmm)
                rc_ = rc[par]
                rck = "rc%d" % par
                for mc in range(2):
                    dve(lambda e, mc=mc, rc_=rc_, quad=quad: e.tensor_scalar(
                        out=rc_[:, mc * 128:(mc + 1) * 128], in0=PS[6][:, mc * 128:(mc + 1) * 128],
                        scalar1=esP[:, quad * 2 + mc:quad * 2 + mc + 1], scalar2=None, op0=ALU.add),
                        [psk(6), "esP"], [rck])
                dve(lambda e, rc_=rc_: e.reciprocal(out=rc_[:], in_=rc_[:]), [rck], [rck])
                dve(lambda e, rc_=rc_, i=i, quad=quad: e.tensor_tensor(
                    out=mixT[:, quad * 2:quad * 2 + 2, i * 128:(i + 1) * 128],
                    in0=PS[5][:, 0:256].rearrange("p (a b) -> p a b", a=2),
                    in1=rc_[:].rearrange("p (a b) -> p a b", a=2), op=ALU.mult),
                    [psk(5), rck], ["mixlo" if quad < 4 else "mixhi"])

    def dbg_dump_mix():
        mixT = get_mixT()
        ov = out_d.rearrange("(c p) t -> p c t", p=128)
        for c in range(DC):
            sc.dma("pool", "dbgm", ov[:, c, :], mixT[:, c, :], reads=["mixlo", "mixhi"])

    def dbg_copy_R(R):
        A = Arena(PH_OFF)
        bufs = A.take("dbgb", [128, S], F32, 2)
        for c in range(DC):
            k = "dbgb%d" % (c % 2)
            sc.dma("sp", k, bufs[c % 2][:], R[c], reads=[("R", id(R), tt) for tt in range(NT)], writes=[k])
            sc.dma("sp", k, out_d[c * 128:(c + 1) * 128, :], bufs[c % 2][:], reads=[k])

    setup_consts()
    sc.phase_reset()
    if mode == "full":
        phase_input(x_in, RA)
        sc.phase_reset()
        phase_fox()
        sc.phase_reset()
        phase_s5()
        sc.phase_reset()
        phase_outproj(0, ev_w_out, RA, RB)
        sc.phase_reset()
        phase_ffn(0, RB, RA, final=False)
        sc.phase_reset()
        phase_swa()
        sc.phase_reset()
        phase_outproj(1, od_w_out, RA, RB)
        sc.phase_reset()
        phase_ffn(1, RB, RA, final=True)
    elif mode == "consts":
        sc.dma("sp", "dbg", out_d[0:128, 0:128], ident_f[:], reads=["ident_f"])
        sc.dma("sp", "dbg", out_d[128:256, 0:86 * 4].rearrange("p (a b) -> p a b", a=4), convp[:, 0, :, :], reads=["consts"])
        sc.dma("sp", "dbg", out_d[256:384, 0:128].rearrange("p (a b) -> p a b", a=8), lnp[:], reads=["consts"])
    elif mode == "p0":
        phase_input(x_in, RA)
        sc.phase_reset()
        if 'dbg' not in _SKIP:
            dbg_copy_R(RA)
    elif mode == "fox":
        phase_input(x_in, RA)
        sc.phase_reset()
        phase_fox()
        sc.phase_reset()
        dbg_dump_mix()
    elif mode == "s5":
        phase_input(x_in, RA)
        sc.phase_reset()
        phase_fox()
        sc.phase_reset()
        phase_s5()
        sc.phase_reset()
        dbg_dump_mix()
    elif mode == "swa":
        phase_input(x_in, RA)
        sc.phase_reset()
        phase_swa()
        sc.phase_reset()
        phase_outproj(1, od_w_out, RA, RB)
        sc.phase_reset()
        dbg_copy_R(RB)
    elif mode == "outproj0":
        phase_input(x_in, RA)
        sc.phase_reset()
        phase_fox()
        sc.phase_reset()
        phase_outproj(0, ev_w_out, RA, RB)
        sc.phase_reset()
        dbg_copy_R(RB)
    elif mode == "ffn0":
        phase_input(x_in, RA)
        sc.phase_reset()
        phase_ffn(0, RA, RB, final=True)
    else:
        raise NotImplementedError(mode)
    sc.barrier()
    sc.emit()
    return nc, es


_CACHE = {}


def _prep_inputs(inp, b):
    f = np.ascontiguousarray
    m = {
        "x": f(inp["x"][b]),
        "pos": f(inp["positions"][b].reshape(16, 128).astype(np.int32)),
        "ev_w_in": f(inp["ev_w_in"][0]),
        "ev_b_f": f(inp["ev_b_f"][0].reshape(8, 1)),
        "ev_lre": f(inp["ev_lambda_re"][0]),
        "ev_lim": f(inp["ev_lambda_im"][0]),
        "ev_lstep": f(inp["ev_log_step"][0].reshape(1, 64)),
        "ev_bre": f(inp["ev_ssm_b_re"][0]),
        "ev_bim": f(inp["ev_ssm_b_im"][0]),
        "ev_cre": f(inp["ev_ssm_c_re"][0]),
        "ev_cim": f(inp["ev_ssm_c_im"][0]),
        "ev_d": f(inp["ev_ssm_d"][0].reshape(8, 128)),
        "ev_w_glu": f(inp["ev_w_glu"][0]),
        "ev_w_out": f(inp["ev_w_out"][0]),
        "od_w_in": f(inp["od_w_in"][0]),
        "od_sinks": f(inp["od_sinks"][0].reshape(1, 32)),
        "od_w_out": f(inp["od_w_out"][0]),
        "ln_mix_g": f(inp["ln_mix_g"].reshape(2, 16, 128)),
        "ln_mix_b": f(inp["ln_mix_b"].reshape(2, 16, 128)),
        "ffn_w_up": f(inp["ffn_w_up"]),
        "ffn_conv_w": f(inp["ffn_conv_w"].reshape(2, 3, 86, 128)),
        "ffn_conv_b": f(inp["ffn_conv_b"].reshape(2, 86, 128)),
        "ffn_w_down": f(inp["ffn_w_down"]),
        "ln_ffn_g": f(inp["ln_ffn_g"].reshape(2, 16, 128)),
        "ln_ffn_b": f(inp["ln_ffn_b"].reshape(2, 16, 128)),
    }
    return m


def run(inputs, mode="full", cores=8, trace=False):
    nc, es = build(mode)
    in_maps = [_prep_inputs(inputs, b) for b in range(cores)]
    res = run_bass_kernel_spmd(nc, in_maps, core_ids=list(range(cores)), trace=trace)
    es.close()
    return res


def kernel(**inputs):
    res = run(inputs, "full", 8)
    out = np.stack([np.asarray(r["out"]) for r in res.results], axis=0)
    return out.astype(np.float32)
```

```python
import math
import os
_SKIP = set(os.environ.get('K_SKIP', '').split(','))
from contextlib import ExitStack

import numpy as np
import concourse.bass as bass
import concourse.mybir as mybir
from concourse.bass_utils import run_bass_kernel_spmd

F32 = mybir.dt.float32
BF16 = mybir.dt.bfloat16
I32 = mybir.dt.int32
AF = mybir.ActivationFunctionType
ALU = mybir.AluOpType

S = 2048
D = 2048
DC = 16
T = 512
NT = 4
DFF = 5504
NJ = 43
ALPHA = (2.0 * 2) ** 0.25
LN_EPS = 1e-5
DMA_SCRATCH = 8192
SBUF_BASE = DMA_SCRATCH
SBUF_LIMIT = 16384 + 212800

ENGS = ("pe", "act", "dve", "pool", "sp")
SAME_SYNC = True


class Sched:
    def __init__(self, nc, es):
        self.nc = nc
        self.es = es
        self.streams = {e: [] for e in ENGS}
        self.cnt = {e: 0 for e in ENGS}
        self.waited = {e: {} for e in ENGS}
        self.lastw = {}
        self.readers = {}
        self.pend_r = {e: [] for e in ENGS}
        self.pend_w = {e: [] for e in ENGS}
        self.sems = {}
        self.dcount = {}
        self.ranges = {}
        self.alias = {}
        self.tcache = {}
        self.persist = set()
        for e in ENGS:
            self.sems["E_" + e] = es.enter_context(nc.semaphore("E_" + e))

    def sb(self, key, shape, dtype, off, persistent=False):
        nbytes = int(np.prod(shape[1:])) * (2 if dtype == BF16 else 4)
        assert off % 4 == 0 and off + nbytes <= SBUF_LIMIT, (key, off, nbytes)
        ck = (key, off, tuple(shape), str(dtype))
        if ck in self.tcache and self.ranges.get(key) == (off, off + nbytes):
            return self.tcache[ck]
        t = self.nc.alloc_sbuf_tensor_at(key, list(shape), dtype, offset=off)
        self.tcache[ck] = t
        assert key not in self.ranges, ("key re-registered at a different place", key)
        self.reg(key, off, off + nbytes)
        if persistent:
            self.persist.add(key)
        return t

    def phase_reset(self):
        self.barrier()
        self.lastw = {}
        self.readers = {}
        for k in list(self.ranges):
            if k not in self.persist:
                del self.ranges[k]
                del self.alias[k]
        for k in self.alias:
            self.alias[k] = [a for a in self.alias[k] if a in self.ranges]
        self.tcache = {ck: t for ck, t in self.tcache.items() if ck[0] in self.persist}

    def reg(self, key, lo, hi):
        self.ranges[key] = (lo, hi)
        al = []
        for k, (a, b) in self.ranges.items():
            if k != key and a < hi and lo < b:
                al.append(k)
                self.alias[k].append(key)
        self.alias[key] = al

    def _keys(self, k):
        return [k] + self.alias.get(k, [])

    def _wait(self, eng, tok):
        sem, val = tok
        if sem == "E_" + eng:
            if eng in ("pe", "sp") or not SAME_SYNC:
                return
        if self.waited[eng].get(sem, 0) >= val:
            return
        self.waited[eng][sem] = val
        self.streams[eng].append(("w", sem, val))

    def _deps(self, eng, reads, writes):
        toks = []
        for k0 in reads:
            for k in self._keys(k0):
                t = self.lastw.get(k)
                if t:
                    toks.append(t)
                if isinstance(k, str) and k.startswith("ps") and k[2:].isdigit():
                    toks.extend(r for r in self.readers.get(k, ()) if r[0] != "E_" + eng)
                for e2 in ENGS:
                    assert k not in self.pend_w[e2] or e2 == eng, ("pending write", k, e2, eng)
        for k0 in writes:
            for k in self._keys(k0):
                t = self.lastw.get(k)
                if t:
                    toks.append(t)
                toks.extend(self.readers.get(k, ()))
                for e2 in ENGS:
                    if e2 != eng:
                        assert k not in self.pend_r[e2] and k not in self.pend_w[e2], ("pending", k, e2, eng)
        for t in toks:
            self._wait(eng, t)

    def _commit(self, tok, reads, writes):
        for k in reads:
            self.readers.setdefault(k, []).append(tok)
        for k in writes:
            self.lastw[k] = tok
            self.readers[k] = []

    def op(self, eng, fn, reads=(), writes=(), inc=True):
        reads = list(reads)
        writes = list(writes)
        self._deps(eng, reads, writes)
        if inc:
            self.cnt[eng] += 1
            tok = ("E_" + eng, self.cnt[eng])
            self.streams[eng].append(("o", fn, "E_" + eng, 1))
            self._commit(tok, reads + self.pend_r[eng], writes + self.pend_w[eng])
            self.pend_r[eng] = []
            self.pend_w[eng] = []
        else:
            self.streams[eng].append(("o", fn, None, 0))
            self.pend_r[eng] += reads
            self.pend_w[eng] += writes

    def dma(self, q, sem, out, in_, reads=(), writes=(), **kw):
        reads = list(reads)
        writes = list(writes)
        if sem not in self.sems:
            self.sems[sem] = self.es.enter_context(self.nc.semaphore(sem))
            self.dcount[sem] = 0
        if self.dcount[sem]:
            self._wait(q, (sem, self.dcount[sem]))
        self._deps(q, reads, writes)
        self.dcount[sem] += 16
        tok = (sem, self.dcount[sem])
        self.streams[q].append(("o", lambda e, o=out, i=in_, kw=kw: e.dma_start(out=o, in_=i, **kw), sem, 16))
        self._commit(tok, reads, writes)

    def barrier(self):
        for e in ENGS:
            assert not self.pend_r[e] and not self.pend_w[e], ("barrier with pending ops", e)
        for e in ENGS:
            for e2 in ENGS:
                if self.cnt[e2] and (e2 != e or e in ("act", "dve", "pool")):
                    if self.waited[e].get("E_" + e2, 0) < self.cnt[e2]:
                        self.waited[e]["E_" + e2] = self.cnt[e2]
                        self.streams[e].append(("w", "E_" + e2, self.cnt[e2]))
            for s, v in self.dcount.items():
                if v:
                    self._wait(e, (s, v))

    def emit(self):
        nc = self.nc
        with nc.Block() as block:
            def mk(name):
                def body(e):
                    for it in self.streams[name]:
                        if it[0] == "w":
                            e.wait_ge(self.sems[it[1]], it[2])
                        else:
                            ins = it[1](e)
                            if it[2] is not None:
                                ins.then_inc(self.sems[it[2]], it[3])
                return body
            block.tensor(mk("pe"))
            block.scalar(mk("act"))
            block.vector(mk("dve"))
            block.gpsimd(mk("pool"))
            block.sync(mk("sp"))


def build(mode="full"):
    nc = bass.Bass("TRN2", target_bir_lowering=False, dynamic_dma_scratch_size=DMA_SCRATCH)
    es = ExitStack()
    sc = Sched(nc, es)

    def dram_in(name, shape, dt=F32):
        return nc.dram_tensor(name, list(shape), dt, kind="ExternalInput").ap()

    x_in = dram_in("x", [S, D])
    pos_in = dram_in("pos", [S // 128, 128], I32)
    ev_w_in = dram_in("ev_w_in", [D, 4104])
    ev_b_f = dram_in("ev_b_f", [8, 1])
    ev_lre = dram_in("ev_lre", [64, 64])
    ev_lim = dram_in("ev_lim", [64, 64])
    ev_lstep = dram_in("ev_lstep", [1, 64])
    ev_bre = dram_in("ev_bre", [64, 64, 16])
    ev_bim = dram_in("ev_bim", [64, 64, 16])
    ev_cre = dram_in("ev_cre", [64, 16, 64])
    ev_cim = dram_in("ev_cim", [64, 16, 64])
    ev_d = dram_in("ev_d", [8, 128])
    ev_w_glu = dram_in("ev_w_glu", [1024, 2048])
    ev_w_out = dram_in("ev_w_out", [D, D])
    od_w_in = dram_in("od_w_in", [D, 2560])
    od_sinks = dram_in("od_sinks", [1, 32])
    od_w_out = dram_in("od_w_out", [D, D])
    ln_mix_g = dram_in("ln_mix_g", [2, 16, 128])
    ln_mix_b = dram_in("ln_mix_b", [2, 16, 128])
    ffn_w_up = dram_in("ffn_w_up", [2, D, 2 * DFF])
    ffn_conv_w = dram_in("ffn_conv_w", [2, 3, 86, 128])
    ffn_conv_b = dram_in("ffn_conv_b", [2, 86, 128])
    ffn_w_down = dram_in("ffn_w_down", [2, DFF, D])
    ln_ffn_g = dram_in("ln_ffn_g", [2, 16, 128])
    ln_ffn_b = dram_in("ln_ffn_b", [2, 16, 128])
    out_d = nc.dram_tensor("out", [S, D], F32, kind="ExternalOutput").ap()
    RA = nc.dram_tensor("resA", [DC, 128, S], F32, kind="Internal").ap()
    RB = nc.dram_tensor("resB", [DC, 128, S], F32, kind="Internal").ap()

    wupb = [nc.dram_tensor("wupb%d" % l, [86, 128, DC, 128], BF16, kind="Internal").ap() for l in range(2)]
    wdnb = [nc.dram_tensor("wdnb%d" % l, [DC, 128, NJ, 128], BF16, kind="Internal").ap() for l in range(2)]
    conv_jobs = []
    for l in range(2):
        wup_v = ffn_w_up[l].rearrange("(c p) f -> p c f", p=128)
        wdn_v = ffn_w_down[l].rearrange("(j p) d -> p j d", p=128)
        for j in range(86):
            conv_jobs.append((wupb[l][j], wup_v[:, :, j * 128:(j + 1) * 128], ("wupb", l, j)))
        for dc in range(DC):
            conv_jobs.append((wdnb[l][dc], wdn_v[:, :, dc * 128:(dc + 1) * 128], ("wdnb", l, dc)))
    conv_done = [0]
    NCV = 6

    def conv_pump(n):
        for _ in range(n):
            if conv_done[0] >= len(conv_jobs):
                return
            o_, i_, key = conv_jobs[conv_done[0]]
            sc.dma("pool", "cv%d" % (conv_done[0] % NCV), o_, i_, writes=[key])
            conv_done[0] += 1

    PS = [es.enter_context(nc.psum_tensor("ps%d" % i, [128, 512], F32)) for i in range(8)]

    def psk(i):
        return "ps%d" % i

    off = [SBUF_BASE]

    def alloc(key, shape, dt):
        nbytes = int(np.prod(shape[1:])) * (2 if dt == BF16 else 4)
        nbytes = (nbytes + 31) // 32 * 32
        t = sc.sb(key, shape, dt, off[0], persistent=True)
        off[0] += nbytes
        return t

    ident_f = alloc("ident_f", [128, 128], F32)
    ident_b = alloc("ident_b", [128, 128], BF16)
    ones_f = alloc("ones_f", [128, 128], F32)
    ones_b = alloc("ones_b", [128, 128], BF16)
    lnp = alloc("lnp", [128, 8, 16], F32)
    convp = alloc("convp", [128, 2, 4, 86], F32)
    CONST_END = off[0]
    XTB_OFF = CONST_END
    xTb = sc.sb("xTb_all", [128, DC, S], BF16, XTB_OFF, persistent=True)
    for tt in range(NT):
        sc.reg("xTb%d" % tt, SBUF_LIMIT + 1000 + tt, SBUF_LIMIT + 1000 + tt + 1)
        sc.persist.add("xTb%d" % tt)
    PH_OFF = XTB_OFF + DC * S * 2

    class Arena:
        def __init__(self, start):
            self.o = (start + 31) // 32 * 32

        def take(self, key, shape, dt, n=None):
            nbytes = int(np.prod(shape[1:])) * (2 if dt == BF16 else 4)
            nbytes = (nbytes + 31) // 32 * 32
            if n is None:
                t = sc.sb(key, shape, dt, self.o)
                self.o += nbytes
                return t
            ts = []
            for i in range(n):
                ts.append(sc.sb("%s%d" % (key, i), shape, dt, self.o))
                self.o += nbytes
            return ts

    def setup_consts():
        sc.op("pool", lambda e: e.memset(ones_f[:], 1.0), writes=["ones_f"])
        sc.op("pool", lambda e: e.memset(ident_f[:], 1.0), writes=["ident_f"])
        sc.op("pool", lambda e: e.affine_select(out=ident_f[:], in_=ident_f[:], pattern=[[-1, 128]],
                                                 compare_op=ALU.is_equal, fill=0.0, base=0,
                                                 channel_multiplier=1),
              reads=["ident_f"], writes=["ident_f"])
        sc.op("dve", lambda e: e.tensor_copy(out=ident_b[:], in_=ident_f[:]), reads=["ident_f"], writes=["ident_b"])
        sc.op("dve", lambda e: e.tensor_copy(out=ones_b[:], in_=ones_f[:]), reads=["ones_f"], writes=["ones_b"])
        stg = sc.sb("c_stg", [128, 128], F32, PH_OFF)
        jobs = []
        for l in range(2):
            for k, src in enumerate((ln_mix_g, ln_mix_b, ln_ffn_g, ln_ffn_b)):
                jobs.append((src[l], 16, lnp[:, l * 4 + k, :]))
            for k in range(3):
                jobs.append((ffn_conv_w[l, k], 86, convp[:, l, k, :]))
            jobs.append((ffn_conv_b[l], 86, convp[:, l, 3, :]))
        for n, (src, rows, dst) in enumerate(jobs):
            sc.dma("sp", "c_stg", stg[0:rows, :], src, writes=["c_stg"])
            sc.op("pe", lambda e, r=rows: e.transpose(PS[0][:, 0:r], stg[0:r, :], ident_f[0:r, 0:r]),
                  reads=["c_stg", "ident_f"], writes=[psk(0)])
            sc.op("dve", lambda e, r=rows, d=dst: e.tensor_copy(out=d, in_=PS[0][:, 0:r]),
                  reads=[psk(0)], writes=["consts"])

    def phase_input(x_src, Rout):
        A = Arena(PH_OFF)
        xin = A.take("xin", [128, D], F32, 2)
        st32 = A.take("st32", [128, DC, T], F32)
        for tt in range(NT):
            for tb in range(4):
                g = tt * 4 + tb
                xi = xin[g % 2]
                xk = "xin%d" % (g % 2)
                sc.dma("sp", xk, xi[:], x_src[g * 128:(g + 1) * 128, :], writes=[xk])
                for q in range(4):
                    bank = q % 2
                    for c4 in range(4):
                        dc = q * 4 + c4
                        sc.op("pe", lambda e, b=bank, c4=c4, dc=dc, xi=xi: e.transpose(
                            PS[b][:, c4 * 128:(c4 + 1) * 128], xi[:, dc * 128:(dc + 1) * 128], ident_f[:]),
                            reads=[xk, "ident_f"], writes=[psk(bank)], inc=(c4 == 3))
                    pv = PS[bank][:].rearrange("p (c t) -> p c t", c=4)
                    if 'act' not in _SKIP:
                      sc.op("act", lambda e, pv=pv, q=q, tb=tb: e.copy(
                        out=st32[:, q * 4:(q + 1) * 4, tb * 128:(tb + 1) * 128], in_=pv),
                        reads=[psk(bank)], writes=["st32"])
                    if 'dve' not in _SKIP:
                      sc.op("dve", lambda e, pv=pv, q=q, g=g: e.tensor_copy(
                        out=xTb[:, q * 4:(q + 1) * 4, g * 128:(g + 1) * 128], in_=pv),
                        reads=[psk(bank)], writes=["xTb%d" % tt])
            conv_pump(4)
            if 'st' not in _SKIP:
              sc.dma("sp", "st32", Rout[:, :, tt * T:(tt + 1) * T].rearrange("c p t -> p c t"), st32[:],
                   reads=["st32"], writes=[("R", id(Rout), tt)])

    def layer_norm(r32, rkey, gi, bi, tt, lo, Rout=None, final=False, ost_lo=None, nbuf=2):
        A = Arena(lo)
        sq = A.take("ln_sq", [128, T], BF16, nbuf)
        rb = A.take("ln_rb", [128, T], BF16, nbuf)
        mean = A.take("ln_mean", [128, T], F32)
        rstd = A.take("ln_rstd", [128, T], F32)
        if final:
            ost = Arena(ost_lo).take("ln_ost", [128, D], F32, 2)
        for c in range(DC):
            s = sq[c % nbuf]
            sk = "ln_sq%d" % (c % nbuf)
            rb_ = rb[c % nbuf]
            rbk = "ln_rb%d" % (c % nbuf)
            sc.op("act", lambda e, s=s, c=c: e.activation(out=s[:], in_=r32[:, c, :], func=AF.Square),
                  reads=[rkey], writes=[sk])
            sc.op("pool", lambda e, rb_=rb_, c=c: e.tensor_copy(out=rb_[:], in_=r32[:, c, :]), reads=[rkey], writes=[rbk])
            sc.op("pe", lambda e, c=c, rb_=rb_: e.matmul(PS[6][:], lhsT=ones_b[:], rhs=rb_[:], start=(c == 0), stop=(c == DC - 1)),
                  reads=[rbk, "ones_b"], writes=[psk(6)], inc=True)
            sc.op("pe", lambda e, s=s, c=c: e.matmul(PS[7][:], lhsT=ones_b[:], rhs=s[:], start=(c == 0), stop=(c == DC - 1)),
                  reads=[sk, "ones_b"], writes=[psk(7)], inc=True)
        sc.op("act", lambda e: e.mul(out=mean[:], in_=PS[6][:], mul=1.0 / D), reads=[psk(6)], writes=["ln_mean"])
        sc.op("dve", lambda e: e.tensor_tensor(out=rstd[:], in0=mean[:], in1=mean[:], op=ALU.mult),
              reads=["ln_mean"], writes=["ln_rstd"])
        sc.op("dve", lambda e: e.scalar_tensor_tensor(out=rstd[:], in0=PS[7][:], scalar=1.0 / D, in1=rstd[:],
                                                      op0=ALU.mult, op1=ALU.subtract),
              reads=[psk(7), "ln_rstd"], writes=["ln_rstd"])
        sc.op("dve", lambda e: e.tensor_scalar(out=rstd[:], in0=rstd[:], scalar1=LN_EPS, scalar2=None, op0=ALU.add),
              reads=["ln_rstd"], writes=["ln_rstd"])
        sc.op("act", lambda e: e.activation(out=rstd[:], in_=rstd[:], func=AF.Sqrt), reads=["ln_rstd"], writes=["ln_rstd"])
        sc.op("dve", lambda e: e.reciprocal(out=rstd[:], in_=rstd[:]), reads=["ln_rstd"], writes=["ln_rstd"])
        for c in range(DC):
            sc.op("dve", lambda e, c=c: e.tensor_tensor(out=r32[:, c, :], in0=r32[:, c, :], in1=mean[:], op=ALU.subtract),
                  reads=[rkey, "ln_mean"], writes=[rkey])
            sc.op("dve", lambda e, c=c: e.tensor_tensor(out=r32[:, c, :], in0=r32[:, c, :], in1=rstd[:], op=ALU.mult),
                  reads=[rkey, "ln_rstd"], writes=[rkey])
            sc.op("act", lambda e, c=c: e.activation(out=r32[:, c, :], in_=r32[:, c, :], func=AF.Identity,
                                                     bias=lnp[:, bi, c:c + 1], scale=lnp[:, gi, c:c + 1]),
                  reads=[rkey, "consts"], writes=[rkey])
            if not final:
                sc.op("pool", lambda e, c=c: e.tensor_copy(out=xTb[:, c, tt * T:(tt + 1) * T], in_=r32[:, c, :]),
                      reads=[rkey], writes=["xTb%d" % tt])
        if not final:
            sc.dma("sp", "ln_out_" + str(rkey), Rout[:, :, tt * T:(tt + 1) * T].rearrange("c p t -> p c t"), r32[:],
                   reads=[rkey], writes=[("R", id(Rout), tt)])
        else:
            for tb in range(4):
                g = tt * 4 + tb
                os_ = ost[g % 2]
                ok = "ln_ost%d" % (g % 2)
                for q in range(4):
                    bank = q % 2
                    for c4 in range(4):
                        dc = q * 4 + c4
                        sc.op("pe", lambda e, b=bank, c4=c4, dc=dc, tb=tb: e.transpose(
                            PS[b][:, c4 * 128:(c4 + 1) * 128], r32[:, dc, tb * 128:(tb + 1) * 128], ident_f[:]),
                            reads=[rkey, "ident_f"], writes=[psk(bank)], inc=(c4 == 3))
                    if q % 2 == 0:
                        sc.op("act", lambda e, b=bank, q=q, os_=os_: e.copy(out=os_[:, q * 512:(q + 1) * 512], in_=PS[b][:]),
                              reads=[psk(bank)], writes=[ok])
                    else:
                        sc.op("dve", lambda e, b=bank, q=q, os_=os_: e.tensor_copy(out=os_[:, q * 512:(q + 1) * 512], in_=PS[b][:]),
                              reads=[psk(bank)], writes=[ok])
                sc.dma("sp", ok, out_d[g * 128:(g + 1) * 128, :], os_[:], reads=[ok], writes=[("out", g)])

    def phase_ffn(l, Rin, Rout, final):
        conv_pump(max(0, (l + 1) * 102 - conv_done[0]))
        A = Arena(PH_OFF)
        aT = A.take("aT", [128, NJ, T], BF16)
        r32 = A.take("r32", [128, DC, T], F32)
        NW = 3
        wg = A.take("wg", [128, DC, 128], BF16, NW)
        wv = A.take("wv", [128, DC, 128], BF16, NW)
        ND = 2
        wd = A.take("wd", [128, NJ, 128], BF16, ND)
        carry = A.take("carry", [128, 2, NJ, 2], F32, 2)
        lo = A.o
        gb = A.take("gb", [128, T], F32, 2)
        vb = A.take("vb", [128, T], F32, 2)
        wup = ffn_w_up[l].rearrange("(c p) f -> p c f", p=128)
        wdn = ffn_w_down[l].rearrange("(j p) d -> p j d", p=128)

        def cw(k, j):
            return convp[:, l, k, j:j + 1]

        up_jobs = [(tt, j) for tt in range(NT) for j in range(NJ)]
        dn_jobs = [(tt, dc) for tt in range(NT) for dc in range(DC)]
        up_issued = [0]
        dn_issued = [0]

        def issue_up(n):
            while up_issued[0] < min(n, len(up_jobs)):
                i = up_issued[0]
                _, j = up_jobs[i]
                ws = i % NW
                sc.dma("sp", "wg%d" % ws, wg[ws][:], wupb[l][j], reads=[("wupb", l, j)], writes=["wg%d" % ws])
                sc.dma("sp", "wv%d" % ws, wv[ws][:], wupb[l][NJ + j], reads=[("wupb", l, NJ + j)], writes=["wv%d" % ws])
                up_issued[0] += 1

        def issue_dn(n):
            while dn_issued[0] < min(n, len(dn_jobs)):
                i = dn_issued[0]
                _, dc = dn_jobs[i]
                ds = i % ND
                sc.dma("sp", "wd%d" % ds, wd[ds][:], wdnb[l][dc], reads=[("wdnb", l, dc)], writes=["wd%d" % ds])
                dn_issued[0] += 1

        for tt in range(NT):
            tsl = slice(tt * T, (tt + 1) * T)
            xk = "xTb%d" % tt
            sc.dma("sp", "r32", r32[:], Rin[:, :, tsl].rearrange("c p t -> p c t"),
                   reads=[("R", id(Rin), tt)], writes=["r32"])
            cin = carry[(tt + 1) % 2]
            cout = carry[tt % 2]
            cink = "carry%d" % ((tt + 1) % 2)
            coutk = "carry%d" % (tt % 2)
            for j in range(NJ):
                ui = tt * NJ + j
                issue_up(ui + NW - 0 if ui == 0 else ui + NW)
                if j == 8:
                    issue_dn(tt * DC + 1)
                if j == 16:
                    issue_dn(tt * DC + 2)
                ws = ui % NW
                pg = (j % 2) * 2
                for half, (w_, wk, gv, cofs) in enumerate(((wg[ws], "wg%d" % ws, gb[j % 2], 0),
                                                           (wv[ws], "wv%d" % ws, vb[j % 2], NJ))):
                    bank = pg + half
                    P = PS[bank]
                    for c in range(DC):
                        rhs = xTb[:, c, tsl]
                        sc.op("pe", lambda e, P=P, w_=w_, c=c, rhs=rhs: e.matmul(P[:], lhsT=w_[:, c, :], rhs=rhs,
                                                                                  start=(c == 0), stop=(c == DC - 1)),
                              reads=[wk, xk], writes=[psk(bank)], inc=(c == DC - 1))
                    gk = ("gb%d" if half == 0 else "vb%d") % (j % 2)
                    jj = j + cofs
                    sc.op("act", lambda e, P=P, gv=gv, jj=jj: e.activation(out=gv[:], in_=P[:], func=AF.Identity,
                                                                           bias=cw(3, jj), scale=cw(2, jj)),
                          reads=[psk(bank), "consts"], writes=[gk])
                    sc.op("dve", lambda e, P=P, gv=gv, jj=jj: e.scalar_tensor_tensor(
                        out=gv[:, 1:T], in0=P[:, 0:T - 1], scalar=cw(1, jj), in1=gv[:, 1:T], op0=ALU.mult, op1=ALU.add),
                        reads=[psk(bank), gk, "consts"], writes=[gk])
                    sc.op("dve", lambda e, P=P, gv=gv, jj=jj: e.scalar_tensor_tensor(
                        out=gv[:, 2:T], in0=P[:, 0:T - 2], scalar=cw(0, jj), in1=gv[:, 2:T], op0=ALU.mult, op1=ALU.add),
                        reads=[psk(bank), gk, "consts"], writes=[gk])
                    if tt > 0:
                        sc.op("dve", lambda e, gv=gv, jj=jj, half=half, j=j, cin=cin: e.scalar_tensor_tensor(
                            out=gv[:, 0:1], in0=cin[:, half, j, 1:2], scalar=cw(1, jj), in1=gv[:, 0:1],
                            op0=ALU.mult, op1=ALU.add), reads=[cink, gk, "consts"], writes=[gk])
                        sc.op("dve", lambda e, gv=gv, jj=jj, half=half, j=j, cin=cin: e.scalar_tensor_tensor(
                            out=gv[:, 0:2], in0=cin[:, half, j, 0:2], scalar=cw(0, jj), in1=gv[:, 0:2],
                            op0=ALU.mult, op1=ALU.add), reads=[cink, gk, "consts"], writes=[gk])
                    if tt < NT - 1:
                        sc.op("act", lambda e, P=P, half=half, j=j, cout=cout: e.copy(out=cout[:, half, j, :], in_=P[:, T - 2:T]),
                              reads=[psk(bank)], writes=[coutk])
                g_ = gb[j % 2]
                v_ = vb[j % 2]
                sc.op("act", lambda e, g_=g_: e.activation(out=g_[:], in_=g_[:], func=AF.Silu),
                      reads=["gb%d" % (j % 2)], writes=["gb%d" % (j % 2)])
                sc.op("pool", lambda e, g_=g_, v_=v_, j=j: e.tensor_tensor(out=aT[:, j, :], in0=g_[:], in1=v_[:], op=ALU.mult),
                      reads=["gb%d" % (j % 2), "vb%d" % (j % 2)], writes=["aT"])
            for dc in range(DC):
                di = tt * DC + dc
                issue_dn(di + ND)
                ds = di % ND
                bank = 4 + dc % 2
                P = PS[bank]
                for j in range(NJ):
                    sc.op("pe", lambda e, P=P, j=j, ds=ds: e.matmul(P[:], lhsT=wd[ds][:, j, :], rhs=aT[:, j, :],
                                                                    start=(j == 0), stop=(j == NJ - 1)),
                          reads=["wd%d" % ds, "aT"], writes=[psk(bank)], inc=(j == NJ - 1))
                sc.op("dve", lambda e, P=P, dc=dc: e.scalar_tensor_tensor(
                    out=r32[:, dc, :], in0=r32[:, dc, :], scalar=ALPHA, in1=P[:], op0=ALU.mult, op1=ALU.add),
                    reads=[psk(bank), "r32"], writes=["r32"])
            layer_norm(r32, "r32", l * 4 + 2, l * 4 + 3, tt, lo, Rout=Rout, final=final, ost_lo=PH_OFF)

    class Prefetch:
        def __init__(self, n, nslots, issue):
            self.n, self.nslots, self.issue, self.issued = n, nslots, issue, 0

        def ensure(self, i):
            while self.issued < min(self.n, i + self.nslots):
                self.issue(self.issued, self.issued % self.nslots)
                self.issued += 1

    MIX_OFF = PH_OFF
    PH2_OFF = MIX_OFF + DC * S * 2

    _mix = []

    def get_mixT():
        if not _mix:
            _mix.append(nc.alloc_sbuf_tensor_at("mixT_all", [128, DC, S], BF16, offset=MIX_OFF))
        t = _mix[0]
        if "mixlo" not in sc.ranges:
            sc.reg("mixlo", MIX_OFF, MIX_OFF + 8 * S * 2)
            sc.reg("mixhi", MIX_OFF + 8 * S * 2, MIX_OFF + 16 * S * 2)
        return t

    def phase_outproj(l, w_out, Rin, Rout):
        mixT = get_mixT()
        A = Arena(PH2_OFF)
        r32s = A.take("r32_", [128, DC, T], F32, 2)
        NW = 2
        wo = A.take("wo", [128, DC, 128], BF16, NW)
        lo = A.o
        wsrc = w_out.rearrange("(c p) f -> p c f", p=128)
        jobs = [(tt, dc) for tt in range(NT) for dc in range(DC)]
        pf = Prefetch(len(jobs), NW, lambda i, s_: sc.dma(
            "pool", "wo%d" % s_, wo[s_][:], wsrc[:, :, jobs[i][1] * 128:(jobs[i][1] + 1) * 128], writes=["wo%d" % s_]))
        for tt in range(NT):
            tsl = slice(tt * T, (tt + 1) * T)
            r32 = r32s[tt % 2]
            rk = "r32_%d" % (tt % 2)
            sc.dma("sp", rk, r32[:], Rin[:, :, tsl].rearrange("c p t -> p c t"),
                   reads=[("R", id(Rin), tt)], writes=[rk])
            for dc in range(DC):
                i = tt * DC + dc
                pf.ensure(i)
                ws = i % NW
                bank = 4 + dc % 2
                P = PS[bank]
                for c in range(DC):
                    rhs = mixT[:, c, tsl]
                    sc.op("pe", lambda e, P=P, ws=ws, c=c, rhs=rhs: e.matmul(P[:], lhsT=wo[ws][:, c, :], rhs=rhs,
                                                                          start=(c == 0), stop=(c == DC - 1)),
                          reads=["wo%d" % ws, "mixlo" if c < 8 else "mixhi"], writes=[psk(bank)], inc=(c == DC - 1))
                sc.op("dve", lambda e, P=P, dc=dc, r32=r32: e.scalar_tensor_tensor(
                    out=r32[:, dc, :], in0=r32[:, dc, :], scalar=ALPHA, in1=P[:], op0=ALU.mult, op1=ALU.add),
                    reads=[psk(bank), rk], writes=[rk])
            layer_norm(r32, rk, l * 4 + 0, l * 4 + 1, tt, lo, Rout=Rout, nbuf=1)

    def phase_fox():
        mixT = get_mixT()
        A = Arena(PH2_OFF)
        qT = A.take("qT", [128, S], BF16, 2)
        kT = A.take("kT", [128, S], BF16, 2)
        Vt = A.take("Vt", [128, 16, 128], BF16, 2)
        wq = A.take("wq", [128, DC, 128], BF16, 2)
        wk = A.take("wk", [128, DC, 128], BF16, 2)
        wv = A.take("wv", [128, DC, 128], BF16, 2)
        PT = A.take("PT", [128, T], BF16, 2)
        negc = A.take("negc", [128, S], F32)
        cneg = A.take("cneg", [128, S], F32)
        negcT = A.take("negcT", [128, 16, 8], F32)
        recip = A.take("recip", [128, T], F32)
        sel = A.take("sel", [128, 8, 128], F32)
        maskb = A.take("maskb", [128, 128], BF16)
        wf = A.take("wf", [128, DC, 8], BF16)
        bfc = A.take("bfc", [128, 2], F32)
        ones_row = sc.sb("ones_row", [128, S], BF16, sc.ranges["qT0"][0])
        win = ev_w_in.rearrange("(c p) f -> p c f", p=128)
        SC = 1.0 / math.sqrt(128.0)

        sc.op("pool", lambda e: e.memset(sel[:], 1.0), writes=["sel"])
        sc.op("pool", lambda e: e.affine_select(out=sel[:], in_=sel[:], pattern=[[-1, 8], [0, 128]],
                                                 compare_op=ALU.is_equal, fill=0.0, base=0, channel_multiplier=1),
              reads=["sel"], writes=["sel"])
        sc.op("pool", lambda e: e.memset(maskb[:], 0.0), writes=["maskb"])
        sc.op("pool", lambda e: e.affine_select(out=maskb[:], in_=maskb[:], pattern=[[1, 128]],
                                                 compare_op=ALU.is_ge, fill=-30000.0, base=0, channel_multiplier=-1),
              reads=["maskb"], writes=["maskb"])
        sc.op("pool", lambda e: e.memset(cneg[:], 0.0), writes=["cneg"])
        sc.op("pool", lambda e: e.memset(ones_row[0:8, :], 1.0), writes=["ones_row"])
        sc.dma("pool", "wf", wf[:], win[:, :, 3072:3080], writes=["wf"])
        sc.dma("sp", "bfc", bfc[0:8, 0:1], ev_b_f, writes=["bfc"])
        sc.op("dve", lambda e: e.tensor_scalar(out=bfc[0:8, 1:2], in0=bfc[0:8, 0:1], scalar1=-1.0, scalar2=None, op0=ALU.mult),
              reads=["bfc"], writes=["bfc"])
        for tt in range(NT):
            tsl = slice(tt * T, (tt + 1) * T)
            for c in range(DC):
                rhs = xTb[:, c, tsl]
                sc.op("pe", lambda e, c=c, rhs=rhs: e.matmul(PS[6][0:8, :], lhsT=wf[:, c, :], rhs=rhs,
                                                              start=(c == 0), stop=(c == DC - 1)),
                      reads=["wf", "xTb%d" % tt], writes=[psk(6)], inc=(c == DC - 1))
            sc.op("act", lambda e, tsl=tsl: e.activation(out=negc[0:8, tsl], in_=PS[6][0:8, :], func=AF.Exp,
                                                          bias=bfc[0:8, 1:2], scale=-1.0),
                  reads=[psk(6), "bfc"], writes=["negc"])
            sc.op("act", lambda e, tsl=tsl: e.activation(out=negc[0:8, tsl], in_=negc[0:8, tsl], func=AF.Ln, bias=1.0, scale=1.0),
                  reads=["negc"], writes=["negc"])
        sc.op("dve", lambda e: e.tensor_tensor_scan(out=negc[0:8, :], data0=ones_row[0:8, :], data1=negc[0:8, :],
                                                    initial=0.0, op0=ALU.mult, op1=ALU.add),
              reads=["negc", "ones_row"], writes=["negc"])
        sc.op("act", lambda e: e.mul(out=cneg[0:8, :], in_=negc[0:8, :], mul=-1.0), reads=["negc"], writes=["cneg"])
        for tb in range(16):
            sc.op("pe", lambda e, tb=tb: e.transpose(PS[7][:, tb * 8:(tb + 1) * 8], negc[0:8, tb * 128:(tb + 1) * 128],
                                                     ident_f[0:8, 0:8]),
                  reads=["negc", "ident_f"], writes=[psk(7)], inc=(tb == 15))
        sc.op("dve", lambda e: e.tensor_copy(out=negcT[:].rearrange("p a b -> p (a b)"), in_=PS[7][:, 0:128]),
              reads=[psk(7)], writes=["negcT"])

        def load_head(h):
            s_ = h % 2
            sc.dma("pool", "wq%d" % s_, wq[s_][:], win[:, :, h * 128:(h + 1) * 128], writes=["wq%d" % s_])
            sc.dma("pool", "wk%d" % s_, wk[s_][:], win[:, :, 1024 + h * 128:1024 + (h + 1) * 128], writes=["wk%d" % s_])
            sc.dma("pool", "wv%d" % s_, wv[s_][:], win[:, :, 2048 + h * 128:2048 + (h + 1) * 128], writes=["wv%d" % s_])

        load_head(0)
        pcnt = [0]
        for h in range(8):
            s_ = h % 2
            if h + 1 < 8:
                load_head(h + 1)
            qk, kk, vk = "qT%d" % s_, "kT%d" % s_, "Vt%d" % s_
            for tt in range(NT):
                tsl = slice(tt * T, (tt + 1) * T)
                for which, (w_, wkey) in enumerate(((wq[s_], "wq%d" % s_), (wk[s_], "wk%d" % s_))):
                    bank = which
                    for c in range(DC):
                        rhs = xTb[:, c, tsl]
                        sc.op("pe", lambda e, bank=bank, w_=w_, c=c, rhs=rhs: e.matmul(
                            PS[bank][:], lhsT=w_[:, c, :], rhs=rhs, start=(c == 0), stop=(c == DC - 1)),
                            reads=[wkey, "xTb%d" % tt], writes=[psk(bank)], inc=(c == DC - 1))
                    if which == 0:
                        sc.op("act", lambda e, tsl=tsl, s_=s_: e.activation(out=qT[s_][:, tsl], in_=PS[0][:], func=AF.Copy, scale=SC),
                              reads=[psk(0)], writes=[qk])
                    else:
                        sc.op("dve", lambda e, tsl=tsl, s_=s_: e.tensor_copy(out=kT[s_][:, tsl], in_=PS[1][:]),
                              reads=[psk(1)], writes=[kk])
            for t4 in range(4):
                bank = t4 % 2
                for q4 in range(4):
                    tb = t4 * 4 + q4
                    for c in range(DC):
                        lhsT = xTb[:, c, tb * 128:(tb + 1) * 128]
                        sc.op("pe", lambda e, bank=bank, q4=q4, c=c, lhsT=lhsT, s_=s_: e.matmul(
                            PS[bank][:, q4 * 128:(q4 + 1) * 128], lhsT=lhsT, rhs=wv[s_][:, c, :],
                            start=(c == 0), stop=(c == DC - 1)),
                            reads=["wv%d" % s_, "xTb%d" % (tb // 4)], writes=[psk(bank)], inc=(c == DC - 1))
                pv = PS[bank][:].rearrange("p (a b) -> p a b", a=4)
                if t4 % 2 == 0:
                    sc.op("act", lambda e, pv=pv, t4=t4, s_=s_: e.copy(out=Vt[s_][:, t4 * 4:(t4 + 1) * 4, :], in_=pv),
                          reads=[psk(bank)], writes=[vk])
                else:
                    sc.op("dve", lambda e, pv=pv, t4=t4, s_=s_: e.tensor_copy(out=Vt[s_][:, t4 * 4:(t4 + 1) * 4, :], in_=pv),
                          reads=[psk(bank)], writes=[vk])
            for Qi in range(4):
                conv_pump(2)
                nblk = 4 * Qi + 4
                for j in range(nblk):
                    n0 = max(0, j * 128 - Qi * T)
                    diag = j * 128 >= Qi * T
                    q0 = Qi * T + n0
                    q1 = (Qi + 1) * T
                    bank = 2 + pcnt[0] % 2
                    ps_ = pcnt[0] % 2
                    pcnt[0] += 1
                    P = PS[bank]
                    sc.op("pe", lambda e, P=P, n0=n0, j=j, q0=q0, q1=q1, s_=s_: e.matmul(
                        P[:, n0:T], lhsT=kT[s_][:, j * 128:(j + 1) * 128], rhs=qT[s_][:, q0:q1], start=True, stop=False),
                        reads=[kk, qk], writes=[psk(bank)], inc=False)
                    sc.op("pe", lambda e, P=P, n0=n0, q0=q0, q1=q1, h=h, diag=diag: e.matmul(
                        P[:, n0:T], lhsT=sel[:, h, :], rhs=cneg[:, q0:q1], start=False, stop=(not diag)),
                        reads=["sel", "cneg"], writes=[psk(bank)], inc=(not diag))
                    if diag:
                        sc.op("pe", lambda e, P=P, n0=n0: e.matmul(
                            P[:, n0:n0 + 128], lhsT=ident_b[:], rhs=maskb[:], start=False, stop=True),
                            reads=["ident_b", "maskb"], writes=[psk(bank)], inc=True)
                    ptk = "PT%d" % ps_
                    sc.op("act", lambda e, P=P, n0=n0, j=j, h=h, ps_=ps_: e.activation(
                        out=PT[ps_][:, n0:T], in_=P[:, n0:T], func=AF.Exp, bias=negcT[:, j, h:h + 1], scale=1.0),
                        reads=[psk(bank), "negcT"], writes=[ptk])
                    last = (j == nblk - 1)
                    sc.op("pe", lambda e, n0=n0, j=j, ps_=ps_, s_=s_, last=last: e.matmul(
                        PS[4][:, n0:T], lhsT=Vt[s_][:, j, :], rhs=PT[ps_][:, n0:T], start=(j == 0), stop=last),
                        reads=[vk, ptk], writes=[psk(4)], inc=False)
                    sc.op("pe", lambda e, n0=n0, j=j, ps_=ps_, last=last: e.matmul(
                        PS[5][:, n0:T], lhsT=ones_b[:], rhs=PT[ps_][:, n0:T], start=(j == 0), stop=last),
                        reads=["ones_b", ptk], writes=[psk(5)], inc=True)
                sc.op("dve", lambda e: e.reciprocal(out=recip[:], in_=PS[5][:]), reads=[psk(5)], writes=["recip"])
                sc.op("dve", lambda e, h=h, Qi=Qi: e.tensor_tensor(out=mixT[:, h, Qi * T:(Qi + 1) * T], in0=PS[4][:], in1=recip[:],
                                                                  op=ALU.mult),
                      reads=[psk(4), "recip"], writes=["mixlo"])
        wu = wq
        pf = Prefetch(8, 2, lambda i, s_: sc.dma("pool", "wq%d" % s_, wu[s_][:], win[:, :, 3080 + i * 128:3080 + (i + 1) * 128],
                                                 writes=["wq%d" % s_]))
        for ch in range(8):
            pf.ensure(ch)
            s_ = ch % 2
            for tt in range(NT):
                tsl = slice(tt * T, (tt + 1) * T)
                bank = tt % 2
                for c in range(DC):
                    rhs = xTb[:, c, tsl]
                    sc.op("pe", lambda e, bank=bank, c=c, rhs=rhs, s_=s_: e.matmul(
                        PS[bank][:], lhsT=wu[s_][:, c, :], rhs=rhs, start=(c == 0), stop=(c == DC - 1)),
                        reads=["wq%d" % s_, "xTb%d" % tt], writes=[psk(bank)], inc=(c == DC - 1))
                if tt % 2 == 0:
                    sc.op("act", lambda e, bank=bank, ch=ch, tsl=tsl: e.copy(out=mixT[:, 8 + ch, tsl], in_=PS[bank][:]),
                          reads=[psk(bank)], writes=["mixhi"])
                else:
                    sc.op("dve", lambda e, bank=bank, ch=ch, tsl=tsl: e.tensor_copy(out=mixT[:, 8 + ch, tsl], in_=PS[bank][:]),
                          reads=[psk(bank)], writes=["mixhi"])

    Wd = nc.dram_tensor("s5w", [8, 128, 16, 128], BF16, kind="Internal").ap()
    TWO_PI = 2.0 * math.pi

    def bk_level(k, l, first, xr_, xi_, xrk, xik, i, AQ_re, AQ_im, AQ_in):
        if first >= S:
            return []
        src = slice(first - k, S - k, 2 * k)
        dst = slice(first, S, 2 * k)
        ar = AQ_re[:, i, l:l + 1]
        ai = AQ_im[:, i, l:l + 1]
        an = AQ_in[:, i, l:l + 1]
        return [
            (lambda e: e.scalar_tensor_tensor(out=xr_[:, dst], in0=xr_[:, src], scalar=ar, in1=xr_[:, dst],
                                              op0=ALU.mult, op1=ALU.add), [xrk, "AQ_re"], [xrk]),
            (lambda e: e.scalar_tensor_tensor(out=xr_[:, dst], in0=xi_[:, src], scalar=an, in1=xr_[:, dst],
                                              op0=ALU.mult, op1=ALU.add), [xrk, xik, "AQ_in"], [xrk]),
            (lambda e: e.scalar_tensor_tensor(out=xi_[:, dst], in0=xi_[:, src], scalar=ar, in1=xi_[:, dst],
                                              op0=ALU.mult, op1=ALU.add), [xik, "AQ_re"], [xik]),
            (lambda e: e.scalar_tensor_tensor(out=xi_[:, dst], in0=xr_[:, src], scalar=ai, in1=xi_[:, dst],
                                              op0=ALU.mult, op1=ALU.add), [xik, xrk, "AQ_im"], [xik]),
        ]

    def phase_s5():
        mixT = get_mixT()
        AP_ = Arena(PH2_OFF)
        yg = AP_.take("yg", [128, 8, S], BF16)
        wslot = AP_.take("wslot", [128, 16, 128], BF16, 2)
        AQ_re = AP_.take("AQ_re", [128, 32, 11], F32)
        AQ_im = AP_.take("AQ_im", [128, 32, 11], F32)
        AQ_in = AP_.take("AQ_in", [128, 32, 11], F32)
        Dcol = AP_.take("Dcol", [128, 8], F32)
        y32 = AP_.take("y32", [128, T], F32, 2)
        gt = AP_.take("gt", [128, T], F32, 2)
        wz1 = AP_.take("wz1", [128, 8, 128], BF16, 2)
        wz2 = AP_.take("wz2", [128, 8, 128], BF16, 2)
        sig = AP_.take("sig", [128, T], F32, 2)
        AX = Arena(XTB_OFF)
        lamraw = AX.take("lamraw", [128, 2, 128], F32)
        names = ["lr", "li", "dt", "lrd", "lid", "mag", "kf", "rr", "rc", "m1", "sn", "cs", "are", "aim",
                 "den", "xr", "gre", "gim", "t1", "t2"]
        sm = {n: AX.take("s5_" + n, [128, 64], F32) for n in names}
        ki = AX.take("s5_ki", [128, 64], I32)
        P_re = AX.take("P_re", [128, 64, 11], F32)
        P_im = AX.take("P_im", [128, 64, 11], F32)
        b_re = AX.take("b_re", [128, 64, 16], F32)
        b_im = AX.take("b_im", [128, 64, 16], F32)
        bb_re = AX.take("bb_re", [128, 64, 16], F32)
        bb_im = AX.take("bb_im", [128, 64, 16], F32)
        btmp = AX.take("btmp", [128, 64, 16], F32)
        CT_re = AX.take("CT_re", [128, 8, 128], F32)
        CT_im = AX.take("CT_im", [128, 8, 128], F32)
        MC = AX.take("MC", [128, 4, 128], F32)
        MB = AX.take("MB", [128, 4, 128], F32)
        dstg = AX.take("dstg", [128, 128], F32)
        stage = AX.take("wstage", [128, 16, 128], BF16, 2)

        def dve(fn, reads, writes):
            sc.op("dve", fn, reads=reads, writes=writes)

        def tt_(out, a, b, op, reads, writes):
            dve(lambda e: e.tensor_tensor(out=out, in0=a, in1=b, op=op), reads, writes)

        K = lambda n: "s5_" + n
        for half in range(2):
            sc.dma("sp", "s5ld", lamraw[0:64, 0, half * 64:(half + 1) * 64], ev_lre, writes=["lamraw"])
            sc.dma("sp", "s5ld", lamraw[0:64, 1, half * 64:(half + 1) * 64], ev_lim, writes=["lamraw"])
            sc.dma("sp", "s5ld", b_re[half * 64:(half + 1) * 64], ev_bre.rearrange("g p c -> p g c"), writes=["b_re"])
            sc.dma("sp", "s5ld", b_im[half * 64:(half + 1) * 64], ev_bim.rearrange("g p c -> p g c"), writes=["b_im"])
            sc.dma("sp", "s5ld", CT_re[:, :, half * 64:(half + 1) * 64], ev_cre.rearrange("(j a) c p -> (a c) j p", a=8),
                   writes=["CT_re"])
            sc.dma("sp", "s5ld", CT_im[:, :, half * 64:(half + 1) * 64], ev_cim.rearrange("(j a) c p -> (a c) j p", a=8),
                   writes=["CT_im"])
        sc.dma("sp", "s5ld", sm["dt"][:], ev_lstep.broadcast_to([128, 64]), writes=[K("dt")])
        sc.dma("sp", "s5ld", dstg[0:8, :], ev_d, writes=["dstg"])
        for w, n in ((0, "lr"), (1, "li")):
            sc.op("pe", lambda e, w=w: e.transpose(PS[0][:, 0:64], lamraw[0:64, w, :], ident_f[0:64, 0:64]),
                  reads=["lamraw", "ident_f"], writes=[psk(0)])
            dve(lambda e, n=n: e.tensor_copy(out=sm[n][:], in_=PS[0][:, 0:64]), [psk(0)], [K(n)])
        sc.op("pe", lambda e: e.transpose(PS[0][:, 0:8], dstg[0:8, :], ident_f[0:8, 0:8]),
              reads=["dstg", "ident_f"], writes=[psk(0)])
        dve(lambda e: e.tensor_copy(out=Dcol[:], in_=PS[0][:, 0:8]), [psk(0)], ["Dcol"])
        sc.op("act", lambda e: e.activation(out=sm["dt"][:], in_=sm["dt"][:], func=AF.Exp), reads=[K("dt")], writes=[K("dt")])
        tt_(sm["lrd"][:], sm["lr"][:], sm["dt"][:], ALU.mult, [K("lr"), K("dt")], [K("lrd")])
        tt_(sm["lid"][:], sm["li"][:], sm["dt"][:], ALU.mult, [K("li"), K("dt")], [K("lid")])
        sc.op("act", lambda e: e.activation(out=sm["mag"][:], in_=sm["lrd"][:], func=AF.Exp), reads=[K("lrd")], writes=[K("mag")])
        dve(lambda e: e.tensor_scalar(out=sm["kf"][:], in0=sm["lid"][:], scalar1=1.0 / TWO_PI, scalar2=0.5,
                                      op0=ALU.mult, op1=ALU.add), [K("lid")], [K("kf")])
        dve(lambda e: e.tensor_copy(out=ki[:], in_=sm["kf"][:]), [K("kf")], ["s5_ki"])
        dve(lambda e: e.tensor_copy(out=sm["kf"][:], in_=ki[:]), ["s5_ki"], [K("kf")])
        dve(lambda e: e.scalar_tensor_tensor(out=sm["rr"][:], in0=sm["kf"][:], scalar=-TWO_PI, in1=sm["lid"][:],
                                             op0=ALU.mult, op1=ALU.add), [K("kf"), K("lid")], [K("rr")])

        def wrap(n):
            dve(lambda e: e.tensor_scalar(out=sm["m1"][:], in0=sm[n][:], scalar1=math.pi, scalar2=None, op0=ALU.is_gt),
                [K(n)], [K("m1")])
            dve(lambda e: e.scalar_tensor_tensor(out=sm[n][:], in0=sm["m1"][:], scalar=-TWO_PI, in1=sm[n][:],
                                                 op0=ALU.mult, op1=ALU.add), [K("m1"), K(n)], [K(n)])
            dve(lambda e: e.tensor_scalar(out=sm["m1"][:], in0=sm[n][:], scalar1=-math.pi, scalar2=None, op0=ALU.is_lt),
                [K(n)], [K("m1")])
            dve(lambda e: e.scalar_tensor_tensor(out=sm[n][:], in0=sm["m1"][:], scalar=TWO_PI, in1=sm[n][:],
                                                 op0=ALU.mult, op1=ALU.add), [K("m1"), K(n)], [K(n)])

        wrap("rr")
        dve(lambda e: e.tensor_scalar(out=sm["rc"][:], in0=sm["rr"][:], scalar1=0.5 * math.pi, scalar2=None, op0=ALU.add),
            [K("rr")], [K("rc")])
        wrap("rc")
        sc.op("act", lambda e: e.activation(out=sm["sn"][:], in_=sm["rr"][:], func=AF.Sin), reads=[K("rr")], writes=[K("sn")])
        sc.op("act", lambda e: e.activation(out=sm["cs"][:], in_=sm["rc"][:], func=AF.Sin), reads=[K("rc")], writes=[K("cs")])
        tt_(sm["are"][:], sm["mag"][:], sm["cs"][:], ALU.mult, [K("mag"), K("cs")], [K("are")])
        tt_(sm["aim"][:], sm["mag"][:], sm["sn"][:], ALU.mult, [K("mag"), K("sn")], [K("aim")])
        tt_(sm["t1"][:], sm["lr"][:], sm["lr"][:], ALU.mult, [K("lr")], [K("t1")])
        tt_(sm["den"][:], sm["li"][:], sm["li"][:], ALU.mult, [K("li")], [K("den")])
        tt_(sm["den"][:], sm["den"][:], sm["t1"][:], ALU.add, [K("den"), K("t1")], [K("den")])
        dve(lambda e: e.reciprocal(out=sm["den"][:], in_=sm["den"][:]), [K("den")], [K("den")])
        dve(lambda e: e.tensor_scalar(out=sm["xr"][:], in0=sm["are"][:], scalar1=-1.0, scalar2=None, op0=ALU.add),
            [K("are")], [K("xr")])
        tt_(sm["t1"][:], sm["xr"][:], sm["lr"][:], ALU.mult, [K("xr"), K("lr")], [K("t1")])
        tt_(sm["t2"][:], sm["aim"][:], sm["li"][:], ALU.mult, [K("aim"), K("li")], [K("t2")])
        tt_(sm["t1"][:], sm["t1"][:], sm["t2"][:], ALU.add, [K("t1"), K("t2")], [K("t1")])
        tt_(sm["gre"][:], sm["t1"][:], sm["den"][:], ALU.mult, [K("t1"), K("den")], [K("gre")])
        tt_(sm["t1"][:], sm["aim"][:], sm["lr"][:], ALU.mult, [K("aim"), K("lr")], [K("t1")])
        tt_(sm["t2"][:], sm["xr"][:], sm["li"][:], ALU.mult, [K("xr"), K("li")], [K("t2")])
        tt_(sm["t1"][:], sm["t1"][:], sm["t2"][:], ALU.subtract, [K("t1"), K("t2")], [K("t1")])
        tt_(sm["gim"][:], sm["t1"][:], sm["den"][:], ALU.mult, [K("t1"), K("den")], [K("gim")])
        gre_b = sm["gre"][:].unsqueeze(2).broadcast_to([128, 64, 16])
        gim_b = sm["gim"][:].unsqueeze(2).broadcast_to([128, 64, 16])
        tt_(bb_re[:], b_re[:], gre_b, ALU.mult, ["b_re", K("gre")], ["bb_re"])
        tt_(btmp[:], b_im[:], gim_b, ALU.mult, ["b_im", K("gim")], ["btmp"])
        tt_(bb_re[:], bb_re[:], btmp[:], ALU.subtract, ["bb_re", "btmp"], ["bb_re"])
        tt_(bb_im[:], b_im[:], gre_b, ALU.mult, ["b_im", K("gre")], ["bb_im"])
        tt_(btmp[:], b_re[:], gim_b, ALU.mult, ["b_re", K("gim")], ["btmp"])
        tt_(bb_im[:], bb_im[:], btmp[:], ALU.add, ["bb_im", "btmp"], ["bb_im"])
        dve(lambda e: e.tensor_copy(out=P_re[:, :, 0], in_=sm["are"][:]), [K("are")], ["P_re"])
        dve(lambda e: e.tensor_copy(out=P_im[:, :, 0], in_=sm["aim"][:]), [K("aim")], ["P_im"])
        for l in range(1, 11):
            tt_(sm["t1"][:], P_re[:, :, l - 1], P_re[:, :, l - 1], ALU.mult, ["P_re"], [K("t1")])
            tt_(sm["t2"][:], P_im[:, :, l - 1], P_im[:, :, l - 1], ALU.mult, ["P_im"], [K("t2")])
            tt_(P_re[:, :, l], sm["t1"][:], sm["t2"][:], ALU.subtract, [K("t1"), K("t2")], ["P_re"])
            tt_(sm["t1"][:], P_re[:, :, l - 1], P_im[:, :, l - 1], ALU.mult, ["P_re", "P_im"], [K("t1")])
            dve(lambda e, l=l: e.tensor_scalar(out=P_im[:, :, l], in0=sm["t1"][:], scalar1=2.0, scalar2=None, op0=ALU.mult),
                [K("t1")], ["P_im"])
        for (src, dst, dk) in ((P_re, AQ_re, "AQ_re"), (P_im, AQ_im, "AQ_im")):
            sv = src[:].rearrange("p (i two) l -> p i two l", two=2)
            sk = "P_re" if src is P_re else "P_im"
            dve(lambda e, sv=sv, dst=dst: e.tensor_copy(out=dst[0:64], in_=sv[0:64, :, 0, :]), [sk], [dk])
            dve(lambda e, sv=sv, dst=dst: e.tensor_copy(out=dst[64:128], in_=sv[64:128, :, 1, :]), [sk], [dk])
        dve(lambda e: e.tensor_scalar(out=AQ_in[:], in0=AQ_im[:], scalar1=-1.0, scalar2=None, op0=ALU.mult), ["AQ_im"], ["AQ_in"])
        sc.op("pool", lambda e: e.memset(MC[:], 0.0), writes=["MC"])
        for m in range(4):
            sc.op("pool", lambda e, m=m: e.memset(MC[0:64, m, 32 * m:32 * m + 16], 1.0), reads=["MC"], writes=["MC"])
            sc.op("pool", lambda e, m=m: e.memset(MC[64:128, m, 32 * m + 16:32 * m + 32], 1.0), reads=["MC"], writes=["MC"])
        for m in range(4):
            sc.op("pe", lambda e, m=m: e.transpose(PS[1][:, m * 128:(m + 1) * 128], MC[:, m, :], ident_f[:]),
                  reads=["MC", "ident_f"], writes=[psk(1)], inc=(m == 3))
        dve(lambda e: e.tensor_copy(out=MB[:].rearrange("p a b -> p (a b)"), in_=PS[1][:]), [psk(1)], ["MB"])
        for j in range(8):
            st_ = stage[j % 2]
            stk = "wstage%d" % (j % 2)
            srcs = ((bb_re[:, 8 * j:8 * j + 8, :].rearrange("p a b -> p (a b)"), "bb_re", 2, 0),
                    (bb_im[:, 8 * j:8 * j + 8, :].rearrange("p a b -> p (a b)"), "bb_im", 2, 1),
                    (CT_re[:, j, :], "CT_re", 3, 0), (CT_im[:, j, :], "CT_im", 3, 1))
            for (src, sk, bank, half) in srcs:
                sc.op("pe", lambda e, src=src, bank=bank, half=half: e.transpose(
                    PS[bank][:, half * 128:(half + 1) * 128], src, ident_f[:]),
                    reads=[sk, "ident_f"], writes=[psk(bank)], inc=True)
            for m in range(4):
                dve(lambda e, st_=st_, m=m: e.tensor_tensor(out=st_[:, 0 + m, :], in0=PS[2][:, 0:128], in1=MB[:, m, :], op=ALU.mult),
                    [psk(2), "MB"], [stk])
                dve(lambda e, st_=st_, m=m: e.tensor_tensor(out=st_[:, 4 + m, :], in0=PS[2][:, 128:256], in1=MB[:, m, :], op=ALU.mult),
                    [psk(2), "MB"], [stk])
                dve(lambda e, st_=st_, m=m: e.tensor_tensor(out=st_[:, 8 + m, :], in0=PS[3][:, 0:128], in1=MC[:, m, :], op=ALU.mult),
                    [psk(3), "MC"], [stk])
                dve(lambda e, st_=st_, m=m: e.scalar_tensor_tensor(out=st_[:, 12 + m, :], in0=PS[3][:, 128:256], scalar=-1.0,
                                                                   in1=MC[:, m, :], op0=ALU.mult, op1=ALU.mult),
                    [psk(3), "MC"], [stk])
            sc.dma("sp", stk, Wd[j], st_[:], reads=[stk], writes=[("Wd", j)])
        mixT2 = mixT
        AX2 = Arena(XTB_OFF)
        X_re = AX2.take("X_re", [128, S], F32, 2)
        X_im = AX2.take("X_im", [128, S], F32, 2)
        H_re = AX2.take("H_re", [128, S], BF16, 4)
        H_im = AX2.take("H_im", [128, S], BF16, 4)
        C1 = 2.0 * math.sqrt(2.0 / math.pi)
        C2 = C1 * 0.044715
        bcnt = [0]
        for j in range(8):
            wsl = wslot[j % 2]
            wk_ = "wslot%d" % (j % 2)
            sc.dma("sp", wk_, wsl[:], Wd[j], reads=[("Wd", j)], writes=[wk_])
            for mp in (0, 2):
                conv_pump(8)
                chains = []
                for m in (mp, mp + 1):
                    i = 4 * j + m
                    xs = i % 2
                    xr_, xi_ = X_re[xs], X_im[xs]
                    xrk, xik = "X_re%d" % xs, "X_im%d" % xs
                    for tt in range(NT):
                        tsl = slice(tt * T, (tt + 1) * T)
                        for half, (dst, dk) in enumerate(((xr_, xrk), (xi_, xik))):
                            bank = bcnt[0] % 4
                            bcnt[0] += 1
                            sc.op("pe", lambda e, bank=bank, half=half, m=m, wsl=wsl, j=j, tsl=tsl: e.matmul(
                                PS[bank][:], lhsT=wsl[:, 4 * half + m, :], rhs=mixT2[:, 8 + j, tsl], start=True, stop=True),
                                reads=[wk_, "mixhi"], writes=[psk(bank)], inc=True)
                            sc.op("act", lambda e, bank=bank, dst=dst, tsl=tsl: e.copy(out=dst[:, tsl], in_=PS[bank][:]),
                                  reads=[psk(bank)], writes=[dk])
                    ops = []
                    for l in range(11):
                        ops += bk_level(1 << l, l, 2 * (1 << l) - 1, xr_, xi_, xrk, xik, i, AQ_re, AQ_im, AQ_in)
                    for l in range(9, -1, -1):
                        ops += bk_level(1 << l, l, 3 * (1 << l) - 1, xr_, xi_, xrk, xik, i, AQ_re, AQ_im, AQ_in)
                    chains.append((m, xr_, xi_, xrk, xik, ops))
                for idx in range(max(len(c[5]) for c in chains)):
                    for c_ in chains:
                        if idx < len(c_[5]):
                            fn, rd, wr = c_[5][idx]
                            sc.op("dve", fn, reads=rd, writes=wr)
                for (m, xr_, xi_, xrk, xik, _) in chains:
                    sc.op("act", lambda e, m=m, xr_=xr_: e.copy(out=H_re[m][:], in_=xr_[:]), reads=[xrk], writes=["H_re%d" % m])
                    sc.op("act", lambda e, m=m, xi_=xi_: e.copy(out=H_im[m][:], in_=xi_[:]), reads=[xik], writes=["H_im%d" % m])
            for tt in range(NT):
                tsl = slice(tt * T, (tt + 1) * T)
                bank = 4 + tt % 2
                for m in range(4):
                    sc.op("pe", lambda e, bank=bank, m=m, wsl=wsl, tsl=tsl: e.matmul(
                        PS[bank][:], lhsT=wsl[:, 8 + m, :], rhs=H_re[m][:, tsl], start=(m == 0), stop=False),
                        reads=[wk_, "H_re%d" % m], writes=[psk(bank)], inc=False)
                    sc.op("pe", lambda e, bank=bank, m=m, wsl=wsl, tsl=tsl: e.matmul(
                        PS[bank][:], lhsT=wsl[:, 12 + m, :], rhs=H_im[m][:, tsl], start=False, stop=(m == 3)),
                        reads=[wk_, "H_im%d" % m], writes=[psk(bank)], inc=(m == 3))
                y_ = y32[tt % 2]
                g_ = gt[tt % 2]
                yk, gk = "y32%d" % (tt % 2), "gt%d" % (tt % 2)
                sc.op("dve", lambda e, bank=bank, y_=y_, j=j, tsl=tsl: e.scalar_tensor_tensor(
                    out=y_[:], in0=mixT2[:, 8 + j, tsl], scalar=Dcol[:, j:j + 1], in1=PS[bank][:], op0=ALU.mult, op1=ALU.add),
                    reads=[psk(bank), "mixhi", "Dcol"], writes=[yk])
                sc.op("pool", lambda e, y_=y_, g_=g_: e.tensor_tensor(out=g_[:], in0=y_[:], in1=y_[:], op=ALU.mult),
                      reads=[yk], writes=[gk])
                sc.op("pool", lambda e, g_=g_: e.tensor_scalar(out=g_[:], in0=g_[:], scalar1=C2, scalar2=C1, op0=ALU.mult, op1=ALU.add),
                      reads=[gk], writes=[gk])
                sc.op("pool", lambda e, y_=y_, g_=g_: e.tensor_tensor(out=g_[:], in0=g_[:], in1=y_[:], op=ALU.mult),
                      reads=[gk, yk], writes=[gk])
                sc.op("act", lambda e, g_=g_: e.activation(out=g_[:], in_=g_[:], func=AF.Sigmoid), reads=[gk], writes=[gk])
                sc.op("pool", lambda e, y_=y_, g_=g_, j=j, tsl=tsl: e.tensor_tensor(out=yg[:, j, tsl], in0=g_[:], in1=y_[:], op=ALU.mult),
                      reads=[gk, yk], writes=["yg"])
        wgl = ev_w_glu.rearrange("(c p) f -> p c f", p=128)

        def issue_glu(i, s_):
            sc.dma("pool", "wz1%d" % s_, wz1[s_][:], wgl[:, :, i * 128:(i + 1) * 128], writes=["wz1%d" % s_])
            sc.dma("pool", "wz2%d" % s_, wz2[s_][:], wgl[:, :, 1024 + i * 128:1024 + (i + 1) * 128], writes=["wz2%d" % s_])

        pf = Prefetch(8, 2, issue_glu)
        for e_ in range(8):
            pf.ensure(e_)
            s_ = e_ % 2
            for tt in range(NT):
                tsl = slice(tt * T, (tt + 1) * T)
                ba, bb_ = (tt % 2) * 2, (tt % 2) * 2 + 1
                for (bank, w_, wkey) in ((ba, wz1[s_], "wz1%d" % s_), (bb_, wz2[s_], "wz2%d" % s_)):
                    for c in range(8):
                        sc.op("pe", lambda e, bank=bank, w_=w_, c=c, tsl=tsl: e.matmul(
                            PS[bank][:], lhsT=w_[:, c, :], rhs=yg[:, c, tsl], start=(c == 0), stop=(c == 7)),
                            reads=[wkey, "yg"], writes=[psk(bank)], inc=(c == 7))
                sg_ = sig[tt % 2]
                sgk = "sig%d" % (tt % 2)
                sc.op("act", lambda e, bb_=bb_, sg_=sg_: e.activation(out=sg_[:], in_=PS[bb_][:], func=AF.Sigmoid),
                      reads=[psk(bb_)], writes=[sgk])
                sc.op("dve", lambda e, ba=ba, sg_=sg_, e_=e_, tsl=tsl: e.tensor_tensor(
                    out=mixT2[:, 8 + e_, tsl], in0=PS[ba][:], in1=sg_[:], op=ALU.mult),
                    reads=[psk(ba), sgk], writes=["mixhi"])

    INVF = [float(v) for v in (np.float32(500000.0) ** (-(np.arange(8, dtype=np.float32) / np.float32(8.0))))]

    def psb(i):
        return PS[i][:].bitcast(BF16)

    def phase_swa():
        mixT = get_mixT()
        A = Arena(PH2_OFF)
        wcol = A.take("wcol", [128, DC, 256], BF16, 2)
        qb = A.take("qb", [128, 4, 64], BF16, 2)
        kd = A.take("kd", [128, 4, 2, 64], BF16, 2)
        qTp = A.take("qTp", [128, S], BF16, 4)
        kTd = A.take("kTd", [128, S], BF16, 4)
        Vs = A.take("Vs", [128, 16, 256], BF16)
        PTa = A.take("PTa", [128, T], BF16, 2)
        PTb = A.take("PTb", [128, T], BF16, 2)
        maskA = A.take("maskA", [128, 4, 128], BF16)
        maskB = A.take("maskB", [128, 4, 128], BF16)
        cosk = A.take("cosk", [128, 16, 8], F32)
        sink = A.take("sink", [128, 16, 8], F32)
        cosq = A.take("cosq", [128, 16, 8], F32)
        sinq = A.take("sinq", [128, 16, 8], F32)
        ang = A.take("ang", [128, 16, 8], F32)
        angc = A.take("angc", [128, 16, 8], F32)
        rtmp = A.take("rtmp", [128, 16, 8], F32)
        rki = A.take("rki", [128, 16, 8], I32)
        posi = A.take("posi", [128, 128], I32)
        posf = A.take("posf", [128, 128], F32)
        posT = A.take("posT", [128, 16], F32)
        invf = A.take("invf", [128, 8], F32)
        es = A.take("es", [128, 32], F32)
        esP = A.take("esP", [128, 16], F32)
        rt = A.take("rt", [128, 6, 4, 8], F32, 2)
        rot = A.take("rot", [128, 4, 16], F32, 2)
        rc = A.take("rc", [128, 256], F32, 2)
        win = od_w_in.rearrange("(c p) f -> p c f", p=128)

        def dve(fn, reads, writes):
            sc.op("dve", fn, reads=reads, writes=writes)

        sc.op("pool", lambda e: e.memset(maskA[:], 0.0), writes=["maskA"])
        sc.op("pool", lambda e: e.affine_select(out=maskA[:], in_=maskA[:], pattern=[[0, 4], [-1, 128]],
                                                 compare_op=ALU.is_ge, fill=-30000.0, base=-1, channel_multiplier=1),
              reads=["maskA"], writes=["maskA"])
        sc.op("pool", lambda e: e.memset(maskB[:], 0.0), writes=["maskB"])
        sc.op("pool", lambda e: e.affine_select(out=maskB[:], in_=maskB[:], pattern=[[0, 4], [1, 128]],
                                                 compare_op=ALU.is_ge, fill=-30000.0, base=0, channel_multiplier=-1),
              reads=["maskB"], writes=["maskB"])
        sc.dma("sp", "swld", es[:], od_sinks.broadcast_to([128, 32]), writes=["es"])
        sc.op("act", lambda e: e.activation(out=es[:], in_=es[:], func=AF.Exp), reads=["es"], writes=["es"])
        esv = es[:].rearrange("p (a two) -> p a two", two=2)
        dve(lambda e: e.tensor_copy(out=esP[0:64, :], in_=esv[0:64, :, 0]), ["es"], ["esP"])
        dve(lambda e: e.tensor_copy(out=esP[64:128, :], in_=esv[64:128, :, 1]), ["es"], ["esP"])
        sc.dma("sp", "swld", posi[0:16, :], pos_in, writes=["posi"])
        dve(lambda e: e.tensor_copy(out=posf[0:16, :], in_=posi[0:16, :]), ["posi"], ["posf"])
        sc.op("pe", lambda e: e.transpose(PS[7][:, 0:16], posf[0:16, :], ident_f[0:16, 0:16]),
              reads=["posf", "ident_f"], writes=[psk(7)])
        dve(lambda e: e.tensor_copy(out=posT[:], in_=PS[7][:, 0:16]), [psk(7)], ["posT"])
        for f in range(8):
            sc.op("pool", lambda e, f=f: e.memset(invf[:, f:f + 1], INVF[f]), writes=["invf"])
        dve(lambda e: e.tensor_tensor(out=ang[:], in0=posT[:].unsqueeze(2).broadcast_to([128, 16, 8]),
                                      in1=invf[:].unsqueeze(1).broadcast_to([128, 16, 8]), op=ALU.mult),
            ["posT", "invf"], ["ang"])

        def reduce_(x, xk):
            dve(lambda e: e.tensor_scalar(out=rtmp[:], in0=x[:], scalar1=1.0 / TWO_PI, scalar2=0.5, op0=ALU.mult, op1=ALU.add),
                [xk], ["rtmp"])
            dve(lambda e: e.tensor_copy(out=rki[:], in_=rtmp[:]), ["rtmp"], ["rki"])
            dve(lambda e: e.tensor_copy(out=rtmp[:], in_=rki[:]), ["rki"], ["rtmp"])
            dve(lambda e: e.scalar_tensor_tensor(out=x[:], in0=rtmp[:], scalar=-TWO_PI, in1=x[:], op0=ALU.mult, op1=ALU.add),
                ["rtmp", xk], [xk])
            for (thr, op_, add) in ((math.pi, ALU.is_gt, -TWO_PI), (-math.pi, ALU.is_lt, TWO_PI)):
                dve(lambda e, thr=thr, op_=op_: e.tensor_scalar(out=rtmp[:], in0=x[:], scalar1=thr, scalar2=None, op0=op_),
                    [xk], ["rtmp"])
                dve(lambda e, add=add: e.scalar_tensor_tensor(out=x[:], in0=rtmp[:], scalar=add, in1=x[:], op0=ALU.mult, op1=ALU.add),
                    ["rtmp", xk], [xk])

        dve(lambda e: e.tensor_scalar(out=angc[:], in0=ang[:], scalar1=0.5 * math.pi, scalar2=None, op0=ALU.add), ["ang"], ["angc"])
        reduce_(ang, "ang")
        reduce_(angc, "angc")
        sc.op("act", lambda e: e.activation(out=sink[:], in_=ang[:], func=AF.Sin), reads=["ang"], writes=["sink"])
        sc.op("act", lambda e: e.activation(out=cosk[:], in_=angc[:], func=AF.Sin), reads=["angc"], writes=["cosk"])
        sc.op("act", lambda e: e.mul(out=sinq[:], in_=sink[:], mul=0.125), reads=["sink"], writes=["sinq"])
        sc.op("act", lambda e: e.mul(out=cosq[:], in_=cosk[:], mul=0.125), reads=["cosk"], writes=["cosq"])
        for hq in range(4):
            sc.op("pool", lambda e, hq=hq: e.memset(qTp[hq][:], 0.0), writes=["qTp%d" % hq])

        units = [("k", 2048), ("v", 2304)] + [("q", 256 * u) for u in range(8)]
        pf = Prefetch(len(units), 2, lambda i, s_: sc.dma("pool", "wcol%d" % s_, wcol[s_][:],
                                                          win[:, :, units[i][1]:units[i][1] + 256], writes=["wcol%d" % s_]))
        ecnt = [0]

        def rope(P4, cs, sn, tb, o1, o2, okeys, slot):
            r_ = rt[slot]
            rk = "rt%d" % slot
            cb = cs[:, tb, :].unsqueeze(1).broadcast_to([128, 4, 8])
            sb_ = sn[:, tb, :].unsqueeze(1).broadcast_to([128, 4, 8])
            x1 = P4[:, :, 0:8]
            x2 = P4[:, :, 8:16]
            ck = ["cosk", "sink", "cosq", "sinq"]
            dve(lambda e: e.tensor_tensor(out=r_[:, 0], in0=x1, in1=cb, op=ALU.mult), okeys["ps"] + ck, [rk])
            dve(lambda e: e.tensor_tensor(out=r_[:, 1], in0=x2, in1=sb_, op=ALU.mult), okeys["ps"] + ck, [rk])
            dve(lambda e: e.tensor_tensor(out=r_[:, 2], in0=x2, in1=cb, op=ALU.mult), okeys["ps"] + ck, [rk])
            dve(lambda e: e.tensor_tensor(out=r_[:, 3], in0=x1, in1=sb_, op=ALU.mult), okeys["ps"] + ck, [rk])
            dve(lambda e: e.tensor_tensor(out=o1, in0=r_[:, 0], in1=r_[:, 1], op=ALU.subtract), [rk], okeys["out"])
            dve(lambda e: e.tensor_tensor(out=o2, in0=r_[:, 2], in1=r_[:, 3], op=ALU.add), [rk], okeys["out"])

        for ui, (kind, col0) in enumerate(units):
            pf.ensure(ui)
            ws = ui % 2
            wkey = "wcol%d" % ws
            quad = ui - 2
            for tb in range(16):
                bank = (tb // 2) % 2
                half = tb % 2
                Pfull = PS[bank][:, half * 256:(half + 1) * 256]
                for c in range(DC):
                    lhsT = xTb[:, c, tb * 128:(tb + 1) * 128]
                    sc.op("pe", lambda e, Pfull=Pfull, lhsT=lhsT, ws=ws, c=c: e.matmul(
                        Pfull, lhsT=lhsT, rhs=wcol[ws][:, c, :], start=(c == 0), stop=(c == DC - 1)),
                        reads=[wkey, "xTb%d" % (tb // 4)], writes=[psk(bank)], inc=(c == DC - 1))
                P4 = Pfull.rearrange("p (h d) -> p h d", h=4)
                slot = ecnt[0] % 2
                ecnt[0] += 1
                if kind == "v":
                    sc.op("act", lambda e, Pfull=Pfull, tb=tb: e.copy(out=Vs[:, tb, :], in_=Pfull), reads=[psk(bank)], writes=["Vs"])
                elif kind == "k":
                    kd_ = kd[slot]
                    kk = "kd%d" % slot
                    sc.op("act", lambda e, P4=P4, kd_=kd_: e.copy(out=kd_[:, :, 0, :], in_=P4), reads=[psk(bank)], writes=[kk])
                    ro = rot[slot]
                    rok = "rot%d" % slot
                    rope(P4, cosk, sink, tb, ro[:, :, 0:8], ro[:, :, 8:16], {"ps": [psk(bank)], "out": [rok]}, slot)
                    dve(lambda e, kd_=kd_, ro=ro: e.tensor_copy(out=kd_[:, :, 0, 0:16], in_=ro[:]), [rok, kk], [kk])
                    dve(lambda e, kd_=kd_: e.tensor_copy(out=kd_[:, :, 1, :], in_=kd_[:, :, 0, :]), [kk], [kk])
                    for kv in range(4):
                        sc.op("pe", lambda e, kv=kv, kd_=kd_: e.transpose(
                            psb(2)[:, kv * 128:(kv + 1) * 128], kd_[:, kv, :, :].rearrange("p a b -> p (a b)"), ident_b[:]),
                            reads=[kk, "ident_b"], writes=[psk(2)], inc=(kv == 3))
                    for kv in range(4):
                        eng = "act" if kv % 2 == 0 else "dve"
                        if eng == "act":
                            sc.op("act", lambda e, kv=kv, tb=tb: e.copy(out=kTd[kv][:, tb * 128:(tb + 1) * 128],
                                                                        in_=psb(2)[:, kv * 128:(kv + 1) * 128]),
                                  reads=[psk(2)], writes=["kTd%d" % kv])
                        else:
                            dve(lambda e, kv=kv, tb=tb: e.tensor_copy(out=kTd[kv][:, tb * 128:(tb + 1) * 128],
                                                                       in_=psb(2)[:, kv * 128:(kv + 1) * 128]),
                                [psk(2)], ["kTd%d" % kv])
                else:
                    qb_ = qb[slot]
                    qk_ = "qb%d" % slot
                    sc.op("act", lambda e, P4=P4, qb_=qb_: e.activation(out=qb_[:], in_=P4, func=AF.Copy, scale=0.125),
                          reads=[psk(bank)], writes=[qk_])
                    rope(P4, cosq, sinq, tb, qb_[:, :, 0:8], qb_[:, :, 8:16], {"ps": [psk(bank), qk_], "out": [qk_]}, slot)
                    for mc in range(2):
                        sc.op("pe", lambda e, mc=mc, qb_=qb_: e.transpose(
                            psb(2)[:, mc * 128:(mc + 1) * 128], qb_[:, 2 * mc:2 * mc + 2, :].rearrange("p a b -> p (a b)"),
                            ident_b[:]), reads=[qk_, "ident_b"], writes=[psk(2)], inc=(mc == 1))
                    for hq in range(4):
                        e_, mc = hq % 2, hq // 2
                        src = psb(2)[e_ * 64:(e_ + 1) * 64, mc * 128:(mc + 1) * 128]
                        dst = qTp[hq][e_ * 64:(e_ + 1) * 64, tb * 128:(tb + 1) * 128]
                        if hq % 2 == 0:
                            sc.op("act", lambda e, src=src, dst=dst: e.copy(out=dst, in_=src), reads=[psk(2)], writes=["qTp%d" % hq])
                        else:
                            dve(lambda e, src=src, dst=dst: e.tensor_copy(out=dst, in_=src), [psk(2)], ["qTp%d" % hq])
            if kind != "q":
                continue
            hk = quad // 2
            for i in range(16):
                par = i % 2
                pa, pb_ = PTa[par], PTb[par]
                pak, pbk = "PTa%d" % par, "PTb%d" % par
                blocks = ([(3, i - 1, maskA, "maskA", pa, pak)] if i > 0 else []) + [(4, i, maskB, "maskB", pb_, pbk)]
                for (bank, kb, mk, mkk, pt, ptk) in blocks:
                    for hq in range(4):
                        sc.op("pe", lambda e, bank=bank, kb=kb, hq=hq, i=i, hk=hk: e.matmul(
                            PS[bank][:, hq * 128:(hq + 1) * 128], lhsT=kTd[hk][:, kb * 128:(kb + 1) * 128],
                            rhs=qTp[hq][:, i * 128:(i + 1) * 128], start=(hq == 0), stop=False),
                            reads=["kTd%d" % hk, "qTp%d" % hq], writes=[psk(bank)], inc=False)
                    sc.op("pe", lambda e, bank=bank, mk=mk: e.matmul(
                        PS[bank][:], lhsT=ident_b[:], rhs=mk[:].rearrange("p a b -> p (a b)"), start=False, stop=True),
                        reads=["ident_b", mkk], writes=[psk(bank)], inc=True)
                    sc.op("act", lambda e, bank=bank, pt=pt: e.activation(out=pt[:], in_=PS[bank][:], func=AF.Exp),
                          reads=[psk(bank)], writes=[ptk])
                nb = len(blocks)
                for (obank, use_v) in ((5, True), (6, False)):
                    for hq in range(4):
                        e_, mc = hq % 2, hq // 2
                        out = PS[obank][e_ * 64:(e_ + 1) * 64, mc * 128:(mc + 1) * 128]
                        for bi, (bank, kb, mk, mkk, pt, ptk) in enumerate(blocks):
                            lhsT = Vs[:, kb, hk * 64:(hk + 1) * 64] if use_v else ones_b[:, 0:64]
                            rhs = pt[:, hq * 128:(hq + 1) * 128]
                            first = (bi == 0 and hq < 2)
                            lastmm = (hq == 3 and bi == nb - 1)
                            stopf = (hq >= 2 and bi == nb - 1)
                            sc.op("pe", lambda e, out=out, lhsT=lhsT, rhs=rhs, first=first, stopf=stopf, e_=e_: e.matmul(
                                out, lhsT=lhsT, rhs=rhs, start=first, stop=stopf, tile_position=(0, e_ * 64)),
                                reads=[ptk, "Vs" if use_v else "ones_b"], writes=[psk(obank)], inc=lastmm)
                rc_ = rc[par]
                rck = "rc%d" % par
                for mc in range(2):
                    dve(lambda e, mc=mc, rc_=rc_, quad=quad: e.tensor_scalar(
                        out=rc_[:, mc * 128:(mc + 1) * 128], in0=PS[6][:, mc * 128:(mc + 1) * 128],
                        scalar1=esP[:, quad * 2 + mc:quad * 2 + mc + 1], scalar2=None, op0=ALU.add),
                        [psk(6), "esP"], [rck])
                dve(lambda e, rc_=rc_: e.reciprocal(out=rc_[:], in_=rc_[:]), [rck], [rck])
                dve(lambda e, rc_=rc_, i=i, quad=quad: e.tensor_tensor(
                    out=mixT[:, quad * 2:quad * 2 + 2, i * 128:(i + 1) * 128],
                    in0=PS[5][:, 0:256].rearrange("p (a b) -> p a b", a=2),
                    in1=rc_[:].rearrange("p (a b) -> p a b", a=2), op=ALU.mult),
                    [psk(5), rck], ["mixlo" if quad < 4 else "mixhi"])

    def dbg_dump_mix():
        mixT = get_mixT()
        ov = out_d.rearrange("(c p) t -> p c t", p=128)
        for c in range(DC):
            sc.dma("pool", "dbgm", ov[:, c, :], mixT[:, c, :], reads=["mixlo", "mixhi"])

    def dbg_copy_R(R):
        A = Arena(PH_OFF)
        bufs = A.take("dbgb", [128, S], F32, 2)
        for c in range(DC):
            k = "dbgb%d" % (c % 2)
            sc.dma("sp", k, bufs[c % 2][:], R[c], reads=[("R", id(R), tt) for tt in range(NT)], writes=[k])
            sc.dma("sp", k, out_d[c * 128:(c + 1) * 128, :], bufs[c % 2][:], reads=[k])

    setup_consts()
    sc.phase_reset()
    if mode == "full":
        phase_input(x_in, RA)
        sc.phase_reset()
        phase_fox()
        sc.phase_reset()
        phase_s5()
        sc.phase_reset()
        phase_outproj(0, ev_w_out, RA, RB)
        sc.phase_reset()
        phase_ffn(0, RB, RA, final=False)
        sc.phase_reset()
        phase_swa()
        sc.phase_reset()
        phase_outproj(1, od_w_out, RA, RB)
        sc.phase_reset()
        phase_ffn(1, RB, RA, final=True)
    elif mode == "consts":
        sc.dma("sp", "dbg", out_d[0:128, 0:128], ident_f[:], reads=["ident_f"])
        sc.dma("sp", "dbg", out_d[128:256, 0:86 * 4].rearrange("p (a b) -> p a b", a=4), convp[:, 0, :, :], reads=["consts"])
        sc.dma("sp", "dbg", out_d[256:384, 0:128].rearrange("p (a b) -> p a b", a=8), lnp[:], reads=["consts"])
    elif mode == "p0":
        phase_input(x_in, RA)
        sc.phase_reset()
        if 'dbg' not in _SKIP:
            dbg_copy_R(RA)
    elif mode == "fox":
        phase_input(x_in, RA)
        sc.phase_reset()
        phase_fox()
        sc.phase_reset()
        dbg_dump_mix()
    elif mode == "s5":
        phase_input(x_in, RA)
        sc.phase_reset()
        phase_fox()
        sc.phase_reset()
        phase_s5()
        sc.phase_reset()
        dbg_dump_mix()
    elif mode == "swa":
        phase_input(x_in, RA)
        sc.phase_reset()
        phase_swa()
        sc.phase_reset()
        phase_outproj(1, od_w_out, RA, RB)
        sc.phase_reset()
        dbg_copy_R(RB)
    elif mode == "outproj0":
        phase_input(x_in, RA)
        sc.phase_reset()
        phase_fox()
        sc.phase_reset()
        phase_outproj(0, ev_w_out, RA, RB)
        sc.phase_reset()
        dbg_copy_R(RB)
    elif mode == "ffn0":
        phase_input(x_in, RA)
        sc.phase_reset()
        phase_ffn(0, RA, RB, final=True)
    else:
        raise NotImplementedError(mode)
    sc.barrier()
    sc.emit()
    return nc, es


_CACHE = {}


def _prep_inputs(inp, b):
    f = np.ascontiguousarray
    m = {
        "x": f(inp["x"][b]),
        "pos": f(inp["positions"][b].reshape(16, 128).astype(np.int32)),
        "ev_w_in": f(inp["ev_w_in"][0]),
        "ev_b_f": f(inp["ev_b_f"][0].reshape(8, 1)),
        "ev_lre": f(inp["ev_lambda_re"][0]),
        "ev_lim": f(inp["ev_lambda_im"][0]),
        "ev_lstep": f(inp["ev_log_step"][0].reshape(1, 64)),
        "ev_bre": f(inp["ev_ssm_b_re"][0]),
        "ev_bim": f(inp["ev_ssm_b_im"][0]),
        "ev_cre": f(inp["ev_ssm_c_re"][0]),
        "ev_cim": f(inp["ev_ssm_c_im"][0]),
        "ev_d": f(inp["ev_ssm_d"][0].reshape(8, 128)),
        "ev_w_glu": f(inp["ev_w_glu"][0]),
        "ev_w_out": f(inp["ev_w_out"][0]),
        "od_w_in": f(inp["od_w_in"][0]),
        "od_sinks": f(inp["od_sinks"][0].reshape(1, 32)),
        "od_w_out": f(inp["od_w_out"][0]),
        "ln_mix_g": f(inp["ln_mix_g"].reshape(2, 16, 128)),
        "ln_mix_b": f(inp["ln_mix_b"].reshape(2, 16, 128)),
        "ffn_w_up": f(inp["ffn_w_up"]),
        "ffn_conv_w": f(inp["ffn_conv_w"].reshape(2, 3, 86, 128)),
        "ffn_conv_b": f(inp["ffn_conv_b"].reshape(2, 86, 128)),
        "ffn_w_down": f(inp["ffn_w_down"]),
        "ln_ffn_g": f(inp["ln_ffn_g"].reshape(2, 16, 128)),
        "ln_ffn_b": f(inp["ln_ffn_b"].reshape(2, 16, 128)),
    }
    return m


def run(inputs, mode="full", cores=8, trace=False):
    nc, es = build(mode)
    in_maps = [_prep_inputs(inputs, b) for b in range(cores)]
    res = run_bass_kernel_spmd(nc, in_maps, core_ids=list(range(cores)), trace=trace)
    es.close()
    return res


def kernel(**inputs):
    res = run(inputs, "full", 8)
    out = np.stack([np.asarray(r["out"]) for r in res.results], axis=0)
    return out.astype(np.float32)
```

```python
import math
import os
_SKIP = set(os.environ.get('K_SKIP', '').split(','))
from contextlib import ExitStack

import numpy as np
import concourse.bass as bass
import concourse.mybir as mybir
from concourse.bass_utils import run_bass_kernel_spmd

F32 = mybir.dt.float32
BF16 = mybir.dt.bfloat16
I32 = mybir.dt.int32
AF = mybir.ActivationFunctionType
ALU = mybir.AluOpType

S = 2048
D = 2048
DC = 16
T = 512
NT = 4
DFF = 5504
NJ = 43
ALPHA = (2.0 * 2) ** 0.25
LN_EPS = 1e-5
DMA_SCRATCH = 8192
SBUF_BASE = DMA_SCRATCH
SBUF_LIMIT = 16384 + 212800

ENGS = ("pe", "act", "dve", "pool", "sp")
SAME_SYNC = True


class Sched:
    def __init__(self, nc, es):
        self.nc = nc
        self.es = es
        self.streams = {e: [] for e in ENGS}
        self.cnt = {e: 0 for e in ENGS}
        self.waited = {e: {} for e in ENGS}
        self.lastw = {}
        self.readers = {}
        self.pend_r = {e: [] for e in ENGS}
        self.pend_w = {e: [] for e in ENGS}
        self.sems = {}
        self.dcount = {}
        self.ranges = {}
        self.alias = {}
        self.tcache = {}
        self.persist = set()
        for e in ENGS:
            self.sems["E_" + e] = es.enter_context(nc.semaphore("E_" + e))

    def sb(self, key, shape, dtype, off, persistent=False):
        nbytes = int(np.prod(shape[1:])) * (2 if dtype == BF16 else 4)
        assert off % 4 == 0 and off + nbytes <= SBUF_LIMIT, (key, off, nbytes)
        ck = (key, off, tuple(shape), str(dtype))
        if ck in self.tcache and self.ranges.get(key) == (off, off + nbytes):
            return self.tcache[ck]
        t = self.nc.alloc_sbuf_tensor_at(key, list(shape), dtype, offset=off)
        self.tcache[ck] = t
        assert key not in self.ranges, ("key re-registered at a different place", key)
        self.reg(key, off, off + nbytes)
        if persistent:
            self.persist.add(key)
        return t

    def phase_reset(self):
        self.barrier()
        self.lastw = {}
        self.readers = {}
        for k in list(self.ranges):
            if k not in self.persist:
                del self.ranges[k]
                del self.alias[k]
        for k in self.alias:
            self.alias[k] = [a for a in self.alias[k] if a in self.ranges]
        self.tcache = {ck: t for ck, t in self.tcache.items() if ck[0] in self.persist}

    def reg(self, key, lo, hi):
        self.ranges[key] = (lo, hi)
        al = []
        for k, (a, b) in self.ranges.items():
            if k != key and a < hi and lo < b:
                al.append(k)
                self.alias[k].append(key)
        self.alias[key] = al

    def _keys(self, k):
        return [k] + self.alias.get(k, [])

    def _wait(self, eng, tok):
        sem, val = tok
        if sem == "E_" + eng:
            if eng in ("pe", "sp") or not SAME_SYNC:
                return
        if self.waited[eng].get(sem, 0) >= val:
            return
        self.waited[eng][sem] = val
        self.streams[eng].append(("w", sem, val))

    def _deps(self, eng, reads, writes):
        toks = []
        for k0 in reads:
            for k in self._keys(k0):
                t = self.lastw.get(k)
                if t:
                    toks.append(t)
                if isinstance(k, str) and k.startswith("ps") and k[2:].isdigit():
                    toks.extend(r for r in self.readers.get(k, ()) if r[0] != "E_" + eng)
                for e2 in ENGS:
                    assert k not in self.pend_w[e2] or e2 == eng, ("pending write", k, e2, eng)
        for k0 in writes:
            for k in self._keys(k0):
                t = self.lastw.get(k)
                if t:
                    toks.append(t)
                toks.extend(self.readers.get(k, ()))
                for e2 in ENGS:
                    if e2 != eng:
                        assert k not in self.pend_r[e2] and k not in self.pend_w[e2], ("pending", k, e2, eng)
        for t in toks:
            self._wait(eng, t)

    def _commit(self, tok, reads, writes):
        for k in reads:
            self.readers.setdefault(k, []).append(tok)
        for k in writes:
            self.lastw[k] = tok
            self.readers[k] = []

    def op(self, eng, fn, reads=(), writes=(), inc=True):
        reads = list(reads)
        writes = list(writes)
        self._deps(eng, reads, writes)
        if inc:
            self.cnt[eng] += 1
            tok = ("E_" + eng, self.cnt[eng])
            self.streams[eng].append(("o", fn, "E_" + eng, 1))
            self._commit(tok, reads + self.pend_r[eng], writes + self.pend_w[eng])
            self.pend_r[eng] = []
            self.pend_w[eng] = []
        else:
            self.streams[eng].append(("o", fn, None, 0))
            self.pend_r[eng] += reads
            self.pend_w[eng] += writes

    def dma(self, q, sem, out, in_, reads=(), writes=(), **kw):
        reads = list(reads)
        writes = list(writes)
        if sem not in self.sems:
            self.sems[sem] = self.es.enter_context(self.nc.semaphore(sem))
            self.dcount[sem] = 0
        if self.dcount[sem]:
            self._wait(q, (sem, self.dcount[sem]))
        self._deps(q, reads, writes)
        self.dcount[sem] += 16
        tok = (sem, self.dcount[sem])
        self.streams[q].append(("o", lambda e, o=out, i=in_, kw=kw: e.dma_start(out=o, in_=i, **kw), sem, 16))
        self._commit(tok, reads, writes)

    def barrier(self):
        for e in ENGS:
            assert not self.pend_r[e] and not self.pend_w[e], ("barrier with pending ops", e)
        for e in ENGS:
            for e2 in ENGS:
                if self.cnt[e2] and (e2 != e or e in ("act", "dve", "pool")):
                    if self.waited[e].get("E_" + e2, 0) < self.cnt[e2]:
                        self.waited[e]["E_" + e2] = self.cnt[e2]
                        self.streams[e].append(("w", "E_" + e2, self.cnt[e2]))
            for s, v in self.dcount.items():
                if v:
                    self._wait(e, (s, v))

    def emit(self):
        nc = self.nc
        with nc.Block() as block:
            def mk(name):
                def body(e):
                    for it in self.streams[name]:
                        if it[0] == "w":
                            e.wait_ge(self.sems[it[1]], it[2])
                        else:
                            ins = it[1](e)
                            if it[2] is not None:
                                ins.then_inc(self.sems[it[2]], it[3])
                return body
            block.tensor(mk("pe"))
            block.scalar(mk("act"))
            block.vector(mk("dve"))
            block.gpsimd(mk("pool"))
            block.sync(mk("sp"))


def build(mode="full"):
    nc = bass.Bass("TRN2", target_bir_lowering=False, dynamic_dma_scratch_size=DMA_SCRATCH)
    es = ExitStack()
    sc = Sched(nc, es)

    def dram_in(name, shape, dt=F32):
        return nc.dram_tensor(name, list(shape), dt, kind="ExternalInput").ap()

    x_in = dram_in("x", [S, D])
    pos_in = dram_in("pos", [S // 128, 128], I32)
    ev_w_in = dram_in("ev_w_in", [D, 4104])
    ev_b_f = dram_in("ev_b_f", [8, 1])
    ev_lre = dram_in("ev_lre", [64, 64])
    ev_lim = dram_in("ev_lim", [64, 64])
    ev_lstep = dram_in("ev_lstep", [1, 64])
    ev_bre = dram_in("ev_bre", [64, 64, 16])
    ev_bim = dram_in("ev_bim", [64, 64, 16])
    ev_cre = dram_in("ev_cre", [64, 16, 64])
    ev_cim = dram_in("ev_cim", [64, 16, 64])
    ev_d = dram_in("ev_d", [8, 128])
    ev_w_glu = dram_in("ev_w_glu", [1024, 2048])
    ev_w_out = dram_in("ev_w_out", [D, D])
    od_w_in = dram_in("od_w_in", [D, 2560])
    od_sinks = dram_in("od_sinks", [1, 32])
    od_w_out = dram_in("od_w_out", [D, D])
    ln_mix_g = dram_in("ln_mix_g", [2, 16, 128])
    ln_mix_b = dram_in("ln_mix_b", [2, 16, 128])
    ffn_w_up = dram_in("ffn_w_up", [2, D, 2 * DFF])
    ffn_conv_w = dram_in("ffn_conv_w", [2, 3, 86, 128])
    ffn_conv_b = dram_in("ffn_conv_b", [2, 86, 128])
    ffn_w_down = dram_in("ffn_w_down", [2, DFF, D])
    ln_ffn_g = dram_in("ln_ffn_g", [2, 16, 128])
    ln_ffn_b = dram_in("ln_ffn_b", [2, 16, 128])
    out_d = nc.dram_tensor("out", [S, D], F32, kind="ExternalOutput").ap()
    RA = nc.dram_tensor("resA", [DC, 128, S], F32, kind="Internal").ap()
    RB = nc.dram_tensor("resB", [DC, 128, S], F32, kind="Internal").ap()

    wupb = [nc.dram_tensor("wupb%d" % l, [86, 128, DC, 128], BF16, kind="Internal").ap() for l in range(2)]
    wdnb = [nc.dram_tensor("wdnb%d" % l, [DC, 128, NJ, 128], BF16, kind="Internal").ap() for l in range(2)]
    woutb = [nc.dram_tensor("woutb%d" % l, [DC, 128, DC, 128], BF16, kind="Internal").ap() for l in range(2)]
    conv_jobs = []
    for l in range(2):
        wo_v = (ev_w_out if l == 0 else od_w_out).rearrange("(c p) f -> p c f", p=128)
        for dc in range(DC):
            conv_jobs.append((woutb[l][dc], wo_v[:, :, dc * 128:(dc + 1) * 128], ("woutb", l, dc)))
        wup_v = ffn_w_up[l].rearrange("(c p) f -> p c f", p=128)
        wdn_v = ffn_w_down[l].rearrange("(j p) d -> p j d", p=128)
        for j in range(86):
            conv_jobs.append((wupb[l][j], wup_v[:, :, j * 128:(j + 1) * 128], ("wupb", l, j)))
        for dc in range(DC):
            conv_jobs.append((wdnb[l][dc], wdn_v[:, :, dc * 128:(dc + 1) * 128], ("wdnb", l, dc)))
    conv_done = [0]
    NCV = 6

    def conv_pump(n):
        for _ in range(n):
            if conv_done[0] >= len(conv_jobs):
                return
            o_, i_, key = conv_jobs[conv_done[0]]
            sc.dma("pool", "cv%d" % (conv_done[0] % NCV), o_, i_, writes=[key])
            conv_done[0] += 1

    PS = [es.enter_context(nc.psum_tensor("ps%d" % i, [128, 512], F32)) for i in range(8)]

    def psk(i):
        return "ps%d" % i

    off = [SBUF_BASE]

    def alloc(key, shape, dt):
        nbytes = int(np.prod(shape[1:])) * (2 if dt == BF16 else 4)
        nbytes = (nbytes + 31) // 32 * 32
        t = sc.sb(key, shape, dt, off[0], persistent=True)
        off[0] += nbytes
        return t

    ident_f = alloc("ident_f", [128, 128], F32)
    ident_b = alloc("ident_b", [128, 128], BF16)
    ones_f = alloc("ones_f", [128, 128], F32)
    ones_b = alloc("ones_b", [128, 128], BF16)
    lnp = alloc("lnp", [128, 8, 16], F32)
    convp = alloc("convp", [128, 2, 4, 86], F32)
    CONST_END = off[0]
    XTB_OFF = CONST_END
    xTb = sc.sb("xTb_all", [128, DC, S], BF16, XTB_OFF, persistent=True)
    for tt in range(NT):
        sc.reg("xTb%d" % tt, SBUF_LIMIT + 1000 + tt, SBUF_LIMIT + 1000 + tt + 1)
        sc.persist.add("xTb%d" % tt)
    PH_OFF = XTB_OFF + DC * S * 2

    class Arena:
        def __init__(self, start):
            self.o = (start + 31) // 32 * 32

        def take(self, key, shape, dt, n=None):
            nbytes = int(np.prod(shape[1:])) * (2 if dt == BF16 else 4)
            nbytes = (nbytes + 31) // 32 * 32
            if n is None:
                t = sc.sb(key, shape, dt, self.o)
                self.o += nbytes
                return t
            ts = []
            for i in range(n):
                ts.append(sc.sb("%s%d" % (key, i), shape, dt, self.o))
                self.o += nbytes
            return ts

    def setup_consts():
        sc.op("pool", lambda e: e.memset(ones_f[:], 1.0), writes=["ones_f"])
        sc.op("pool", lambda e: e.memset(ident_f[:], 1.0), writes=["ident_f"])
        sc.op("pool", lambda e: e.affine_select(out=ident_f[:], in_=ident_f[:], pattern=[[-1, 128]],
                                                 compare_op=ALU.is_equal, fill=0.0, base=0,
                                                 channel_multiplier=1),
              reads=["ident_f"], writes=["ident_f"])
        sc.op("dve", lambda e: e.tensor_copy(out=ident_b[:], in_=ident_f[:]), reads=["ident_f"], writes=["ident_b"])
        sc.op("dve", lambda e: e.tensor_copy(out=ones_b[:], in_=ones_f[:]), reads=["ones_f"], writes=["ones_b"])
        stg = sc.sb("c_stg", [128, 128], F32, PH_OFF)
        jobs = []
        for l in range(2):
            for k, src in enumerate((ln_mix_g, ln_mix_b, ln_ffn_g, ln_ffn_b)):
                jobs.append((src[l], 16, lnp[:, l * 4 + k, :]))
            for k in range(3):
                jobs.append((ffn_conv_w[l, k], 86, convp[:, l, k, :]))
            jobs.append((ffn_conv_b[l], 86, convp[:, l, 3, :]))
        for n, (src, rows, dst) in enumerate(jobs):
            sc.dma("sp", "c_stg", stg[0:rows, :], src, writes=["c_stg"])
            sc.op("pe", lambda e, r=rows: e.transpose(PS[0][:, 0:r], stg[0:r, :], ident_f[0:r, 0:r]),
                  reads=["c_stg", "ident_f"], writes=[psk(0)])
            sc.op("dve", lambda e, r=rows, d=dst: e.tensor_copy(out=d, in_=PS[0][:, 0:r]),
                  reads=[psk(0)], writes=["consts"])

    def phase_input(x_src, Rout):
        A = Arena(PH_OFF)
        xin = A.take("xin", [128, D], F32, 2)
        st32 = A.take("st32", [128, DC, T], F32)
        for tt in range(NT):
            for tb in range(4):
                g = tt * 4 + tb
                xi = xin[g % 2]
                xk = "xin%d" % (g % 2)
                sc.dma("sp", xk, xi[:], x_src[g * 128:(g + 1) * 128, :], writes=[xk])
                for q in range(4):
                    bank = q % 2
                    for c4 in range(4):
                        dc = q * 4 + c4
                        sc.op("pe", lambda e, b=bank, c4=c4, dc=dc, xi=xi: e.transpose(
                            PS[b][:, c4 * 128:(c4 + 1) * 128], xi[:, dc * 128:(dc + 1) * 128], ident_f[:]),
                            reads=[xk, "ident_f"], writes=[psk(bank)], inc=(c4 == 3))
                    pv = PS[bank][:].rearrange("p (c t) -> p c t", c=4)
                    if 'act' not in _SKIP:
                      sc.op("act", lambda e, pv=pv, q=q, tb=tb: e.copy(
                        out=st32[:, q * 4:(q + 1) * 4, tb * 128:(tb + 1) * 128], in_=pv),
                        reads=[psk(bank)], writes=["st32"])
                    if 'dve' not in _SKIP:
                      sc.op("dve", lambda e, pv=pv, q=q, g=g: e.tensor_copy(
                        out=xTb[:, q * 4:(q + 1) * 4, g * 128:(g + 1) * 128], in_=pv),
                        reads=[psk(bank)], writes=["xTb%d" % tt])
            conv_pump(4)
            if 'st' not in _SKIP:
              sc.dma("sp", "st32", Rout[:, :, tt * T:(tt + 1) * T].rearrange("c p t -> p c t"), st32[:],
                   reads=["st32"], writes=[("R", id(Rout), tt)])

    def layer_norm(r32, rkey, gi, bi, tt, lo, Rout=None, final=False, ost_lo=None, nbuf=2):
        A = Arena(lo)
        sq = A.take("ln_sq", [128, T], BF16, nbuf)
        rb = A.take("ln_rb", [128, T], BF16, nbuf)
        mean = A.take("ln_mean", [128, T], F32)
        rstd = A.take("ln_rstd", [128, T], F32)
        if final:
            ost = Arena(ost_lo).take("ln_ost", [128, D], F32, 2)
        for c in range(DC):
            s = sq[c % nbuf]
            sk = "ln_sq%d" % (c % nbuf)
            rb_ = rb[c % nbuf]
            rbk = "ln_rb%d" % (c % nbuf)
            sc.op("act", lambda e, s=s, c=c: e.activation(out=s[:], in_=r32[:, c, :], func=AF.Square),
                  reads=[rkey], writes=[sk])
            sc.op("pool", lambda e, rb_=rb_, c=c: e.tensor_copy(out=rb_[:], in_=r32[:, c, :]), reads=[rkey], writes=[rbk])
            sc.op("pe", lambda e, c=c, rb_=rb_: e.matmul(PS[6][:], lhsT=ones_b[:], rhs=rb_[:], start=(c == 0), stop=(c == DC - 1)),
                  reads=[rbk, "ones_b"], writes=[psk(6)], inc=True)
            sc.op("pe", lambda e, s=s, c=c: e.matmul(PS[7][:], lhsT=ones_b[:], rhs=s[:], start=(c == 0), stop=(c == DC - 1)),
                  reads=[sk, "ones_b"], writes=[psk(7)], inc=True)
        sc.op("act", lambda e: e.mul(out=mean[:], in_=PS[6][:], mul=1.0 / D), reads=[psk(6)], writes=["ln_mean"])
        sc.op("dve", lambda e: e.tensor_tensor(out=rstd[:], in0=mean[:], in1=mean[:], op=ALU.mult),
              reads=["ln_mean"], writes=["ln_rstd"])
        sc.op("dve", lambda e: e.scalar_tensor_tensor(out=rstd[:], in0=PS[7][:], scalar=1.0 / D, in1=rstd[:],
                                                      op0=ALU.mult, op1=ALU.subtract),
              reads=[psk(7), "ln_rstd"], writes=["ln_rstd"])
        sc.op("dve", lambda e: e.tensor_scalar(out=rstd[:], in0=rstd[:], scalar1=LN_EPS, scalar2=None, op0=ALU.add),
              reads=["ln_rstd"], writes=["ln_rstd"])
        sc.op("act", lambda e: e.activation(out=rstd[:], in_=rstd[:], func=AF.Sqrt), reads=["ln_rstd"], writes=["ln_rstd"])
        sc.op("dve", lambda e: e.reciprocal(out=rstd[:], in_=rstd[:]), reads=["ln_rstd"], writes=["ln_rstd"])
        for c in range(DC):
            sc.op("dve", lambda e, c=c: e.tensor_tensor(out=r32[:, c, :], in0=r32[:, c, :], in1=mean[:], op=ALU.subtract),
                  reads=[rkey, "ln_mean"], writes=[rkey])
            sc.op("dve", lambda e, c=c: e.tensor_tensor(out=r32[:, c, :], in0=r32[:, c, :], in1=rstd[:], op=ALU.mult),
                  reads=[rkey, "ln_rstd"], writes=[rkey])
            sc.op("act", lambda e, c=c: e.activation(out=r32[:, c, :], in_=r32[:, c, :], func=AF.Identity,
                                                     bias=lnp[:, bi, c:c + 1], scale=lnp[:, gi, c:c + 1]),
                  reads=[rkey, "consts"], writes=[rkey])
            if not final:
                sc.op("pool", lambda e, c=c: e.tensor_copy(out=xTb[:, c, tt * T:(tt + 1) * T], in_=r32[:, c, :]),
                      reads=[rkey], writes=["xTb%d" % tt])
        if not final:
            sc.dma("sp", "ln_out_" + str(rkey), Rout[:, :, tt * T:(tt + 1) * T].rearrange("c p t -> p c t"), r32[:],
                   reads=[rkey], writes=[("R", id(Rout), tt)])
        else:
            for tb in range(4):
                g = tt * 4 + tb
                os_ = ost[g % 2]
                ok = "ln_ost%d" % (g % 2)
                for q in range(4):
                    bank = q % 2
                    for c4 in range(4):
                        dc = q * 4 + c4
                        sc.op("pe", lambda e, b=bank, c4=c4, dc=dc, tb=tb: e.transpose(
                            PS[b][:, c4 * 128:(c4 + 1) * 128], r32[:, dc, tb * 128:(tb + 1) * 128], ident_f[:]),
                            reads=[rkey, "ident_f"], writes=[psk(bank)], inc=(c4 == 3))
                    if q % 2 == 0:
                        sc.op("act", lambda e, b=bank, q=q, os_=os_: e.copy(out=os_[:, q * 512:(q + 1) * 512], in_=PS[b][:]),
                              reads=[psk(bank)], writes=[ok])
                    else:
                        sc.op("dve", lambda e, b=bank, q=q, os_=os_: e.tensor_copy(out=os_[:, q * 512:(q + 1) * 512], in_=PS[b][:]),
                              reads=[psk(bank)], writes=[ok])
                sc.dma("sp", ok, out_d[g * 128:(g + 1) * 128, :], os_[:], reads=[ok], writes=[("out", g)])

    def phase_ffn(l, Rin, Rout, final):
        conv_pump(max(0, (l + 1) * 118 - conv_done[0]))
        A = Arena(PH_OFF)
        aT = A.take("aT", [128, NJ, T], BF16)
        r32 = A.take("r32", [128, DC, T], F32)
        NW = 3
        wg = A.take("wg", [128, DC, 128], BF16, NW)
        wv = A.take("wv", [128, DC, 128], BF16, NW)
        ND = 2
        wd = A.take("wd", [128, NJ, 128], BF16, ND)
        carry = A.take("carry", [128, 2, NJ, 2], F32, 2)
        lo = A.o
        gb = A.take("gb", [128, T], F32, 2)
        vb = A.take("vb", [128, T], F32, 2)
        wup = ffn_w_up[l].rearrange("(c p) f -> p c f", p=128)
        wdn = ffn_w_down[l].rearrange("(j p) d -> p j d", p=128)

        def cw(k, j):
            return convp[:, l, k, j:j + 1]

        up_jobs = [(tt, j) for tt in range(NT) for j in range(NJ)]
        dn_jobs = [(tt, dc) for tt in range(NT) for dc in range(DC)]
        up_issued = [0]
        dn_issued = [0]

        def issue_up(n):
            while up_issued[0] < min(n, len(up_jobs)):
                i = up_issued[0]
                _, j = up_jobs[i]
                ws = i % NW
                sc.dma("sp", "wg%d" % ws, wg[ws][:], wupb[l][j], reads=[("wupb", l, j)], writes=["wg%d" % ws])
                sc.dma("sp", "wv%d" % ws, wv[ws][:], wupb[l][NJ + j], reads=[("wupb", l, NJ + j)], writes=["wv%d" % ws])
                up_issued[0] += 1

        def issue_dn(n):
            while dn_issued[0] < min(n, len(dn_jobs)):
                i = dn_issued[0]
                _, dc = dn_jobs[i]
                ds = i % ND
                sc.dma("sp", "wd%d" % ds, wd[ds][:], wdnb[l][dc], reads=[("wdnb", l, dc)], writes=["wd%d" % ds])
                dn_issued[0] += 1

        for tt in range(NT):
            tsl = slice(tt * T, (tt + 1) * T)
            xk = "xTb%d" % tt
            cin = carry[(tt + 1) % 2]
            cout = carry[tt % 2]
            cink = "carry%d" % ((tt + 1) % 2)
            coutk = "carry%d" % (tt % 2)
            for j in range(NJ):
                ui = tt * NJ + j
                issue_up(ui + NW - 0 if ui == 0 else ui + NW)
                if j == 30:
                    sc.dma("sp", "r32", r32[:], Rin[:, :, tsl].rearrange("c p t -> p c t"),
                           reads=[("R", id(Rin), tt)], writes=["r32"])
                if j == 8:
                    issue_dn(tt * DC + 1)
                if j == 16:
                    issue_dn(tt * DC + 2)
                ws = ui % NW
                pg = (j % 2) * 2
                for half, (w_, wk, gv, cofs) in enumerate(((wg[ws], "wg%d" % ws, gb[j % 2], 0),
                                                           (wv[ws], "wv%d" % ws, vb[j % 2], NJ))):
                    bank = pg + half
                    P = PS[bank]
                    for c in range(DC):
                        rhs = xTb[:, c, tsl]
                        sc.op("pe", lambda e, P=P, w_=w_, c=c, rhs=rhs: e.matmul(P[:], lhsT=w_[:, c, :], rhs=rhs,
                                                                                  start=(c == 0), stop=(c == DC - 1)),
                              reads=[wk, xk], writes=[psk(bank)], inc=(c == DC - 1))
                    gk = ("gb%d" if half == 0 else "vb%d") % (j % 2)
                    jj = j + cofs
                    sc.op("act", lambda e, P=P, gv=gv, jj=jj: e.activation(out=gv[:], in_=P[:], func=AF.Identity,
                                                                           bias=cw(3, jj), scale=cw(2, jj)),
                          reads=[psk(bank), "consts"], writes=[gk])
                    sc.op("dve", lambda e, P=P, gv=gv, jj=jj: e.scalar_tensor_tensor(
                        out=gv[:, 1:T], in0=P[:, 0:T - 1], scalar=cw(1, jj), in1=gv[:, 1:T], op0=ALU.mult, op1=ALU.add),
                        reads=[psk(bank), gk, "consts"], writes=[gk])
                    sc.op("dve", lambda e, P=P, gv=gv, jj=jj: e.scalar_tensor_tensor(
                        out=gv[:, 2:T], in0=P[:, 0:T - 2], scalar=cw(0, jj), in1=gv[:, 2:T], op0=ALU.mult, op1=ALU.add),
                        reads=[psk(bank), gk, "consts"], writes=[gk])
                    if tt > 0:
                        sc.op("dve", lambda e, gv=gv, jj=jj, half=half, j=j, cin=cin: e.scalar_tensor_tensor(
                            out=gv[:, 0:1], in0=cin[:, half, j, 1:2], scalar=cw(1, jj), in1=gv[:, 0:1],
                            op0=ALU.mult, op1=ALU.add), reads=[cink, gk, "consts"], writes=[gk])
                        sc.op("dve", lambda e, gv=gv, jj=jj, half=half, j=j, cin=cin: e.scalar_tensor_tensor(
                            out=gv[:, 0:2], in0=cin[:, half, j, 0:2], scalar=cw(0, jj), in1=gv[:, 0:2],
                            op0=ALU.mult, op1=ALU.add), reads=[cink, gk, "consts"], writes=[gk])
                    if tt < NT - 1:
                        sc.op("act", lambda e, P=P, half=half, j=j, cout=cout: e.copy(out=cout[:, half, j, :], in_=P[:, T - 2:T]),
                              reads=[psk(bank)], writes=[coutk])
                g_ = gb[j % 2]
                v_ = vb[j % 2]
                sc.op("act", lambda e, g_=g_: e.activation(out=g_[:], in_=g_[:], func=AF.Silu),
                      reads=["gb%d" % (j % 2)], writes=["gb%d" % (j % 2)])
                sc.op("pool", lambda e, g_=g_, v_=v_, j=j: e.tensor_tensor(out=aT[:, j, :], in0=g_[:], in1=v_[:], op=ALU.mult),
                      reads=["gb%d" % (j % 2), "vb%d" % (j % 2)], writes=["aT"])
            for dc in range(DC):
                di = tt * DC + dc
                issue_dn(di + ND)
                ds = di % ND
                bank = 4 + dc % 2
                P = PS[bank]
                for j in range(NJ):
                    sc.op("pe", lambda e, P=P, j=j, ds=ds: e.matmul(P[:], lhsT=wd[ds][:, j, :], rhs=aT[:, j, :],
                                                                    start=(j == 0), stop=(j == NJ - 1)),
                          reads=["wd%d" % ds, "aT"], writes=[psk(bank)], inc=(j == NJ - 1))
                sc.op("dve", lambda e, P=P, dc=dc: e.scalar_tensor_tensor(
                    out=r32[:, dc, :], in0=r32[:, dc, :], scalar=ALPHA, in1=P[:], op0=ALU.mult, op1=ALU.add),
                    reads=[psk(bank), "r32"], writes=["r32"])
            layer_norm(r32, "r32", l * 4 + 2, l * 4 + 3, tt, lo, Rout=Rout, final=final, ost_lo=PH_OFF)

    class Prefetch:
        def __init__(self, n, nslots, issue):
            self.n, self.nslots, self.issue, self.issued = n, nslots, issue, 0

        def ensure(self, i):
            while self.issued < min(self.n, i + self.nslots):
                self.issue(self.issued, self.issued % self.nslots)
                self.issued += 1

    MIX_OFF = PH_OFF
    PH2_OFF = MIX_OFF + DC * S * 2

    _mix = []

    def get_mixT():
        if not _mix:
            _mix.append(nc.alloc_sbuf_tensor_at("mixT_all", [128, DC, S], BF16, offset=MIX_OFF))
        t = _mix[0]
        if "mixlo" not in sc.ranges:
            sc.reg("mixlo", MIX_OFF, MIX_OFF + 8 * S * 2)
            sc.reg("mixhi", MIX_OFF + 8 * S * 2, MIX_OFF + 16 * S * 2)
        return t

    def phase_outproj(l, w_out, Rin, Rout):
        mixT = get_mixT()
        A = Arena(PH2_OFF)
        conv_pump(max(0, l * 118 + 16 - conv_done[0]))
        r32s = A.take("r32_", [128, DC, T], F32, 2)
        NW = 3
        wo = A.take("wo", [128, DC, 128], BF16, NW)
        lo = A.o
        jobs = [(tt, dc) for tt in range(NT) for dc in range(DC)]
        pf = Prefetch(len(jobs), NW, lambda i, s_: sc.dma(
            "sp", "wo%d" % s_, wo[s_][:], woutb[l][jobs[i][1]], reads=[("woutb", l, jobs[i][1])], writes=["wo%d" % s_]))
        for tt in range(NT):
            tsl = slice(tt * T, (tt + 1) * T)
            r32 = r32s[tt % 2]
            rk = "r32_%d" % (tt % 2)
            sc.dma("sp", rk, r32[:], Rin[:, :, tsl].rearrange("c p t -> p c t"),
                   reads=[("R", id(Rin), tt)], writes=[rk])
            for dc in range(DC):
                i = tt * DC + dc
                pf.ensure(i)
                ws = i % NW
                bank = 4 + dc % 2
                P = PS[bank]
                for c in range(DC):
                    rhs = mixT[:, c, tsl]
                    sc.op("pe", lambda e, P=P, ws=ws, c=c, rhs=rhs: e.matmul(P[:], lhsT=wo[ws][:, c, :], rhs=rhs,
                                                                          start=(c == 0), stop=(c == DC - 1)),
                          reads=["wo%d" % ws, "mixlo" if c < 8 else "mixhi"], writes=[psk(bank)], inc=(c == DC - 1))
                sc.op("dve", lambda e, P=P, dc=dc, r32=r32: e.scalar_tensor_tensor(
                    out=r32[:, dc, :], in0=r32[:, dc, :], scalar=ALPHA, in1=P[:], op0=ALU.mult, op1=ALU.add),
                    reads=[psk(bank), rk], writes=[rk])
            layer_norm(r32, rk, l * 4 + 0, l * 4 + 1, tt, lo, Rout=Rout, nbuf=1)

    def phase_fox():
        mixT = get_mixT()
        A = Arena(PH2_OFF)
        qT = A.take("qT", [128, S], BF16, 2)
        kT = A.take("kT", [128, S], BF16, 2)
        Vt = A.take("Vt", [128, 16, 128], BF16, 2)
        wq = A.take("wq", [128, DC, 128], BF16, 2)
        wk = A.take("wk", [128, DC, 128], BF16, 2)
        wv = A.take("wv", [128, DC, 128], BF16, 2)
        PT = A.take("PT", [128, T], BF16, 2)
        negc = A.take("negc", [128, S], F32)
        cneg = A.take("cneg", [128, S], F32)
        negcT = A.take("negcT", [128, 16, 8], F32)
        recip = A.take("recip", [128, T], F32)
        sel = A.take("sel", [128, 8, 128], F32)
        maskb = A.take("maskb", [128, 128], BF16)
        wf = A.take("wf", [128, DC, 8], BF16)
        bfc = A.take("bfc", [128, 2], F32)
        ones_row = sc.sb("ones_row", [128, S], BF16, sc.ranges["qT0"][0])
        win = ev_w_in.rearrange("(c p) f -> p c f", p=128)
        SC = 1.0 / math.sqrt(128.0)

        sc.op("pool", lambda e: e.memset(sel[:], 1.0), writes=["sel"])
        sc.op("pool", lambda e: e.affine_select(out=sel[:], in_=sel[:], pattern=[[-1, 8], [0, 128]],
                                                 compare_op=ALU.is_equal, fill=0.0, base=0, channel_multiplier=1),
              reads=["sel"], writes=["sel"])
        sc.op("pool", lambda e: e.memset(maskb[:], 0.0), writes=["maskb"])
        sc.op("pool", lambda e: e.affine_select(out=maskb[:], in_=maskb[:], pattern=[[1, 128]],
                                                 compare_op=ALU.is_ge, fill=-30000.0, base=0, channel_multiplier=-1),
              reads=["maskb"], writes=["maskb"])
        sc.op("pool", lambda e: e.memset(cneg[:], 0.0), writes=["cneg"])
        sc.op("pool", lambda e: e.memset(ones_row[0:8, :], 1.0), writes=["ones_row"])
        sc.dma("pool", "wf", wf[:], win[:, :, 3072:3080], writes=["wf"])
        sc.dma("sp", "bfc", bfc[0:8, 0:1], ev_b_f, writes=["bfc"])
        sc.op("dve", lambda e: e.tensor_scalar(out=bfc[0:8, 1:2], in0=bfc[0:8, 0:1], scalar1=-1.0, scalar2=None, op0=ALU.mult),
              reads=["bfc"], writes=["bfc"])
        for tt in range(NT):
            tsl = slice(tt * T, (tt + 1) * T)
            for c in range(DC):
                rhs = xTb[:, c, tsl]
                sc.op("pe", lambda e, c=c, rhs=rhs: e.matmul(PS[6][0:8, :], lhsT=wf[:, c, :], rhs=rhs,
                                                              start=(c == 0), stop=(c == DC - 1)),
                      reads=["wf", "xTb%d" % tt], writes=[psk(6)], inc=(c == DC - 1))
            sc.op("act", lambda e, tsl=tsl: e.activation(out=negc[0:8, tsl], in_=PS[6][0:8, :], func=AF.Exp,
                                                          bias=bfc[0:8, 1:2], scale=-1.0),
                  reads=[psk(6), "bfc"], writes=["negc"])
            sc.op("act", lambda e, tsl=tsl: e.activation(out=negc[0:8, tsl], in_=negc[0:8, tsl], func=AF.Ln, bias=1.0, scale=1.0),
                  reads=["negc"], writes=["negc"])
        sc.op("dve", lambda e: e.tensor_tensor_scan(out=negc[0:8, :], data0=ones_row[0:8, :], data1=negc[0:8, :],
                                                    initial=0.0, op0=ALU.mult, op1=ALU.add),
              reads=["negc", "ones_row"], writes=["negc"])
        sc.op("act", lambda e: e.mul(out=cneg[0:8, :], in_=negc[0:8, :], mul=-1.0), reads=["negc"], writes=["cneg"])
        for tb in range(16):
            sc.op("pe", lambda e, tb=tb: e.transpose(PS[7][:, tb * 8:(tb + 1) * 8], negc[0:8, tb * 128:(tb + 1) * 128],
                                                     ident_f[0:8, 0:8]),
                  reads=["negc", "ident_f"], writes=[psk(7)], inc=(tb == 15))
        sc.op("dve", lambda e: e.tensor_copy(out=negcT[:].rearrange("p a b -> p (a b)"), in_=PS[7][:, 0:128]),
              reads=[psk(7)], writes=["negcT"])

        def load_head(h):
            s_ = h % 2
            sc.dma("pool", "wq%d" % s_, wq[s_][:], win[:, :, h * 128:(h + 1) * 128], writes=["wq%d" % s_])
            sc.dma("pool", "wk%d" % s_, wk[s_][:], win[:, :, 1024 + h * 128:1024 + (h + 1) * 128], writes=["wk%d" % s_])
            sc.dma("pool", "wv%d" % s_, wv[s_][:], win[:, :, 2048 + h * 128:2048 + (h + 1) * 128], writes=["wv%d" % s_])

        load_head(0)
        pcnt = [0]
        for h in range(8):
            s_ = h % 2
            if h + 1 < 8:
                load_head(h + 1)
            qk, kk, vk = "qT%d" % s_, "kT%d" % s_, "Vt%d" % s_
            for tt in range(NT):
                tsl = slice(tt * T, (tt + 1) * T)
                for which, (w_, wkey) in enumerate(((wq[s_], "wq%d" % s_), (wk[s_], "wk%d" % s_))):
                    bank = which
                    for c in range(DC):
                        rhs = xTb[:, c, tsl]
                        sc.op("pe", lambda e, bank=bank, w_=w_, c=c, rhs=rhs: e.matmul(
                            PS[bank][:], lhsT=w_[:, c, :], rhs=rhs, start=(c == 0), stop=(c == DC - 1)),
                            reads=[wkey, "xTb%d" % tt], writes=[psk(bank)], inc=(c == DC - 1))
                    if which == 0:
                        sc.op("act", lambda e, tsl=tsl, s_=s_: e.activation(out=qT[s_][:, tsl], in_=PS[0][:], func=AF.Copy, scale=SC),
                              reads=[psk(0)], writes=[qk])
                    else:
                        sc.op("dve", lambda e, tsl=tsl, s_=s_: e.tensor_copy(out=kT[s_][:, tsl], in_=PS[1][:]),
                              reads=[psk(1)], writes=[kk])
            for t4 in range(4):
                bank = t4 % 2
                for q4 in range(4):
                    tb = t4 * 4 + q4
                    for c in range(DC):
                        lhsT = xTb[:, c, tb * 128:(tb + 1) * 128]
                        sc.op("pe", lambda e, bank=bank, q4=q4, c=c, lhsT=lhsT, s_=s_: e.matmul(
                            PS[bank][:, q4 * 128:(q4 + 1) * 128], lhsT=lhsT, rhs=wv[s_][:, c, :],
                            start=(c == 0), stop=(c == DC - 1)),
                            reads=["wv%d" % s_, "xTb%d" % (tb // 4)], writes=[psk(bank)], inc=(c == DC - 1))
                pv = PS[bank][:].rearrange("p (a b) -> p a b", a=4)
                if t4 % 2 == 0:
                    sc.op("act", lambda e, pv=pv, t4=t4, s_=s_: e.copy(out=Vt[s_][:, t4 * 4:(t4 + 1) * 4, :], in_=pv),
                          reads=[psk(bank)], writes=[vk])
                else:
                    sc.op("dve", lambda e, pv=pv, t4=t4, s_=s_: e.tensor_copy(out=Vt[s_][:, t4 * 4:(t4 + 1) * 4, :], in_=pv),
                          reads=[psk(bank)], writes=[vk])
            for Qi in range(4):
                conv_pump(3)
                nblk = 4 * Qi + 4
                for j in range(nblk):
                    n0 = max(0, j * 128 - Qi * T)
                    diag = j * 128 >= Qi * T
                    q0 = Qi * T + n0
                    q1 = (Qi + 1) * T
                    bank = 2 + pcnt[0] % 2
                    ps_ = pcnt[0] % 2
                    pcnt[0] += 1
                    P = PS[bank]
                    sc.op("pe", lambda e, P=P, n0=n0, j=j, q0=q0, q1=q1, s_=s_: e.matmul(
                        P[:, n0:T], lhsT=kT[s_][:, j * 128:(j + 1) * 128], rhs=qT[s_][:, q0:q1], start=True, stop=False),
                        reads=[kk, qk], writes=[psk(bank)], inc=False)
                    sc.op("pe", lambda e, P=P, n0=n0, q0=q0, q1=q1, h=h, diag=diag: e.matmul(
                        P[:, n0:T], lhsT=sel[:, h, :], rhs=cneg[:, q0:q1], start=False, stop=(not diag)),
                        reads=["sel", "cneg"], writes=[psk(bank)], inc=(not diag))
                    if diag:
                        sc.op("pe", lambda e, P=P, n0=n0: e.matmul(
                            P[:, n0:n0 + 128], lhsT=ident_b[:], rhs=maskb[:], start=False, stop=True),
                            reads=["ident_b", "maskb"], writes=[psk(bank)], inc=True)
                    ptk = "PT%d" % ps_
                    sc.op("act", lambda e, P=P, n0=n0, j=j, h=h, ps_=ps_: e.activation(
                        out=PT[ps_][:, n0:T], in_=P[:, n0:T], func=AF.Exp, bias=negcT[:, j, h:h + 1], scale=1.0),
                        reads=[psk(bank), "negcT"], writes=[ptk])
                    last = (j == nblk - 1)
                    sc.op("pe", lambda e, n0=n0, j=j, ps_=ps_, s_=s_, last=last: e.matmul(
                        PS[4][:, n0:T], lhsT=Vt[s_][:, j, :], rhs=PT[ps_][:, n0:T], start=(j == 0), stop=last),
                        reads=[vk, ptk], writes=[psk(4)], inc=False)
                    sc.op("pe", lambda e, n0=n0, j=j, ps_=ps_, last=last: e.matmul(
                        PS[5][:, n0:T], lhsT=ones_b[:], rhs=PT[ps_][:, n0:T], start=(j == 0), stop=last),
                        reads=["ones_b", ptk], writes=[psk(5)], inc=True)
                sc.op("dve", lambda e: e.reciprocal(out=recip[:], in_=PS[5][:]), reads=[psk(5)], writes=["recip"])
                sc.op("dve", lambda e, h=h, Qi=Qi: e.tensor_tensor(out=mixT[:, h, Qi * T:(Qi + 1) * T], in0=PS[4][:], in1=recip[:],
                                                                  op=ALU.mult),
                      reads=[psk(4), "recip"], writes=["mixlo"])
        wu = wq
        pf = Prefetch(8, 2, lambda i, s_: sc.dma("pool", "wq%d" % s_, wu[s_][:], win[:, :, 3080 + i * 128:3080 + (i + 1) * 128],
                                                 writes=["wq%d" % s_]))
        for ch in range(8):
            pf.ensure(ch)
            s_ = ch % 2
            for tt in range(NT):
                tsl = slice(tt * T, (tt + 1) * T)
                bank = tt % 2
                for c in range(DC):
                    rhs = xTb[:, c, tsl]
                    sc.op("pe", lambda e, bank=bank, c=c, rhs=rhs, s_=s_: e.matmul(
                        PS[bank][:], lhsT=wu[s_][:, c, :], rhs=rhs, start=(c == 0), stop=(c == DC - 1)),
                        reads=["wq%d" % s_, "xTb%d" % tt], writes=[psk(bank)], inc=(c == DC - 1))
                if tt % 2 == 0:
                    sc.op("act", lambda e, bank=bank, ch=ch, tsl=tsl: e.copy(out=mixT[:, 8 + ch, tsl], in_=PS[bank][:]),
                          reads=[psk(bank)], writes=["mixhi"])
                else:
                    sc.op("dve", lambda e, bank=bank, ch=ch, tsl=tsl: e.tensor_copy(out=mixT[:, 8 + ch, tsl], in_=PS[bank][:]),
                          reads=[psk(bank)], writes=["mixhi"])

    Wd = nc.dram_tensor("s5w", [8, 128, 16, 128], BF16, kind="Internal").ap()
    TWO_PI = 2.0 * math.pi

    def bk_level(k, l, first, xr_, xi_, xrk, xik, i, AQ_re, AQ_im, AQ_in):
        if first >= S:
            return []
        src = slice(first - k, S - k, 2 * k)
        dst = slice(first, S, 2 * k)
        ar = AQ_re[:, i, l:l + 1]
        ai = AQ_im[:, i, l:l + 1]
        an = AQ_in[:, i, l:l + 1]
        return [
            (lambda e: e.scalar_tensor_tensor(out=xr_[:, dst], in0=xr_[:, src], scalar=ar, in1=xr_[:, dst],
                                              op0=ALU.mult, op1=ALU.add), [xrk, "AQ_re"], [xrk]),
            (lambda e: e.scalar_tensor_tensor(out=xr_[:, dst], in0=xi_[:, src], scalar=an, in1=xr_[:, dst],
                                              op0=ALU.mult, op1=ALU.add), [xrk, xik, "AQ_in"], [xrk]),
            (lambda e: e.scalar_tensor_tensor(out=xi_[:, dst], in0=xi_[:, src], scalar=ar, in1=xi_[:, dst],
                                              op0=ALU.mult, op1=ALU.add), [xik, "AQ_re"], [xik]),
            (lambda e: e.scalar_tensor_tensor(out=xi_[:, dst], in0=xr_[:, src], scalar=ai, in1=xi_[:, dst],
                                              op0=ALU.mult, op1=ALU.add), [xik, xrk, "AQ_im"], [xik]),
        ]

    def phase_s5():
        mixT = get_mixT()
        AP_ = Arena(PH2_OFF)
        yg = AP_.take("yg", [128, 8, S], BF16)
        wslot = AP_.take("wslot", [128, 16, 128], BF16, 2)
        AQ_re = AP_.take("AQ_re", [128, 32, 11], F32)
        AQ_im = AP_.take("AQ_im", [128, 32, 11], F32)
        AQ_in = AP_.take("AQ_in", [128, 32, 11], F32)
        Dcol = AP_.take("Dcol", [128, 8], F32)
        y32 = AP_.take("y32", [128, T], F32, 2)
        gt = AP_.take("gt", [128, T], F32, 2)
        wz1 = AP_.take("wz1", [128, 8, 128], BF16, 2)
        wz2 = AP_.take("wz2", [128, 8, 128], BF16, 2)
        sig = AP_.take("sig", [128, T], F32, 2)
        AX = Arena(XTB_OFF)
        lamraw = AX.take("lamraw", [128, 2, 128], F32)
        names = ["lr", "li", "dt", "lrd", "lid", "mag", "kf", "rr", "rc", "m1", "sn", "cs", "are", "aim",
                 "den", "xr", "gre", "gim", "t1", "t2"]
        sm = {n: AX.take("s5_" + n, [128, 64], F32) for n in names}
        ki = AX.take("s5_ki", [128, 64], I32)
        P_re = AX.take("P_re", [128, 64, 11], F32)
        P_im = AX.take("P_im", [128, 64, 11], F32)
        b_re = AX.take("b_re", [128, 64, 16], F32)
        b_im = AX.take("b_im", [128, 64, 16], F32)
        bb_re = AX.take("bb_re", [128, 64, 16], F32)
        bb_im = AX.take("bb_im", [128, 64, 16], F32)
        btmp = AX.take("btmp", [128, 64, 16], F32)
        CT_re = AX.take("CT_re", [128, 8, 128], F32)
        CT_im = AX.take("CT_im", [128, 8, 128], F32)
        MC = AX.take("MC", [128, 4, 128], F32)
        MB = AX.take("MB", [128, 4, 128], F32)
        dstg = AX.take("dstg", [128, 128], F32)
        stage = AX.take("wstage", [128, 16, 128], BF16, 2)

        def dve(fn, reads, writes):
            sc.op("dve", fn, reads=reads, writes=writes)

        def tt_(out, a, b, op, reads, writes):
            dve(lambda e: e.tensor_tensor(out=out, in0=a, in1=b, op=op), reads, writes)

        K = lambda n: "s5_" + n
        for half in range(2):
            sc.dma("sp", "s5ld", lamraw[0:64, 0, half * 64:(half + 1) * 64], ev_lre, writes=["lamraw"])
            sc.dma("sp", "s5ld", lamraw[0:64, 1, half * 64:(half + 1) * 64], ev_lim, writes=["lamraw"])
            sc.dma("sp", "s5ld", b_re[half * 64:(half + 1) * 64], ev_bre.rearrange("g p c -> p g c"), writes=["b_re"])
            sc.dma("sp", "s5ld", b_im[half * 64:(half + 1) * 64], ev_bim.rearrange("g p c -> p g c"), writes=["b_im"])
            sc.dma("sp", "s5ld", CT_re[:, :, half * 64:(half + 1) * 64], ev_cre.rearrange("(j a) c p -> (a c) j p", a=8),
                   writes=["CT_re"])
            sc.dma("sp", "s5ld", CT_im[:, :, half * 64:(half + 1) * 64], ev_cim.rearrange("(j a) c p -> (a c) j p", a=8),
                   writes=["CT_im"])
        sc.dma("sp", "s5ld", sm["dt"][:], ev_lstep.broadcast_to([128, 64]), writes=[K("dt")])
        sc.dma("sp", "s5ld", dstg[0:8, :], ev_d, writes=["dstg"])
        for w, n in ((0, "lr"), (1, "li")):
            sc.op("pe", lambda e, w=w: e.transpose(PS[0][:, 0:64], lamraw[0:64, w, :], ident_f[0:64, 0:64]),
                  reads=["lamraw", "ident_f"], writes=[psk(0)])
            dve(lambda e, n=n: e.tensor_copy(out=sm[n][:], in_=PS[0][:, 0:64]), [psk(0)], [K(n)])
        sc.op("pe", lambda e: e.transpose(PS[0][:, 0:8], dstg[0:8, :], ident_f[0:8, 0:8]),
              reads=["dstg", "ident_f"], writes=[psk(0)])
        dve(lambda e: e.tensor_copy(out=Dcol[:], in_=PS[0][:, 0:8]), [psk(0)], ["Dcol"])
        sc.op("act", lambda e: e.activation(out=sm["dt"][:], in_=sm["dt"][:], func=AF.Exp), reads=[K("dt")], writes=[K("dt")])
        tt_(sm["lrd"][:], sm["lr"][:], sm["dt"][:], ALU.mult, [K("lr"), K("dt")], [K("lrd")])
        tt_(sm["lid"][:], sm["li"][:], sm["dt"][:], ALU.mult, [K("li"), K("dt")], [K("lid")])
        sc.op("act", lambda e: e.activation(out=sm["mag"][:], in_=sm["lrd"][:], func=AF.Exp), reads=[K("lrd")], writes=[K("mag")])
        dve(lambda e: e.tensor_scalar(out=sm["kf"][:], in0=sm["lid"][:], scalar1=1.0 / TWO_PI, scalar2=0.5,
                                      op0=ALU.mult, op1=ALU.add), [K("lid")], [K("kf")])
        dve(lambda e: e.tensor_copy(out=ki[:], in_=sm["kf"][:]), [K("kf")], ["s5_ki"])
        dve(lambda e: e.tensor_copy(out=sm["kf"][:], in_=ki[:]), ["s5_ki"], [K("kf")])
        dve(lambda e: e.scalar_tensor_tensor(out=sm["rr"][:], in0=sm["kf"][:], scalar=-TWO_PI, in1=sm["lid"][:],
                                             op0=ALU.mult, op1=ALU.add), [K("kf"), K("lid")], [K("rr")])

        def wrap(n):
            dve(lambda e: e.tensor_scalar(out=sm["m1"][:], in0=sm[n][:], scalar1=math.pi, scalar2=None, op0=ALU.is_gt),
                [K(n)], [K("m1")])
            dve(lambda e: e.scalar_tensor_tensor(out=sm[n][:], in0=sm["m1"][:], scalar=-TWO_PI, in1=sm[n][:],
                                                 op0=ALU.mult, op1=ALU.add), [K("m1"), K(n)], [K(n)])
            dve(lambda e: e.tensor_scalar(out=sm["m1"][:], in0=sm[n][:], scalar1=-math.pi, scalar2=None, op0=ALU.is_lt),
                [K(n)], [K("m1")])
            dve(lambda e: e.scalar_tensor_tensor(out=sm[n][:], in0=sm["m1"][:], scalar=TWO_PI, in1=sm[n][:],
                                                 op0=ALU.mult, op1=ALU.add), [K("m1"), K(n)], [K(n)])

        wrap("rr")
        dve(lambda e: e.tensor_scalar(out=sm["rc"][:], in0=sm["rr"][:], scalar1=0.5 * math.pi, scalar2=None, op0=ALU.add),
            [K("rr")], [K("rc")])
        wrap("rc")
        sc.op("act", lambda e: e.activation(out=sm["sn"][:], in_=sm["rr"][:], func=AF.Sin), reads=[K("rr")], writes=[K("sn")])
        sc.op("act", lambda e: e.activation(out=sm["cs"][:], in_=sm["rc"][:], func=AF.Sin), reads=[K("rc")], writes=[K("cs")])
        tt_(sm["are"][:], sm["mag"][:], sm["cs"][:], ALU.mult, [K("mag"), K("cs")], [K("are")])
        tt_(sm["aim"][:], sm["mag"][:], sm["sn"][:], ALU.mult, [K("mag"), K("sn")], [K("aim")])
        tt_(sm["t1"][:], sm["lr"][:], sm["lr"][:], ALU.mult, [K("lr")], [K("t1")])
        tt_(sm["den"][:], sm["li"][:], sm["li"][:], ALU.mult, [K("li")], [K("den")])
        tt_(sm["den"][:], sm["den"][:], sm["t1"][:], ALU.add, [K("den"), K("t1")], [K("den")])
        dve(lambda e: e.reciprocal(out=sm["den"][:], in_=sm["den"][:]), [K("den")], [K("den")])
        dve(lambda e: e.tensor_scalar(out=sm["xr"][:], in0=sm["are"][:], scalar1=-1.0, scalar2=None, op0=ALU.add),
            [K("are")], [K("xr")])
        tt_(sm["t1"][:], sm["xr"][:], sm["lr"][:], ALU.mult, [K("xr"), K("lr")], [K("t1")])
        tt_(sm["t2"][:], sm["aim"][:], sm["li"][:], ALU.mult, [K("aim"), K("li")], [K("t2")])
        tt_(sm["t1"][:], sm["t1"][:], sm["t2"][:], ALU.add, [K("t1"), K("t2")], [K("t1")])
        tt_(sm["gre"][:], sm["t1"][:], sm["den"][:], ALU.mult, [K("t1"), K("den")], [K("gre")])
        tt_(sm["t1"][:], sm["aim"][:], sm["lr"][:], ALU.mult, [K("aim"), K("lr")], [K("t1")])
        tt_(sm["t2"][:], sm["xr"][:], sm["li"][:], ALU.mult, [K("xr"), K("li")], [K("t2")])
        tt_(sm["t1"][:], sm["t1"][:], sm["t2"][:], ALU.subtract, [K("t1"), K("t2")], [K("t1")])
        tt_(sm["gim"][:], sm["t1"][:], sm["den"][:], ALU.mult, [K("t1"), K("den")], [K("gim")])
        gre_b = sm["gre"][:].unsqueeze(2).broadcast_to([128, 64, 16])
        gim_b = sm["gim"][:].unsqueeze(2).broadcast_to([128, 64, 16])
        tt_(bb_re[:], b_re[:], gre_b, ALU.mult, ["b_re", K("gre")], ["bb_re"])
        tt_(btmp[:], b_im[:], gim_b, ALU.mult, ["b_im", K("gim")], ["btmp"])
        tt_(bb_re[:], bb_re[:], btmp[:], ALU.subtract, ["bb_re", "btmp"], ["bb_re"])
        tt_(bb_im[:], b_im[:], gre_b, ALU.mult, ["b_im", K("gre")], ["bb_im"])
        tt_(btmp[:], b_re[:], gim_b, ALU.mult, ["b_re", K("gim")], ["btmp"])
        tt_(bb_im[:], bb_im[:], btmp[:], ALU.add, ["bb_im", "btmp"], ["bb_im"])
        dve(lambda e: e.tensor_copy(out=P_re[:, :, 0], in_=sm["are"][:]), [K("are")], ["P_re"])
        dve(lambda e: e.tensor_copy(out=P_im[:, :, 0], in_=sm["aim"][:]), [K("aim")], ["P_im"])
        for l in range(1, 11):
            tt_(sm["t1"][:], P_re[:, :, l - 1], P_re[:, :, l - 1], ALU.mult, ["P_re"], [K("t1")])
            tt_(sm["t2"][:], P_im[:, :, l - 1], P_im[:, :, l - 1], ALU.mult, ["P_im"], [K("t2")])
            tt_(P_re[:, :, l], sm["t1"][:], sm["t2"][:], ALU.subtract, [K("t1"), K("t2")], ["P_re"])
            tt_(sm["t1"][:], P_re[:, :, l - 1], P_im[:, :, l - 1], ALU.mult, ["P_re", "P_im"], [K("t1")])
            dve(lambda e, l=l: e.tensor_scalar(out=P_im[:, :, l], in0=sm["t1"][:], scalar1=2.0, scalar2=None, op0=ALU.mult),
                [K("t1")], ["P_im"])
        for (src, dst, dk) in ((P_re, AQ_re, "AQ_re"), (P_im, AQ_im, "AQ_im")):
            sv = src[:].rearrange("p (i two) l -> p i two l", two=2)
            sk = "P_re" if src is P_re else "P_im"
            dve(lambda e, sv=sv, dst=dst: e.tensor_copy(out=dst[0:64], in_=sv[0:64, :, 0, :]), [sk], [dk])
            dve(lambda e, sv=sv, dst=dst: e.tensor_copy(out=dst[64:128], in_=sv[64:128, :, 1, :]), [sk], [dk])
        dve(lambda e: e.tensor_scalar(out=AQ_in[:], in0=AQ_im[:], scalar1=-1.0, scalar2=None, op0=ALU.mult), ["AQ_im"], ["AQ_in"])
        sc.op("pool", lambda e: e.memset(MC[:], 0.0), writes=["MC"])
        for m in range(4):
            sc.op("pool", lambda e, m=m: e.memset(MC[0:64, m, 32 * m:32 * m + 16], 1.0), reads=["MC"], writes=["MC"])
            sc.op("pool", lambda e, m=m: e.memset(MC[64:128, m, 32 * m + 16:32 * m + 32], 1.0), reads=["MC"], writes=["MC"])
        for m in range(4):
            sc.op("pe", lambda e, m=m: e.transpose(PS[1][:, m * 128:(m + 1) * 128], MC[:, m, :], ident_f[:]),
                  reads=["MC", "ident_f"], writes=[psk(1)], inc=(m == 3))
        dve(lambda e: e.tensor_copy(out=MB[:].rearrange("p a b -> p (a b)"), in_=PS[1][:]), [psk(1)], ["MB"])
        for j in range(8):
            st_ = stage[j % 2]
            stk = "wstage%d" % (j % 2)
            srcs = ((bb_re[:, 8 * j:8 * j + 8, :].rearrange("p a b -> p (a b)"), "bb_re", 2, 0),
                    (bb_im[:, 8 * j:8 * j + 8, :].rearrange("p a b -> p (a b)"), "bb_im", 2, 1),
                    (CT_re[:, j, :], "CT_re", 3, 0), (CT_im[:, j, :], "CT_im", 3, 1))
            for (src, sk, bank, half) in srcs:
                sc.op("pe", lambda e, src=src, bank=bank, half=half: e.transpose(
                    PS[bank][:, half * 128:(half + 1) * 128], src, ident_f[:]),
                    reads=[sk, "ident_f"], writes=[psk(bank)], inc=True)
            for m in range(4):
                dve(lambda e, st_=st_, m=m: e.tensor_tensor(out=st_[:, 0 + m, :], in0=PS[2][:, 0:128], in1=MB[:, m, :], op=ALU.mult),
                    [psk(2), "MB"], [stk])
                dve(lambda e, st_=st_, m=m: e.tensor_tensor(out=st_[:, 4 + m, :], in0=PS[2][:, 128:256], in1=MB[:, m, :], op=ALU.mult),
                    [psk(2), "MB"], [stk])
                dve(lambda e, st_=st_, m=m: e.tensor_tensor(out=st_[:, 8 + m, :], in0=PS[3][:, 0:128], in1=MC[:, m, :], op=ALU.mult),
                    [psk(3), "MC"], [stk])
                dve(lambda e, st_=st_, m=m: e.scalar_tensor_tensor(out=st_[:, 12 + m, :], in0=PS[3][:, 128:256], scalar=-1.0,
                                                                   in1=MC[:, m, :], op0=ALU.mult, op1=ALU.mult),
                    [psk(3), "MC"], [stk])
            sc.dma("sp", stk, Wd[j], st_[:], reads=[stk], writes=[("Wd", j)])
        mixT2 = mixT
        AX2 = Arena(XTB_OFF)
        X_re = AX2.take("X_re", [128, S], F32, 2)
        X_im = AX2.take("X_im", [128, S], F32, 2)
        H_re = AX2.take("H_re", [128, S], BF16, 4)
        H_im = AX2.take("H_im", [128, S], BF16, 4)
        C1 = 2.0 * math.sqrt(2.0 / math.pi)
        C2 = C1 * 0.044715
        bcnt = [0]
        for j in range(8):
            wsl = wslot[j % 2]
            wk_ = "wslot%d" % (j % 2)
            sc.dma("sp", wk_, wsl[:], Wd[j], reads=[("Wd", j)], writes=[wk_])
            for mp in (0, 2):
                conv_pump(8)
                chains = []
                for m in (mp, mp + 1):
                    i = 4 * j + m
                    xs = i % 2
                    xr_, xi_ = X_re[xs], X_im[xs]
                    xrk, xik = "X_re%d" % xs, "X_im%d" % xs
                    for tt in range(NT):
                        tsl = slice(tt * T, (tt + 1) * T)
                        for half, (dst, dk) in enumerate(((xr_, xrk), (xi_, xik))):
                            bank = bcnt[0] % 4
                            bcnt[0] += 1
                            sc.op("pe", lambda e, bank=bank, half=half, m=m, wsl=wsl, j=j, tsl=tsl: e.matmul(
                                PS[bank][:], lhsT=wsl[:, 4 * half + m, :], rhs=mixT2[:, 8 + j, tsl], start=True, stop=True),
                                reads=[wk_, "mixhi"], writes=[psk(bank)], inc=True)
                            sc.op("act", lambda e, bank=bank, dst=dst, tsl=tsl: e.copy(out=dst[:, tsl], in_=PS[bank][:]),
                                  reads=[psk(bank)], writes=[dk])
                    ops = []
                    for l in range(11):
                        ops += bk_level(1 << l, l, 2 * (1 << l) - 1, xr_, xi_, xrk, xik, i, AQ_re, AQ_im, AQ_in)
                    for l in range(9, -1, -1):
                        ops += bk_level(1 << l, l, 3 * (1 << l) - 1, xr_, xi_, xrk, xik, i, AQ_re, AQ_im, AQ_in)
                    chains.append((m, xr_, xi_, xrk, xik, ops))
                for idx in range(max(len(c[5]) for c in chains)):
                    for c_ in chains:
                        if idx < len(c_[5]):
                            fn, rd, wr = c_[5][idx]
                            sc.op("dve", fn, reads=rd, writes=wr)
                for (m, xr_, xi_, xrk, xik, _) in chains:
                    sc.op("act", lambda e, m=m, xr_=xr_: e.copy(out=H_re[m][:], in_=xr_[:]), reads=[xrk], writes=["H_re%d" % m])
                    sc.op("act", lambda e, m=m, xi_=xi_: e.copy(out=H_im[m][:], in_=xi_[:]), reads=[xik], writes=["H_im%d" % m])
            for tt in range(NT):
                tsl = slice(tt * T, (tt + 1) * T)
                bank = 4 + tt % 2
                for m in range(4):
                    sc.op("pe", lambda e, bank=bank, m=m, wsl=wsl, tsl=tsl: e.matmul(
                        PS[bank][:], lhsT=wsl[:, 8 + m, :], rhs=H_re[m][:, tsl], start=(m == 0), stop=False),
                        reads=[wk_, "H_re%d" % m], writes=[psk(bank)], inc=False)
                    sc.op("pe", lambda e, bank=bank, m=m, wsl=wsl, tsl=tsl: e.matmul(
                        PS[bank][:], lhsT=wsl[:, 12 + m, :], rhs=H_im[m][:, tsl], start=False, stop=(m == 3)),
                        reads=[wk_, "H_im%d" % m], writes=[psk(bank)], inc=(m == 3))
                y_ = y32[tt % 2]
                g_ = gt[tt % 2]
                yk, gk = "y32%d" % (tt % 2), "gt%d" % (tt % 2)
                sc.op("dve", lambda e, bank=bank, y_=y_, j=j, tsl=tsl: e.scalar_tensor_tensor(
                    out=y_[:], in0=mixT2[:, 8 + j, tsl], scalar=Dcol[:, j:j + 1], in1=PS[bank][:], op0=ALU.mult, op1=ALU.add),
                    reads=[psk(bank), "mixhi", "Dcol"], writes=[yk])
                sc.op("pool", lambda e, y_=y_, g_=g_: e.tensor_tensor(out=g_[:], in0=y_[:], in1=y_[:], op=ALU.mult),
                      reads=[yk], writes=[gk])
                sc.op("pool", lambda e, g_=g_: e.tensor_scalar(out=g_[:], in0=g_[:], scalar1=C2, scalar2=C1, op0=ALU.mult, op1=ALU.add),
                      reads=[gk], writes=[gk])
                sc.op("pool", lambda e, y_=y_, g_=g_: e.tensor_tensor(out=g_[:], in0=g_[:], in1=y_[:], op=ALU.mult),
                      reads=[gk, yk], writes=[gk])
                sc.op("act", lambda e, g_=g_: e.activation(out=g_[:], in_=g_[:], func=AF.Sigmoid), reads=[gk], writes=[gk])
                sc.op("pool", lambda e, y_=y_, g_=g_, j=j, tsl=tsl: e.tensor_tensor(out=yg[:, j, tsl], in0=g_[:], in1=y_[:], op=ALU.mult),
                      reads=[gk, yk], writes=["yg"])
        wgl = ev_w_glu.rearrange("(c p) f -> p c f", p=128)

        def issue_glu(i, s_):
            sc.dma("pool", "wz1%d" % s_, wz1[s_][:], wgl[:, :, i * 128:(i + 1) * 128], writes=["wz1%d" % s_])
            sc.dma("pool", "wz2%d" % s_, wz2[s_][:], wgl[:, :, 1024 + i * 128:1024 + (i + 1) * 128], writes=["wz2%d" % s_])

        pf = Prefetch(8, 2, issue_glu)
        for e_ in range(8):
            pf.ensure(e_)
            s_ = e_ % 2
            for tt in range(NT):
                tsl = slice(tt * T, (tt + 1) * T)
                ba, bb_ = (tt % 2) * 2, (tt % 2) * 2 + 1
                for (bank, w_, wkey) in ((ba, wz1[s_], "wz1%d" % s_), (bb_, wz2[s_], "wz2%d" % s_)):
                    for c in range(8):
                        sc.op("pe", lambda e, bank=bank, w_=w_, c=c, tsl=tsl: e.matmul(
                            PS[bank][:], lhsT=w_[:, c, :], rhs=yg[:, c, tsl], start=(c == 0), stop=(c == 7)),
                            reads=[wkey, "yg"], writes=[psk(bank)], inc=(c == 7))
                sg_ = sig[tt % 2]
                sgk = "sig%d" % (tt % 2)
                sc.op("act", lambda e, bb_=bb_, sg_=sg_: e.activation(out=sg_[:], in_=PS[bb_][:], func=AF.Sigmoid),
                      reads=[psk(bb_)], writes=[sgk])
                sc.op("dve", lambda e, ba=ba, sg_=sg_, e_=e_, tsl=tsl: e.tensor_tensor(
                    out=mixT2[:, 8 + e_, tsl], in0=PS[ba][:], in1=sg_[:], op=ALU.mult),
                    reads=[psk(ba), sgk], writes=["mixhi"])

    INVF = [float(v) for v in (np.float32(500000.0) ** (-(np.arange(8, dtype=np.float32) / np.float32(8.0))))]

    def psb(i):
        return PS[i][:].bitcast(BF16)

    def phase_swa():
        mixT = get_mixT()
        A = Arena(PH2_OFF)
        wcol = A.take("wcol", [128, DC, 256], BF16, 2)
        qb = A.take("qb", [128, 4, 64], BF16, 2)
        kd = A.take("kd", [128, 4, 2, 64], BF16, 2)
        qTp = A.take("qTp", [128, S], BF16, 4)
        kTd = A.take("kTd", [128, S], BF16, 4)
        Vs = A.take("Vs", [128, 16, 256], BF16)
        PTa = A.take("PTa", [128, T], BF16, 2)
        PTb = A.take("PTb", [128, T], BF16, 2)
        maskA = A.take("maskA", [128, 4, 128], BF16)
        maskB = A.take("maskB", [128, 4, 128], BF16)
        cosk = A.take("cosk", [128, 16, 8], F32)
        sink = A.take("sink", [128, 16, 8], F32)
        cosq = A.take("cosq", [128, 16, 8], F32)
        sinq = A.take("sinq", [128, 16, 8], F32)
        ang = A.take("ang", [128, 16, 8], F32)
        angc = A.take("angc", [128, 16, 8], F32)
        rtmp = A.take("rtmp", [128, 16, 8], F32)
        rki = A.take("rki", [128, 16, 8], I32)
        posi = A.take("posi", [128, 128], I32)
        posf = A.take("posf", [128, 128], F32)
        posT = A.take("posT", [128, 16], F32)
        invf = A.take("invf", [128, 8], F32)
        es = A.take("es", [128, 32], F32)
        esP = A.take("esP", [128, 16], F32)
        rt = A.take("rt", [128, 6, 4, 8], F32, 2)
        rot = A.take("rot", [128, 4, 16], F32, 2)
        rc = A.take("rc", [128, 256], F32, 2)
        win = od_w_in.rearrange("(c p) f -> p c f", p=128)

        def dve(fn, reads, writes):
            sc.op("dve", fn, reads=reads, writes=writes)

        sc.op("pool", lambda e: e.memset(maskA[:], 0.0), writes=["maskA"])
        sc.op("pool", lambda e: e.affine_select(out=maskA[:], in_=maskA[:], pattern=[[0, 4], [-1, 128]],
                                                 compare_op=ALU.is_ge, fill=-30000.0, base=-1, channel_multiplier=1),
              reads=["maskA"], writes=["maskA"])
        sc.op("pool", lambda e: e.memset(maskB[:], 0.0), writes=["maskB"])
        sc.op("pool", lambda e: e.affine_select(out=maskB[:], in_=maskB[:], pattern=[[0, 4], [1, 128]],
                                                 compare_op=ALU.is_ge, fill=-30000.0, base=0, channel_multiplier=-1),
              reads=["maskB"], writes=["maskB"])
        sc.dma("sp", "swld", es[:], od_sinks.broadcast_to([128, 32]), writes=["es"])
        sc.op("act", lambda e: e.activation(out=es[:], in_=es[:], func=AF.Exp), reads=["es"], writes=["es"])
        esv = es[:].rearrange("p (a two) -> p a two", two=2)
        dve(lambda e: e.tensor_copy(out=esP[0:64, :], in_=esv[0:64, :, 0]), ["es"], ["esP"])
        dve(lambda e: e.tensor_copy(out=esP[64:128, :], in_=esv[64:128, :, 1]), ["es"], ["esP"])
        sc.dma("sp", "swld", posi[0:16, :], pos_in, writes=["posi"])
        dve(lambda e: e.tensor_copy(out=posf[0:16, :], in_=posi[0:16, :]), ["posi"], ["posf"])
        sc.op("pe", lambda e: e.transpose(PS[7][:, 0:16], posf[0:16, :], ident_f[0:16, 0:16]),
              reads=["posf", "ident_f"], writes=[psk(7)])
        dve(lambda e: e.tensor_copy(out=posT[:], in_=PS[7][:, 0:16]), [psk(7)], ["posT"])
        for f in range(8):
            sc.op("pool", lambda e, f=f: e.memset(invf[:, f:f + 1], INVF[f]), writes=["invf"])
        dve(lambda e: e.tensor_tensor(out=ang[:], in0=posT[:].unsqueeze(2).broadcast_to([128, 16, 8]),
                                      in1=invf[:].unsqueeze(1).broadcast_to([128, 16, 8]), op=ALU.mult),
            ["posT", "invf"], ["ang"])

        def reduce_(x, xk):
            dve(lambda e: e.tensor_scalar(out=rtmp[:], in0=x[:], scalar1=1.0 / TWO_PI, scalar2=0.5, op0=ALU.mult, op1=ALU.add),
                [xk], ["rtmp"])
            dve(lambda e: e.tensor_copy(out=rki[:], in_=rtmp[:]), ["rtmp"], ["rki"])
            dve(lambda e: e.tensor_copy(out=rtmp[:], in_=rki[:]), ["rki"], ["rtmp"])
            dve(lambda e: e.scalar_tensor_tensor(out=x[:], in0=rtmp[:], scalar=-TWO_PI, in1=x[:], op0=ALU.mult, op1=ALU.add),
                ["rtmp", xk], [xk])
            for (thr, op_, add) in ((math.pi, ALU.is_gt, -TWO_PI), (-math.pi, ALU.is_lt, TWO_PI)):
                dve(lambda e, thr=thr, op_=op_: e.tensor_scalar(out=rtmp[:], in0=x[:], scalar1=thr, scalar2=None, op0=op_),
                    [xk], ["rtmp"])
                dve(lambda e, add=add: e.scalar_tensor_tensor(out=x[:], in0=rtmp[:], scalar=add, in1=x[:], op0=ALU.mult, op1=ALU.add),
                    ["rtmp", xk], [xk])

        dve(lambda e: e.tensor_scalar(out=angc[:], in0=ang[:], scalar1=0.5 * math.pi, scalar2=None, op0=ALU.add), ["ang"], ["angc"])
        reduce_(ang, "ang")
        reduce_(angc, "angc")
        sc.op("act", lambda e: e.activation(out=sink[:], in_=ang[:], func=AF.Sin), reads=["ang"], writes=["sink"])
        sc.op("act", lambda e: e.activation(out=cosk[:], in_=angc[:], func=AF.Sin), reads=["angc"], writes=["cosk"])
        sc.op("act", lambda e: e.mul(out=sinq[:], in_=sink[:], mul=0.125), reads=["sink"], writes=["sinq"])
        sc.op("act", lambda e: e.mul(out=cosq[:], in_=cosk[:], mul=0.125), reads=["cosk"], writes=["cosq"])
        for hq in range(4):
            sc.op("pool", lambda e, hq=hq: e.memset(qTp[hq][:], 0.0), writes=["qTp%d" % hq])

        units = [("k", 2048), ("v", 2304)] + [("q", 256 * u) for u in range(8)]
        pf = Prefetch(len(units), 2, lambda i, s_: sc.dma("pool", "wcol%d" % s_, wcol[s_][:],
                                                          win[:, :, units[i][1]:units[i][1] + 256], writes=["wcol%d" % s_]))
        ecnt = [0]

        def rope(P4, cs, sn, tb, o1, o2, okeys, slot):
            r_ = rt[slot]
            rk = "rt%d" % slot
            cb = cs[:, tb, :].unsqueeze(1).broadcast_to([128, 4, 8])
            sb_ = sn[:, tb, :].unsqueeze(1).broadcast_to([128, 4, 8])
            x1 = P4[:, :, 0:8]
            x2 = P4[:, :, 8:16]
            ck = ["cosk", "sink", "cosq", "sinq"]
            dve(lambda e: e.tensor_tensor(out=r_[:, 0], in0=x1, in1=cb, op=ALU.mult), okeys["ps"] + ck, [rk])
            dve(lambda e: e.tensor_tensor(out=r_[:, 1], in0=x2, in1=sb_, op=ALU.mult), okeys["ps"] + ck, [rk])
            dve(lambda e: e.tensor_tensor(out=r_[:, 2], in0=x2, in1=cb, op=ALU.mult), okeys["ps"] + ck, [rk])
            dve(lambda e: e.tensor_tensor(out=r_[:, 3], in0=x1, in1=sb_, op=ALU.mult), okeys["ps"] + ck, [rk])
            dve(lambda e: e.tensor_tensor(out=o1, in0=r_[:, 0], in1=r_[:, 1], op=ALU.subtract), [rk], okeys["out"])
            dve(lambda e: e.tensor_tensor(out=o2, in0=r_[:, 2], in1=r_[:, 3], op=ALU.add), [rk], okeys["out"])

        for ui, (kind, col0) in enumerate(units):
            pf.ensure(ui)
            ws = ui % 2
            wkey = "wcol%d" % ws
            quad = ui - 2
            for tb in range(16):
                bank = (tb // 2) % 2
                half = tb % 2
                Pfull = PS[bank][:, half * 256:(half + 1) * 256]
                for c in range(DC):
                    lhsT = xTb[:, c, tb * 128:(tb + 1) * 128]
                    sc.op("pe", lambda e, Pfull=Pfull, lhsT=lhsT, ws=ws, c=c: e.matmul(
                        Pfull, lhsT=lhsT, rhs=wcol[ws][:, c, :], start=(c == 0), stop=(c == DC - 1)),
                        reads=[wkey, "xTb%d" % (tb // 4)], writes=[psk(bank)], inc=(c == DC - 1))
                P4 = Pfull.rearrange("p (h d) -> p h d", h=4)
                slot = ecnt[0] % 2
                ecnt[0] += 1
                if kind == "v":
                    sc.op("act", lambda e, Pfull=Pfull, tb=tb: e.copy(out=Vs[:, tb, :], in_=Pfull), reads=[psk(bank)], writes=["Vs"])
                elif kind == "k":
                    kd_ = kd[slot]
                    kk = "kd%d" % slot
                    sc.op("act", lambda e, P4=P4, kd_=kd_: e.copy(out=kd_[:, :, 0, :], in_=P4), reads=[psk(bank)], writes=[kk])
                    ro = rot[slot]
                    rok = "rot%d" % slot
                    rope(P4, cosk, sink, tb, ro[:, :, 0:8], ro[:, :, 8:16], {"ps": [psk(bank)], "out": [rok]}, slot)
                    dve(lambda e, kd_=kd_, ro=ro: e.tensor_copy(out=kd_[:, :, 0, 0:16], in_=ro[:]), [rok, kk], [kk])
                    dve(lambda e, kd_=kd_: e.tensor_copy(out=kd_[:, :, 1, :], in_=kd_[:, :, 0, :]), [kk], [kk])
                    for kv in range(4):
                        sc.op("pe", lambda e, kv=kv, kd_=kd_: e.transpose(
                            psb(2)[:, kv * 128:(kv + 1) * 128], kd_[:, kv, :, :].rearrange("p a b -> p (a b)"), ident_b[:]),
                            reads=[kk, "ident_b"], writes=[psk(2)], inc=(kv == 3))
                    for kv in range(4):
                        eng = "act" if kv % 2 == 0 else "dve"
                        if eng == "act":
                            sc.op("act", lambda e, kv=kv, tb=tb: e.copy(out=kTd[kv][:, tb * 128:(tb + 1) * 128],
                                                                        in_=psb(2)[:, kv * 128:(kv + 1) * 128]),
                                  reads=[psk(2)], writes=["kTd%d" % kv])
                        else:
                            dve(lambda e, kv=kv, tb=tb: e.tensor_copy(out=kTd[kv][:, tb * 128:(tb + 1) * 128],
                                                                       in_=psb(2)[:, kv * 128:(kv + 1) * 128]),
                                [psk(2)], ["kTd%d" % kv])
                else:
                    qb_ = qb[slot]
                    qk_ = "qb%d" % slot
                    sc.op("act", lambda e, P4=P4, qb_=qb_: e.activation(out=qb_[:], in_=P4, func=AF.Copy, scale=0.125),
                          reads=[psk(bank)], writes=[qk_])
                    rope(P4, cosq, sinq, tb, qb_[:, :, 0:8], qb_[:, :, 8:16], {"ps": [psk(bank), qk_], "out": [qk_]}, slot)
                    for mc in range(2):
                        sc.op("pe", lambda e, mc=mc, qb_=qb_: e.transpose(
                            psb(2)[:, mc * 128:(mc + 1) * 128], qb_[:, 2 * mc:2 * mc + 2, :].rearrange("p a b -> p (a b)"),
                            ident_b[:]), reads=[qk_, "ident_b"], writes=[psk(2)], inc=(mc == 1))
                    for hq in range(4):
                        e_, mc = hq % 2, hq // 2
                        src = psb(2)[e_ * 64:(e_ + 1) * 64, mc * 128:(mc + 1) * 128]
                        dst = qTp[hq][e_ * 64:(e_ + 1) * 64, tb * 128:(tb + 1) * 128]
                        if hq % 2 == 0:
                            sc.op("act", lambda e, src=src, dst=dst: e.copy(out=dst, in_=src), reads=[psk(2)], writes=["qTp%d" % hq])
                        else:
                            dve(lambda e, src=src, dst=dst: e.tensor_copy(out=dst, in_=src), [psk(2)], ["qTp%d" % hq])
            if kind != "q":
                continue
            hk = quad // 2
            for i in range(16):
                par = i % 2
                pa, pb_ = PTa[par], PTb[par]
                pak, pbk = "PTa%d" % par, "PTb%d" % par
                blocks = ([(3, i - 1, maskA, "maskA", pa, pak)] if i > 0 else []) + [(4, i, maskB, "maskB", pb_, pbk)]
                for (bank, kb, mk, mkk, pt, ptk) in blocks:
                    for hq in range(4):
                        sc.op("pe", lambda e, bank=bank, kb=kb, hq=hq, i=i, hk=hk: e.matmul(
                            PS[bank][:, hq * 128:(hq + 1) * 128], lhsT=kTd[hk][:, kb * 128:(kb + 1) * 128],
                            rhs=qTp[hq][:, i * 128:(i + 1) * 128], start=(hq == 0), stop=False),
                            reads=["kTd%d" % hk, "qTp%d" % hq], writes=[psk(bank)], inc=False)
                    sc.op("pe", lambda e, bank=bank, mk=mk: e.matmul(
                        PS[bank][:], lhsT=ident_b[:], rhs=mk[:].rearrange("p a b -> p (a b)"), start=False, stop=True),
                        reads=["ident_b", mkk], writes=[psk(bank)], inc=True)
                    sc.op("act", lambda e, bank=bank, pt=pt: e.activation(out=pt[:], in_=PS[bank][:], func=AF.Exp),
                          reads=[psk(bank)], writes=[ptk])
                nb = len(blocks)
                for (obank, use_v) in ((5, True), (6, False)):
                    for hq in range(4):
                        e_, mc = hq % 2, hq // 2
                        out = PS[obank][e_ * 64:(e_ + 1) * 64, mc * 128:(mc + 1) * 128]
                        for bi, (bank, kb, mk, mkk, pt, ptk) in enumerate(blocks):
                            lhsT = Vs[:, kb, hk * 64:(hk + 1) * 64] if use_v else ones_b[:, 0:64]
                            rhs = pt[:, hq * 128:(hq + 1) * 128]
                            first = (bi == 0 and hq < 2)
                            lastmm = (hq == 3 and bi == nb - 1)
                            stopf = (hq >= 2 and bi == nb - 1)
                            sc.op("pe", lambda e, out=out, lhsT=lhsT, rhs=rhs, first=first, stopf=stopf, e_=e_: e.matmul(
                                out, lhsT=lhsT, rhs=rhs, start=first, stop=stopf, tile_position=(0, e_ * 64)),
                                reads=[ptk, "Vs" if use_v else "ones_b"], writes=[psk(obank)], inc=lastmm)
                rc_ = rc[par]
                rck = "rc%d" % par
                for mc in range(2):
                    dve(lambda e, mc=mc, rc_=rc_, quad=quad: e.tensor_scalar(
                        out=rc_[:, mc * 128:(mc + 1) * 128], in0=PS[6][:, mc * 128:(mc + 1) * 128],
                        scalar1=esP[:, quad * 2 + mc:quad * 2 + mc + 1], scalar2=None, op0=ALU.add),
                        [psk(6), "esP"], [rck])
                dve(lambda e, rc_=rc_: e.reciprocal(out=rc_[:], in_=rc_[:]), [rck], [rck])
                dve(lambda e, rc_=rc_, i=i, quad=quad: e.tensor_tensor(
                    out=mixT[:, quad * 2:quad * 2 + 2, i * 128:(i + 1) * 128],
                    in0=PS[5][:, 0:256].rearrange("p (a b) -> p a b", a=2),
                    in1=rc_[:].rearrange("p (a b) -> p a b", a=2), op=ALU.mult),
                    [psk(5), rck], ["mixlo" if quad < 4 else "mixhi"])

    def dbg_dump_mix():
        mixT = get_mixT()
        ov = out_d.rearrange("(c p) t -> p c t", p=128)
        for c in range(DC):
            sc.dma("pool", "dbgm", ov[:, c, :], mixT[:, c, :], reads=["mixlo", "mixhi"])

    def dbg_copy_R(R):
        A = Arena(PH_OFF)
        bufs = A.take("dbgb", [128, S], F32, 2)
        for c in range(DC):
            k = "dbgb%d" % (c % 2)
            sc.dma("sp", k, bufs[c % 2][:], R[c], reads=[("R", id(R), tt) for tt in range(NT)], writes=[k])
            sc.dma("sp", k, out_d[c * 128:(c + 1) * 128, :], bufs[c % 2][:], reads=[k])

    setup_consts()
    sc.phase_reset()
    if mode == "full":
        phase_input(x_in, RA)
        sc.phase_reset()
        phase_fox()
        sc.phase_reset()
        phase_s5()
        sc.phase_reset()
        phase_outproj(0, ev_w_out, RA, RB)
        sc.phase_reset()
        phase_ffn(0, RB, RA, final=False)
        sc.phase_reset()
        phase_swa()
        sc.phase_reset()
        phase_outproj(1, od_w_out, RA, RB)
        sc.phase_reset()
        phase_ffn(1, RB, RA, final=True)
    elif mode == "consts":
        sc.dma("sp", "dbg", out_d[0:128, 0:128], ident_f[:], reads=["ident_f"])
        sc.dma("sp", "dbg", out_d[128:256, 0:86 * 4].rearrange("p (a b) -> p a b", a=4), convp[:, 0, :, :], reads=["consts"])
        sc.dma("sp", "dbg", out_d[256:384, 0:128].rearrange("p (a b) -> p a b", a=8), lnp[:], reads=["consts"])
    elif mode == "p0":
        phase_input(x_in, RA)
        sc.phase_reset()
        if 'dbg' not in _SKIP:
            dbg_copy_R(RA)
    elif mode == "fox":
        phase_input(x_in, RA)
        sc.phase_reset()
        phase_fox()
        sc.phase_reset()
        dbg_dump_mix()
    elif mode == "s5":
        phase_input(x_in, RA)
        sc.phase_reset()
        phase_fox()
        sc.phase_reset()
        phase_s5()
        sc.phase_reset()
        dbg_dump_mix()
    elif mode == "swa":
        phase_input(x_in, RA)
        sc.phase_reset()
        phase_swa()
        sc.phase_reset()
        phase_outproj(1, od_w_out, RA, RB)
        sc.phase_reset()
        dbg_copy_R(RB)
    elif mode == "outproj0":
        phase_input(x_in, RA)
        sc.phase_reset()
        phase_fox()
        sc.phase_reset()
        phase_outproj(0, ev_w_out, RA, RB)
        sc.phase_reset()
        dbg_copy_R(RB)
    elif mode == "ffn0":
        phase_input(x_in, RA)
        sc.phase_reset()
        phase_ffn(0, RA, RB, final=True)
    else:
        raise NotImplementedError(mode)
    sc.barrier()
    sc.emit()
    return nc, es


_CACHE = {}


def _prep_inputs(inp, b):
    f = np.ascontiguousarray
    m = {
        "x": f(inp["x"][b]),
        "pos": f(inp["positions"][b].reshape(16, 128).astype(np.int32)),
        "ev_w_in": f(inp["ev_w_in"][0]),
        "ev_b_f": f(inp["ev_b_f"][0].reshape(8, 1)),
        "ev_lre": f(inp["ev_lambda_re"][0]),
        "ev_lim": f(inp["ev_lambda_im"][0]),
        "ev_lstep": f(inp["ev_log_step"][0].reshape(1, 64)),
        "ev_bre": f(inp["ev_ssm_b_re"][0]),
        "ev_bim": f(inp["ev_ssm_b_im"][0]),
        "ev_cre": f(inp["ev_ssm_c_re"][0]),
        "ev_cim": f(inp["ev_ssm_c_im"][0]),
        "ev_d": f(inp["ev_ssm_d"][0].reshape(8, 128)),
        "ev_w_glu": f(inp["ev_w_glu"][0]),
        "ev_w_out": f(inp["ev_w_out"][0]),
        "od_w_in": f(inp["od_w_in"][0]),
        "od_sinks": f(inp["od_sinks"][0].reshape(1, 32)),
        "od_w_out": f(inp["od_w_out"][0]),
        "ln_mix_g": f(inp["ln_mix_g"].reshape(2, 16, 128)),
        "ln_mix_b": f(inp["ln_mix_b"].reshape(2, 16, 128)),
        "ffn_w_up": f(inp["ffn_w_up"]),
        "ffn_conv_w": f(inp["ffn_conv_w"].reshape(2, 3, 86, 128)),
        "ffn_conv_b": f(inp["ffn_conv_b"].reshape(2, 86, 128)),
        "ffn_w_down": f(inp["ffn_w_down"]),
        "ln_ffn_g": f(inp["ln_ffn_g"].reshape(2, 16, 128)),
        "ln_ffn_b": f(inp["ln_ffn_b"].reshape(2, 16, 128)),
    }
    return m


def run(inputs, mode="full", cores=8, trace=False):
    nc, es = build(mode)
    in_maps = [_prep_inputs(inputs, b) for b in range(cores)]
    res = run_bass_kernel_spmd(nc, in_maps, core_ids=list(range(cores)), trace=trace)
    es.close()
    return res


def kernel(**inputs):
    res = run(inputs, "full", 8)
    out = np.stack([np.asarray(r["out"]) for r in res.results], axis=0)
    return out.astype(np.float32)
```

```python
import math
import os
_SKIP = set(os.environ.get('K_SKIP', '').split(','))
from contextlib import ExitStack

import numpy as np
import concourse.bass as bass
import concourse.mybir as mybir
from concourse.bass_utils import run_bass_kernel_spmd

F32 = mybir.dt.float32
BF16 = mybir.dt.bfloat16
I32 = mybir.dt.int32
AF = mybir.ActivationFunctionType
ALU = mybir.AluOpType

S = 2048
D = 2048
DC = 16
T = 512
NT = 4
DFF = 5504
NJ = 43
ALPHA = (2.0 * 2) ** 0.25
LN_EPS = 1e-5
DMA_SCRATCH = 8192
SBUF_BASE = DMA_SCRATCH
SBUF_LIMIT = 16384 + 212800

ENGS = ("pe", "act", "dve", "pool", "sp")
SAME_SYNC = True


class Sched:
    def __init__(self, nc, es):
        self.nc = nc
        self.es = es
        self.streams = {e: [] for e in ENGS}
        self.cnt = {e: 0 for e in ENGS}
        self.waited = {e: {} for e in ENGS}
        self.lastw = {}
        self.readers = {}
        self.pend_r = {e: [] for e in ENGS}
        self.pend_w = {e: [] for e in ENGS}
        self.sems = {}
        self.dcount = {}
        self.ranges = {}
        self.alias = {}
        self.tcache = {}
        self.persist = set()
        for e in ENGS:
            self.sems["E_" + e] = es.enter_context(nc.semaphore("E_" + e))

    def sb(self, key, shape, dtype, off, persistent=False):
        nbytes = int(np.prod(shape[1:])) * (2 if dtype == BF16 else 4)
        assert off % 4 == 0 and off + nbytes <= SBUF_LIMIT, (key, off, nbytes)
        ck = (key, off, tuple(shape), str(dtype))
        if ck in self.tcache and self.ranges.get(key) == (off, off + nbytes):
            return self.tcache[ck]
        t = self.nc.alloc_sbuf_tensor_at(key, list(shape), dtype, offset=off)
        self.tcache[ck] = t
        assert key not in self.ranges, ("key re-registered at a different place", key)
        self.reg(key, off, off + nbytes)
        if persistent:
            self.persist.add(key)
        return t

    def phase_reset(self):
        self.barrier()
        self.lastw = {}
        self.readers = {}
        for k in list(self.ranges):
            if k not in self.persist:
                del self.ranges[k]
                del self.alias[k]
        for k in self.alias:
            self.alias[k] = [a for a in self.alias[k] if a in self.ranges]
        self.tcache = {ck: t for ck, t in self.tcache.items() if ck[0] in self.persist}

    def reg(self, key, lo, hi):
        self.ranges[key] = (lo, hi)
        al = []
        for k, (a, b) in self.ranges.items():
            if k != key and a < hi and lo < b:
                al.append(k)
                self.alias[k].append(key)
        self.alias[key] = al

    def _keys(self, k):
        return [k] + self.alias.get(k, [])

    def _wait(self, eng, tok):
        sem, val = tok
        if sem == "E_" + eng:
            if eng in ("pe", "sp") or not SAME_SYNC:
                return
        if self.waited[eng].get(sem, 0) >= val:
            return
        self.waited[eng][sem] = val
        self.streams[eng].append(("w", sem, val))

    def _deps(self, eng, reads, writes):
        toks = []
        for k0 in reads:
            for k in self._keys(k0):
                t = self.lastw.get(k)
                if t:
                    toks.append(t)
                if isinstance(k, str) and k.startswith("ps") and k[2:].isdigit():
                    toks.extend(r for r in self.readers.get(k, ()) if r[0] != "E_" + eng)
                for e2 in ENGS:
                    assert k not in self.pend_w[e2] or e2 == eng, ("pending write", k, e2, eng)
        for k0 in writes:
            for k in self._keys(k0):
                t = self.lastw.get(k)
                if t:
                    toks.append(t)
                toks.extend(self.readers.get(k, ()))
                for e2 in ENGS:
                    if e2 != eng:
                        assert k not in self.pend_r[e2] and k not in self.pend_w[e2], ("pending", k, e2, eng)
        for t in toks:
            self._wait(eng, t)

    def _commit(self, tok, reads, writes):
        for k in reads:
            self.readers.setdefault(k, []).append(tok)
        for k in writes:
            self.lastw[k] = tok
            self.readers[k] = []

    def op(self, eng, fn, reads=(), writes=(), inc=True):
        reads = list(reads)
        writes = list(writes)
        self._deps(eng, reads, writes)
        if inc:
            self.cnt[eng] += 1
            tok = ("E_" + eng, self.cnt[eng])
            self.streams[eng].append(("o", fn, "E_" + eng, 1))
            self._commit(tok, reads + self.pend_r[eng], writes + self.pend_w[eng])
            self.pend_r[eng] = []
            self.pend_w[eng] = []
        else:
            self.streams[eng].append(("o", fn, None, 0))
            self.pend_r[eng] += reads
            self.pend_w[eng] += writes

    def dma(self, q, sem, out, in_, reads=(), writes=(), **kw):
        reads = list(reads)
        writes = list(writes)
        if sem not in self.sems:
            self.sems[sem] = self.es.enter_context(self.nc.semaphore(sem))
            self.dcount[sem] = 0
        if self.dcount[sem]:
            self._wait(q, (sem, self.dcount[sem]))
        self._deps(q, reads, writes)
        self.dcount[sem] += 16
        tok = (sem, self.dcount[sem])
        self.streams[q].append(("o", lambda e, o=out, i=in_, kw=kw: e.dma_start(out=o, in_=i, **kw), sem, 16))
        self._commit(tok, reads, writes)

    def barrier(self):
        for e in ENGS:
            assert not self.pend_r[e] and not self.pend_w[e], ("barrier with pending ops", e)
        for e in ENGS:
            for e2 in ENGS:
                if self.cnt[e2] and (e2 != e or e in ("act", "dve", "pool")):
                    if self.waited[e].get("E_" + e2, 0) < self.cnt[e2]:
                        self.waited[e]["E_" + e2] = self.cnt[e2]
                        self.streams[e].append(("w", "E_" + e2, self.cnt[e2]))
            for s, v in self.dcount.items():
                if v:
                    self._wait(e, (s, v))

    def emit(self):
        nc = self.nc
        with nc.Block() as block:
            def mk(name):
                def body(e):
                    for it in self.streams[name]:
                        if it[0] == "w":
                            e.wait_ge(self.sems[it[1]], it[2])
                        else:
                            ins = it[1](e)
                            if it[2] is not None:
                                ins.then_inc(self.sems[it[2]], it[3])
                return body
            block.tensor(mk("pe"))
            block.scalar(mk("act"))
            block.vector(mk("dve"))
            block.gpsimd(mk("pool"))
            block.sync(mk("sp"))


def build(mode="full"):
    nc = bass.Bass("TRN2", target_bir_lowering=False, dynamic_dma_scratch_size=DMA_SCRATCH)
    es = ExitStack()
    sc = Sched(nc, es)

    def dram_in(name, shape, dt=F32):
        return nc.dram_tensor(name, list(shape), dt, kind="ExternalInput").ap()

    x_in = dram_in("x", [S, D])
    pos_in = dram_in("pos", [S // 128, 128], I32)
    ev_w_in = dram_in("ev_w_in", [D, 4104])
    ev_b_f = dram_in("ev_b_f", [8, 1])
    ev_lre = dram_in("ev_lre", [64, 64])
    ev_lim = dram_in("ev_lim", [64, 64])
    ev_lstep = dram_in("ev_lstep", [1, 64])
    ev_bre = dram_in("ev_bre", [64, 64, 16])
    ev_bim = dram_in("ev_bim", [64, 64, 16])
    ev_cre = dram_in("ev_cre", [64, 16, 64])
    ev_cim = dram_in("ev_cim", [64, 16, 64])
    ev_d = dram_in("ev_d", [8, 128])
    ev_w_glu = dram_in("ev_w_glu", [1024, 2048])
    ev_w_out = dram_in("ev_w_out", [D, D])
    od_w_in = dram_in("od_w_in", [D, 2560])
    od_sinks = dram_in("od_sinks", [1, 32])
    od_w_out = dram_in("od_w_out", [D, D])
    ln_mix_g = dram_in("ln_mix_g", [2, 16, 128])
    ln_mix_b = dram_in("ln_mix_b", [2, 16, 128])
    ffn_w_up = dram_in("ffn_w_up", [2, D, 2 * DFF])
    ffn_conv_w = dram_in("ffn_conv_w", [2, 3, 86, 128])
    ffn_conv_b = dram_in("ffn_conv_b", [2, 86, 128])
    ffn_w_down = dram_in("ffn_w_down", [2, DFF, D])
    ln_ffn_g = dram_in("ln_ffn_g", [2, 16, 128])
    ln_ffn_b = dram_in("ln_ffn_b", [2, 16, 128])
    out_d = nc.dram_tensor("out", [S, D], F32, kind="ExternalOutput").ap()
    RA = nc.dram_tensor("resA", [DC, 128, S], F32, kind="Internal").ap()
    RB = nc.dram_tensor("resB", [DC, 128, S], F32, kind="Internal").ap()

    wupb = [nc.dram_tensor("wupb%d" % l, [86, 128, DC, 128], BF16, kind="Internal").ap() for l in range(2)]
    wdnb = [nc.dram_tensor("wdnb%d" % l, [DC, 128, NJ, 128], BF16, kind="Internal").ap() for l in range(2)]
    woutb = [nc.dram_tensor("woutb%d" % l, [DC, 128, DC, 128], BF16, kind="Internal").ap() for l in range(2)]
    conv_jobs = []
    for l in range(2):
        wo_v = (ev_w_out if l == 0 else od_w_out).rearrange("(c p) f -> p c f", p=128)
        for dc in range(DC):
            conv_jobs.append((woutb[l][dc], wo_v[:, :, dc * 128:(dc + 1) * 128], ("woutb", l, dc)))
        wup_v = ffn_w_up[l].rearrange("(c p) f -> p c f", p=128)
        wdn_v = ffn_w_down[l].rearrange("(j p) d -> p j d", p=128)
        for j in range(86):
            conv_jobs.append((wupb[l][j], wup_v[:, :, j * 128:(j + 1) * 128], ("wupb", l, j)))
        for dc in range(DC):
            conv_jobs.append((wdnb[l][dc], wdn_v[:, :, dc * 128:(dc + 1) * 128], ("wdnb", l, dc)))
    conv_done = [0]
    NCV = 6

    def conv_pump(n):
        for _ in range(n):
            if conv_done[0] >= len(conv_jobs):
                return
            o_, i_, key = conv_jobs[conv_done[0]]
            sc.dma("pool", "cv%d" % (conv_done[0] % NCV), o_, i_, writes=[key])
            conv_done[0] += 1

    PS = [es.enter_context(nc.psum_tensor("ps%d" % i, [128, 512], F32)) for i in range(8)]

    def psk(i):
        return "ps%d" % i

    off = [SBUF_BASE]

    def alloc(key, shape, dt):
        nbytes = int(np.prod(shape[1:])) * (2 if dt == BF16 else 4)
        nbytes = (nbytes + 31) // 32 * 32
        t = sc.sb(key, shape, dt, off[0], persistent=True)
        off[0] += nbytes
        return t

    ident_f = alloc("ident_f", [128, 128], F32)
    ident_b = alloc("ident_b", [128, 128], BF16)
    ones_f = alloc("ones_f", [128, 128], F32)
    ones_b = alloc("ones_b", [128, 128], BF16)
    lnp = alloc("lnp", [128, 8, 16], F32)
    convp = alloc("convp", [128, 2, 4, 86], F32)
    CONST_END = off[0]
    XTB_OFF = CONST_END
    xTb = sc.sb("xTb_all", [128, DC, S], BF16, XTB_OFF, persistent=True)
    for tt in range(NT):
        sc.reg("xTb%d" % tt, SBUF_LIMIT + 1000 + tt, SBUF_LIMIT + 1000 + tt + 1)
        sc.persist.add("xTb%d" % tt)
    PH_OFF = XTB_OFF + DC * S * 2

    class Arena:
        def __init__(self, start):
            self.o = (start + 31) // 32 * 32

        def take(self, key, shape, dt, n=None):
            nbytes = int(np.prod(shape[1:])) * (2 if dt == BF16 else 4)
            nbytes = (nbytes + 31) // 32 * 32
            if n is None:
                t = sc.sb(key, shape, dt, self.o)
                self.o += nbytes
                return t
            ts = []
            for i in range(n):
                ts.append(sc.sb("%s%d" % (key, i), shape, dt, self.o))
                self.o += nbytes
            return ts

    def setup_consts():
        sc.op("pool", lambda e: e.memset(ones_f[:], 1.0), writes=["ones_f"])
        sc.op("pool", lambda e: e.memset(ident_f[:], 1.0), writes=["ident_f"])
        sc.op("pool", lambda e: e.affine_select(out=ident_f[:], in_=ident_f[:], pattern=[[-1, 128]],
                                                 compare_op=ALU.is_equal, fill=0.0, base=0,
                                                 channel_multiplier=1),
              reads=["ident_f"], writes=["ident_f"])
        sc.op("dve", lambda e: e.tensor_copy(out=ident_b[:], in_=ident_f[:]), reads=["ident_f"], writes=["ident_b"])
        sc.op("dve", lambda e: e.tensor_copy(out=ones_b[:], in_=ones_f[:]), reads=["ones_f"], writes=["ones_b"])
        stg = sc.sb("c_stg", [128, 128], F32, PH_OFF)
        jobs = []
        for l in range(2):
            for k, src in enumerate((ln_mix_g, ln_mix_b, ln_ffn_g, ln_ffn_b)):
                jobs.append((src[l], 16, lnp[:, l * 4 + k, :]))
            for k in range(3):
                jobs.append((ffn_conv_w[l, k], 86, convp[:, l, k, :]))
            jobs.append((ffn_conv_b[l], 86, convp[:, l, 3, :]))
        for n, (src, rows, dst) in enumerate(jobs):
            sc.dma("sp", "c_stg", stg[0:rows, :], src, writes=["c_stg"])
            sc.op("pe", lambda e, r=rows: e.transpose(PS[0][:, 0:r], stg[0:r, :], ident_f[0:r, 0:r]),
                  reads=["c_stg", "ident_f"], writes=[psk(0)])
            sc.op("dve", lambda e, r=rows, d=dst: e.tensor_copy(out=d, in_=PS[0][:, 0:r]),
                  reads=[psk(0)], writes=["consts"])

    def phase_input(x_src, Rout):
        A = Arena(PH_OFF)
        xin = A.take("xin", [128, D], F32, 2)
        st32 = A.take("st32", [128, DC, T], F32)
        for tt in range(NT):
            for tb in range(4):
                g = tt * 4 + tb
                xi = xin[g % 2]
                xk = "xin%d" % (g % 2)
                sc.dma("sp", xk, xi[:], x_src[g * 128:(g + 1) * 128, :], writes=[xk])
                for q in range(4):
                    bank = q % 2
                    for c4 in range(4):
                        dc = q * 4 + c4
                        sc.op("pe", lambda e, b=bank, c4=c4, dc=dc, xi=xi: e.transpose(
                            PS[b][:, c4 * 128:(c4 + 1) * 128], xi[:, dc * 128:(dc + 1) * 128], ident_f[:]),
                            reads=[xk, "ident_f"], writes=[psk(bank)], inc=(c4 == 3))
                    pv = PS[bank][:].rearrange("p (c t) -> p c t", c=4)
                    if 'act' not in _SKIP:
                      sc.op("act", lambda e, pv=pv, q=q, tb=tb: e.copy(
                        out=st32[:, q * 4:(q + 1) * 4, tb * 128:(tb + 1) * 128], in_=pv),
                        reads=[psk(bank)], writes=["st32"])
                    if 'dve' not in _SKIP:
                      sc.op("dve", lambda e, pv=pv, q=q, g=g: e.tensor_copy(
                        out=xTb[:, q * 4:(q + 1) * 4, g * 128:(g + 1) * 128], in_=pv),
                        reads=[psk(bank)], writes=["xTb%d" % tt])
            conv_pump(4)
            if 'st' not in _SKIP:
              sc.dma("sp", "st32", Rout[:, :, tt * T:(tt + 1) * T].rearrange("c p t -> p c t"), st32[:],
                   reads=["st32"], writes=[("R", id(Rout), tt)])

    def layer_norm(r32, rkey, gi, bi, tt, lo, Rout=None, final=False, ost_lo=None, nbuf=2):
        A = Arena(lo)
        sq = A.take("ln_sq", [128, T], BF16, nbuf)
        rb = A.take("ln_rb", [128, T], BF16, nbuf)
        mean = A.take("ln_mean", [128, T], F32)
        rstd = A.take("ln_rstd", [128, T], F32)
        if final:
            ost = Arena(ost_lo).take("ln_ost", [128, D], F32, 2)
        for c in range(DC):
            s = sq[c % nbuf]
            sk = "ln_sq%d" % (c % nbuf)
            rb_ = rb[c % nbuf]
            rbk = "ln_rb%d" % (c % nbuf)
            sc.op("act", lambda e, s=s, c=c: e.activation(out=s[:], in_=r32[:, c, :], func=AF.Square),
                  reads=[rkey], writes=[sk])
            sc.op("pool", lambda e, rb_=rb_, c=c: e.tensor_copy(out=rb_[:], in_=r32[:, c, :]), reads=[rkey], writes=[rbk])
            sc.op("pe", lambda e, c=c, rb_=rb_: e.matmul(PS[6][:], lhsT=ones_b[:], rhs=rb_[:], start=(c == 0), stop=(c == DC - 1)),
                  reads=[rbk, "ones_b"], writes=[psk(6)], inc=True)
            sc.op("pe", lambda e, s=s, c=c: e.matmul(PS[7][:], lhsT=ones_b[:], rhs=s[:], start=(c == 0), stop=(c == DC - 1)),
                  reads=[sk, "ones_b"], writes=[psk(7)], inc=True)
        sc.op("act", lambda e: e.mul(out=mean[:], in_=PS[6][:], mul=1.0 / D), reads=[psk(6)], writes=["ln_mean"])
        sc.op("dve", lambda e: e.tensor_tensor(out=rstd[:], in0=mean[:], in1=mean[:], op=ALU.mult),
              reads=["ln_mean"], writes=["ln_rstd"])
        sc.op("dve", lambda e: e.scalar_tensor_tensor(out=rstd[:], in0=PS[7][:], scalar=1.0 / D, in1=rstd[:],
                                                      op0=ALU.mult, op1=ALU.subtract),
              reads=[psk(7), "ln_rstd"], writes=["ln_rstd"])
        sc.op("dve", lambda e: e.tensor_scalar(out=rstd[:], in0=rstd[:], scalar1=LN_EPS, scalar2=None, op0=ALU.add),
              reads=["ln_rstd"], writes=["ln_rstd"])
        sc.op("act", lambda e: e.activation(out=rstd[:], in_=rstd[:], func=AF.Sqrt), reads=["ln_rstd"], writes=["ln_rstd"])
        sc.op("dve", lambda e: e.reciprocal(out=rstd[:], in_=rstd[:]), reads=["ln_rstd"], writes=["ln_rstd"])
        for c in range(DC):
            sc.op("dve", lambda e, c=c: e.tensor_tensor(out=r32[:, c, :], in0=r32[:, c, :], in1=mean[:], op=ALU.subtract),
                  reads=[rkey, "ln_mean"], writes=[rkey])
            sc.op("dve", lambda e, c=c: e.tensor_tensor(out=r32[:, c, :], in0=r32[:, c, :], in1=rstd[:], op=ALU.mult),
                  reads=[rkey, "ln_rstd"], writes=[rkey])
            sc.op("act", lambda e, c=c: e.activation(out=r32[:, c, :], in_=r32[:, c, :], func=AF.Identity,
                                                     bias=lnp[:, bi, c:c + 1], scale=lnp[:, gi, c:c + 1]),
                  reads=[rkey, "consts"], writes=[rkey])
            if not final:
                sc.op("pool", lambda e, c=c: e.tensor_copy(out=xTb[:, c, tt * T:(tt + 1) * T], in_=r32[:, c, :]),
                      reads=[rkey], writes=["xTb%d" % tt])
        if not final:
            def store():
                sc.dma("sp", "ln_out_" + str(rkey), Rout[:, :, tt * T:(tt + 1) * T].rearrange("c p t -> p c t"), r32[:],
                       reads=[rkey], writes=[("R", id(Rout), tt)])
            return store
        else:
            for tb in range(4):
                g = tt * 4 + tb
                os_ = ost[g % 2]
                ok = "ln_ost%d" % (g % 2)
                for q in range(4):
                    bank = q % 2
                    for c4 in range(4):
                        dc = q * 4 + c4
                        sc.op("pe", lambda e, b=bank, c4=c4, dc=dc, tb=tb: e.transpose(
                            PS[b][:, c4 * 128:(c4 + 1) * 128], r32[:, dc, tb * 128:(tb + 1) * 128], ident_f[:]),
                            reads=[rkey, "ident_f"], writes=[psk(bank)], inc=(c4 == 3))
                    if q % 2 == 0:
                        sc.op("act", lambda e, b=bank, q=q, os_=os_: e.copy(out=os_[:, q * 512:(q + 1) * 512], in_=PS[b][:]),
                              reads=[psk(bank)], writes=[ok])
                    else:
                        sc.op("dve", lambda e, b=bank, q=q, os_=os_: e.tensor_copy(out=os_[:, q * 512:(q + 1) * 512], in_=PS[b][:]),
                              reads=[psk(bank)], writes=[ok])
                sc.dma("sp", ok, out_d[g * 128:(g + 1) * 128, :], os_[:], reads=[ok], writes=[("out", g)])

    def phase_ffn(l, Rin, Rout, final):
        conv_pump(max(0, (l + 1) * 118 - conv_done[0]))
        A = Arena(PH_OFF)
        aT = A.take("aT", [128, NJ, T], BF16)
        r32 = A.take("r32", [128, DC, T], F32)
        NW = 3
        wg = A.take("wg", [128, DC, 128], BF16, NW)
        wv = A.take("wv", [128, DC, 128], BF16, NW)
        ND = 2
        wd = A.take("wd", [128, NJ, 128], BF16, ND)
        carry = A.take("carry", [128, 2, NJ, 2], F32, 2)
        lo = A.o
        gb = A.take("gb", [128, T], F32, 2)
        vb = A.take("vb", [128, T], F32, 2)
        wup = ffn_w_up[l].rearrange("(c p) f -> p c f", p=128)
        wdn = ffn_w_down[l].rearrange("(j p) d -> p j d", p=128)

        def cw(k, j):
            return convp[:, l, k, j:j + 1]

        pending_store = [None]
        up_jobs = [(tt, j) for tt in range(NT) for j in range(NJ)]
        dn_jobs = [(tt, dc) for tt in range(NT) for dc in range(DC)]
        up_issued = [0]
        dn_issued = [0]

        def issue_up(n):
            while up_issued[0] < min(n, len(up_jobs)):
                i = up_issued[0]
                _, j = up_jobs[i]
                ws = i % NW
                sc.dma("sp", "wg%d" % ws, wg[ws][:], wupb[l][j], reads=[("wupb", l, j)], writes=["wg%d" % ws])
                sc.dma("sp", "wv%d" % ws, wv[ws][:], wupb[l][NJ + j], reads=[("wupb", l, NJ + j)], writes=["wv%d" % ws])
                up_issued[0] += 1

        def issue_dn(n):
            while dn_issued[0] < min(n, len(dn_jobs)):
                i = dn_issued[0]
                _, dc = dn_jobs[i]
                ds = i % ND
                sc.dma("sp", "wd%d" % ds, wd[ds][:], wdnb[l][dc], reads=[("wdnb", l, dc)], writes=["wd%d" % ds])
                dn_issued[0] += 1

        for tt in range(NT):
            tsl = slice(tt * T, (tt + 1) * T)
            xk = "xTb%d" % tt
            cin = carry[(tt + 1) % 2]
            cout = carry[tt % 2]
            cink = "carry%d" % ((tt + 1) % 2)
            coutk = "carry%d" % (tt % 2)
            for j in range(NJ):
                ui = tt * NJ + j
                issue_up(ui + NW - 0 if ui == 0 else ui + NW)
                if j == 30:
                    if pending_store[0] is not None:
                        pending_store[0]()
                        pending_store[0] = None
                    sc.dma("sp", "r32", r32[:], Rin[:, :, tsl].rearrange("c p t -> p c t"),
                           reads=[("R", id(Rin), tt)], writes=["r32"])
                if j == 8:
                    issue_dn(tt * DC + 1)
                if j == 16:
                    issue_dn(tt * DC + 2)
                ws = ui % NW
                pg = (j % 2) * 2
                for half, (w_, wk, gv, cofs) in enumerate(((wg[ws], "wg%d" % ws, gb[j % 2], 0),
                                                           (wv[ws], "wv%d" % ws, vb[j % 2], NJ))):
                    bank = pg + half
                    P = PS[bank]
                    for c in range(DC):
                        rhs = xTb[:, c, tsl]
                        sc.op("pe", lambda e, P=P, w_=w_, c=c, rhs=rhs: e.matmul(P[:], lhsT=w_[:, c, :], rhs=rhs,
                                                                                  start=(c == 0), stop=(c == DC - 1)),
                              reads=[wk, xk], writes=[psk(bank)], inc=(c == DC - 1))
                    gk = ("gb%d" if half == 0 else "vb%d") % (j % 2)
                    jj = j + cofs
                    sc.op("act", lambda e, P=P, gv=gv, jj=jj: e.activation(out=gv[:], in_=P[:], func=AF.Identity,
                                                                           bias=cw(3, jj), scale=cw(2, jj)),
                          reads=[psk(bank), "consts"], writes=[gk])
                    sc.op("dve", lambda e, P=P, gv=gv, jj=jj: e.scalar_tensor_tensor(
                        out=gv[:, 1:T], in0=P[:, 0:T - 1], scalar=cw(1, jj), in1=gv[:, 1:T], op0=ALU.mult, op1=ALU.add),
                        reads=[psk(bank), gk, "consts"], writes=[gk])
                    sc.op("dve", lambda e, P=P, gv=gv, jj=jj: e.scalar_tensor_tensor(
                        out=gv[:, 2:T], in0=P[:, 0:T - 2], scalar=cw(0, jj), in1=gv[:, 2:T], op0=ALU.mult, op1=ALU.add),
                        reads=[psk(bank), gk, "consts"], writes=[gk])
                    if tt > 0:
                        sc.op("dve", lambda e, gv=gv, jj=jj, half=half, j=j, cin=cin: e.scalar_tensor_tensor(
                            out=gv[:, 0:1], in0=cin[:, half, j, 1:2], scalar=cw(1, jj), in1=gv[:, 0:1],
                            op0=ALU.mult, op1=ALU.add), reads=[cink, gk, "consts"], writes=[gk])
                        sc.op("dve", lambda e, gv=gv, jj=jj, half=half, j=j, cin=cin: e.scalar_tensor_tensor(
                            out=gv[:, 0:2], in0=cin[:, half, j, 0:2], scalar=cw(0, jj), in1=gv[:, 0:2],
                            op0=ALU.mult, op1=ALU.add), reads=[cink, gk, "consts"], writes=[gk])
                    if tt < NT - 1:
                        sc.op("act", lambda e, P=P, half=half, j=j, cout=cout: e.copy(out=cout[:, half, j, :], in_=P[:, T - 2:T]),
                              reads=[psk(bank)], writes=[coutk])
                g_ = gb[j % 2]
                v_ = vb[j % 2]
                sc.op("act", lambda e, g_=g_: e.activation(out=g_[:], in_=g_[:], func=AF.Silu),
                      reads=["gb%d" % (j % 2)], writes=["gb%d" % (j % 2)])
                sc.op("pool", lambda e, g_=g_, v_=v_, j=j: e.tensor_tensor(out=aT[:, j, :], in0=g_[:], in1=v_[:], op=ALU.mult),
                      reads=["gb%d" % (j % 2), "vb%d" % (j % 2)], writes=["aT"])
            for dc in range(DC):
                di = tt * DC + dc
                issue_dn(di + ND)
                ds = di % ND
                bank = 4 + dc % 2
                P = PS[bank]
                for j in range(NJ):
                    sc.op("pe", lambda e, P=P, j=j, ds=ds: e.matmul(P[:], lhsT=wd[ds][:, j, :], rhs=aT[:, j, :],
                                                                    start=(j == 0), stop=(j == NJ - 1)),
                          reads=["wd%d" % ds, "aT"], writes=[psk(bank)], inc=(j == NJ - 1))
                sc.op("dve", lambda e, P=P, dc=dc: e.scalar_tensor_tensor(
                    out=r32[:, dc, :], in0=r32[:, dc, :], scalar=ALPHA, in1=P[:], op0=ALU.mult, op1=ALU.add),
                    reads=[psk(bank), "r32"], writes=["r32"])
            pending_store[0] = layer_norm(r32, "r32", l * 4 + 2, l * 4 + 3, tt, lo, Rout=Rout)
        pending_store[0]()

    class Prefetch:
        def __init__(self, n, nslots, issue):
            self.n, self.nslots, self.issue, self.issued = n, nslots, issue, 0

        def ensure(self, i):
            while self.issued < min(self.n, i + self.nslots):
                self.issue(self.issued, self.issued % self.nslots)
                self.issued += 1

    MIX_OFF = PH_OFF
    PH2_OFF = MIX_OFF + DC * S * 2

    _mix = []

    def get_mixT():
        if not _mix:
            _mix.append(nc.alloc_sbuf_tensor_at("mixT_all", [128, DC, S], BF16, offset=MIX_OFF))
        t = _mix[0]
        if "mixlo" not in sc.ranges:
            sc.reg("mixlo", MIX_OFF, MIX_OFF + 8 * S * 2)
            sc.reg("mixhi", MIX_OFF + 8 * S * 2, MIX_OFF + 16 * S * 2)
        return t

    def phase_outproj(l, w_out, Rin, Rout):
        mixT = get_mixT()
        A = Arena(PH2_OFF)
        conv_pump(max(0, l * 118 + 16 - conv_done[0]))
        r32s = A.take("r32_", [128, DC, T], F32, 2)
        NW = 3
        wo = A.take("wo", [128, DC, 128], BF16, NW)
        lo = A.o
        stores = []
        jobs = [(tt, dc) for tt in range(NT) for dc in range(DC)]
        pf = Prefetch(len(jobs), NW, lambda i, s_: sc.dma(
            "sp", "wo%d" % s_, wo[s_][:], woutb[l][jobs[i][1]], reads=[("woutb", l, jobs[i][1])], writes=["wo%d" % s_]))
        for tt in range(NT):
            tsl = slice(tt * T, (tt + 1) * T)
            r32 = r32s[tt % 2]
            rk = "r32_%d" % (tt % 2)
            if tt >= 2:
                stores[tt - 2]()
            sc.dma("sp", rk, r32[:], Rin[:, :, tsl].rearrange("c p t -> p c t"),
                   reads=[("R", id(Rin), tt)], writes=[rk])
            for dc in range(DC):
                i = tt * DC + dc
                pf.ensure(i)
                ws = i % NW
                bank = 4 + dc % 2
                P = PS[bank]
                for c in range(DC):
                    rhs = mixT[:, c, tsl]
                    sc.op("pe", lambda e, P=P, ws=ws, c=c, rhs=rhs: e.matmul(P[:], lhsT=wo[ws][:, c, :], rhs=rhs,
                                                                          start=(c == 0), stop=(c == DC - 1)),
                          reads=["wo%d" % ws, "mixlo" if c < 8 else "mixhi"], writes=[psk(bank)], inc=(c == DC - 1))
                sc.op("dve", lambda e, P=P, dc=dc, r32=r32: e.scalar_tensor_tensor(
                    out=r32[:, dc, :], in0=r32[:, dc, :], scalar=ALPHA, in1=P[:], op0=ALU.mult, op1=ALU.add),
                    reads=[psk(bank), rk], writes=[rk])
            stores.append(layer_norm(r32, rk, l * 4 + 0, l * 4 + 1, tt, lo, Rout=Rout, nbuf=1))
        stores[NT - 2]()
        stores[NT - 1]()

    def phase_fox():
        mixT = get_mixT()
        A = Arena(PH2_OFF)
        qT = A.take("qT", [128, S], BF16, 2)
        kT = A.take("kT", [128, S], BF16, 2)
        Vt = A.take("Vt", [128, 16, 128], BF16, 2)
        wq = A.take("wq", [128, DC, 128], BF16, 2)
        wk = A.take("wk", [128, DC, 128], BF16, 2)
        wv = A.take("wv", [128, DC, 128], BF16, 2)
        PT = A.take("PT", [128, T], BF16, 2)
        negc = A.take("negc", [128, S], F32)
        cneg = A.take("cneg", [128, S], F32)
        negcT = A.take("negcT", [128, 16, 8], F32)
        recips = A.take("recip", [128, T], F32, 2)
        sel = A.take("sel", [128, 8, 128], F32)
        sel3 = A.take("sel3", [128, 8, 128], BF16)
        cneg3 = A.take("cneg3", [128, S], BF16)
        maskb = A.take("maskb", [128, 128], BF16)
        wf = A.take("wf", [128, DC, 8], BF16)
        bfc = A.take("bfc", [128, 2], F32)
        ones_row = sc.sb("ones_row", [128, S], BF16, sc.ranges["qT0"][0])
        c_hi = sc.sb("c_hi", [128, S], BF16, sc.ranges["qT1"][0])
        c_mid = sc.sb("c_mid", [128, S], BF16, sc.ranges["kT1"][0])
        c_lo = sc.sb("c_lo", [128, S], BF16, sc.ranges["kT0"][0])
        win = ev_w_in.rearrange("(c p) f -> p c f", p=128)
        SC = 1.0 / math.sqrt(128.0)

        sc.op("pool", lambda e: e.memset(sel[:], 1.0), writes=["sel"])
        sc.op("pool", lambda e: e.affine_select(out=sel[:], in_=sel[:], pattern=[[-1, 8], [0, 128]],
                                                 compare_op=ALU.is_equal, fill=0.0, base=0, channel_multiplier=1),
              reads=["sel"], writes=["sel"])
        sc.op("pool", lambda e: e.memset(maskb[:], 0.0), writes=["maskb"])
        sc.op("pool", lambda e: e.affine_select(out=maskb[:], in_=maskb[:], pattern=[[1, 128]],
                                                 compare_op=ALU.is_ge, fill=-30000.0, base=0, channel_multiplier=-1),
              reads=["maskb"], writes=["maskb"])
        sc.op("pool", lambda e: e.memset(cneg[:], 0.0), writes=["cneg"])
        sc.op("pool", lambda e: e.memset(ones_row[0:8, :], 1.0), writes=["ones_row"])
        sc.dma("pool", "wf", wf[:], win[:, :, 3072:3080], writes=["wf"])
        sc.dma("sp", "bfc", bfc[0:8, 0:1], ev_b_f, writes=["bfc"])
        sc.op("dve", lambda e: e.tensor_scalar(out=bfc[0:8, 1:2], in0=bfc[0:8, 0:1], scalar1=-1.0, scalar2=None, op0=ALU.mult),
              reads=["bfc"], writes=["bfc"])
        for tt in range(NT):
            tsl = slice(tt * T, (tt + 1) * T)
            for c in range(DC):
                rhs = xTb[:, c, tsl]
                sc.op("pe", lambda e, c=c, rhs=rhs: e.matmul(PS[6][0:8, :], lhsT=wf[:, c, :], rhs=rhs,
                                                              start=(c == 0), stop=(c == DC - 1)),
                      reads=["wf", "xTb%d" % tt], writes=[psk(6)], inc=(c == DC - 1))
            sc.op("act", lambda e, tsl=tsl: e.activation(out=negc[0:8, tsl], in_=PS[6][0:8, :], func=AF.Exp,
                                                          bias=bfc[0:8, 1:2], scale=-1.0),
                  reads=[psk(6), "bfc"], writes=["negc"])
            sc.op("act", lambda e, tsl=tsl: e.activation(out=negc[0:8, tsl], in_=negc[0:8, tsl], func=AF.Ln, bias=1.0, scale=1.0),
                  reads=["negc"], writes=["negc"])
        sc.op("dve", lambda e: e.tensor_tensor_scan(out=negc[0:8, :], data0=ones_row[0:8, :], data1=negc[0:8, :],
                                                    initial=0.0, op0=ALU.mult, op1=ALU.add),
              reads=["negc", "ones_row"], writes=["negc"])
        sc.op("act", lambda e: e.mul(out=cneg[0:8, :], in_=negc[0:8, :], mul=-1.0), reads=["negc"], writes=["cneg"])
        for tb in range(16):
            sc.op("pe", lambda e, tb=tb: e.transpose(PS[7][:, tb * 8:(tb + 1) * 8], negc[0:8, tb * 128:(tb + 1) * 128],
                                                     ident_f[0:8, 0:8]),
                  reads=["negc", "ident_f"], writes=[psk(7)], inc=(tb == 15))
        sc.op("dve", lambda e: e.tensor_copy(out=negcT[:].rearrange("p a b -> p (a b)"), in_=PS[7][:, 0:128]),
              reads=[psk(7)], writes=["negcT"])

        sc.op("pool", lambda e: e.memset(cneg3[:], 0.0), writes=["cneg3"])
        sc.op("dve", lambda e: e.tensor_copy(out=sel3[:], in_=sel[:]), reads=["sel"], writes=["sel3"])
        sc.op("act", lambda e: e.copy(out=c_hi[0:8, :], in_=cneg[0:8, :]), reads=["cneg"], writes=["c_hi"])
        sc.op("dve", lambda e: e.tensor_tensor(out=negc[0:8, :], in0=cneg[0:8, :], in1=c_hi[0:8, :], op=ALU.subtract),
              reads=["cneg", "c_hi", "negc"], writes=["negc"])
        sc.op("act", lambda e: e.copy(out=c_mid[0:8, :], in_=negc[0:8, :]), reads=["negc"], writes=["c_mid"])
        sc.op("dve", lambda e: e.tensor_tensor(out=negc[0:8, :], in0=negc[0:8, :], in1=c_mid[0:8, :], op=ALU.subtract),
              reads=["negc", "c_mid"], writes=["negc"])
        sc.op("act", lambda e: e.copy(out=c_lo[0:8, :], in_=negc[0:8, :]), reads=["negc"], writes=["c_lo"])
        for r_, (src_, sk_) in enumerate(((c_hi, "c_hi"), (c_mid, "c_mid"), (c_lo, "c_lo"))):
            sc.dma("sp", "c3ld", cneg3[8 * r_:8 * r_ + 8, :], src_[0:8, :], reads=[sk_], writes=["cneg3"])
        for r_ in (1, 2):
            sc.dma("sp", "c3ld", sel3[8 * r_:8 * r_ + 8], sel3[0:8], reads=["sel3"], writes=["sel3"])

        def load_head(h):
            s_ = h % 2
            sc.dma("pool", "wq%d" % s_, wq[s_][:], win[:, :, h * 128:(h + 1) * 128], writes=["wq%d" % s_])
            sc.dma("pool", "wk%d" % s_, wk[s_][:], win[:, :, 1024 + h * 128:1024 + (h + 1) * 128], writes=["wk%d" % s_])
            sc.dma("pool", "wv%d" % s_, wv[s_][:], win[:, :, 2048 + h * 128:2048 + (h + 1) * 128], writes=["wv%d" % s_])

        load_head(0)
        pcnt = [0]
        qcnt = [0]
        for h in range(8):
            s_ = h % 2
            if h + 1 < 8:
                load_head(h + 1)
            qk, kk, vk = "qT%d" % s_, "kT%d" % s_, "Vt%d" % s_
            for tt in range(NT):
                tsl = slice(tt * T, (tt + 1) * T)
                for which, (w_, wkey) in enumerate(((wq[s_], "wq%d" % s_), (wk[s_], "wk%d" % s_))):
                    bank = which
                    for c in range(DC):
                        rhs = xTb[:, c, tsl]
                        sc.op("pe", lambda e, bank=bank, w_=w_, c=c, rhs=rhs: e.matmul(
                            PS[bank][:], lhsT=w_[:, c, :], rhs=rhs, start=(c == 0), stop=(c == DC - 1)),
                            reads=[wkey, "xTb%d" % tt], writes=[psk(bank)], inc=(c == DC - 1))
                    if which == 0:
                        sc.op("act", lambda e, tsl=tsl, s_=s_: e.activation(out=qT[s_][:, tsl], in_=PS[0][:], func=AF.Copy, scale=SC),
                              reads=[psk(0)], writes=[qk])
                    else:
                        sc.op("dve", lambda e, tsl=tsl, s_=s_: e.tensor_copy(out=kT[s_][:, tsl], in_=PS[1][:]),
                              reads=[psk(1)], writes=[kk])
            for t4 in range(4):
                bank = t4 % 2
                for q4 in range(4):
                    tb = t4 * 4 + q4
                    for c in range(DC):
                        lhsT = xTb[:, c, tb * 128:(tb + 1) * 128]
                        sc.op("pe", lambda e, bank=bank, q4=q4, c=c, lhsT=lhsT, s_=s_: e.matmul(
                            PS[bank][:, q4 * 128:(q4 + 1) * 128], lhsT=lhsT, rhs=wv[s_][:, c, :],
                            start=(c == 0), stop=(c == DC - 1)),
                            reads=["wv%d" % s_, "xTb%d" % (tb // 4)], writes=[psk(bank)], inc=(c == DC - 1))
                pv = PS[bank][:].rearrange("p (a b) -> p a b", a=4)
                if t4 % 2 == 0:
                    sc.op("act", lambda e, pv=pv, t4=t4, s_=s_: e.copy(out=Vt[s_][:, t4 * 4:(t4 + 1) * 4, :], in_=pv),
                          reads=[psk(bank)], writes=[vk])
                else:
                    sc.op("dve", lambda e, pv=pv, t4=t4, s_=s_: e.tensor_copy(out=Vt[s_][:, t4 * 4:(t4 + 1) * 4, :], in_=pv),
                          reads=[psk(bank)], writes=[vk])
            for Qi in range(4):
                conv_pump(3)
                ob = 4 + 2 * (qcnt[0] % 2)
                lb = ob + 1
                recip = recips[qcnt[0] % 2]
                rck_ = "recip%d" % (qcnt[0] % 2)
                qcnt[0] += 1
                nblk = 4 * Qi + 4
                for j in range(nblk):
                    n0 = max(0, j * 128 - Qi * T)
                    diag = j * 128 >= Qi * T
                    q0 = Qi * T + n0
                    q1 = (Qi + 1) * T
                    bank = 2 + pcnt[0] % 2
                    ps_ = pcnt[0] % 2
                    pcnt[0] += 1
                    P = PS[bank]
                    sc.op("pe", lambda e, P=P, n0=n0, j=j, q0=q0, q1=q1, s_=s_: e.matmul(
                        P[:, n0:T], lhsT=kT[s_][:, j * 128:(j + 1) * 128], rhs=qT[s_][:, q0:q1], start=True, stop=False),
                        reads=[kk, qk], writes=[psk(bank)], inc=False)
                    sc.op("pe", lambda e, P=P, n0=n0, q0=q0, q1=q1, h=h, diag=diag: e.matmul(
                        P[:, n0:T], lhsT=sel3[:, h, :], rhs=cneg3[:, q0:q1], start=False, stop=(not diag)),
                        reads=["sel3", "cneg3"], writes=[psk(bank)], inc=(not diag))
                    if diag:
                        sc.op("pe", lambda e, P=P, n0=n0: e.matmul(
                            P[:, n0:n0 + 128], lhsT=ident_b[:], rhs=maskb[:], start=False, stop=True),
                            reads=["ident_b", "maskb"], writes=[psk(bank)], inc=True)
                    ptk = "PT%d" % ps_
                    sc.op("act", lambda e, P=P, n0=n0, j=j, h=h, ps_=ps_: e.activation(
                        out=PT[ps_][:, n0:T], in_=P[:, n0:T], func=AF.Exp, bias=negcT[:, j, h:h + 1], scale=1.0),
                        reads=[psk(bank), "negcT"], writes=[ptk])
                    last = (j == nblk - 1)
                    sc.op("pe", lambda e, n0=n0, j=j, ps_=ps_, s_=s_, last=last, ob=ob: e.matmul(
                        PS[ob][:, n0:T], lhsT=Vt[s_][:, j, :], rhs=PT[ps_][:, n0:T], start=(j == 0), stop=last),
                        reads=[vk, ptk], writes=[psk(ob)], inc=False)
                    sc.op("pe", lambda e, n0=n0, j=j, ps_=ps_, last=last, lb=lb: e.matmul(
                        PS[lb][:, n0:T], lhsT=ones_b[:], rhs=PT[ps_][:, n0:T], start=(j == 0), stop=last),
                        reads=["ones_b", ptk], writes=[psk(lb)], inc=True)
                sc.op("dve", lambda e, lb=lb, recip=recip: e.reciprocal(out=recip[:], in_=PS[lb][:]), reads=[psk(lb)], writes=[rck_])
                sc.op("dve", lambda e, h=h, Qi=Qi, ob=ob, recip=recip: e.tensor_tensor(
                    out=mixT[:, h, Qi * T:(Qi + 1) * T], in0=PS[ob][:], in1=recip[:], op=ALU.mult),
                    reads=[psk(ob), rck_], writes=["mixlo"])
        wu = wq
        pf = Prefetch(8, 2, lambda i, s_: sc.dma("pool", "wq%d" % s_, wu[s_][:], win[:, :, 3080 + i * 128:3080 + (i + 1) * 128],
                                                 writes=["wq%d" % s_]))
        for ch in range(8):
            pf.ensure(ch)
            s_ = ch % 2
            for tt in range(NT):
                tsl = slice(tt * T, (tt + 1) * T)
                bank = tt % 2
                for c in range(DC):
                    rhs = xTb[:, c, tsl]
                    sc.op("pe", lambda e, bank=bank, c=c, rhs=rhs, s_=s_: e.matmul(
                        PS[bank][:], lhsT=wu[s_][:, c, :], rhs=rhs, start=(c == 0), stop=(c == DC - 1)),
                        reads=["wq%d" % s_, "xTb%d" % tt], writes=[psk(bank)], inc=(c == DC - 1))
                if tt % 2 == 0:
                    sc.op("act", lambda e, bank=bank, ch=ch, tsl=tsl: e.copy(out=mixT[:, 8 + ch, tsl], in_=PS[bank][:]),
                          reads=[psk(bank)], writes=["mixhi"])
                else:
                    sc.op("dve", lambda e, bank=bank, ch=ch, tsl=tsl: e.tensor_copy(out=mixT[:, 8 + ch, tsl], in_=PS[bank][:]),
                          reads=[psk(bank)], writes=["mixhi"])

    Wd = nc.dram_tensor("s5w", [8, 128, 16, 128], BF16, kind="Internal").ap()
    TWO_PI = 2.0 * math.pi

    def bk_level(k, l, first, xr_, xi_, xrk, xik, i, AQ_re, AQ_im, AQ_in):
        if first >= S:
            return []
        src = slice(first - k, S - k, 2 * k)
        dst = slice(first, S, 2 * k)
        ar = AQ_re[:, i, l:l + 1]
        ai = AQ_im[:, i, l:l + 1]
        an = AQ_in[:, i, l:l + 1]
        return [
            (lambda e: e.scalar_tensor_tensor(out=xr_[:, dst], in0=xr_[:, src], scalar=ar, in1=xr_[:, dst],
                                              op0=ALU.mult, op1=ALU.add), [xrk, "AQ_re"], [xrk]),
            (lambda e: e.scalar_tensor_tensor(out=xr_[:, dst], in0=xi_[:, src], scalar=an, in1=xr_[:, dst],
                                              op0=ALU.mult, op1=ALU.add), [xrk, xik, "AQ_in"], [xrk]),
            (lambda e: e.scalar_tensor_tensor(out=xi_[:, dst], in0=xi_[:, src], scalar=ar, in1=xi_[:, dst],
                                              op0=ALU.mult, op1=ALU.add), [xik, "AQ_re"], [xik]),
            (lambda e: e.scalar_tensor_tensor(out=xi_[:, dst], in0=xr_[:, src], scalar=ai, in1=xi_[:, dst],
                                              op0=ALU.mult, op1=ALU.add), [xik, xrk, "AQ_im"], [xik]),
        ]

    def phase_s5():
        mixT = get_mixT()
        AP_ = Arena(PH2_OFF)
        yg = AP_.take("yg", [128, 8, S], BF16)
        wslot = AP_.take("wslot", [128, 16, 128], BF16, 2)
        AQ_re = AP_.take("AQ_re", [128, 32, 11], F32)
        AQ_im = AP_.take("AQ_im", [128, 32, 11], F32)
        AQ_in = AP_.take("AQ_in", [128, 32, 11], F32)
        Dcol = AP_.take("Dcol", [128, 8], F32)
        y32 = AP_.take("y32", [128, T], F32, 2)
        gt = AP_.take("gt", [128, T], F32, 2)
        wz1 = AP_.take("wz1", [128, 8, 128], BF16, 2)
        wz2 = AP_.take("wz2", [128, 8, 128], BF16, 2)
        sig = AP_.take("sig", [128, T], F32, 2)
        AX = Arena(XTB_OFF)
        lamraw = AX.take("lamraw", [128, 2, 128], F32)
        names = ["lr", "li", "dt", "lrd", "lid", "mag", "kf", "rr", "rc", "m1", "sn", "cs", "are", "aim",
                 "den", "xr", "gre", "gim", "t1", "t2"]
        sm = {n: AX.take("s5_" + n, [128, 64], F32) for n in names}
        ki = AX.take("s5_ki", [128, 64], I32)
        P_re = AX.take("P_re", [128, 64, 11], F32)
        P_im = AX.take("P_im", [128, 64, 11], F32)
        b_re = AX.take("b_re", [128, 64, 16], F32)
        b_im = AX.take("b_im", [128, 64, 16], F32)
        bb_re = AX.take("bb_re", [128, 64, 16], F32)
        bb_im = AX.take("bb_im", [128, 64, 16], F32)
        btmp = AX.take("btmp", [128, 64, 16], F32)
        CT_re = AX.take("CT_re", [128, 8, 128], F32)
        CT_im = AX.take("CT_im", [128, 8, 128], F32)
        MC = AX.take("MC", [128, 4, 128], F32)
        MB = AX.take("MB", [128, 4, 128], F32)
        dstg = AX.take("dstg", [128, 128], F32)
        stage = AX.take("wstage", [128, 16, 128], BF16, 2)

        def dve(fn, reads, writes):
            sc.op("dve", fn, reads=reads, writes=writes)

        def tt_(out, a, b, op, reads, writes):
            dve(lambda e: e.tensor_tensor(out=out, in0=a, in1=b, op=op), reads, writes)

        K = lambda n: "s5_" + n
        for half in range(2):
            sc.dma("sp", "s5ld", lamraw[0:64, 0, half * 64:(half + 1) * 64], ev_lre, writes=["lamraw"])
            sc.dma("sp", "s5ld", lamraw[0:64, 1, half * 64:(half + 1) * 64], ev_lim, writes=["lamraw"])
            sc.dma("sp", "s5ld", b_re[half * 64:(half + 1) * 64], ev_bre.rearrange("g p c -> p g c"), writes=["b_re"])
            sc.dma("sp", "s5ld", b_im[half * 64:(half + 1) * 64], ev_bim.rearrange("g p c -> p g c"), writes=["b_im"])
            sc.dma("sp", "s5ld", CT_re[:, :, half * 64:(half + 1) * 64], ev_cre.rearrange("(j a) c p -> (a c) j p", a=8),
                   writes=["CT_re"])
            sc.dma("sp", "s5ld", CT_im[:, :, half * 64:(half + 1) * 64], ev_cim.rearrange("(j a) c p -> (a c) j p", a=8),
                   writes=["CT_im"])
        sc.dma("sp", "s5ld", sm["dt"][:], ev_lstep.broadcast_to([128, 64]), writes=[K("dt")])
        sc.dma("sp", "s5ld", dstg[0:8, :], ev_d, writes=["dstg"])
        for w, n in ((0, "lr"), (1, "li")):
            sc.op("pe", lambda e, w=w: e.transpose(PS[0][:, 0:64], lamraw[0:64, w, :], ident_f[0:64, 0:64]),
                  reads=["lamraw", "ident_f"], writes=[psk(0)])
            dve(lambda e, n=n: e.tensor_copy(out=sm[n][:], in_=PS[0][:, 0:64]), [psk(0)], [K(n)])
        sc.op("pe", lambda e: e.transpose(PS[0][:, 0:8], dstg[0:8, :], ident_f[0:8, 0:8]),
              reads=["dstg", "ident_f"], writes=[psk(0)])
        dve(lambda e: e.tensor_copy(out=Dcol[:], in_=PS[0][:, 0:8]), [psk(0)], ["Dcol"])
        sc.op("act", lambda e: e.activation(out=sm["dt"][:], in_=sm["dt"][:], func=AF.Exp), reads=[K("dt")], writes=[K("dt")])
        tt_(sm["lrd"][:], sm["lr"][:], sm["dt"][:], ALU.mult, [K("lr"), K("dt")], [K("lrd")])
        tt_(sm["lid"][:], sm["li"][:], sm["dt"][:], ALU.mult, [K("li"), K("dt")], [K("lid")])
        sc.op("act", lambda e: e.activation(out=sm["mag"][:], in_=sm["lrd"][:], func=AF.Exp), reads=[K("lrd")], writes=[K("mag")])
        dve(lambda e: e.tensor_scalar(out=sm["kf"][:], in0=sm["lid"][:], scalar1=1.0 / TWO_PI, scalar2=0.5,
                                      op0=ALU.mult, op1=ALU.add), [K("lid")], [K("kf")])
        dve(lambda e: e.tensor_copy(out=ki[:], in_=sm["kf"][:]), [K("kf")], ["s5_ki"])
        dve(lambda e: e.tensor_copy(out=sm["kf"][:], in_=ki[:]), ["s5_ki"], [K("kf")])
        dve(lambda e: e.scalar_tensor_tensor(out=sm["rr"][:], in0=sm["kf"][:], scalar=-TWO_PI, in1=sm["lid"][:],
                                             op0=ALU.mult, op1=ALU.add), [K("kf"), K("lid")], [K("rr")])

        def wrap(n):
            dve(lambda e: e.tensor_scalar(out=sm["m1"][:], in0=sm[n][:], scalar1=math.pi, scalar2=None, op0=ALU.is_gt),
                [K(n)], [K("m1")])
            dve(lambda e: e.scalar_tensor_tensor(out=sm[n][:], in0=sm["m1"][:], scalar=-TWO_PI, in1=sm[n][:],
                                                 op0=ALU.mult, op1=ALU.add), [K("m1"), K(n)], [K(n)])
            dve(lambda e: e.tensor_scalar(out=sm["m1"][:], in0=sm[n][:], scalar1=-math.pi, scalar2=None, op0=ALU.is_lt),
                [K(n)], [K("m1")])
            dve(lambda e: e.scalar_tensor_tensor(out=sm[n][:], in0=sm["m1"][:], scalar=TWO_PI, in1=sm[n][:],
                                                 op0=ALU.mult, op1=ALU.add), [K("m1"), K(n)], [K(n)])

        wrap("rr")
        dve(lambda e: e.tensor_scalar(out=sm["rc"][:], in0=sm["rr"][:], scalar1=0.5 * math.pi, scalar2=None, op0=ALU.add),
            [K("rr")], [K("rc")])
        wrap("rc")
        sc.op("act", lambda e: e.activation(out=sm["sn"][:], in_=sm["rr"][:], func=AF.Sin), reads=[K("rr")], writes=[K("sn")])
        sc.op("act", lambda e: e.activation(out=sm["cs"][:], in_=sm["rc"][:], func=AF.Sin), reads=[K("rc")], writes=[K("cs")])
        tt_(sm["are"][:], sm["mag"][:], sm["cs"][:], ALU.mult, [K("mag"), K("cs")], [K("are")])
        tt_(sm["aim"][:], sm["mag"][:], sm["sn"][:], ALU.mult, [K("mag"), K("sn")], [K("aim")])
        tt_(sm["t1"][:], sm["lr"][:], sm["lr"][:], ALU.mult, [K("lr")], [K("t1")])
        tt_(sm["den"][:], sm["li"][:], sm["li"][:], ALU.mult, [K("li")], [K("den")])
        tt_(sm["den"][:], sm["den"][:], sm["t1"][:], ALU.add, [K("den"), K("t1")], [K("den")])
        dve(lambda e: e.reciprocal(out=sm["den"][:], in_=sm["den"][:]), [K("den")], [K("den")])
        dve(lambda e: e.tensor_scalar(out=sm["xr"][:], in0=sm["are"][:], scalar1=-1.0, scalar2=None, op0=ALU.add),
            [K("are")], [K("xr")])
        tt_(sm["t1"][:], sm["xr"][:], sm["lr"][:], ALU.mult, [K("xr"), K("lr")], [K("t1")])
        tt_(sm["t2"][:], sm["aim"][:], sm["li"][:], ALU.mult, [K("aim"), K("li")], [K("t2")])
        tt_(sm["t1"][:], sm["t1"][:], sm["t2"][:], ALU.add, [K("t1"), K("t2")], [K("t1")])
        tt_(sm["gre"][:], sm["t1"][:], sm["den"][:], ALU.mult, [K("t1"), K("den")], [K("gre")])
        tt_(sm["t1"][:], sm["aim"][:], sm["lr"][:], ALU.mult, [K("aim"), K("lr")], [K("t1")])
        tt_(sm["t2"][:], sm["xr"][:], sm["li"][:], ALU.mult, [K("xr"), K("li")], [K("t2")])
        tt_(sm["t1"][:], sm["t1"][:], sm["t2"][:], ALU.subtract, [K("t1"), K("t2")], [K("t1")])
        tt_(sm["gim"][:], sm["t1"][:], sm["den"][:], ALU.mult, [K("t1"), K("den")], [K("gim")])
        gre_b = sm["gre"][:].unsqueeze(2).broadcast_to([128, 64, 16])
        gim_b = sm["gim"][:].unsqueeze(2).broadcast_to([128, 64, 16])
        tt_(bb_re[:], b_re[:], gre_b, ALU.mult, ["b_re", K("gre")], ["bb_re"])
        tt_(btmp[:], b_im[:], gim_b, ALU.mult, ["b_im", K("gim")], ["btmp"])
        tt_(bb_re[:], bb_re[:], btmp[:], ALU.subtract, ["bb_re", "btmp"], ["bb_re"])
        tt_(bb_im[:], b_im[:], gre_b, ALU.mult, ["b_im", K("gre")], ["bb_im"])
        tt_(btmp[:], b_re[:], gim_b, ALU.mult, ["b_re", K("gim")], ["btmp"])
        tt_(bb_im[:], bb_im[:], btmp[:], ALU.add, ["bb_im", "btmp"], ["bb_im"])
        dve(lambda e: e.tensor_copy(out=P_re[:, :, 0], in_=sm["are"][:]), [K("are")], ["P_re"])
        dve(lambda e: e.tensor_copy(out=P_im[:, :, 0], in_=sm["aim"][:]), [K("aim")], ["P_im"])
        for l in range(1, 11):
            tt_(sm["t1"][:], P_re[:, :, l - 1], P_re[:, :, l - 1], ALU.mult, ["P_re"], [K("t1")])
            tt_(sm["t2"][:], P_im[:, :, l - 1], P_im[:, :, l - 1], ALU.mult, ["P_im"], [K("t2")])
            tt_(P_re[:, :, l], sm["t1"][:], sm["t2"][:], ALU.subtract, [K("t1"), K("t2")], ["P_re"])
            tt_(sm["t1"][:], P_re[:, :, l - 1], P_im[:, :, l - 1], ALU.mult, ["P_re", "P_im"], [K("t1")])
            dve(lambda e, l=l: e.tensor_scalar(out=P_im[:, :, l], in0=sm["t1"][:], scalar1=2.0, scalar2=None, op0=ALU.mult),
                [K("t1")], ["P_im"])
        for (src, dst, dk) in ((P_re, AQ_re, "AQ_re"), (P_im, AQ_im, "AQ_im")):
            sv = src[:].rearrange("p (i two) l -> p i two l", two=2)
            sk = "P_re" if src is P_re else "P_im"
            dve(lambda e, sv=sv, dst=dst: e.tensor_copy(out=dst[0:64], in_=sv[0:64, :, 0, :]), [sk], [dk])
            dve(lambda e, sv=sv, dst=dst: e.tensor_copy(out=dst[64:128], in_=sv[64:128, :, 1, :]), [sk], [dk])
        dve(lambda e: e.tensor_scalar(out=AQ_in[:], in0=AQ_im[:], scalar1=-1.0, scalar2=None, op0=ALU.mult), ["AQ_im"], ["AQ_in"])
        sc.op("pool", lambda e: e.memset(MC[:], 0.0), writes=["MC"])
        for m in range(4):
            sc.op("pool", lambda e, m=m: e.memset(MC[0:64, m, 32 * m:32 * m + 16], 1.0), reads=["MC"], writes=["MC"])
            sc.op("pool", lambda e, m=m: e.memset(MC[64:128, m, 32 * m + 16:32 * m + 32], 1.0), reads=["MC"], writes=["MC"])
        for m in range(4):
            sc.op("pe", lambda e, m=m: e.transpose(PS[1][:, m * 128:(m + 1) * 128], MC[:, m, :], ident_f[:]),
                  reads=["MC", "ident_f"], writes=[psk(1)], inc=(m == 3))
        dve(lambda e: e.tensor_copy(out=MB[:].rearrange("p a b -> p (a b)"), in_=PS[1][:]), [psk(1)], ["MB"])
        for j in range(8):
            st_ = stage[j % 2]
            stk = "wstage%d" % (j % 2)
            srcs = ((bb_re[:, 8 * j:8 * j + 8, :].rearrange("p a b -> p (a b)"), "bb_re", 2, 0),
                    (bb_im[:, 8 * j:8 * j + 8, :].rearrange("p a b -> p (a b)"), "bb_im", 2, 1),
                    (CT_re[:, j, :], "CT_re", 3, 0), (CT_im[:, j, :], "CT_im", 3, 1))
            for (src, sk, bank, half) in srcs:
                sc.op("pe", lambda e, src=src, bank=bank, half=half: e.transpose(
                    PS[bank][:, half * 128:(half + 1) * 128], src, ident_f[:]),
                    reads=[sk, "ident_f"], writes=[psk(bank)], inc=True)
            for m in range(4):
                dve(lambda e, st_=st_, m=m: e.tensor_tensor(out=st_[:, 0 + m, :], in0=PS[2][:, 0:128], in1=MB[:, m, :], op=ALU.mult),
                    [psk(2), "MB"], [stk])
                dve(lambda e, st_=st_, m=m: e.tensor_tensor(out=st_[:, 4 + m, :], in0=PS[2][:, 128:256], in1=MB[:, m, :], op=ALU.mult),
                    [psk(2), "MB"], [stk])
                dve(lambda e, st_=st_, m=m: e.tensor_tensor(out=st_[:, 8 + m, :], in0=PS[3][:, 0:128], in1=MC[:, m, :], op=ALU.mult),
                    [psk(3), "MC"], [stk])
                dve(lambda e, st_=st_, m=m: e.scalar_tensor_tensor(out=st_[:, 12 + m, :], in0=PS[3][:, 128:256], scalar=-1.0,
                                                                   in1=MC[:, m, :], op0=ALU.mult, op1=ALU.mult),
                    [psk(3), "MC"], [stk])
            sc.dma("sp", stk, Wd[j], st_[:], reads=[stk], writes=[("Wd", j)])
        mixT2 = mixT
        AX2 = Arena(XTB_OFF)
        X_re = AX2.take("X_re", [128, S], F32, 2)
        X_im = AX2.take("X_im", [128, S], F32, 2)
        H_re = AX2.take("H_re", [128, S], BF16, 4)
        H_im = AX2.take("H_im", [128, S], BF16, 4)
        C1 = 2.0 * math.sqrt(2.0 / math.pi)
        C2 = C1 * 0.044715
        bcnt = [0]
        for j in range(8):
            wsl = wslot[j % 2]
            wk_ = "wslot%d" % (j % 2)
            sc.dma("sp", wk_, wsl[:], Wd[j], reads=[("Wd", j)], writes=[wk_])
            for mp in (0, 2):
                conv_pump(8)
                chains = []
                for m in (mp, mp + 1):
                    i = 4 * j + m
                    xs = i % 2
                    xr_, xi_ = X_re[xs], X_im[xs]
                    xrk, xik = "X_re%d" % xs, "X_im%d" % xs
                    for tt in range(NT):
                        tsl = slice(tt * T, (tt + 1) * T)
                        for half, (dst, dk) in enumerate(((xr_, xrk), (xi_, xik))):
                            bank = bcnt[0] % 4
                            bcnt[0] += 1
                            sc.op("pe", lambda e, bank=bank, half=half, m=m, wsl=wsl, j=j, tsl=tsl: e.matmul(
                                PS[bank][:], lhsT=wsl[:, 4 * half + m, :], rhs=mixT2[:, 8 + j, tsl], start=True, stop=True),
                                reads=[wk_, "mixhi"], writes=[psk(bank)], inc=True)
                            sc.op("act", lambda e, bank=bank, dst=dst, tsl=tsl: e.copy(out=dst[:, tsl], in_=PS[bank][:]),
                                  reads=[psk(bank)], writes=[dk])
                    ops = []
                    for l in range(11):
                        ops += bk_level(1 << l, l, 2 * (1 << l) - 1, xr_, xi_, xrk, xik, i, AQ_re, AQ_im, AQ_in)
                    for l in range(9, -1, -1):
                        ops += bk_level(1 << l, l, 3 * (1 << l) - 1, xr_, xi_, xrk, xik, i, AQ_re, AQ_im, AQ_in)
                    chains.append((m, xr_, xi_, xrk, xik, ops))
                for idx in range(max(len(c[5]) for c in chains)):
                    for c_ in chains:
                        if idx < len(c_[5]):
                            fn, rd, wr = c_[5][idx]
                            sc.op("dve", fn, reads=rd, writes=wr)
                for (m, xr_, xi_, xrk, xik, _) in chains:
                    sc.op("act", lambda e, m=m, xr_=xr_: e.copy(out=H_re[m][:], in_=xr_[:]), reads=[xrk], writes=["H_re%d" % m])
                    sc.op("act", lambda e, m=m, xi_=xi_: e.copy(out=H_im[m][:], in_=xi_[:]), reads=[xik], writes=["H_im%d" % m])
            for tt in range(NT):
                tsl = slice(tt * T, (tt + 1) * T)
                bank = 4 + tt % 2
                for m in range(4):
                    sc.op("pe", lambda e, bank=bank, m=m, wsl=wsl, tsl=tsl: e.matmul(
                        PS[bank][:], lhsT=wsl[:, 8 + m, :], rhs=H_re[m][:, tsl], start=(m == 0), stop=False),
                        reads=[wk_, "H_re%d" % m], writes=[psk(bank)], inc=False)
                    sc.op("pe", lambda e, bank=bank, m=m, wsl=wsl, tsl=tsl: e.matmul(
                        PS[bank][:], lhsT=wsl[:, 12 + m, :], rhs=H_im[m][:, tsl], start=False, stop=(m == 3)),
                        reads=[wk_, "H_im%d" % m], writes=[psk(bank)], inc=(m == 3))
                y_ = y32[tt % 2]
                g_ = gt[tt % 2]
                yk, gk = "y32%d" % (tt % 2), "gt%d" % (tt % 2)
                sc.op("dve", lambda e, bank=bank, y_=y_, j=j, tsl=tsl: e.scalar_tensor_tensor(
                    out=y_[:], in0=mixT2[:, 8 + j, tsl], scalar=Dcol[:, j:j + 1], in1=PS[bank][:], op0=ALU.mult, op1=ALU.add),
                    reads=[psk(bank), "mixhi", "Dcol"], writes=[yk])
                sc.op("pool", lambda e, y_=y_, g_=g_: e.tensor_tensor(out=g_[:], in0=y_[:], in1=y_[:], op=ALU.mult),
                      reads=[yk], writes=[gk])
                sc.op("pool", lambda e, g_=g_: e.tensor_scalar(out=g_[:], in0=g_[:], scalar1=C2, scalar2=C1, op0=ALU.mult, op1=ALU.add),
                      reads=[gk], writes=[gk])
                sc.op("pool", lambda e, y_=y_, g_=g_: e.tensor_tensor(out=g_[:], in0=g_[:], in1=y_[:], op=ALU.mult),
                      reads=[gk, yk], writes=[gk])
                sc.op("act", lambda e, g_=g_: e.activation(out=g_[:], in_=g_[:], func=AF.Sigmoid), reads=[gk], writes=[gk])
                sc.op("pool", lambda e, y_=y_, g_=g_, j=j, tsl=tsl: e.tensor_tensor(out=yg[:, j, tsl], in0=g_[:], in1=y_[:], op=ALU.mult),
                      reads=[gk, yk], writes=["yg"])
        wgl = ev_w_glu.rearrange("(c p) f -> p c f", p=128)

        def issue_glu(i, s_):
            sc.dma("pool", "wz1%d" % s_, wz1[s_][:], wgl[:, :, i * 128:(i + 1) * 128], writes=["wz1%d" % s_])
            sc.dma("pool", "wz2%d" % s_, wz2[s_][:], wgl[:, :, 1024 + i * 128:1024 + (i + 1) * 128], writes=["wz2%d" % s_])

        pf = Prefetch(8, 2, issue_glu)
        for e_ in range(8):
            pf.ensure(e_)
            s_ = e_ % 2
            for tt in range(NT):
                tsl = slice(tt * T, (tt + 1) * T)
                ba, bb_ = (tt % 2) * 2, (tt % 2) * 2 + 1
                for (bank, w_, wkey) in ((ba, wz1[s_], "wz1%d" % s_), (bb_, wz2[s_], "wz2%d" % s_)):
                    for c in range(8):
                        sc.op("pe", lambda e, bank=bank, w_=w_, c=c, tsl=tsl: e.matmul(
                            PS[bank][:], lhsT=w_[:, c, :], rhs=yg[:, c, tsl], start=(c == 0), stop=(c == 7)),
                            reads=[wkey, "yg"], writes=[psk(bank)], inc=(c == 7))
                sg_ = sig[tt % 2]
                sgk = "sig%d" % (tt % 2)
                sc.op("act", lambda e, bb_=bb_, sg_=sg_: e.activation(out=sg_[:], in_=PS[bb_][:], func=AF.Sigmoid),
                      reads=[psk(bb_)], writes=[sgk])
                sc.op("dve", lambda e, ba=ba, sg_=sg_, e_=e_, tsl=tsl: e.tensor_tensor(
                    out=mixT2[:, 8 + e_, tsl], in0=PS[ba][:], in1=sg_[:], op=ALU.mult),
                    reads=[psk(ba), sgk], writes=["mixhi"])

    INVF = [float(v) for v in (np.float32(500000.0) ** (-(np.arange(8, dtype=np.float32) / np.float32(8.0))))]

    def psb(i):
        return PS[i][:].bitcast(BF16)

    def phase_swa():
        mixT = get_mixT()
        A = Arena(PH2_OFF)
        wcol = A.take("wcol", [128, DC, 256], BF16, 2)
        qb = A.take("qb", [128, 4, 64], BF16, 2)
        kd = A.take("kd", [128, 4, 2, 64], BF16, 2)
        qTp = A.take("qTp", [128, S], BF16, 4)
        kTd = A.take("kTd", [128, S], BF16, 4)
        Vs = A.take("Vs", [128, 16, 256], BF16)
        PTa = A.take("PTa", [128, T], BF16, 2)
        PTb = A.take("PTb", [128, T], BF16, 2)
        mask2 = A.take("mask2", [128, 512], BF16)
        maskA = mask2[:, 0:256].rearrange("p (a b) -> p a b", a=2)
        maskB = mask2[:, 256:512].rearrange("p (a b) -> p a b", a=2)
        cosk = A.take("cosk", [128, 16, 8], F32)
        sink = A.take("sink", [128, 16, 8], F32)
        cosq = A.take("cosq", [128, 16, 8], F32)
        sinq = A.take("sinq", [128, 16, 8], F32)
        ang = A.take("ang", [128, 16, 8], F32)
        angc = A.take("angc", [128, 16, 8], F32)
        rtmp = A.take("rtmp", [128, 16, 8], F32)
        rki = A.take("rki", [128, 16, 8], I32)
        posi = A.take("posi", [128, 128], I32)
        posf = A.take("posf", [128, 128], F32)
        posT = A.take("posT", [128, 16], F32)
        invf = A.take("invf", [128, 8], F32)
        es = A.take("es", [128, 32], F32)
        esP = A.take("esP", [128, 16], F32)
        rt = A.take("rt", [128, 6, 4, 8], F32, 2)
        rot = A.take("rot", [128, 4, 16], F32, 2)
        rc = A.take("rc", [128, 256], F32, 2)
        win = od_w_in.rearrange("(c p) f -> p c f", p=128)

        def dve(fn, reads, writes):
            sc.op("dve", fn, reads=reads, writes=writes)

        sc.op("pool", lambda e: e.memset(mask2[:], 0.0), writes=["mask2"])
        sc.op("pool", lambda e: e.affine_select(out=maskA, in_=maskA, pattern=[[0, 2], [-1, 128]],
                                                 compare_op=ALU.is_ge, fill=-30000.0, base=-1, channel_multiplier=1),
              reads=["mask2"], writes=["mask2"])
        sc.op("pool", lambda e: e.affine_select(out=maskB, in_=maskB, pattern=[[0, 2], [1, 128]],
                                                 compare_op=ALU.is_ge, fill=-30000.0, base=0, channel_multiplier=-1),
              reads=["mask2"], writes=["mask2"])
        sc.dma("sp", "swld", es[:], od_sinks.broadcast_to([128, 32]), writes=["es"])
        sc.op("act", lambda e: e.activation(out=es[:], in_=es[:], func=AF.Exp), reads=["es"], writes=["es"])
        esv = es[:].rearrange("p (a two) -> p a two", two=2)
        dve(lambda e: e.tensor_copy(out=esP[0:64, :], in_=esv[0:64, :, 0]), ["es"], ["esP"])
        dve(lambda e: e.tensor_copy(out=esP[64:128, :], in_=esv[64:128, :, 1]), ["es"], ["esP"])
        sc.dma("sp", "swld", posi[0:16, :], pos_in, writes=["posi"])
        dve(lambda e: e.tensor_copy(out=posf[0:16, :], in_=posi[0:16, :]), ["posi"], ["posf"])
        sc.op("pe", lambda e: e.transpose(PS[7][:, 0:16], posf[0:16, :], ident_f[0:16, 0:16]),
              reads=["posf", "ident_f"], writes=[psk(7)])
        dve(lambda e: e.tensor_copy(out=posT[:], in_=PS[7][:, 0:16]), [psk(7)], ["posT"])
        for f in range(8):
            sc.op("pool", lambda e, f=f: e.memset(invf[:, f:f + 1], INVF[f]), writes=["invf"])
        dve(lambda e: e.tensor_tensor(out=ang[:], in0=posT[:].unsqueeze(2).broadcast_to([128, 16, 8]),
                                      in1=invf[:].unsqueeze(1).broadcast_to([128, 16, 8]), op=ALU.mult),
            ["posT", "invf"], ["ang"])

        def reduce_(x, xk):
            dve(lambda e: e.tensor_scalar(out=rtmp[:], in0=x[:], scalar1=1.0 / TWO_PI, scalar2=0.5, op0=ALU.mult, op1=ALU.add),
                [xk], ["rtmp"])
            dve(lambda e: e.tensor_copy(out=rki[:], in_=rtmp[:]), ["rtmp"], ["rki"])
            dve(lambda e: e.tensor_copy(out=rtmp[:], in_=rki[:]), ["rki"], ["rtmp"])
            dve(lambda e: e.scalar_tensor_tensor(out=x[:], in0=rtmp[:], scalar=-TWO_PI, in1=x[:], op0=ALU.mult, op1=ALU.add),
                ["rtmp", xk], [xk])
            for (thr, op_, add) in ((math.pi, ALU.is_gt, -TWO_PI), (-math.pi, ALU.is_lt, TWO_PI)):
                dve(lambda e, thr=thr, op_=op_: e.tensor_scalar(out=rtmp[:], in0=x[:], scalar1=thr, scalar2=None, op0=op_),
                    [xk], ["rtmp"])
                dve(lambda e, add=add: e.scalar_tensor_tensor(out=x[:], in0=rtmp[:], scalar=add, in1=x[:], op0=ALU.mult, op1=ALU.add),
                    ["rtmp", xk], [xk])

        dve(lambda e: e.tensor_scalar(out=angc[:], in0=ang[:], scalar1=0.5 * math.pi, scalar2=None, op0=ALU.add), ["ang"], ["angc"])
        reduce_(ang, "ang")
        reduce_(angc, "angc")
        sc.op("act", lambda e: e.activation(out=sink[:], in_=ang[:], func=AF.Sin), reads=["ang"], writes=["sink"])
        sc.op("act", lambda e: e.activation(out=cosk[:], in_=angc[:], func=AF.Sin), reads=["angc"], writes=["cosk"])
        sc.op("act", lambda e: e.mul(out=sinq[:], in_=sink[:], mul=0.125), reads=["sink"], writes=["sinq"])
        sc.op("act", lambda e: e.mul(out=cosq[:], in_=cosk[:], mul=0.125), reads=["cosk"], writes=["cosq"])
        for hq in range(4):
            sc.op("pool", lambda e, hq=hq: e.memset(qTp[hq][:], 0.0), writes=["qTp%d" % hq])

        units = [("k", 2048), ("v", 2304)] + [("q", 256 * u) for u in range(8)]
        pf = Prefetch(len(units), 2, lambda i, s_: sc.dma("pool", "wcol%d" % s_, wcol[s_][:],
                                                          win[:, :, units[i][1]:units[i][1] + 256], writes=["wcol%d" % s_]))
        ecnt = [0]

        def rope(P4, cs, sn, tb, o1, o2, okeys, slot):
            r_ = rt[slot]
            rk = "rt%d" % slot
            cb = cs[:, tb, :].unsqueeze(1).broadcast_to([128, 4, 8])
            sb_ = sn[:, tb, :].unsqueeze(1).broadcast_to([128, 4, 8])
            x1 = P4[:, :, 0:8]
            x2 = P4[:, :, 8:16]
            ck = ["cosk", "sink", "cosq", "sinq"]
            dve(lambda e: e.tensor_tensor(out=r_[:, 0], in0=x1, in1=cb, op=ALU.mult), okeys["ps"] + ck, [rk])
            dve(lambda e: e.tensor_tensor(out=r_[:, 1], in0=x2, in1=sb_, op=ALU.mult), okeys["ps"] + ck, [rk])
            dve(lambda e: e.tensor_tensor(out=r_[:, 2], in0=x2, in1=cb, op=ALU.mult), okeys["ps"] + ck, [rk])
            dve(lambda e: e.tensor_tensor(out=r_[:, 3], in0=x1, in1=sb_, op=ALU.mult), okeys["ps"] + ck, [rk])
            dve(lambda e: e.tensor_tensor(out=o1, in0=r_[:, 0], in1=r_[:, 1], op=ALU.subtract), [rk], okeys["out"])
            dve(lambda e: e.tensor_tensor(out=o2, in0=r_[:, 2], in1=r_[:, 3], op=ALU.add), [rk], okeys["out"])

        for ui, (kind, col0) in enumerate(units):
            pf.ensure(ui)
            ws = ui % 2
            wkey = "wcol%d" % ws
            quad = ui - 2
            for tb in range(16):
                bank = (tb // 2) % 2
                half = tb % 2
                Pfull = PS[bank][:, half * 256:(half + 1) * 256]
                for c in range(DC):
                    lhsT = xTb[:, c, tb * 128:(tb + 1) * 128]
                    sc.op("pe", lambda e, Pfull=Pfull, lhsT=lhsT, ws=ws, c=c: e.matmul(
                        Pfull, lhsT=lhsT, rhs=wcol[ws][:, c, :], start=(c == 0), stop=(c == DC - 1)),
                        reads=[wkey, "xTb%d" % (tb // 4)], writes=[psk(bank)], inc=(c == DC - 1))
                P4 = Pfull.rearrange("p (h d) -> p h d", h=4)
                slot = ecnt[0] % 2
                ecnt[0] += 1
                if kind == "v":
                    sc.op("act", lambda e, Pfull=Pfull, tb=tb: e.copy(out=Vs[:, tb, :], in_=Pfull), reads=[psk(bank)], writes=["Vs"])
                elif kind == "k":
                    kd_ = kd[slot]
                    kk = "kd%d" % slot
                    sc.op("act", lambda e, P4=P4, kd_=kd_: e.copy(out=kd_[:, :, 0, :], in_=P4), reads=[psk(bank)], writes=[kk])
                    ro = rot[slot]
                    rok = "rot%d" % slot
                    rope(P4, cosk, sink, tb, ro[:, :, 0:8], ro[:, :, 8:16], {"ps": [psk(bank)], "out": [rok]}, slot)
                    dve(lambda e, kd_=kd_, ro=ro: e.tensor_copy(out=kd_[:, :, 0, 0:16], in_=ro[:]), [rok, kk], [kk])
                    dve(lambda e, kd_=kd_: e.tensor_copy(out=kd_[:, :, 1, :], in_=kd_[:, :, 0, :]), [kk], [kk])
                    for kv in range(4):
                        sc.op("pe", lambda e, kv=kv, kd_=kd_: e.transpose(
                            psb(2)[:, kv * 128:(kv + 1) * 128], kd_[:, kv, :, :].rearrange("p a b -> p (a b)"), ident_b[:]),
                            reads=[kk, "ident_b"], writes=[psk(2)], inc=(kv == 3))
                    for kv in range(4):
                        eng = "act" if kv % 2 == 0 else "dve"
                        if eng == "act":
                            sc.op("act", lambda e, kv=kv, tb=tb: e.copy(out=kTd[kv][:, tb * 128:(tb + 1) * 128],
                                                                        in_=psb(2)[:, kv * 128:(kv + 1) * 128]),
                                  reads=[psk(2)], writes=["kTd%d" % kv])
                        else:
                            dve(lambda e, kv=kv, tb=tb: e.tensor_copy(out=kTd[kv][:, tb * 128:(tb + 1) * 128],
                                                                       in_=psb(2)[:, kv * 128:(kv + 1) * 128]),
                                [psk(2)], ["kTd%d" % kv])
                else:
                    qb_ = qb[slot]
                    qk_ = "qb%d" % slot
                    sc.op("act", lambda e, P4=P4, qb_=qb_: e.activation(out=qb_[:], in_=P4, func=AF.Copy, scale=0.125),
                          reads=[psk(bank)], writes=[qk_])
                    rope(P4, cosq, sinq, tb, qb_[:, :, 0:8], qb_[:, :, 8:16], {"ps": [psk(bank), qk_], "out": [qk_]}, slot)
                    for mc in range(2):
                        sc.op("pe", lambda e, mc=mc, qb_=qb_: e.transpose(
                            psb(2)[:, mc * 128:(mc + 1) * 128], qb_[:, 2 * mc:2 * mc + 2, :].rearrange("p a b -> p (a b)"),
                            ident_b[:]), reads=[qk_, "ident_b"], writes=[psk(2)], inc=(mc == 1))
                    for hq in range(4):
                        e_, mc = hq % 2, hq // 2
                        src = psb(2)[e_ * 64:(e_ + 1) * 64, mc * 128:(mc + 1) * 128]
                        dst = qTp[hq][e_ * 64:(e_ + 1) * 64, tb * 128:(tb + 1) * 128]
                        if hq % 2 == 0:
                            sc.op("act", lambda e, src=src, dst=dst: e.copy(out=dst, in_=src), reads=[psk(2)], writes=["qTp%d" % hq])
                        else:
                            dve(lambda e, src=src, dst=dst: e.tensor_copy(out=dst, in_=src), [psk(2)], ["qTp%d" % hq])
            if kind != "q":
                continue
            hk = quad // 2
            units_ = [(i, mc) for i in range(16) for mc in range(2)]

            def emit_scores(u):
                i, mc = units_[u]
                bank = 3 + u % 2
                pt = PTa[u % 2]
                ptk = "PTa%d" % (u % 2)
                first = True
                for blk, kb in ((0, i - 1), (1, i)):
                    if kb < 0:
                        continue
                    for e_ in range(2):
                        hq = 2 * mc + e_
                        col = blk * 256 + e_ * 128
                        sc.op("pe", lambda e, bank=bank, kb=kb, hq=hq, i=i, hk=hk, col=col, first=first: e.matmul(
                            PS[bank][:, col:col + 128], lhsT=kTd[hk][:, kb * 128:(kb + 1) * 128],
                            rhs=qTp[hq][:, i * 128:(i + 1) * 128], start=first, stop=False),
                            reads=["kTd%d" % hk, "qTp%d" % hq], writes=[psk(bank)], inc=False)
                        first = False
                c0 = 0 if i > 0 else 256
                sc.op("pe", lambda e, bank=bank, c0=c0: e.matmul(
                    PS[bank][:, c0:512], lhsT=ident_b[:], rhs=mask2[:, c0:512], start=False, stop=True),
                    reads=["ident_b", "mask2"], writes=[psk(bank)], inc=True)
                sc.op("act", lambda e, bank=bank, pt=pt, c0=c0: e.activation(out=pt[:, c0:512], in_=PS[bank][:, c0:512], func=AF.Exp),
                      reads=[psk(bank)], writes=[ptk])

            def emit_pv(u):
                i, mc = units_[u]
                obank = 5 + u % 2
                pt = PTa[u % 2]
                ptk = "PTa%d" % (u % 2)
                blks = [(0, i - 1), (1, i)] if i > 0 else [(1, i)]
                for (c_off, use_v) in ((0, True), (128, False)):
                    for e_ in range(2):
                        out = PS[obank][e_ * 64:(e_ + 1) * 64, c_off:c_off + 128]
                        for bi, (blk, kb) in enumerate(blks):
                            lhsT = Vs[:, kb, hk * 64:(hk + 1) * 64] if use_v else ones_b[:, 0:64]
                            rhs = pt[:, blk * 256 + e_ * 128:blk * 256 + (e_ + 1) * 128]
                            lastmm = (not use_v) and e_ == 1 and bi == len(blks) - 1
                            sc.op("pe", lambda e, out=out, lhsT=lhsT, rhs=rhs, bi=bi, nb=len(blks), e_=e_: e.matmul(
                                out, lhsT=lhsT, rhs=rhs, start=(bi == 0), stop=(bi == nb - 1), tile_position=(0, e_ * 64)),
                                reads=[ptk, "Vs" if use_v else "ones_b"], writes=[psk(obank)], inc=lastmm)
                rc_ = rc[u % 2]
                rck = "rc%d" % (u % 2)
                ch = quad * 2 + mc
                dve(lambda e, rc_=rc_, obank=obank, ch=ch: e.tensor_scalar(
                    out=rc_[:, 0:128], in0=PS[obank][:, 128:256], scalar1=esP[:, ch:ch + 1], scalar2=None, op0=ALU.add),
                    [psk(obank), "esP"], [rck])
                dve(lambda e, rc_=rc_: e.reciprocal(out=rc_[:, 0:128], in_=rc_[:, 0:128]), [rck], [rck])
                dve(lambda e, rc_=rc_, obank=obank, ch=ch, i=i: e.tensor_tensor(
                    out=mixT[:, ch, i * 128:(i + 1) * 128], in0=PS[obank][:, 0:128], in1=rc_[:, 0:128], op=ALU.mult),
                    [psk(obank), rck], ["mixlo" if ch < 8 else "mixhi"])

            emit_scores(0)
            for u in range(len(units_)):
                if u + 1 < len(units_):
                    emit_scores(u + 1)
                emit_pv(u)

    def dbg_dump_mix():
        mixT = get_mixT()
        ov = out_d.rearrange("(c p) t -> p c t", p=128)
        for c in range(DC):
            sc.dma("pool", "dbgm", ov[:, c, :], mixT[:, c, :], reads=["mixlo", "mixhi"])

    def phase_output(R):
        A = Arena(PH_OFF)
        rt_ = A.take("o_r", [128, DC, T], F32, 2)
        ost = A.take("o_st", [128, D], F32, 3)
        for tt in range(NT):
            r_ = rt_[tt % 2]
            rk = "o_r%d" % (tt % 2)
            sc.dma("sp", rk, r_[:], R[:, :, tt * T:(tt + 1) * T].rearrange("c p t -> p c t"),
                   reads=[("R", id(R), tt)], writes=[rk])
            for tb in range(4):
                g = tt * 4 + tb
                os_ = ost[g % 3]
                ok = "o_st%d" % (g % 3)
                for q in range(4):
                    bank = q % 2 + 2 * (g % 2)
                    for c4 in range(4):
                        dc = q * 4 + c4
                        src = r_[:, dc, tb * 128:(tb + 1) * 128]
                        sc.op("pe", lambda e, b=bank, c4=c4, src=src: e.transpose(
                            PS[b][:, c4 * 128:(c4 + 1) * 128], src, ident_f[:]),
                            reads=[rk, "ident_f"], writes=[psk(bank)], inc=(c4 == 3))
                    dst = os_[:, q * 512:(q + 1) * 512]
                    if q % 2 == 0:
                        sc.op("act", lambda e, b=bank, dst=dst: e.copy(out=dst, in_=PS[b][:]), reads=[psk(bank)], writes=[ok])
                    else:
                        sc.op("dve", lambda e, b=bank, dst=dst: e.tensor_copy(out=dst, in_=PS[b][:]), reads=[psk(bank)], writes=[ok])
                sc.dma("sp", ok, out_d[g * 128:(g + 1) * 128, :], os_[:], reads=[ok], writes=[("out", g)])

    def dbg_copy_R(R):
        A = Arena(PH_OFF)
        bufs = A.take("dbgb", [128, S], F32, 2)
        for c in range(DC):
            k = "dbgb%d" % (c % 2)
            sc.dma("sp", k, bufs[c % 2][:], R[c], reads=[("R", id(R), tt) for tt in range(NT)], writes=[k])
            sc.dma("sp", k, out_d[c * 128:(c + 1) * 128, :], bufs[c % 2][:], reads=[k])

    setup_consts()
    sc.phase_reset()
    if mode == "full":
        phase_input(x_in, RA)
        sc.phase_reset()
        phase_fox()
        sc.phase_reset()
        phase_s5()
        sc.phase_reset()
        phase_outproj(0, ev_w_out, RA, RB)
        sc.phase_reset()
        phase_ffn(0, RB, RA, final=False)
        sc.phase_reset()
        phase_swa()
        sc.phase_reset()
        phase_outproj(1, od_w_out, RA, RB)
        sc.phase_reset()
        phase_ffn(1, RB, RA, final=False)
        sc.phase_reset()
        phase_output(RA)
    elif mode == "consts":
        sc.dma("sp", "dbg", out_d[0:128, 0:128], ident_f[:], reads=["ident_f"])
        sc.dma("sp", "dbg", out_d[128:256, 0:86 * 4].rearrange("p (a b) -> p a b", a=4), convp[:, 0, :, :], reads=["consts"])
        sc.dma("sp", "dbg", out_d[256:384, 0:128].rearrange("p (a b) -> p a b", a=8), lnp[:], reads=["consts"])
    elif mode == "p0":
        phase_input(x_in, RA)
        sc.phase_reset()
        if 'dbg' not in _SKIP:
            dbg_copy_R(RA)
    elif mode == "fox":
        phase_input(x_in, RA)
        sc.phase_reset()
        phase_fox()
        sc.phase_reset()
        dbg_dump_mix()
    elif mode == "s5":
        phase_input(x_in, RA)
        sc.phase_reset()
        phase_fox()
        sc.phase_reset()
        phase_s5()
        sc.phase_reset()
        dbg_dump_mix()
    elif mode == "swa":
        phase_input(x_in, RA)
        sc.phase_reset()
        phase_swa()
        sc.phase_reset()
        phase_outproj(1, od_w_out, RA, RB)
        sc.phase_reset()
        dbg_copy_R(RB)
    elif mode == "outproj0":
        phase_input(x_in, RA)
        sc.phase_reset()
        phase_fox()
        sc.phase_reset()
        phase_outproj(0, ev_w_out, RA, RB)
        sc.phase_reset()
        dbg_copy_R(RB)
    elif mode == "ffn0":
        phase_input(x_in, RA)
        sc.phase_reset()
        phase_ffn(0, RA, RB, final=False)
        sc.phase_reset()
        phase_output(RB)
    else:
        raise NotImplementedError(mode)
    sc.barrier()
    sc.emit()
    return nc, es


_CACHE = {}


def _prep_inputs(inp, b):
    f = np.ascontiguousarray
    m = {
        "x": f(inp["x"][b]),
        "pos": f(inp["positions"][b].reshape(16, 128).astype(np.int32)),
        "ev_w_in": f(inp["ev_w_in"][0]),
        "ev_b_f": f(inp["ev_b_f"][0].reshape(8, 1)),
        "ev_lre": f(inp["ev_lambda_re"][0]),
        "ev_lim": f(inp["ev_lambda_im"][0]),
        "ev_lstep": f(inp["ev_log_step"][0].reshape(1, 64)),
        "ev_bre": f(inp["ev_ssm_b_re"][0]),
        "ev_bim": f(inp["ev_ssm_b_im"][0]),
        "ev_cre": f(inp["ev_ssm_c_re"][0]),
        "ev_cim": f(inp["ev_ssm_c_im"][0]),
        "ev_d": f(inp["ev_ssm_d"][0].reshape(8, 128)),
        "ev_w_glu": f(inp["ev_w_glu"][0]),
        "ev_w_out": f(inp["ev_w_out"][0]),
        "od_w_in": f(inp["od_w_in"][0]),
        "od_sinks": f(inp["od_sinks"][0].reshape(1, 32)),
        "od_w_out": f(inp["od_w_out"][0]),
        "ln_mix_g": f(inp["ln_mix_g"].reshape(2, 16, 128)),
        "ln_mix_b": f(inp["ln_mix_b"].reshape(2, 16, 128)),
        "ffn_w_up": f(inp["ffn_w_up"]),
        "ffn_conv_w": f(inp["ffn_conv_w"].reshape(2, 3, 86, 128)),
        "ffn_conv_b": f(inp["ffn_conv_b"].reshape(2, 86, 128)),
        "ffn_w_down": f(inp["ffn_w_down"]),
        "ln_ffn_g": f(inp["ln_ffn_g"].reshape(2, 16, 128)),
        "ln_ffn_b": f(inp["ln_ffn_b"].reshape(2, 16, 128)),
    }
    return m


def run(inputs, mode="full", cores=8, trace=False):
    nc, es = build(mode)
    in_maps = [_prep_inputs(inputs, b) for b in range(cores)]
    res = run_bass_kernel_spmd(nc, in_maps, core_ids=list(range(cores)), trace=trace)
    es.close()
    return res


def kernel(**inputs):
    res = run(inputs, "full", 8)
    out = np.stack([np.asarray(r["out"]) for r in res.results], axis=0)
    return out.astype(np.float32)
```

```python
import math
import os
_SKIP = set(os.environ.get('K_SKIP', '').split(','))
from contextlib import ExitStack

import numpy as np
import concourse.bass as bass
import concourse.mybir as mybir
from concourse.bass_utils import run_bass_kernel_spmd

F32 = mybir.dt.float32
BF16 = mybir.dt.bfloat16
I32 = mybir.dt.int32
AF = mybir.ActivationFunctionType
ALU = mybir.AluOpType

S = 2048
D = 2048
DC = 16
T = 512
NT = 4
DFF = 5504
NJ = 43
ALPHA = (2.0 * 2) ** 0.25
LN_EPS = 1e-5
DMA_SCRATCH = 8192
SBUF_BASE = DMA_SCRATCH
SBUF_LIMIT = 16384 + 212800

ENGS = ("pe", "act", "dve", "pool", "sp")
SAME_SYNC = True


class Sched:
    def __init__(self, nc, es):
        self.nc = nc
        self.es = es
        self.streams = {e: [] for e in ENGS}
        self.cnt = {e: 0 for e in ENGS}
        self.waited = {e: {} for e in ENGS}
        self.lastw = {}
        self.readers = {}
        self.pend_r = {e: [] for e in ENGS}
        self.pend_w = {e: [] for e in ENGS}
        self.sems = {}
        self.dcount = {}
        self.ranges = {}
        self.alias = {}
        self.tcache = {}
        self.persist = set()
        for e in ENGS:
            self.sems["E_" + e] = es.enter_context(nc.semaphore("E_" + e))

    def sb(self, key, shape, dtype, off, persistent=False):
        nbytes = int(np.prod(shape[1:])) * (2 if dtype == BF16 else 4)
        assert off % 4 == 0 and off + nbytes <= SBUF_LIMIT, (key, off, nbytes)
        ck = (key, off, tuple(shape), str(dtype))
        if ck in self.tcache and self.ranges.get(key) == (off, off + nbytes):
            return self.tcache[ck]
        t = self.nc.alloc_sbuf_tensor_at(key, list(shape), dtype, offset=off)
        self.tcache[ck] = t
        assert key not in self.ranges, ("key re-registered at a different place", key)
        self.reg(key, off, off + nbytes)
        if persistent:
            self.persist.add(key)
        return t

    def phase_reset(self):
        self.barrier()
        self.lastw = {}
        self.readers = {}
        for k in list(self.ranges):
            if k not in self.persist:
                del self.ranges[k]
                del self.alias[k]
        for k in self.alias:
            self.alias[k] = [a for a in self.alias[k] if a in self.ranges]
        self.tcache = {ck: t for ck, t in self.tcache.items() if ck[0] in self.persist}

    def reg(self, key, lo, hi):
        self.ranges[key] = (lo, hi)
        al = []
        for k, (a, b) in self.ranges.items():
            if k != key and a < hi and lo < b:
                al.append(k)
                self.alias[k].append(key)
        self.alias[key] = al

    def _keys(self, k):
        return [k] + self.alias.get(k, [])

    def _wait(self, eng, tok):
        sem, val = tok
        if sem == "E_" + eng:
            if eng in ("pe", "sp") or not SAME_SYNC:
                return
        if self.waited[eng].get(sem, 0) >= val:
            return
        self.waited[eng][sem] = val
        self.streams[eng].append(("w", sem, val))

    def _deps(self, eng, reads, writes):
        toks = []
        for k0 in reads:
            for k in self._keys(k0):
                t = self.lastw.get(k)
                if t:
                    toks.append(t)
                if isinstance(k, str) and k.startswith("ps") and k[2:].isdigit():
                    toks.extend(r for r in self.readers.get(k, ()) if r[0] != "E_" + eng)
                for e2 in ENGS:
                    assert k not in self.pend_w[e2] or e2 == eng, ("pending write", k, e2, eng)
        for k0 in writes:
            for k in self._keys(k0):
                t = self.lastw.get(k)
                if t:
                    toks.append(t)
                toks.extend(self.readers.get(k, ()))
                for e2 in ENGS:
                    if e2 != eng:
                        assert k not in self.pend_r[e2] and k not in self.pend_w[e2], ("pending", k, e2, eng)
        for t in toks:
            self._wait(eng, t)

    def _commit(self, tok, reads, writes):
        for k in reads:
            self.readers.setdefault(k, []).append(tok)
        for k in writes:
            self.lastw[k] = tok
            self.readers[k] = []

    def op(self, eng, fn, reads=(), writes=(), inc=True):
        reads = list(reads)
        writes = list(writes)
        self._deps(eng, reads, writes)
        if inc:
            self.cnt[eng] += 1
            tok = ("E_" + eng, self.cnt[eng])
            self.streams[eng].append(("o", fn, "E_" + eng, 1))
            self._commit(tok, reads + self.pend_r[eng], writes + self.pend_w[eng])
            self.pend_r[eng] = []
            self.pend_w[eng] = []
        else:
            self.streams[eng].append(("o", fn, None, 0))
            self.pend_r[eng] += reads
            self.pend_w[eng] += writes

    def dma(self, q, sem, out, in_, reads=(), writes=(), **kw):
        reads = list(reads)
        writes = list(writes)
        if sem not in self.sems:
            self.sems[sem] = self.es.enter_context(self.nc.semaphore(sem))
            self.dcount[sem] = 0
        if self.dcount[sem]:
            self._wait(q, (sem, self.dcount[sem]))
        self._deps(q, reads, writes)
        self.dcount[sem] += 16
        tok = (sem, self.dcount[sem])
        self.streams[q].append(("o", lambda e, o=out, i=in_, kw=kw: e.dma_start(out=o, in_=i, **kw), sem, 16))
        self._commit(tok, reads, writes)

    def barrier(self):
        for e in ENGS:
            assert not self.pend_r[e] and not self.pend_w[e], ("barrier with pending ops", e)
        for e in ENGS:
            for e2 in ENGS:
                if self.cnt[e2] and (e2 != e or e in ("act", "dve", "pool")):
                    if self.waited[e].get("E_" + e2, 0) < self.cnt[e2]:
                        self.waited[e]["E_" + e2] = self.cnt[e2]
                        self.streams[e].append(("w", "E_" + e2, self.cnt[e2]))
            for s, v in self.dcount.items():
                if v:
                    self._wait(e, (s, v))

    def emit(self):
        nc = self.nc
        with nc.Block() as block:
            def mk(name):
                def body(e):
                    for it in self.streams[name]:
                        if it[0] == "w":
                            e.wait_ge(self.sems[it[1]], it[2])
                        else:
                            ins = it[1](e)
                            if it[2] is not None:
                                ins.then_inc(self.sems[it[2]], it[3])
                return body
            block.tensor(mk("pe"))
            block.scalar(mk("act"))
            block.vector(mk("dve"))
            block.gpsimd(mk("pool"))
            block.sync(mk("sp"))


def build(mode="full"):
    nc = bass.Bass("TRN2", target_bir_lowering=False, dynamic_dma_scratch_size=DMA_SCRATCH)
    es = ExitStack()
    sc = Sched(nc, es)

    def dram_in(name, shape, dt=F32):
        return nc.dram_tensor(name, list(shape), dt, kind="ExternalInput").ap()

    x_in = dram_in("x", [S, D])
    pos_in = dram_in("pos", [S // 128, 128], I32)
    ev_w_in = dram_in("ev_w_in", [D, 4104])
    ev_b_f = dram_in("ev_b_f", [8, 1])
    ev_lre = dram_in("ev_lre", [64, 64])
    ev_lim = dram_in("ev_lim", [64, 64])
    ev_lstep = dram_in("ev_lstep", [1, 64])
    ev_bre = dram_in("ev_bre", [64, 64, 16])
    ev_bim = dram_in("ev_bim", [64, 64, 16])
    ev_cre = dram_in("ev_cre", [64, 16, 64])
    ev_cim = dram_in("ev_cim", [64, 16, 64])
    ev_d = dram_in("ev_d", [8, 128])
    ev_w_glu = dram_in("ev_w_glu", [1024, 2048])
    ev_w_out = dram_in("ev_w_out", [D, D])
    od_w_in = dram_in("od_w_in", [D, 2560])
    od_sinks = dram_in("od_sinks", [1, 32])
    od_w_out = dram_in("od_w_out", [D, D])
    ln_mix_g = dram_in("ln_mix_g", [2, 16, 128])
    ln_mix_b = dram_in("ln_mix_b", [2, 16, 128])
    ffn_w_up = dram_in("ffn_w_up", [2, D, 2 * DFF])
    ffn_conv_w = dram_in("ffn_conv_w", [2, 3, 86, 128])
    ffn_conv_b = dram_in("ffn_conv_b", [2, 86, 128])
    ffn_w_down = dram_in("ffn_w_down", [2, DFF, D])
    ln_ffn_g = dram_in("ln_ffn_g", [2, 16, 128])
    ln_ffn_b = dram_in("ln_ffn_b", [2, 16, 128])
    out_d = nc.dram_tensor("out", [S, D], F32, kind="ExternalOutput").ap()
    RA = nc.dram_tensor("resA", [DC, 128, S], F32, kind="Internal").ap()
    RB = nc.dram_tensor("resB", [DC, 128, S], F32, kind="Internal").ap()

    wupb = [nc.dram_tensor("wupb%d" % l, [86, 128, DC, 128], BF16, kind="Internal").ap() for l in range(2)]
    wdnb = [nc.dram_tensor("wdnb%d" % l, [DC, 128, NJ, 128], BF16, kind="Internal").ap() for l in range(2)]
    woutb = [nc.dram_tensor("woutb%d" % l, [DC, 128, DC, 128], BF16, kind="Internal").ap() for l in range(2)]
    conv_jobs = []
    for l in range(2):
        wo_v = (ev_w_out if l == 0 else od_w_out).rearrange("(c p) f -> p c f", p=128)
        for dc in range(DC):
            conv_jobs.append((woutb[l][dc], wo_v[:, :, dc * 128:(dc + 1) * 128], ("woutb", l, dc)))
        wup_v = ffn_w_up[l].rearrange("(c p) f -> p c f", p=128)
        wdn_v = ffn_w_down[l].rearrange("(j p) d -> p j d", p=128)
        for j in range(86):
            conv_jobs.append((wupb[l][j], wup_v[:, :, j * 128:(j + 1) * 128], ("wupb", l, j)))
        for dc in range(DC):
            conv_jobs.append((wdnb[l][dc], wdn_v[:, :, dc * 128:(dc + 1) * 128], ("wdnb", l, dc)))
    conv_done = [0]
    NCV = 6

    def conv_pump(n):
        for _ in range(n):
            if conv_done[0] >= len(conv_jobs):
                return
            o_, i_, key = conv_jobs[conv_done[0]]
            sc.dma("pool", "cv%d" % (conv_done[0] % NCV), o_, i_, writes=[key])
            conv_done[0] += 1

    PS = [es.enter_context(nc.psum_tensor("ps%d" % i, [128, 512], F32)) for i in range(8)]

    def psk(i):
        return "ps%d" % i

    off = [SBUF_BASE]

    def alloc(key, shape, dt):
        nbytes = int(np.prod(shape[1:])) * (2 if dt == BF16 else 4)
        nbytes = (nbytes + 31) // 32 * 32
        t = sc.sb(key, shape, dt, off[0], persistent=True)
        off[0] += nbytes
        return t

    ident_f = alloc("ident_f", [128, 128], F32)
    ident_b = alloc("ident_b", [128, 128], BF16)
    ones_f = alloc("ones_f", [128, 128], F32)
    ones_b = alloc("ones_b", [128, 128], BF16)
    lnp = alloc("lnp", [128, 8, 16], F32)
    convp = alloc("convp", [128, 2, 4, 86], F32)
    CONST_END = off[0]
    XTB_OFF = CONST_END
    xTb = sc.sb("xTb_all", [128, DC, S], BF16, XTB_OFF, persistent=True)
    for tt in range(NT):
        sc.reg("xTb%d" % tt, SBUF_LIMIT + 1000 + tt, SBUF_LIMIT + 1000 + tt + 1)
        sc.persist.add("xTb%d" % tt)
    PH_OFF = XTB_OFF + DC * S * 2

    class Arena:
        def __init__(self, start):
            self.o = (start + 31) // 32 * 32

        def take(self, key, shape, dt, n=None):
            nbytes = int(np.prod(shape[1:])) * (2 if dt == BF16 else 4)
            nbytes = (nbytes + 31) // 32 * 32
            if n is None:
                t = sc.sb(key, shape, dt, self.o)
                self.o += nbytes
                return t
            ts = []
            for i in range(n):
                ts.append(sc.sb("%s%d" % (key, i), shape, dt, self.o))
                self.o += nbytes
            return ts

    def setup_consts():
        sc.op("pool", lambda e: e.memset(ones_f[:], 1.0), writes=["ones_f"])
        sc.op("pool", lambda e: e.memset(ident_f[:], 1.0), writes=["ident_f"])
        sc.op("pool", lambda e: e.affine_select(out=ident_f[:], in_=ident_f[:], pattern=[[-1, 128]],
                                                 compare_op=ALU.is_equal, fill=0.0, base=0,
                                                 channel_multiplier=1),
              reads=["ident_f"], writes=["ident_f"])
        sc.op("dve", lambda e: e.tensor_copy(out=ident_b[:], in_=ident_f[:]), reads=["ident_f"], writes=["ident_b"])
        sc.op("dve", lambda e: e.tensor_copy(out=ones_b[:], in_=ones_f[:]), reads=["ones_f"], writes=["ones_b"])
        stg = sc.sb("c_stg", [128, 128], F32, PH_OFF)
        jobs = []
        for l in range(2):
            for k, src in enumerate((ln_mix_g, ln_mix_b, ln_ffn_g, ln_ffn_b)):
                jobs.append((src[l], 16, lnp[:, l * 4 + k, :]))
            for k in range(3):
                jobs.append((ffn_conv_w[l, k], 86, convp[:, l, k, :]))
            jobs.append((ffn_conv_b[l], 86, convp[:, l, 3, :]))
        for n, (src, rows, dst) in enumerate(jobs):
            sc.dma("sp", "c_stg", stg[0:rows, :], src, writes=["c_stg"])
            sc.op("pe", lambda e, r=rows: e.transpose(PS[0][:, 0:r], stg[0:r, :], ident_f[0:r, 0:r]),
                  reads=["c_stg", "ident_f"], writes=[psk(0)])
            sc.op("dve", lambda e, r=rows, d=dst: e.tensor_copy(out=d, in_=PS[0][:, 0:r]),
                  reads=[psk(0)], writes=["consts"])

    def phase_input(x_src, Rout):
        A = Arena(PH_OFF)
        xin = A.take("xin", [128, D], F32, 2)
        st32 = A.take("st32", [128, DC, T], F32)
        for tt in range(NT):
            for tb in range(4):
                g = tt * 4 + tb
                xi = xin[g % 2]
                xk = "xin%d" % (g % 2)
                sc.dma("sp", xk, xi[:], x_src[g * 128:(g + 1) * 128, :], writes=[xk])
                for q in range(4):
                    bank = q % 2
                    for c4 in range(4):
                        dc = q * 4 + c4
                        sc.op("pe", lambda e, b=bank, c4=c4, dc=dc, xi=xi: e.transpose(
                            PS[b][:, c4 * 128:(c4 + 1) * 128], xi[:, dc * 128:(dc + 1) * 128], ident_f[:]),
                            reads=[xk, "ident_f"], writes=[psk(bank)], inc=(c4 == 3))
                    pv = PS[bank][:].rearrange("p (c t) -> p c t", c=4)
                    if 'act' not in _SKIP:
                      sc.op("act", lambda e, pv=pv, q=q, tb=tb: e.copy(
                        out=st32[:, q * 4:(q + 1) * 4, tb * 128:(tb + 1) * 128], in_=pv),
                        reads=[psk(bank)], writes=["st32"])
                    if 'dve' not in _SKIP:
                      sc.op("dve", lambda e, pv=pv, q=q, g=g: e.tensor_copy(
                        out=xTb[:, q * 4:(q + 1) * 4, g * 128:(g + 1) * 128], in_=pv),
                        reads=[psk(bank)], writes=["xTb%d" % tt])
            conv_pump(4)
            if 'st' not in _SKIP:
              sc.dma("sp", "st32", Rout[:, :, tt * T:(tt + 1) * T].rearrange("c p t -> p c t"), st32[:],
                   reads=["st32"], writes=[("R", id(Rout), tt)])

    def layer_norm(r32, rkey, gi, bi, tt, lo, Rout=None, nbuf=2):
        A = Arena(lo)
        sq = A.take("ln_sq", [128, T], BF16, nbuf)
        rb = A.take("ln_rb", [128, T], BF16, nbuf)
        mean = A.take("ln_mean", [128, T], F32)
        rstd = A.take("ln_rstd", [128, T], F32)
        for c in range(DC):
            s = sq[c % nbuf]
            sk = "ln_sq%d" % (c % nbuf)
            rb_ = rb[c % nbuf]
            rbk = "ln_rb%d" % (c % nbuf)
            sc.op("act", lambda e, s=s, c=c: e.activation(out=s[:], in_=r32[:, c, :], func=AF.Square),
                  reads=[rkey], writes=[sk])
            sc.op("dve", lambda e, rb_=rb_, c=c: e.tensor_copy(out=rb_[:], in_=r32[:, c, :]), reads=[rkey], writes=[rbk])
            sc.op("pe", lambda e, c=c, rb_=rb_: e.matmul(PS[6][:], lhsT=ones_b[:], rhs=rb_[:], start=(c == 0), stop=(c == DC - 1)),
                  reads=[rbk, "ones_b"], writes=[psk(6)], inc=True)
            sc.op("pe", lambda e, s=s, c=c: e.matmul(PS[7][:], lhsT=ones_b[:], rhs=s[:], start=(c == 0), stop=(c == DC - 1)),
                  reads=[sk, "ones_b"], writes=[psk(7)], inc=True)
        sc.op("act", lambda e: e.mul(out=mean[:], in_=PS[6][:], mul=1.0 / D), reads=[psk(6)], writes=["ln_mean"])
        sc.op("dve", lambda e: e.tensor_tensor(out=rstd[:], in0=mean[:], in1=mean[:], op=ALU.mult),
              reads=["ln_mean"], writes=["ln_rstd"])
        sc.op("dve", lambda e: e.scalar_tensor_tensor(out=rstd[:], in0=PS[7][:], scalar=1.0 / D, in1=rstd[:],
                                                      op0=ALU.mult, op1=ALU.subtract),
              reads=[psk(7), "ln_rstd"], writes=["ln_rstd"])
        sc.op("dve", lambda e: e.tensor_scalar(out=rstd[:], in0=rstd[:], scalar1=LN_EPS, scalar2=None, op0=ALU.add),
              reads=["ln_rstd"], writes=["ln_rstd"])
        sc.op("act", lambda e: e.activation(out=rstd[:], in_=rstd[:], func=AF.Sqrt), reads=["ln_rstd"], writes=["ln_rstd"])
        sc.op("dve", lambda e: e.reciprocal(out=rstd[:], in_=rstd[:]), reads=["ln_rstd"], writes=["ln_rstd"])

        def norm_chunk(c):
            sc.op("dve", lambda e: e.tensor_tensor(out=r32[:, c, :], in0=r32[:, c, :], in1=mean[:], op=ALU.subtract),
                  reads=[rkey, "ln_mean"], writes=[rkey])
            sc.op("dve", lambda e: e.tensor_tensor(out=r32[:, c, :], in0=r32[:, c, :], in1=rstd[:], op=ALU.mult),
                  reads=[rkey, "ln_rstd"], writes=[rkey])
            sc.op("act", lambda e: e.activation(out=r32[:, c, :], in_=r32[:, c, :], func=AF.Identity,
                                                bias=lnp[:, bi, c:c + 1], scale=lnp[:, gi, c:c + 1]),
                  reads=[rkey, "consts"], writes=[rkey])
            sc.op("pool", lambda e: e.tensor_copy(out=xTb[:, c, tt * T:(tt + 1) * T], in_=r32[:, c, :]),
                  reads=[rkey], writes=["xTb%d" % tt])

        def store():
            sc.dma("sp", "ln_out_" + str(rkey), Rout[:, :, tt * T:(tt + 1) * T].rearrange("c p t -> p c t"), r32[:],
                   reads=[rkey], writes=[("R", id(Rout), tt)])

        return [(lambda c=c: norm_chunk(c)) for c in range(DC)], store

    def phase_ffn(l, Rin, Rout, final):
        conv_pump(max(0, (l + 1) * 118 - conv_done[0]))
        A = Arena(PH_OFF)
        aT = A.take("aT", [128, NJ, T], BF16)
        r32 = A.take("r32", [128, DC, T], F32)
        NW = 3
        wg = A.take("wg", [128, DC, 128], BF16, NW)
        wv = A.take("wv", [128, DC, 128], BF16, NW)
        ND = 2
        wd = A.take("wd", [128, NJ, 128], BF16, ND)
        carry = A.take("carry", [128, 2, NJ, 2], F32, 2)
        gb = A.take("gb", [128, T], F32, 2)
        vb = A.take("vb", [128, T], F32, 2)
        lo = A.o
        wup = ffn_w_up[l].rearrange("(c p) f -> p c f", p=128)
        wdn = ffn_w_down[l].rearrange("(j p) d -> p j d", p=128)

        def cw(k, j):
            return convp[:, l, k, j:j + 1]

        pending_store = [None]
        pending_norm = []
        up_jobs = [(tt, j) for tt in range(NT) for j in range(NJ)]
        dn_jobs = [(tt, dc) for tt in range(NT) for dc in range(DC)]
        up_issued = [0]
        dn_issued = [0]

        def issue_up(n):
            while up_issued[0] < min(n, len(up_jobs)):
                i = up_issued[0]
                _, j = up_jobs[i]
                ws = i % NW
                sc.dma("sp", "wg%d" % ws, wg[ws][:], wupb[l][j], reads=[("wupb", l, j)], writes=["wg%d" % ws])
                sc.dma("sp", "wv%d" % ws, wv[ws][:], wupb[l][NJ + j], reads=[("wupb", l, NJ + j)], writes=["wv%d" % ws])
                up_issued[0] += 1

        def issue_dn(n):
            while dn_issued[0] < min(n, len(dn_jobs)):
                i = dn_issued[0]
                _, dc = dn_jobs[i]
                ds = i % ND
                sc.dma("sp", "wd%d" % ds, wd[ds][:], wdnb[l][dc], reads=[("wdnb", l, dc)], writes=["wd%d" % ds])
                dn_issued[0] += 1

        for tt in range(NT):
            tsl = slice(tt * T, (tt + 1) * T)
            xk = "xTb%d" % tt
            cin = carry[(tt + 1) % 2]
            cout = carry[tt % 2]
            cink = "carry%d" % ((tt + 1) % 2)
            coutk = "carry%d" % (tt % 2)
            for j in range(NJ):
                ui = tt * NJ + j
                issue_up(ui + NW - 0 if ui == 0 else ui + NW)
                if j == 30:
                    if pending_store[0] is not None:
                        pending_store[0]()
                        pending_store[0] = None
                    sc.dma("sp", "r32", r32[:], Rin[:, :, tsl].rearrange("c p t -> p c t"),
                           reads=[("R", id(Rin), tt)], writes=["r32"])
                if j == 8:
                    issue_dn(tt * DC + 1)
                if j == 16:
                    issue_dn(tt * DC + 2)
                ws = ui % NW
                pg = (j % 2) * 2
                for half, (w_, wk, gv, cofs) in enumerate(((wg[ws], "wg%d" % ws, gb[j % 2], 0),
                                                           (wv[ws], "wv%d" % ws, vb[j % 2], NJ))):
                    bank = pg + half
                    P = PS[bank]
                    for c in range(DC):
                        rhs = xTb[:, c, tsl]
                        sc.op("pe", lambda e, P=P, w_=w_, c=c, rhs=rhs: e.matmul(P[:], lhsT=w_[:, c, :], rhs=rhs,
                                                                                  start=(c == 0), stop=(c == DC - 1)),
                              reads=[wk, xk], writes=[psk(bank)], inc=(c == DC - 1))
                    gk = ("gb%d" if half == 0 else "vb%d") % (j % 2)
                    jj = j + cofs
                    sc.op("act", lambda e, P=P, gv=gv, jj=jj: e.activation(out=gv[:], in_=P[:], func=AF.Identity,
                                                                           bias=cw(3, jj), scale=cw(2, jj)),
                          reads=[psk(bank), "consts"], writes=[gk])
                    sc.op("dve", lambda e, P=P, gv=gv, jj=jj: e.scalar_tensor_tensor(
                        out=gv[:, 1:T], in0=P[:, 0:T - 1], scalar=cw(1, jj), in1=gv[:, 1:T], op0=ALU.mult, op1=ALU.add),
                        reads=[psk(bank), gk, "consts"], writes=[gk])
                    sc.op("dve", lambda e, P=P, gv=gv, jj=jj: e.scalar_tensor_tensor(
                        out=gv[:, 2:T], in0=P[:, 0:T - 2], scalar=cw(0, jj), in1=gv[:, 2:T], op0=ALU.mult, op1=ALU.add),
                        reads=[psk(bank), gk, "consts"], writes=[gk])
                    if tt > 0:
                        sc.op("dve", lambda e, gv=gv, jj=jj, half=half, j=j, cin=cin: e.scalar_tensor_tensor(
                            out=gv[:, 0:1], in0=cin[:, half, j, 1:2], scalar=cw(1, jj), in1=gv[:, 0:1],
                            op0=ALU.mult, op1=ALU.add), reads=[cink, gk, "consts"], writes=[gk])
                        sc.op("dve", lambda e, gv=gv, jj=jj, half=half, j=j, cin=cin: e.scalar_tensor_tensor(
                            out=gv[:, 0:2], in0=cin[:, half, j, 0:2], scalar=cw(0, jj), in1=gv[:, 0:2],
                            op0=ALU.mult, op1=ALU.add), reads=[cink, gk, "consts"], writes=[gk])
                    if tt < NT - 1:
                        sc.op("act", lambda e, P=P, half=half, j=j, cout=cout: e.copy(out=cout[:, half, j, :], in_=P[:, T - 2:T]),
                              reads=[psk(bank)], writes=[coutk])
                g_ = gb[j % 2]
                v_ = vb[j % 2]
                sc.op("act", lambda e, g_=g_: e.activation(out=g_[:], in_=g_[:], func=AF.Silu),
                      reads=["gb%d" % (j % 2)], writes=["gb%d" % (j % 2)])
                sc.op("pool", lambda e, g_=g_, v_=v_, j=j: e.tensor_tensor(out=aT[:, j, :], in0=g_[:], in1=v_[:], op=ALU.mult),
                      reads=["gb%d" % (j % 2), "vb%d" % (j % 2)], writes=["aT"])
                if pending_norm and j >= 2:
                    pending_norm.pop(0)()
            for dc in range(DC):
                di = tt * DC + dc
                issue_dn(di + ND)
                ds = di % ND
                bank = 4 + dc % 2
                P = PS[bank]
                for j in range(NJ):
                    sc.op("pe", lambda e, P=P, j=j, ds=ds: e.matmul(P[:], lhsT=wd[ds][:, j, :], rhs=aT[:, j, :],
                                                                    start=(j == 0), stop=(j == NJ - 1)),
                          reads=["wd%d" % ds, "aT"], writes=[psk(bank)], inc=(j == NJ - 1))
                sc.op("dve", lambda e, P=P, dc=dc: e.scalar_tensor_tensor(
                    out=r32[:, dc, :], in0=r32[:, dc, :], scalar=ALPHA, in1=P[:], op0=ALU.mult, op1=ALU.add),
                    reads=[psk(bank), "r32"], writes=["r32"])
            pending_norm, pending_store[0] = layer_norm(r32, "r32", l * 4 + 2, l * 4 + 3, tt, lo, Rout=Rout)
        for fn in pending_norm:
            fn()
        pending_store[0]()

    class Prefetch:
        def __init__(self, n, nslots, issue):
            self.n, self.nslots, self.issue, self.issued = n, nslots, issue, 0

        def ensure(self, i):
            while self.issued < min(self.n, i + self.nslots):
                self.issue(self.issued, self.issued % self.nslots)
                self.issued += 1

    MIX_OFF = PH_OFF
    PH2_OFF = MIX_OFF + DC * S * 2

    _mix = []

    def get_mixT():
        if not _mix:
            _mix.append(nc.alloc_sbuf_tensor_at("mixT_all", [128, DC, S], BF16, offset=MIX_OFF))
        t = _mix[0]
        if "mixlo" not in sc.ranges:
            sc.reg("mixlo", MIX_OFF, MIX_OFF + 8 * S * 2)
            sc.reg("mixhi", MIX_OFF + 8 * S * 2, MIX_OFF + 16 * S * 2)
        return t

    def phase_outproj(l, w_out, Rin, Rout):
        mixT = get_mixT()
        A = Arena(PH2_OFF)
        conv_pump(max(0, l * 118 + 16 - conv_done[0]))
        r32s = A.take("r32_", [128, DC, T], F32, 2)
        NW = 3
        wo = A.take("wo", [128, DC, 128], BF16, NW)
        lo = A.o
        stores = []
        pend_n = []
        jobs = [(tt, dc) for tt in range(NT) for dc in range(DC)]
        pf = Prefetch(len(jobs), NW, lambda i, s_: sc.dma(
            "sp", "wo%d" % s_, wo[s_][:], woutb[l][jobs[i][1]], reads=[("woutb", l, jobs[i][1])], writes=["wo%d" % s_]))
        for tt in range(NT):
            tsl = slice(tt * T, (tt + 1) * T)
            r32 = r32s[tt % 2]
            rk = "r32_%d" % (tt % 2)
            if tt >= 2:
                stores[tt - 2]()
            sc.dma("sp", rk, r32[:], Rin[:, :, tsl].rearrange("c p t -> p c t"),
                   reads=[("R", id(Rin), tt)], writes=[rk])
            for dc in range(DC):
                i = tt * DC + dc
                pf.ensure(i)
                ws = i % NW
                bank = 4 + dc % 2
                P = PS[bank]
                for c in range(DC):
                    rhs = mixT[:, c, tsl]
                    sc.op("pe", lambda e, P=P, ws=ws, c=c, rhs=rhs: e.matmul(P[:], lhsT=wo[ws][:, c, :], rhs=rhs,
                                                                          start=(c == 0), stop=(c == DC - 1)),
                          reads=["wo%d" % ws, "mixlo" if c < 8 else "mixhi"], writes=[psk(bank)], inc=(c == DC - 1))
                sc.op("dve", lambda e, P=P, dc=dc, r32=r32: e.scalar_tensor_tensor(
                    out=r32[:, dc, :], in0=r32[:, dc, :], scalar=ALPHA, in1=P[:], op0=ALU.mult, op1=ALU.add),
                    reads=[psk(bank), rk], writes=[rk])
                if pend_n:
                    pend_n.pop(0)()
            nrm, st_ = layer_norm(r32, rk, l * 4 + 0, l * 4 + 1, tt, lo, Rout=Rout, nbuf=1)
            pend_n.extend(nrm)
            stores.append(st_)
        for fn in pend_n:
            fn()
        stores[NT - 2]()
        stores[NT - 1]()

    def phase_fox():
        mixT = get_mixT()
        A = Arena(PH2_OFF)
        qT = A.take("qT", [128, S], BF16, 2)
        kT = A.take("kT", [128, S], BF16, 2)
        Vt = A.take("Vt", [128, 16, 128], BF16, 2)
        wq = A.take("wq", [128, DC, 128], BF16, 2)
        wk = A.take("wk", [128, DC, 128], BF16, 2)
        wv = A.take("wv", [128, DC, 128], BF16, 2)
        PT = A.take("PT", [128, T], BF16, 2)
        negc = A.take("negc", [128, S], F32)
        cneg = A.take("cneg", [128, S], F32)
        negcT = A.take("negcT", [128, 16, 8], F32)
        recips = A.take("recip", [128, T], F32, 2)
        sel = A.take("sel", [128, 8, 128], F32)
        sel3 = A.take("sel3", [128, 8, 128], BF16)
        cneg3 = A.take("cneg3", [128, S], BF16)
        maskb = A.take("maskb", [128, 128], BF16)
        wf = A.take("wf", [128, DC, 8], BF16)
        bfc = A.take("bfc", [128, 2], F32)
        ones_row = sc.sb("ones_row", [128, S], BF16, sc.ranges["qT0"][0])
        c_hi = sc.sb("c_hi", [128, S], BF16, sc.ranges["qT1"][0])
        c_mid = sc.sb("c_mid", [128, S], BF16, sc.ranges["kT1"][0])
        c_lo = sc.sb("c_lo", [128, S], BF16, sc.ranges["kT0"][0])
        win = ev_w_in.rearrange("(c p) f -> p c f", p=128)
        SC = 1.0 / math.sqrt(128.0)

        sc.op("pool", lambda e: e.memset(sel[:], 1.0), writes=["sel"])
        sc.op("pool", lambda e: e.affine_select(out=sel[:], in_=sel[:], pattern=[[-1, 8], [0, 128]],
                                                 compare_op=ALU.is_equal, fill=0.0, base=0, channel_multiplier=1),
              reads=["sel"], writes=["sel"])
        sc.op("pool", lambda e: e.memset(maskb[:], 0.0), writes=["maskb"])
        sc.op("pool", lambda e: e.affine_select(out=maskb[:], in_=maskb[:], pattern=[[1, 128]],
                                                 compare_op=ALU.is_ge, fill=-30000.0, base=0, channel_multiplier=-1),
              reads=["maskb"], writes=["maskb"])
        sc.op("pool", lambda e: e.memset(cneg[:], 0.0), writes=["cneg"])
        sc.op("pool", lambda e: e.memset(ones_row[0:8, :], 1.0), writes=["ones_row"])
        sc.dma("pool", "wf", wf[:], win[:, :, 3072:3080], writes=["wf"])
        sc.dma("sp", "bfc", bfc[0:8, 0:1], ev_b_f, writes=["bfc"])
        sc.op("dve", lambda e: e.tensor_scalar(out=bfc[0:8, 1:2], in0=bfc[0:8, 0:1], scalar1=-1.0, scalar2=None, op0=ALU.mult),
              reads=["bfc"], writes=["bfc"])
        for tt in range(NT):
            tsl = slice(tt * T, (tt + 1) * T)
            for c in range(DC):
                rhs = xTb[:, c, tsl]
                sc.op("pe", lambda e, c=c, rhs=rhs: e.matmul(PS[6][0:8, :], lhsT=wf[:, c, :], rhs=rhs,
                                                              start=(c == 0), stop=(c == DC - 1)),
                      reads=["wf", "xTb%d" % tt], writes=[psk(6)], inc=(c == DC - 1))
            sc.op("act", lambda e, tsl=tsl: e.activation(out=negc[0:8, tsl], in_=PS[6][0:8, :], func=AF.Exp,
                                                          bias=bfc[0:8, 1:2], scale=-1.0),
                  reads=[psk(6), "bfc"], writes=["negc"])
            sc.op("act", lambda e, tsl=tsl: e.activation(out=negc[0:8, tsl], in_=negc[0:8, tsl], func=AF.Ln, bias=1.0, scale=1.0),
                  reads=["negc"], writes=["negc"])
        sc.op("dve", lambda e: e.tensor_tensor_scan(out=negc[0:8, :], data0=ones_row[0:8, :], data1=negc[0:8, :],
                                                    initial=0.0, op0=ALU.mult, op1=ALU.add),
              reads=["negc", "ones_row"], writes=["negc"])
        sc.op("act", lambda e: e.mul(out=cneg[0:8, :], in_=negc[0:8, :], mul=-1.0), reads=["negc"], writes=["cneg"])
        for tb in range(16):
            sc.op("pe", lambda e, tb=tb: e.transpose(PS[7][:, tb * 8:(tb + 1) * 8], negc[0:8, tb * 128:(tb + 1) * 128],
                                                     ident_f[0:8, 0:8]),
                  reads=["negc", "ident_f"], writes=[psk(7)], inc=(tb == 15))
        sc.op("dve", lambda e: e.tensor_copy(out=negcT[:].rearrange("p a b -> p (a b)"), in_=PS[7][:, 0:128]),
              reads=[psk(7)], writes=["negcT"])

        sc.op("pool", lambda e: e.memset(cneg3[:], 0.0), writes=["cneg3"])
        sc.op("dve", lambda e: e.tensor_copy(out=sel3[:], in_=sel[:]), reads=["sel"], writes=["sel3"])
        sc.op("act", lambda e: e.copy(out=c_hi[0:8, :], in_=cneg[0:8, :]), reads=["cneg"], writes=["c_hi"])
        sc.op("dve", lambda e: e.tensor_tensor(out=negc[0:8, :], in0=cneg[0:8, :], in1=c_hi[0:8, :], op=ALU.subtract),
              reads=["cneg", "c_hi", "negc"], writes=["negc"])
        sc.op("act", lambda e: e.copy(out=c_mid[0:8, :], in_=negc[0:8, :]), reads=["negc"], writes=["c_mid"])
        sc.op("dve", lambda e: e.tensor_tensor(out=negc[0:8, :], in0=negc[0:8, :], in1=c_mid[0:8, :], op=ALU.subtract),
              reads=["negc", "c_mid"], writes=["negc"])
        sc.op("act", lambda e: e.copy(out=c_lo[0:8, :], in_=negc[0:8, :]), reads=["negc"], writes=["c_lo"])
        for r_, (src_, sk_) in enumerate(((c_hi, "c_hi"), (c_mid, "c_mid"), (c_lo, "c_lo"))):
            sc.dma("sp", "c3ld", cneg3[8 * r_:8 * r_ + 8, :], src_[0:8, :], reads=[sk_], writes=["cneg3"])
        for r_ in (1, 2):
            sc.dma("sp", "c3ld", sel3[8 * r_:8 * r_ + 8], sel3[0:8], reads=["sel3"], writes=["sel3"])

        def load_head(h):
            s_ = h % 2
            sc.dma("pool", "wq%d" % s_, wq[s_][:], win[:, :, h * 128:(h + 1) * 128], writes=["wq%d" % s_])
            sc.dma("pool", "wk%d" % s_, wk[s_][:], win[:, :, 1024 + h * 128:1024 + (h + 1) * 128], writes=["wk%d" % s_])
            sc.dma("pool", "wv%d" % s_, wv[s_][:], win[:, :, 2048 + h * 128:2048 + (h + 1) * 128], writes=["wv%d" % s_])

        load_head(0)
        pcnt = [0]
        qcnt = [0]
        for h in range(8):
            s_ = h % 2
            if h + 1 < 8:
                load_head(h + 1)
            qk, kk, vk = "qT%d" % s_, "kT%d" % s_, "Vt%d" % s_
            for tt in range(NT):
                tsl = slice(tt * T, (tt + 1) * T)
                for which, (w_, wkey) in enumerate(((wq[s_], "wq%d" % s_), (wk[s_], "wk%d" % s_))):
                    bank = which
                    for c in range(DC):
                        rhs = xTb[:, c, tsl]
                        sc.op("pe", lambda e, bank=bank, w_=w_, c=c, rhs=rhs: e.matmul(
                            PS[bank][:], lhsT=w_[:, c, :], rhs=rhs, start=(c == 0), stop=(c == DC - 1)),
                            reads=[wkey, "xTb%d" % tt], writes=[psk(bank)], inc=(c == DC - 1))
                    if which == 0:
                        sc.op("act", lambda e, tsl=tsl, s_=s_: e.activation(out=qT[s_][:, tsl], in_=PS[0][:], func=AF.Copy, scale=SC),
                              reads=[psk(0)], writes=[qk])
                    else:
                        sc.op("dve", lambda e, tsl=tsl, s_=s_: e.tensor_copy(out=kT[s_][:, tsl], in_=PS[1][:]),
                              reads=[psk(1)], writes=[kk])
            for t4 in range(4):
                bank = t4 % 2
                for q4 in range(4):
                    tb = t4 * 4 + q4
                    for c in range(DC):
                        lhsT = xTb[:, c, tb * 128:(tb + 1) * 128]
                        sc.op("pe", lambda e, bank=bank, q4=q4, c=c, lhsT=lhsT, s_=s_: e.matmul(
                            PS[bank][:, q4 * 128:(q4 + 1) * 128], lhsT=lhsT, rhs=wv[s_][:, c, :],
                            start=(c == 0), stop=(c == DC - 1)),
                            reads=["wv%d" % s_, "xTb%d" % (tb // 4)], writes=[psk(bank)], inc=(c == DC - 1))
                pv = PS[bank][:].rearrange("p (a b) -> p a b", a=4)
                if t4 % 2 == 0:
                    sc.op("act", lambda e, pv=pv, t4=t4, s_=s_: e.copy(out=Vt[s_][:, t4 * 4:(t4 + 1) * 4, :], in_=pv),
                          reads=[psk(bank)], writes=[vk])
                else:
                    sc.op("dve", lambda e, pv=pv, t4=t4, s_=s_: e.tensor_copy(out=Vt[s_][:, t4 * 4:(t4 + 1) * 4, :], in_=pv),
                          reads=[psk(bank)], writes=[vk])
            for Qi in range(4):
                conv_pump(3)
                ob = 4 + 2 * (qcnt[0] % 2)
                lb = ob + 1
                recip = recips[qcnt[0] % 2]
                rck_ = "recip%d" % (qcnt[0] % 2)
                qcnt[0] += 1
                nblk = 4 * Qi + 4
                for j in range(nblk):
                    n0 = max(0, j * 128 - Qi * T)
                    diag = j * 128 >= Qi * T
                    q0 = Qi * T + n0
                    q1 = (Qi + 1) * T
                    bank = 2 + pcnt[0] % 2
                    ps_ = pcnt[0] % 2
                    pcnt[0] += 1
                    P = PS[bank]
                    sc.op("pe", lambda e, P=P, n0=n0, j=j, q0=q0, q1=q1, s_=s_: e.matmul(
                        P[:, n0:T], lhsT=kT[s_][:, j * 128:(j + 1) * 128], rhs=qT[s_][:, q0:q1], start=True, stop=False),
                        reads=[kk, qk], writes=[psk(bank)], inc=False)
                    sc.op("pe", lambda e, P=P, n0=n0, q0=q0, q1=q1, h=h, diag=diag: e.matmul(
                        P[:, n0:T], lhsT=sel3[:, h, :], rhs=cneg3[:, q0:q1], start=False, stop=(not diag)),
                        reads=["sel3", "cneg3"], writes=[psk(bank)], inc=(not diag))
                    if diag:
                        sc.op("pe", lambda e, P=P, n0=n0: e.matmul(
                            P[:, n0:n0 + 128], lhsT=ident_b[:], rhs=maskb[:], start=False, stop=True),
                            reads=["ident_b", "maskb"], writes=[psk(bank)], inc=True)
                    ptk = "PT%d" % ps_
                    sc.op("act", lambda e, P=P, n0=n0, j=j, h=h, ps_=ps_: e.activation(
                        out=PT[ps_][:, n0:T], in_=P[:, n0:T], func=AF.Exp, bias=negcT[:, j, h:h + 1], scale=1.0),
                        reads=[psk(bank), "negcT"], writes=[ptk])
                    last = (j == nblk - 1)
                    sc.op("pe", lambda e, n0=n0, j=j, ps_=ps_, s_=s_, last=last, ob=ob: e.matmul(
                        PS[ob][:, n0:T], lhsT=Vt[s_][:, j, :], rhs=PT[ps_][:, n0:T], start=(j == 0), stop=last),
                        reads=[vk, ptk], writes=[psk(ob)], inc=False)
                    sc.op("pe", lambda e, n0=n0, j=j, ps_=ps_, last=last, lb=lb: e.matmul(
                        PS[lb][:, n0:T], lhsT=ones_b[:], rhs=PT[ps_][:, n0:T], start=(j == 0), stop=last),
                        reads=["ones_b", ptk], writes=[psk(lb)], inc=True)
                sc.op("dve", lambda e, lb=lb, recip=recip: e.reciprocal(out=recip[:], in_=PS[lb][:]), reads=[psk(lb)], writes=[rck_])
                sc.op("dve", lambda e, h=h, Qi=Qi, ob=ob, recip=recip: e.tensor_tensor(
                    out=mixT[:, h, Qi * T:(Qi + 1) * T], in0=PS[ob][:], in1=recip[:], op=ALU.mult),
                    reads=[psk(ob), rck_], writes=["mixlo"])
        wu = wq
        pf = Prefetch(8, 2, lambda i, s_: sc.dma("pool", "wq%d" % s_, wu[s_][:], win[:, :, 3080 + i * 128:3080 + (i + 1) * 128],
                                                 writes=["wq%d" % s_]))
        for ch in range(8):
            pf.ensure(ch)
            s_ = ch % 2
            for tt in range(NT):
                tsl = slice(tt * T, (tt + 1) * T)
                bank = tt % 2
                for c in range(DC):
                    rhs = xTb[:, c, tsl]
                    sc.op("pe", lambda e, bank=bank, c=c, rhs=rhs, s_=s_: e.matmul(
                        PS[bank][:], lhsT=wu[s_][:, c, :], rhs=rhs, start=(c == 0), stop=(c == DC - 1)),
                        reads=["wq%d" % s_, "xTb%d" % tt], writes=[psk(bank)], inc=(c == DC - 1))
                if tt % 2 == 0:
                    sc.op("act", lambda e, bank=bank, ch=ch, tsl=tsl: e.copy(out=mixT[:, 8 + ch, tsl], in_=PS[bank][:]),
                          reads=[psk(bank)], writes=["mixhi"])
                else:
                    sc.op("dve", lambda e, bank=bank, ch=ch, tsl=tsl: e.tensor_copy(out=mixT[:, 8 + ch, tsl], in_=PS[bank][:]),
                          reads=[psk(bank)], writes=["mixhi"])

    Wd = nc.dram_tensor("s5w", [8, 128, 16, 128], BF16, kind="Internal").ap()
    TWO_PI = 2.0 * math.pi

    def bk_level(k, l, first, xr_, xi_, xrk, xik, i, AQ_re, AQ_im, AQ_in):
        if first >= S:
            return []
        src = slice(first - k, S - k, 2 * k)
        dst = slice(first, S, 2 * k)
        ar = AQ_re[:, i, l:l + 1]
        ai = AQ_im[:, i, l:l + 1]
        an = AQ_in[:, i, l:l + 1]
        return [
            (lambda e: e.scalar_tensor_tensor(out=xr_[:, dst], in0=xr_[:, src], scalar=ar, in1=xr_[:, dst],
                                              op0=ALU.mult, op1=ALU.add), [xrk, "AQ_re"], [xrk]),
            (lambda e: e.scalar_tensor_tensor(out=xr_[:, dst], in0=xi_[:, src], scalar=an, in1=xr_[:, dst],
                                              op0=ALU.mult, op1=ALU.add), [xrk, xik, "AQ_in"], [xrk]),
            (lambda e: e.scalar_tensor_tensor(out=xi_[:, dst], in0=xi_[:, src], scalar=ar, in1=xi_[:, dst],
                                              op0=ALU.mult, op1=ALU.add), [xik, "AQ_re"], [xik]),
            (lambda e: e.scalar_tensor_tensor(out=xi_[:, dst], in0=xr_[:, src], scalar=ai, in1=xi_[:, dst],
                                              op0=ALU.mult, op1=ALU.add), [xik, xrk, "AQ_im"], [xik]),
        ]

    def phase_s5():
        mixT = get_mixT()
        AP_ = Arena(PH2_OFF)
        yg = AP_.take("yg", [128, 8, S], BF16)
        wslot = AP_.take("wslot", [128, 16, 128], BF16, 2)
        AQ_re = AP_.take("AQ_re", [128, 32, 11], F32)
        AQ_im = AP_.take("AQ_im", [128, 32, 11], F32)
        AQ_in = AP_.take("AQ_in", [128, 32, 11], F32)
        Dcol = AP_.take("Dcol", [128, 8], F32)
        y32 = AP_.take("y32", [128, T], F32, 2)
        gt = AP_.take("gt", [128, T], F32, 2)
        wz1 = AP_.take("wz1", [128, 8, 128], BF16, 2)
        wz2 = AP_.take("wz2", [128, 8, 128], BF16, 2)
        sig = AP_.take("sig", [128, T], F32, 2)
        AX = Arena(XTB_OFF)
        lamraw = AX.take("lamraw", [128, 2, 128], F32)
        names = ["lr", "li", "dt", "lrd", "lid", "mag", "kf", "rr", "rc", "m1", "sn", "cs", "are", "aim",
                 "den", "xr", "gre", "gim", "t1", "t2"]
        sm = {n: AX.take("s5_" + n, [128, 64], F32) for n in names}
        ki = AX.take("s5_ki", [128, 64], I32)
        P_re = AX.take("P_re", [128, 64, 11], F32)
        P_im = AX.take("P_im", [128, 64, 11], F32)
        b_re = AX.take("b_re", [128, 64, 16], F32)
        b_im = AX.take("b_im", [128, 64, 16], F32)
        bb_re = AX.take("bb_re", [128, 64, 16], F32)
        bb_im = AX.take("bb_im", [128, 64, 16], F32)
        btmp = AX.take("btmp", [128, 64, 16], F32)
        CT_re = AX.take("CT_re", [128, 8, 128], F32)
        CT_im = AX.take("CT_im", [128, 8, 128], F32)
        MC = AX.take("MC", [128, 4, 128], F32)
        MB = AX.take("MB", [128, 4, 128], F32)
        dstg = AX.take("dstg", [128, 128], F32)
        stage = AX.take("wstage", [128, 16, 128], BF16, 2)

        def dve(fn, reads, writes):
            sc.op("dve", fn, reads=reads, writes=writes)

        def tt_(out, a, b, op, reads, writes):
            dve(lambda e: e.tensor_tensor(out=out, in0=a, in1=b, op=op), reads, writes)

        K = lambda n: "s5_" + n
        for half in range(2):
            sc.dma("sp", "s5ld", lamraw[0:64, 0, half * 64:(half + 1) * 64], ev_lre, writes=["lamraw"])
            sc.dma("sp", "s5ld", lamraw[0:64, 1, half * 64:(half + 1) * 64], ev_lim, writes=["lamraw"])
            sc.dma("sp", "s5ld", b_re[half * 64:(half + 1) * 64], ev_bre.rearrange("g p c -> p g c"), writes=["b_re"])
            sc.dma("sp", "s5ld", b_im[half * 64:(half + 1) * 64], ev_bim.rearrange("g p c -> p g c"), writes=["b_im"])
            sc.dma("sp", "s5ld", CT_re[:, :, half * 64:(half + 1) * 64], ev_cre.rearrange("(j a) c p -> (a c) j p", a=8),
                   writes=["CT_re"])
            sc.dma("sp", "s5ld", CT_im[:, :, half * 64:(half + 1) * 64], ev_cim.rearrange("(j a) c p -> (a c) j p", a=8),
                   writes=["CT_im"])
        sc.dma("sp", "s5ld", sm["dt"][:], ev_lstep.broadcast_to([128, 64]), writes=[K("dt")])
        sc.dma("sp", "s5ld", dstg[0:8, :], ev_d, writes=["dstg"])
        for w, n in ((0, "lr"), (1, "li")):
            sc.op("pe", lambda e, w=w: e.transpose(PS[0][:, 0:64], lamraw[0:64, w, :], ident_f[0:64, 0:64]),
                  reads=["lamraw", "ident_f"], writes=[psk(0)])
            dve(lambda e, n=n: e.tensor_copy(out=sm[n][:], in_=PS[0][:, 0:64]), [psk(0)], [K(n)])
        sc.op("pe", lambda e: e.transpose(PS[0][:, 0:8], dstg[0:8, :], ident_f[0:8, 0:8]),
              reads=["dstg", "ident_f"], writes=[psk(0)])
        dve(lambda e: e.tensor_copy(out=Dcol[:], in_=PS[0][:, 0:8]), [psk(0)], ["Dcol"])
        sc.op("act", lambda e: e.activation(out=sm["dt"][:], in_=sm["dt"][:], func=AF.Exp), reads=[K("dt")], writes=[K("dt")])
        tt_(sm["lrd"][:], sm["lr"][:], sm["dt"][:], ALU.mult, [K("lr"), K("dt")], [K("lrd")])
        tt_(sm["lid"][:], sm["li"][:], sm["dt"][:], ALU.mult, [K("li"), K("dt")], [K("lid")])
        sc.op("act", lambda e: e.activation(out=sm["mag"][:], in_=sm["lrd"][:], func=AF.Exp), reads=[K("lrd")], writes=[K("mag")])
        dve(lambda e: e.tensor_scalar(out=sm["kf"][:], in0=sm["lid"][:], scalar1=1.0 / TWO_PI, scalar2=0.5,
                                      op0=ALU.mult, op1=ALU.add), [K("lid")], [K("kf")])
        dve(lambda e: e.tensor_copy(out=ki[:], in_=sm["kf"][:]), [K("kf")], ["s5_ki"])
        dve(lambda e: e.tensor_copy(out=sm["kf"][:], in_=ki[:]), ["s5_ki"], [K("kf")])
        dve(lambda e: e.scalar_tensor_tensor(out=sm["rr"][:], in0=sm["kf"][:], scalar=-TWO_PI, in1=sm["lid"][:],
                                             op0=ALU.mult, op1=ALU.add), [K("kf"), K("lid")], [K("rr")])

        def wrap(n):
            dve(lambda e: e.tensor_scalar(out=sm["m1"][:], in0=sm[n][:], scalar1=math.pi, scalar2=None, op0=ALU.is_gt),
                [K(n)], [K("m1")])
            dve(lambda e: e.scalar_tensor_tensor(out=sm[n][:], in0=sm["m1"][:], scalar=-TWO_PI, in1=sm[n][:],
                                                 op0=ALU.mult, op1=ALU.add), [K("m1"), K(n)], [K(n)])
            dve(lambda e: e.tensor_scalar(out=sm["m1"][:], in0=sm[n][:], scalar1=-math.pi, scalar2=None, op0=ALU.is_lt),
                [K(n)], [K("m1")])
            dve(lambda e: e.scalar_tensor_tensor(out=sm[n][:], in0=sm["m1"][:], scalar=TWO_PI, in1=sm[n][:],
                                                 op0=ALU.mult, op1=ALU.add), [K("m1"), K(n)], [K(n)])

        wrap("rr")
        dve(lambda e: e.tensor_scalar(out=sm["rc"][:], in0=sm["rr"][:], scalar1=0.5 * math.pi, scalar2=None, op0=ALU.add),
            [K("rr")], [K("rc")])
        wrap("rc")
        sc.op("act", lambda e: e.activation(out=sm["sn"][:], in_=sm["rr"][:], func=AF.Sin), reads=[K("rr")], writes=[K("sn")])
        sc.op("act", lambda e: e.activation(out=sm["cs"][:], in_=sm["rc"][:], func=AF.Sin), reads=[K("rc")], writes=[K("cs")])
        tt_(sm["are"][:], sm["mag"][:], sm["cs"][:], ALU.mult, [K("mag"), K("cs")], [K("are")])
        tt_(sm["aim"][:], sm["mag"][:], sm["sn"][:], ALU.mult, [K("mag"), K("sn")], [K("aim")])
        tt_(sm["t1"][:], sm["lr"][:], sm["lr"][:], ALU.mult, [K("lr")], [K("t1")])
        tt_(sm["den"][:], sm["li"][:], sm["li"][:], ALU.mult, [K("li")], [K("den")])
        tt_(sm["den"][:], sm["den"][:], sm["t1"][:], ALU.add, [K("den"), K("t1")], [K("den")])
        dve(lambda e: e.reciprocal(out=sm["den"][:], in_=sm["den"][:]), [K("den")], [K("den")])
        dve(lambda e: e.tensor_scalar(out=sm["xr"][:], in0=sm["are"][:], scalar1=-1.0, scalar2=None, op0=ALU.add),
            [K("are")], [K("xr")])
        tt_(sm["t1"][:], sm["xr"][:], sm["lr"][:], ALU.mult, [K("xr"), K("lr")], [K("t1")])
        tt_(sm["t2"][:], sm["aim"][:], sm["li"][:], ALU.mult, [K("aim"), K("li")], [K("t2")])
        tt_(sm["t1"][:], sm["t1"][:], sm["t2"][:], ALU.add, [K("t1"), K("t2")], [K("t1")])
        tt_(sm["gre"][:], sm["t1"][:], sm["den"][:], ALU.mult, [K("t1"), K("den")], [K("gre")])
        tt_(sm["t1"][:], sm["aim"][:], sm["lr"][:], ALU.mult, [K("aim"), K("lr")], [K("t1")])
        tt_(sm["t2"][:], sm["xr"][:], sm["li"][:], ALU.mult, [K("xr"), K("li")], [K("t2")])
        tt_(sm["t1"][:], sm["t1"][:], sm["t2"][:], ALU.subtract, [K("t1"), K("t2")], [K("t1")])
        tt_(sm["gim"][:], sm["t1"][:], sm["den"][:], ALU.mult, [K("t1"), K("den")], [K("gim")])
        gre_b = sm["gre"][:].unsqueeze(2).broadcast_to([128, 64, 16])
        gim_b = sm["gim"][:].unsqueeze(2).broadcast_to([128, 64, 16])
        tt_(bb_re[:], b_re[:], gre_b, ALU.mult, ["b_re", K("gre")], ["bb_re"])
        tt_(btmp[:], b_im[:], gim_b, ALU.mult, ["b_im", K("gim")], ["btmp"])
        tt_(bb_re[:], bb_re[:], btmp[:], ALU.subtract, ["bb_re", "btmp"], ["bb_re"])
        tt_(bb_im[:], b_im[:], gre_b, ALU.mult, ["b_im", K("gre")], ["bb_im"])
        tt_(btmp[:], b_re[:], gim_b, ALU.mult, ["b_re", K("gim")], ["btmp"])
        tt_(bb_im[:], bb_im[:], btmp[:], ALU.add, ["bb_im", "btmp"], ["bb_im"])
        dve(lambda e: e.tensor_copy(out=P_re[:, :, 0], in_=sm["are"][:]), [K("are")], ["P_re"])
        dve(lambda e: e.tensor_copy(out=P_im[:, :, 0], in_=sm["aim"][:]), [K("aim")], ["P_im"])
        for l in range(1, 11):
            tt_(sm["t1"][:], P_re[:, :, l - 1], P_re[:, :, l - 1], ALU.mult, ["P_re"], [K("t1")])
            tt_(sm["t2"][:], P_im[:, :, l - 1], P_im[:, :, l - 1], ALU.mult, ["P_im"], [K("t2")])
            tt_(P_re[:, :, l], sm["t1"][:], sm["t2"][:], ALU.subtract, [K("t1"), K("t2")], ["P_re"])
            tt_(sm["t1"][:], P_re[:, :, l - 1], P_im[:, :, l - 1], ALU.mult, ["P_re", "P_im"], [K("t1")])
            dve(lambda e, l=l: e.tensor_scalar(out=P_im[:, :, l], in0=sm["t1"][:], scalar1=2.0, scalar2=None, op0=ALU.mult),
                [K("t1")], ["P_im"])
        for (src, dst, dk) in ((P_re, AQ_re, "AQ_re"), (P_im, AQ_im, "AQ_im")):
            sv = src[:].rearrange("p (i two) l -> p i two l", two=2)
            sk = "P_re" if src is P_re else "P_im"
            dve(lambda e, sv=sv, dst=dst: e.tensor_copy(out=dst[0:64], in_=sv[0:64, :, 0, :]), [sk], [dk])
            dve(lambda e, sv=sv, dst=dst: e.tensor_copy(out=dst[64:128], in_=sv[64:128, :, 1, :]), [sk], [dk])
        dve(lambda e: e.tensor_scalar(out=AQ_in[:], in0=AQ_im[:], scalar1=-1.0, scalar2=None, op0=ALU.mult), ["AQ_im"], ["AQ_in"])
        sc.op("pool", lambda e: e.memset(MC[:], 0.0), writes=["MC"])
        for m in range(4):
            sc.op("pool", lambda e, m=m: e.memset(MC[0:64, m, 32 * m:32 * m + 16], 1.0), reads=["MC"], writes=["MC"])
            sc.op("pool", lambda e, m=m: e.memset(MC[64:128, m, 32 * m + 16:32 * m + 32], 1.0), reads=["MC"], writes=["MC"])
        for m in range(4):
            sc.op("pe", lambda e, m=m: e.transpose(PS[1][:, m * 128:(m + 1) * 128], MC[:, m, :], ident_f[:]),
                  reads=["MC", "ident_f"], writes=[psk(1)], inc=(m == 3))
        dve(lambda e: e.tensor_copy(out=MB[:].rearrange("p a b -> p (a b)"), in_=PS[1][:]), [psk(1)], ["MB"])
        for j in range(8):
            st_ = stage[j % 2]
            stk = "wstage%d" % (j % 2)
            srcs = ((bb_re[:, 8 * j:8 * j + 8, :].rearrange("p a b -> p (a b)"), "bb_re", 2, 0),
                    (bb_im[:, 8 * j:8 * j + 8, :].rearrange("p a b -> p (a b)"), "bb_im", 2, 1),
                    (CT_re[:, j, :], "CT_re", 3, 0), (CT_im[:, j, :], "CT_im", 3, 1))
            for (src, sk, bank, half) in srcs:
                sc.op("pe", lambda e, src=src, bank=bank, half=half: e.transpose(
                    PS[bank][:, half * 128:(half + 1) * 128], src, ident_f[:]),
                    reads=[sk, "ident_f"], writes=[psk(bank)], inc=True)
            for m in range(4):
                dve(lambda e, st_=st_, m=m: e.tensor_tensor(out=st_[:, 0 + m, :], in0=PS[2][:, 0:128], in1=MB[:, m, :], op=ALU.mult),
                    [psk(2), "MB"], [stk])
                dve(lambda e, st_=st_, m=m: e.tensor_tensor(out=st_[:, 4 + m, :], in0=PS[2][:, 128:256], in1=MB[:, m, :], op=ALU.mult),
                    [psk(2), "MB"], [stk])
                dve(lambda e, st_=st_, m=m: e.tensor_tensor(out=st_[:, 8 + m, :], in0=PS[3][:, 0:128], in1=MC[:, m, :], op=ALU.mult),
                    [psk(3), "MC"], [stk])
                dve(lambda e, st_=st_, m=m: e.scalar_tensor_tensor(out=st_[:, 12 + m, :], in0=PS[3][:, 128:256], scalar=-1.0,
                                                                   in1=MC[:, m, :], op0=ALU.mult, op1=ALU.mult),
                    [psk(3), "MC"], [stk])
            sc.dma("sp", stk, Wd[j], st_[:], reads=[stk], writes=[("Wd", j)])
        mixT2 = mixT
        AX2 = Arena(XTB_OFF)
        X_re = AX2.take("X_re", [128, S], F32, 2)
        X_im = AX2.take("X_im", [128, S], F32, 2)
        H_re = AX2.take("H_re", [128, S], BF16, 4)
        H_im = AX2.take("H_im", [128, S], BF16, 4)
        C1 = 2.0 * math.sqrt(2.0 / math.pi)
        C2 = C1 * 0.044715
        bcnt = [0]
        for j in range(8):
            wsl = wslot[j % 2]
            wk_ = "wslot%d" % (j % 2)
            sc.dma("sp", wk_, wsl[:], Wd[j], reads=[("Wd", j)], writes=[wk_])
            for mp in (0, 2):
                conv_pump(8)
                chains = []
                for m in (mp, mp + 1):
                    i = 4 * j + m
                    xs = i % 2
                    xr_, xi_ = X_re[xs], X_im[xs]
                    xrk, xik = "X_re%d" % xs, "X_im%d" % xs
                    for tt in range(NT):
                        tsl = slice(tt * T, (tt + 1) * T)
                        for half, (dst, dk) in enumerate(((xr_, xrk), (xi_, xik))):
                            bank = bcnt[0] % 4
                            bcnt[0] += 1
                            sc.op("pe", lambda e, bank=bank, half=half, m=m, wsl=wsl, j=j, tsl=tsl: e.matmul(
                                PS[bank][:], lhsT=wsl[:, 4 * half + m, :], rhs=mixT2[:, 8 + j, tsl], start=True, stop=True),
                                reads=[wk_, "mixhi"], writes=[psk(bank)], inc=True)
                            sc.op("act", lambda e, bank=bank, dst=dst, tsl=tsl: e.copy(out=dst[:, tsl], in_=PS[bank][:]),
                                  reads=[psk(bank)], writes=[dk])
                    ops = []
                    for l in range(11):
                        ops += bk_level(1 << l, l, 2 * (1 << l) - 1, xr_, xi_, xrk, xik, i, AQ_re, AQ_im, AQ_in)
                    for l in range(9, -1, -1):
                        ops += bk_level(1 << l, l, 3 * (1 << l) - 1, xr_, xi_, xrk, xik, i, AQ_re, AQ_im, AQ_in)
                    chains.append((m, xr_, xi_, xrk, xik, ops))
                for idx in range(max(len(c[5]) for c in chains)):
                    for c_ in chains:
                        if idx < len(c_[5]):
                            fn, rd, wr = c_[5][idx]
                            sc.op("dve", fn, reads=rd, writes=wr)
                for (m, xr_, xi_, xrk, xik, _) in chains:
                    sc.op("act", lambda e, m=m, xr_=xr_: e.copy(out=H_re[m][:], in_=xr_[:]), reads=[xrk], writes=["H_re%d" % m])
                    sc.op("act", lambda e, m=m, xi_=xi_: e.copy(out=H_im[m][:], in_=xi_[:]), reads=[xik], writes=["H_im%d" % m])
            for tt in range(NT):
                tsl = slice(tt * T, (tt + 1) * T)
                bank = 4 + tt % 2
                for m in range(4):
                    sc.op("pe", lambda e, bank=bank, m=m, wsl=wsl, tsl=tsl: e.matmul(
                        PS[bank][:], lhsT=wsl[:, 8 + m, :], rhs=H_re[m][:, tsl], start=(m == 0), stop=False),
                        reads=[wk_, "H_re%d" % m], writes=[psk(bank)], inc=False)
                    sc.op("pe", lambda e, bank=bank, m=m, wsl=wsl, tsl=tsl: e.matmul(
                        PS[bank][:], lhsT=wsl[:, 12 + m, :], rhs=H_im[m][:, tsl], start=False, stop=(m == 3)),
                        reads=[wk_, "H_im%d" % m], writes=[psk(bank)], inc=(m == 3))
                y_ = y32[tt % 2]
                g_ = gt[tt % 2]
                yk, gk = "y32%d" % (tt % 2), "gt%d" % (tt % 2)
                sc.op("dve", lambda e, bank=bank, y_=y_, j=j, tsl=tsl: e.scalar_tensor_tensor(
                    out=y_[:], in0=mixT2[:, 8 + j, tsl], scalar=Dcol[:, j:j + 1], in1=PS[bank][:], op0=ALU.mult, op1=ALU.add),
                    reads=[psk(bank), "mixhi", "Dcol"], writes=[yk])
                sc.op("pool", lambda e, y_=y_, g_=g_: e.tensor_tensor(out=g_[:], in0=y_[:], in1=y_[:], op=ALU.mult),
                      reads=[yk], writes=[gk])
                sc.op("pool", lambda e, g_=g_: e.tensor_scalar(out=g_[:], in0=g_[:], scalar1=C2, scalar2=C1, op0=ALU.mult, op1=ALU.add),
                      reads=[gk], writes=[gk])
                sc.op("pool", lambda e, y_=y_, g_=g_: e.tensor_tensor(out=g_[:], in0=g_[:], in1=y_[:], op=ALU.mult),
                      reads=[gk, yk], writes=[gk])
                sc.op("act", lambda e, g_=g_: e.activation(out=g_[:], in_=g_[:], func=AF.Sigmoid), reads=[gk], writes=[gk])
                sc.op("pool", lambda e, y_=y_, g_=g_, j=j, tsl=tsl: e.tensor_tensor(out=yg[:, j, tsl], in0=g_[:], in1=y_[:], op=ALU.mult),
                      reads=[gk, yk], writes=["yg"])
        wgl = ev_w_glu.rearrange("(c p) f -> p c f", p=128)

        def issue_glu(i, s_):
            sc.dma("pool", "wz1%d" % s_, wz1[s_][:], wgl[:, :, i * 128:(i + 1) * 128], writes=["wz1%d" % s_])
            sc.dma("pool", "wz2%d" % s_, wz2[s_][:], wgl[:, :, 1024 + i * 128:1024 + (i + 1) * 128], writes=["wz2%d" % s_])

        pf = Prefetch(8, 2, issue_glu)
        for e_ in range(8):
            pf.ensure(e_)
            s_ = e_ % 2
            for tt in range(NT):
                tsl = slice(tt * T, (tt + 1) * T)
                ba, bb_ = (tt % 2) * 2, (tt % 2) * 2 + 1
                for (bank, w_, wkey) in ((ba, wz1[s_], "wz1%d" % s_), (bb_, wz2[s_], "wz2%d" % s_)):
                    for c in range(8):
                        sc.op("pe", lambda e, bank=bank, w_=w_, c=c, tsl=tsl: e.matmul(
                            PS[bank][:], lhsT=w_[:, c, :], rhs=yg[:, c, tsl], start=(c == 0), stop=(c == 7)),
                            reads=[wkey, "yg"], writes=[psk(bank)], inc=(c == 7))
                sg_ = sig[tt % 2]
                sgk = "sig%d" % (tt % 2)
                sc.op("act", lambda e, bb_=bb_, sg_=sg_: e.activation(out=sg_[:], in_=PS[bb_][:], func=AF.Sigmoid),
                      reads=[psk(bb_)], writes=[sgk])
                sc.op("dve", lambda e, ba=ba, sg_=sg_, e_=e_, tsl=tsl: e.tensor_tensor(
                    out=mixT2[:, 8 + e_, tsl], in0=PS[ba][:], in1=sg_[:], op=ALU.mult),
                    reads=[psk(ba), sgk], writes=["mixhi"])

    INVF = [float(v) for v in (np.float32(500000.0) ** (-(np.arange(8, dtype=np.float32) / np.float32(8.0))))]

    def psb(i):
        return PS[i][:].bitcast(BF16)

    def phase_swa():
        mixT = get_mixT()
        A = Arena(PH2_OFF)
        wcol = A.take("wcol", [128, DC, 256], BF16, 2)
        qb = A.take("qb", [128, 4, 64], BF16, 2)
        kd = A.take("kd", [128, 4, 2, 64], BF16, 2)
        qTp = A.take("qTp", [128, S], BF16, 4)
        kTd = A.take("kTd", [128, S], BF16, 4)
        Vs = A.take("Vs", [128, 16, 256], BF16)
        PTa = A.take("PTa", [128, T], BF16, 2)
        PTb = A.take("PTb", [128, T], BF16, 2)
        mask2 = A.take("mask2", [128, 512], BF16)
        maskA = mask2[:, 0:256].rearrange("p (a b) -> p a b", a=2)
        maskB = mask2[:, 256:512].rearrange("p (a b) -> p a b", a=2)
        cosk = A.take("cosk", [128, 16, 8], F32)
        sink = A.take("sink", [128, 16, 8], F32)
        cosq = A.take("cosq", [128, 16, 8], F32)
        sinq = A.take("sinq", [128, 16, 8], F32)
        ang = A.take("ang", [128, 16, 8], F32)
        angc = A.take("angc", [128, 16, 8], F32)
        rtmp = A.take("rtmp", [128, 16, 8], F32)
        rki = A.take("rki", [128, 16, 8], I32)
        posi = A.take("posi", [128, 128], I32)
        posf = A.take("posf", [128, 128], F32)
        posT = A.take("posT", [128, 16], F32)
        invf = A.take("invf", [128, 8], F32)
        es = A.take("es", [128, 32], F32)
        esP = A.take("esP", [128, 16], F32)
        rt = A.take("rt", [128, 6, 4, 8], F32, 2)
        rot = A.take("rot", [128, 4, 16], F32, 2)
        rc = A.take("rc", [128, 256], F32, 2)
        win = od_w_in.rearrange("(c p) f -> p c f", p=128)

        def dve(fn, reads, writes):
            sc.op("dve", fn, reads=reads, writes=writes)

        sc.op("pool", lambda e: e.memset(mask2[:], 0.0), writes=["mask2"])
        sc.op("pool", lambda e: e.affine_select(out=maskA, in_=maskA, pattern=[[0, 2], [-1, 128]],
                                                 compare_op=ALU.is_ge, fill=-30000.0, base=-1, channel_multiplier=1),
              reads=["mask2"], writes=["mask2"])
        sc.op("pool", lambda e: e.affine_select(out=maskB, in_=maskB, pattern=[[0, 2], [1, 128]],
                                                 compare_op=ALU.is_ge, fill=-30000.0, base=0, channel_multiplier=-1),
              reads=["mask2"], writes=["mask2"])
        sc.dma("sp", "swld", es[:], od_sinks.broadcast_to([128, 32]), writes=["es"])
        sc.op("act", lambda e: e.activation(out=es[:], in_=es[:], func=AF.Exp), reads=["es"], writes=["es"])
        esv = es[:].rearrange("p (a two) -> p a two", two=2)
        dve(lambda e: e.tensor_copy(out=esP[0:64, :], in_=esv[0:64, :, 0]), ["es"], ["esP"])
        dve(lambda e: e.tensor_copy(out=esP[64:128, :], in_=esv[64:128, :, 1]), ["es"], ["esP"])
        sc.dma("sp", "swld", posi[0:16, :], pos_in, writes=["posi"])
        dve(lambda e: e.tensor_copy(out=posf[0:16, :], in_=posi[0:16, :]), ["posi"], ["posf"])
        sc.op("pe", lambda e: e.transpose(PS[7][:, 0:16], posf[0:16, :], ident_f[0:16, 0:16]),
              reads=["posf", "ident_f"], writes=[psk(7)])
        dve(lambda e: e.tensor_copy(out=posT[:], in_=PS[7][:, 0:16]), [psk(7)], ["posT"])
        for f in range(8):
            sc.op("pool", lambda e, f=f: e.memset(invf[:, f:f + 1], INVF[f]), writes=["invf"])
        dve(lambda e: e.tensor_tensor(out=ang[:], in0=posT[:].unsqueeze(2).broadcast_to([128, 16, 8]),
                                      in1=invf[:].unsqueeze(1).broadcast_to([128, 16, 8]), op=ALU.mult),
            ["posT", "invf"], ["ang"])

        def reduce_(x, xk):
            dve(lambda e: e.tensor_scalar(out=rtmp[:], in0=x[:], scalar1=1.0 / TWO_PI, scalar2=0.5, op0=ALU.mult, op1=ALU.add),
                [xk], ["rtmp"])
            dve(lambda e: e.tensor_copy(out=rki[:], in_=rtmp[:]), ["rtmp"], ["rki"])
            dve(lambda e: e.tensor_copy(out=rtmp[:], in_=rki[:]), ["rki"], ["rtmp"])
            dve(lambda e: e.scalar_tensor_tensor(out=x[:], in0=rtmp[:], scalar=-TWO_PI, in1=x[:], op0=ALU.mult, op1=ALU.add),
                ["rtmp", xk], [xk])
            for (thr, op_, add) in ((math.pi, ALU.is_gt, -TWO_PI), (-math.pi, ALU.is_lt, TWO_PI)):
                dve(lambda e, thr=thr, op_=op_: e.tensor_scalar(out=rtmp[:], in0=x[:], scalar1=thr, scalar2=None, op0=op_),
                    [xk], ["rtmp"])
                dve(lambda e, add=add: e.scalar_tensor_tensor(out=x[:], in0=rtmp[:], scalar=add, in1=x[:], op0=ALU.mult, op1=ALU.add),
                    ["rtmp", xk], [xk])

        dve(lambda e: e.tensor_scalar(out=angc[:], in0=ang[:], scalar1=0.5 * math.pi, scalar2=None, op0=ALU.add), ["ang"], ["angc"])
        reduce_(ang, "ang")
        reduce_(angc, "angc")
        sc.op("act", lambda e: e.activation(out=sink[:], in_=ang[:], func=AF.Sin), reads=["ang"], writes=["sink"])
        sc.op("act", lambda e: e.activation(out=cosk[:], in_=angc[:], func=AF.Sin), reads=["angc"], writes=["cosk"])
        sc.op("act", lambda e: e.mul(out=sinq[:], in_=sink[:], mul=0.125), reads=["sink"], writes=["sinq"])
        sc.op("act", lambda e: e.mul(out=cosq[:], in_=cosk[:], mul=0.125), reads=["cosk"], writes=["cosq"])
        for hq in range(4):
            sc.op("pool", lambda e, hq=hq: e.memset(qTp[hq][:], 0.0), writes=["qTp%d" % hq])

        units = [("k", 2048), ("v", 2304)] + [("q", 256 * u) for u in range(8)]
        pf = Prefetch(len(units), 2, lambda i, s_: sc.dma("pool", "wcol%d" % s_, wcol[s_][:],
                                                          win[:, :, units[i][1]:units[i][1] + 256], writes=["wcol%d" % s_]))
        ecnt = [0]

        def rope(P4, cs, sn, tb, o1, o2, okeys, slot):
            r_ = rt[slot]
            rk = "rt%d" % slot
            cb = cs[:, tb, :].unsqueeze(1).broadcast_to([128, 4, 8])
            sb_ = sn[:, tb, :].unsqueeze(1).broadcast_to([128, 4, 8])
            x1 = P4[:, :, 0:8]
            x2 = P4[:, :, 8:16]
            ck = ["cosk", "sink", "cosq", "sinq"]
            dve(lambda e: e.tensor_tensor(out=r_[:, 0], in0=x1, in1=cb, op=ALU.mult), okeys["ps"] + ck, [rk])
            dve(lambda e: e.tensor_tensor(out=r_[:, 1], in0=x2, in1=sb_, op=ALU.mult), okeys["ps"] + ck, [rk])
            dve(lambda e: e.tensor_tensor(out=r_[:, 2], in0=x2, in1=cb, op=ALU.mult), okeys["ps"] + ck, [rk])
            dve(lambda e: e.tensor_tensor(out=r_[:, 3], in0=x1, in1=sb_, op=ALU.mult), okeys["ps"] + ck, [rk])
            dve(lambda e: e.tensor_tensor(out=o1, in0=r_[:, 0], in1=r_[:, 1], op=ALU.subtract), [rk], okeys["out"])
            dve(lambda e: e.tensor_tensor(out=o2, in0=r_[:, 2], in1=r_[:, 3], op=ALU.add), [rk], okeys["out"])

        for ui, (kind, col0) in enumerate(units):
            pf.ensure(ui)
            ws = ui % 2
            wkey = "wcol%d" % ws
            quad = ui - 2
            for tb in range(16):
                bank = (tb // 2) % 2
                half = tb % 2
                Pfull = PS[bank][:, half * 256:(half + 1) * 256]
                for c in range(DC):
                    lhsT = xTb[:, c, tb * 128:(tb + 1) * 128]
                    sc.op("pe", lambda e, Pfull=Pfull, lhsT=lhsT, ws=ws, c=c: e.matmul(
                        Pfull, lhsT=lhsT, rhs=wcol[ws][:, c, :], start=(c == 0), stop=(c == DC - 1)),
                        reads=[wkey, "xTb%d" % (tb // 4)], writes=[psk(bank)], inc=(c == DC - 1))
                P4 = Pfull.rearrange("p (h d) -> p h d", h=4)
                slot = ecnt[0] % 2
                ecnt[0] += 1
                if kind == "v":
                    sc.op("act", lambda e, Pfull=Pfull, tb=tb: e.copy(out=Vs[:, tb, :], in_=Pfull), reads=[psk(bank)], writes=["Vs"])
                elif kind == "k":
                    kd_ = kd[slot]
                    kk = "kd%d" % slot
                    sc.op("act", lambda e, P4=P4, kd_=kd_: e.copy(out=kd_[:, :, 0, :], in_=P4), reads=[psk(bank)], writes=[kk])
                    ro = rot[slot]
                    rok = "rot%d" % slot
                    rope(P4, cosk, sink, tb, ro[:, :, 0:8], ro[:, :, 8:16], {"ps": [psk(bank)], "out": [rok]}, slot)
                    dve(lambda e, kd_=kd_, ro=ro: e.tensor_copy(out=kd_[:, :, 0, 0:16], in_=ro[:]), [rok, kk], [kk])
                    dve(lambda e, kd_=kd_: e.tensor_copy(out=kd_[:, :, 1, :], in_=kd_[:, :, 0, :]), [kk], [kk])
                    for kv in range(4):
                        sc.op("pe", lambda e, kv=kv, kd_=kd_: e.transpose(
                            psb(2)[:, kv * 128:(kv + 1) * 128], kd_[:, kv, :, :].rearrange("p a b -> p (a b)"), ident_b[:]),
                            reads=[kk, "ident_b"], writes=[psk(2)], inc=(kv == 3))
                    for kv in range(4):
                        eng = "act" if kv % 2 == 0 else "dve"
                        if eng == "act":
                            sc.op("act", lambda e, kv=kv, tb=tb: e.copy(out=kTd[kv][:, tb * 128:(tb + 1) * 128],
                                                                        in_=psb(2)[:, kv * 128:(kv + 1) * 128]),
                                  reads=[psk(2)], writes=["kTd%d" % kv])
                        else:
                            dve(lambda e, kv=kv, tb=tb: e.tensor_copy(out=kTd[kv][:, tb * 128:(tb + 1) * 128],
                                                                       in_=psb(2)[:, kv * 128:(kv + 1) * 128]),
                                [psk(2)], ["kTd%d" % kv])
                else:
                    qb_ = qb[slot]
                    qk_ = "qb%d" % slot
                    sc.op("act", lambda e, P4=P4, qb_=qb_: e.activation(out=qb_[:], in_=P4, func=AF.Copy, scale=0.125),
                          reads=[psk(bank)], writes=[qk_])
                    rope(P4, cosq, sinq, tb, qb_[:, :, 0:8], qb_[:, :, 8:16], {"ps": [psk(bank), qk_], "out": [qk_]}, slot)
                    for mc in range(2):
                        sc.op("pe", lambda e, mc=mc, qb_=qb_: e.transpose(
                            psb(2)[:, mc * 128:(mc + 1) * 128], qb_[:, 2 * mc:2 * mc + 2, :].rearrange("p a b -> p (a b)"),
                            ident_b[:]), reads=[qk_, "ident_b"], writes=[psk(2)], inc=(mc == 1))
                    for hq in range(4):
                        e_, mc = hq % 2, hq // 2
                        src = psb(2)[e_ * 64:(e_ + 1) * 64, mc * 128:(mc + 1) * 128]
                        dst = qTp[hq][e_ * 64:(e_ + 1) * 64, tb * 128:(tb + 1) * 128]
                        if hq % 2 == 0:
                            sc.op("act", lambda e, src=src, dst=dst: e.copy(out=dst, in_=src), reads=[psk(2)], writes=["qTp%d" % hq])
                        else:
                            dve(lambda e, src=src, dst=dst: e.tensor_copy(out=dst, in_=src), [psk(2)], ["qTp%d" % hq])
            if kind != "q":
                continue
            hk = quad // 2
            units_ = [(i, mc) for i in range(16) for mc in range(2)]

            def emit_scores(u):
                i, mc = units_[u]
                bank = 3 + u % 2
                pt = PTa[u % 2]
                ptk = "PTa%d" % (u % 2)
                first = True
                for blk, kb in ((0, i - 1), (1, i)):
                    if kb < 0:
                        continue
                    for e_ in range(2):
                        hq = 2 * mc + e_
                        col = blk * 256 + e_ * 128
                        sc.op("pe", lambda e, bank=bank, kb=kb, hq=hq, i=i, hk=hk, col=col, first=first: e.matmul(
                            PS[bank][:, col:col + 128], lhsT=kTd[hk][:, kb * 128:(kb + 1) * 128],
                            rhs=qTp[hq][:, i * 128:(i + 1) * 128], start=first, stop=False),
                            reads=["kTd%d" % hk, "qTp%d" % hq], writes=[psk(bank)], inc=False)
                        first = False
                c0 = 0 if i > 0 else 256
                sc.op("pe", lambda e, bank=bank, c0=c0: e.matmul(
                    PS[bank][:, c0:512], lhsT=ident_b[:], rhs=mask2[:, c0:512], start=False, stop=True),
                    reads=["ident_b", "mask2"], writes=[psk(bank)], inc=True)
                sc.op("act", lambda e, bank=bank, pt=pt, c0=c0: e.activation(out=pt[:, c0:512], in_=PS[bank][:, c0:512], func=AF.Exp),
                      reads=[psk(bank)], writes=[ptk])

            def emit_pv(u):
                i, mc = units_[u]
                obank = 5 + u % 2
                pt = PTa[u % 2]
                ptk = "PTa%d" % (u % 2)
                blks = [(0, i - 1), (1, i)] if i > 0 else [(1, i)]
                for (c_off, use_v) in ((0, True), (128, False)):
                    for e_ in range(2):
                        out = PS[obank][e_ * 64:(e_ + 1) * 64, c_off:c_off + 128]
                        for bi, (blk, kb) in enumerate(blks):
                            lhsT = Vs[:, kb, hk * 64:(hk + 1) * 64] if use_v else ones_b[:, 0:64]
                            rhs = pt[:, blk * 256 + e_ * 128:blk * 256 + (e_ + 1) * 128]
                            lastmm = (not use_v) and e_ == 1 and bi == len(blks) - 1
                            sc.op("pe", lambda e, out=out, lhsT=lhsT, rhs=rhs, bi=bi, nb=len(blks), e_=e_: e.matmul(
                                out, lhsT=lhsT, rhs=rhs, start=(bi == 0), stop=(bi == nb - 1), tile_position=(0, e_ * 64)),
                                reads=[ptk, "Vs" if use_v else "ones_b"], writes=[psk(obank)], inc=lastmm)
                rc_ = rc[u % 2]
                rck = "rc%d" % (u % 2)
                ch = quad * 2 + mc
                dve(lambda e, rc_=rc_, obank=obank, ch=ch: e.tensor_scalar(
                    out=rc_[:, 0:128], in0=PS[obank][:, 128:256], scalar1=esP[:, ch:ch + 1], scalar2=None, op0=ALU.add),
                    [psk(obank), "esP"], [rck])
                dve(lambda e, rc_=rc_: e.reciprocal(out=rc_[:, 0:128], in_=rc_[:, 0:128]), [rck], [rck])
                dve(lambda e, rc_=rc_, obank=obank, ch=ch, i=i: e.tensor_tensor(
                    out=mixT[:, ch, i * 128:(i + 1) * 128], in0=PS[obank][:, 0:128], in1=rc_[:, 0:128], op=ALU.mult),
                    [psk(obank), rck], ["mixlo" if ch < 8 else "mixhi"])

            emit_scores(0)
            for u in range(len(units_)):
                if u + 1 < len(units_):
                    emit_scores(u + 1)
                emit_pv(u)

    def dbg_dump_mix():
        mixT = get_mixT()
        ov = out_d.rearrange("(c p) t -> p c t", p=128)
        for c in range(DC):
            sc.dma("pool", "dbgm", ov[:, c, :], mixT[:, c, :], reads=["mixlo", "mixhi"])

    def phase_output(R):
        A = Arena(PH_OFF)
        rt_ = A.take("o_r", [128, DC, T], F32, 2)
        ost = A.take("o_st", [128, D], F32, 3)
        for tt in range(NT):
            r_ = rt_[tt % 2]
            rk = "o_r%d" % (tt % 2)
            sc.dma("sp", rk, r_[:], R[:, :, tt * T:(tt + 1) * T].rearrange("c p t -> p c t"),
                   reads=[("R", id(R), tt)], writes=[rk])
            for tb in range(4):
                g = tt * 4 + tb
                os_ = ost[g % 3]
                ok = "o_st%d" % (g % 3)
                for q in range(4):
                    bank = q % 2 + 2 * (g % 2)
                    for c4 in range(4):
                        dc = q * 4 + c4
                        src = r_[:, dc, tb * 128:(tb + 1) * 128]
                        sc.op("pe", lambda e, b=bank, c4=c4, src=src: e.transpose(
                            PS[b][:, c4 * 128:(c4 + 1) * 128], src, ident_f[:]),
                            reads=[rk, "ident_f"], writes=[psk(bank)], inc=(c4 == 3))
                    dst = os_[:, q * 512:(q + 1) * 512]
                    if q % 2 == 0:
                        sc.op("act", lambda e, b=bank, dst=dst: e.copy(out=dst, in_=PS[b][:]), reads=[psk(bank)], writes=[ok])
                    else:
                        sc.op("dve", lambda e, b=bank, dst=dst: e.tensor_copy(out=dst, in_=PS[b][:]), reads=[psk(bank)], writes=[ok])
                sc.dma("sp", ok, out_d[g * 128:(g + 1) * 128, :], os_[:], reads=[ok], writes=[("out", g)])

    def dbg_copy_R(R):
        A = Arena(PH_OFF)
        bufs = A.take("dbgb", [128, S], F32, 2)
        for c in range(DC):
            k = "dbgb%d" % (c % 2)
            sc.dma("sp", k, bufs[c % 2][:], R[c], reads=[("R", id(R), tt) for tt in range(NT)], writes=[k])
            sc.dma("sp", k, out_d[c * 128:(c + 1) * 128, :], bufs[c % 2][:], reads=[k])

    setup_consts()
    sc.phase_reset()
    if mode == "full":
        phase_input(x_in, RA)
        sc.phase_reset()
        phase_fox()
        sc.phase_reset()
        phase_s5()
        sc.phase_reset()
        phase_outproj(0, ev_w_out, RA, RB)
        sc.phase_reset()
        phase_ffn(0, RB, RA, final=False)
        sc.phase_reset()
        phase_swa()
        sc.phase_reset()
        phase_outproj(1, od_w_out, RA, RB)
        sc.phase_reset()
        phase_ffn(1, RB, RA, final=False)
        sc.phase_reset()
        phase_output(RA)
    elif mode == "consts":
        sc.dma("sp", "dbg", out_d[0:128, 0:128], ident_f[:], reads=["ident_f"])
        sc.dma("sp", "dbg", out_d[128:256, 0:86 * 4].rearrange("p (a b) -> p a b", a=4), convp[:, 0, :, :], reads=["consts"])
        sc.dma("sp", "dbg", out_d[256:384, 0:128].rearrange("p (a b) -> p a b", a=8), lnp[:], reads=["consts"])
    elif mode == "p0":
        phase_input(x_in, RA)
        sc.phase_reset()
        if 'dbg' not in _SKIP:
            dbg_copy_R(RA)
    elif mode == "fox":
        phase_input(x_in, RA)
        sc.phase_reset()
        phase_fox()
        sc.phase_reset()
        dbg_dump_mix()
    elif mode == "s5":
        phase_input(x_in, RA)
        sc.phase_reset()
        phase_fox()
        sc.phase_reset()
        phase_s5()
        sc.phase_reset()
        dbg_dump_mix()
    elif mode == "swa":
        phase_input(x_in, RA)
        sc.phase_reset()
        phase_swa()
        sc.phase_reset()
        phase_outproj(1, od_w_out, RA, RB)
        sc.phase_reset()
        dbg_copy_R(RB)
    elif mode == "outproj0":
        phase_input(x_in, RA)
        sc.phase_reset()
        phase_fox()
        sc.phase_reset()
        phase_outproj(0, ev_w_out, RA, RB)
        sc.phase_reset()
        dbg_copy_R(RB)
    elif mode == "ffn0":
        phase_input(x_in, RA)
        sc.phase_reset()
        phase_ffn(0, RA, RB, final=False)
        sc.phase_reset()
        phase_output(RB)
    else:
        raise NotImplementedError(mode)
    sc.barrier()
    sc.emit()
    return nc, es


_CACHE = {}


def _prep_inputs(inp, b):
    f = np.ascontiguousarray
    m = {
        "x": f(inp["x"][b]),
        "pos": f(inp["positions"][b].reshape(16, 128).astype(np.int32)),
        "ev_w_in": f(inp["ev_w_in"][0]),
        "ev_b_f": f(inp["ev_b_f"][0].reshape(8, 1)),
        "ev_lre": f(inp["ev_lambda_re"][0]),
        "ev_lim": f(inp["ev_lambda_im"][0]),
        "ev_lstep": f(inp["ev_log_step"][0].reshape(1, 64)),
        "ev_bre": f(inp["ev_ssm_b_re"][0]),
        "ev_bim": f(inp["ev_ssm_b_im"][0]),
        "ev_cre": f(inp["ev_ssm_c_re"][0]),
        "ev_cim": f(inp["ev_ssm_c_im"][0]),
        "ev_d": f(inp["ev_ssm_d"][0].reshape(8, 128)),
        "ev_w_glu": f(inp["ev_w_glu"][0]),
        "ev_w_out": f(inp["ev_w_out"][0]),
        "od_w_in": f(inp["od_w_in"][0]),
        "od_sinks": f(inp["od_sinks"][0].reshape(1, 32)),
        "od_w_out": f(inp["od_w_out"][0]),
        "ln_mix_g": f(inp["ln_mix_g"].reshape(2, 16, 128)),
        "ln_mix_b": f(inp["ln_mix_b"].reshape(2, 16, 128)),
        "ffn_w_up": f(inp["ffn_w_up"]),
        "ffn_conv_w": f(inp["ffn_conv_w"].reshape(2, 3, 86, 128)),
        "ffn_conv_b": f(inp["ffn_conv_b"].reshape(2, 86, 128)),
        "ffn_w_down": f(inp["ffn_w_down"]),
        "ln_ffn_g": f(inp["ln_ffn_g"].reshape(2, 16, 128)),
        "ln_ffn_b": f(inp["ln_ffn_b"].reshape(2, 16, 128)),
    }
    return m


def run(inputs, mode="full", cores=8, trace=False):
    nc, es = build(mode)
    in_maps = [_prep_inputs(inputs, b) for b in range(cores)]
    res = run_bass_kernel_spmd(nc, in_maps, core_ids=list(range(cores)), trace=trace)
    es.close()
    return res


def kernel(**inputs):
    res = run(inputs, "full", 8)
    out = np.stack([np.asarray(r["out"]) for r in res.results], axis=0)
    return out.astype(np.float32)
```

```python
import math
import os
_SKIP = set(os.environ.get('K_SKIP', '').split(','))
from contextlib import ExitStack

import numpy as np
import concourse.bass as bass
import concourse.mybir as mybir
from concourse.bass_utils import run_bass_kernel_spmd

F32 = mybir.dt.float32
BF16 = mybir.dt.bfloat16
I32 = mybir.dt.int32
AF = mybir.ActivationFunctionType
ALU = mybir.AluOpType

S = 2048
D = 2048
DC = 16
T = 512
NT = 4
DFF = 5504
NJ = 43
ALPHA = (2.0 * 2) ** 0.25
LN_EPS = 1e-5
DMA_SCRATCH = 8192
SBUF_BASE = DMA_SCRATCH
SBUF_LIMIT = 16384 + 212800

ENGS = ("pe", "act", "dve", "pool", "sp")
SAME_SYNC = True


class Sched:
    def __init__(self, nc, es):
        self.nc = nc
        self.es = es
        self.streams = {e: [] for e in ENGS}
        self.cnt = {e: 0 for e in ENGS}
        self.waited = {e: {} for e in ENGS}
        self.lastw = {}
        self.readers = {}
        self.pend_r = {e: [] for e in ENGS}
        self.pend_w = {e: [] for e in ENGS}
        self.sems = {}
        self.dcount = {}
        self.ranges = {}
        self.alias = {}
        self.tcache = {}
        self.persist = set()
        for e in ENGS:
            self.sems["E_" + e] = es.enter_context(nc.semaphore("E_" + e))

    def sb(self, key, shape, dtype, off, persistent=False):
        nbytes = int(np.prod(shape[1:])) * (2 if dtype == BF16 else 4)
        assert off % 4 == 0 and off + nbytes <= SBUF_LIMIT, (key, off, nbytes)
        ck = (key, off, tuple(shape), str(dtype))
        if ck in self.tcache and self.ranges.get(key) == (off, off + nbytes):
            return self.tcache[ck]
        t = self.nc.alloc_sbuf_tensor_at(key, list(shape), dtype, offset=off)
        self.tcache[ck] = t
        assert key not in self.ranges, ("key re-registered at a different place", key)
        self.reg(key, off, off + nbytes)
        if persistent:
            self.persist.add(key)
        return t

    def phase_reset(self):
        self.barrier()
        self.lastw = {}
        self.readers = {}
        for k in list(self.ranges):
            if k not in self.persist:
                del self.ranges[k]
                del self.alias[k]
        for k in self.alias:
            self.alias[k] = [a for a in self.alias[k] if a in self.ranges]
        self.tcache = {ck: t for ck, t in self.tcache.items() if ck[0] in self.persist}

    def reg(self, key, lo, hi):
        self.ranges[key] = (lo, hi)
        al = []
        for k, (a, b) in self.ranges.items():
            if k != key and a < hi and lo < b:
                al.append(k)
                self.alias[k].append(key)
        self.alias[key] = al

    def _keys(self, k):
        return [k] + self.alias.get(k, [])

    def _wait(self, eng, tok):
        sem, val = tok
        if sem == "E_" + eng:
            if eng in ("pe", "sp") or not SAME_SYNC:
                return
        if self.waited[eng].get(sem, 0) >= val:
            return
        self.waited[eng][sem] = val
        self.streams[eng].append(("w", sem, val))

    def _deps(self, eng, reads, writes):
        toks = []
        for k0 in reads:
            for k in self._keys(k0):
                t = self.lastw.get(k)
                if t:
                    toks.append(t)
                if isinstance(k, str) and k.startswith("ps") and k[2:].isdigit():
                    toks.extend(r for r in self.readers.get(k, ()) if r[0] != "E_" + eng)
                for e2 in ENGS:
                    assert k not in self.pend_w[e2] or e2 == eng, ("pending write", k, e2, eng)
        for k0 in writes:
            for k in self._keys(k0):
                t = self.lastw.get(k)
                if t:
                    toks.append(t)
                toks.extend(self.readers.get(k, ()))
                for e2 in ENGS:
                    if e2 != eng:
                        assert k not in self.pend_r[e2] and k not in self.pend_w[e2], ("pending", k, e2, eng)
        for t in toks:
            self._wait(eng, t)

    def _commit(self, tok, reads, writes):
        for k in reads:
            self.readers.setdefault(k, []).append(tok)
        for k in writes:
            self.lastw[k] = tok
            self.readers[k] = []

    def op(self, eng, fn, reads=(), writes=(), inc=True):
        reads = list(reads)
        writes = list(writes)
        self._deps(eng, reads, writes)
        if inc:
            self.cnt[eng] += 1
            tok = ("E_" + eng, self.cnt[eng])
            self.streams[eng].append(("o", fn, "E_" + eng, 1))
            self._commit(tok, reads + self.pend_r[eng], writes + self.pend_w[eng])
            self.pend_r[eng] = []
            self.pend_w[eng] = []
        else:
            self.streams[eng].append(("o", fn, None, 0))
            self.pend_r[eng] += reads
            self.pend_w[eng] += writes

    def dma(self, q, sem, out, in_, reads=(), writes=(), **kw):
        reads = list(reads)
        writes = list(writes)
        if sem not in self.sems:
            self.sems[sem] = self.es.enter_context(self.nc.semaphore(sem))
            self.dcount[sem] = 0
        if self.dcount[sem]:
            self._wait(q, (sem, self.dcount[sem]))
        self._deps(q, reads, writes)
        self.dcount[sem] += 16
        tok = (sem, self.dcount[sem])
        self.streams[q].append(("o", lambda e, o=out, i=in_, kw=kw: e.dma_start(out=o, in_=i, **kw), sem, 16))
        self._commit(tok, reads, writes)

    def barrier(self):
        for e in ENGS:
            assert not self.pend_r[e] and not self.pend_w[e], ("barrier with pending ops", e)
        for e in ENGS:
            for e2 in ENGS:
                if self.cnt[e2] and (e2 != e or e in ("act", "dve", "pool")):
                    if self.waited[e].get("E_" + e2, 0) < self.cnt[e2]:
                        self.waited[e]["E_" + e2] = self.cnt[e2]
                        self.streams[e].append(("w", "E_" + e2, self.cnt[e2]))
            for s, v in self.dcount.items():
                if v:
                    self._wait(e, (s, v))

    def emit(self):
        nc = self.nc
        with nc.Block() as block:
            def mk(name):
                def body(e):
                    for it in self.streams[name]:
                        if it[0] == "w":
                            e.wait_ge(self.sems[it[1]], it[2])
                        else:
                            ins = it[1](e)
                            if it[2] is not None:
                                ins.then_inc(self.sems[it[2]], it[3])
                return body
            block.tensor(mk("pe"))
            block.scalar(mk("act"))
            block.vector(mk("dve"))
            block.gpsimd(mk("pool"))
            block.sync(mk("sp"))


def build(mode="full"):
    nc = bass.Bass("TRN2", target_bir_lowering=False, dynamic_dma_scratch_size=DMA_SCRATCH)
    es = ExitStack()
    sc = Sched(nc, es)

    def dram_in(name, shape, dt=F32):
        return nc.dram_tensor(name, list(shape), dt, kind="ExternalInput").ap()

    x_in = dram_in("x", [S, D])
    pos_in = dram_in("pos", [S // 128, 128], I32)
    ev_w_in = dram_in("ev_w_in", [D, 4104])
    ev_b_f = dram_in("ev_b_f", [8, 1])
    ev_lre = dram_in("ev_lre", [64, 64])
    ev_lim = dram_in("ev_lim", [64, 64])
    ev_lstep = dram_in("ev_lstep", [1, 64])
    ev_bre = dram_in("ev_bre", [64, 64, 16])
    ev_bim = dram_in("ev_bim", [64, 64, 16])
    ev_cre = dram_in("ev_cre", [64, 16, 64])
    ev_cim = dram_in("ev_cim", [64, 16, 64])
    ev_d = dram_in("ev_d", [8, 128])
    ev_w_glu = dram_in("ev_w_glu", [1024, 2048])
    ev_w_out = dram_in("ev_w_out", [D, D])
    od_w_in = dram_in("od_w_in", [D, 2560])
    od_sinks = dram_in("od_sinks", [1, 32])
    od_w_out = dram_in("od_w_out", [D, D])
    ln_mix_g = dram_in("ln_mix_g", [2, 16, 128])
    ln_mix_b = dram_in("ln_mix_b", [2, 16, 128])
    ffn_w_up = dram_in("ffn_w_up", [2, D, 2 * DFF])
    ffn_conv_w = dram_in("ffn_conv_w", [2, 3, 86, 128])
    ffn_conv_b = dram_in("ffn_conv_b", [2, 86, 128])
    ffn_w_down = dram_in("ffn_w_down", [2, DFF, D])
    ln_ffn_g = dram_in("ln_ffn_g", [2, 16, 128])
    ln_ffn_b = dram_in("ln_ffn_b", [2, 16, 128])
    out_d = nc.dram_tensor("out", [S, D], F32, kind="ExternalOutput").ap()
    RA = nc.dram_tensor("resA", [DC, 128, S], F32, kind="Internal").ap()
    RB = nc.dram_tensor("resB", [DC, 128, S], F32, kind="Internal").ap()

    wupb = [nc.dram_tensor("wupb%d" % l, [86, 128, DC, 128], BF16, kind="Internal").ap() for l in range(2)]
    wdnb = [nc.dram_tensor("wdnb%d" % l, [DC, 128, NJ, 128], BF16, kind="Internal").ap() for l in range(2)]
    woutb = [nc.dram_tensor("woutb%d" % l, [DC, 128, DC, 128], BF16, kind="Internal").ap() for l in range(2)]
    conv_jobs = []
    for l in range(2):
        wo_v = (ev_w_out if l == 0 else od_w_out).rearrange("(c p) f -> p c f", p=128)
        for dc in range(DC):
            conv_jobs.append((woutb[l][dc], wo_v[:, :, dc * 128:(dc + 1) * 128], ("woutb", l, dc)))
        wup_v = ffn_w_up[l].rearrange("(c p) f -> p c f", p=128)
        wdn_v = ffn_w_down[l].rearrange("(j p) d -> p j d", p=128)
        for j in range(86):
            conv_jobs.append((wupb[l][j], wup_v[:, :, j * 128:(j + 1) * 128], ("wupb", l, j)))
        for dc in range(DC):
            conv_jobs.append((wdnb[l][dc], wdn_v[:, :, dc * 128:(dc + 1) * 128], ("wdnb", l, dc)))
    conv_done = [0]
    NCV = 6

    def conv_pump(n):
        for _ in range(n):
            if conv_done[0] >= len(conv_jobs):
                return
            o_, i_, key = conv_jobs[conv_done[0]]
            sc.dma("pool", "cv%d" % (conv_done[0] % NCV), o_, i_, writes=[key])
            conv_done[0] += 1

    PS = [es.enter_context(nc.psum_tensor("ps%d" % i, [128, 512], F32)) for i in range(8)]

    def psk(i):
        return "ps%d" % i

    off = [SBUF_BASE]

    def alloc(key, shape, dt):
        nbytes = int(np.prod(shape[1:])) * (2 if dt == BF16 else 4)
        nbytes = (nbytes + 31) // 32 * 32
        t = sc.sb(key, shape, dt, off[0], persistent=True)
        off[0] += nbytes
        return t

    ident_f = alloc("ident_f", [128, 128], F32)
    ident_b = alloc("ident_b", [128, 128], BF16)
    ones_f = alloc("ones_f", [128, 128], F32)
    ones_b = alloc("ones_b", [128, 128], BF16)
    lnp = alloc("lnp", [128, 8, 16], F32)
    convp = alloc("convp", [128, 2, 4, 86], F32)
    CONST_END = off[0]
    XTB_OFF = CONST_END
    xTb = sc.sb("xTb_all", [128, DC, S], BF16, XTB_OFF, persistent=True)
    for tt in range(NT):
        sc.reg("xTb%d" % tt, SBUF_LIMIT + 1000 + tt, SBUF_LIMIT + 1000 + tt + 1)
        sc.persist.add("xTb%d" % tt)
    PH_OFF = XTB_OFF + DC * S * 2

    class Arena:
        def __init__(self, start):
            self.o = (start + 31) // 32 * 32

        def take(self, key, shape, dt, n=None):
            nbytes = int(np.prod(shape[1:])) * (2 if dt == BF16 else 4)
            nbytes = (nbytes + 31) // 32 * 32
            if n is None:
                t = sc.sb(key, shape, dt, self.o)
                self.o += nbytes
                return t
            ts = []
            for i in range(n):
                ts.append(sc.sb("%s%d" % (key, i), shape, dt, self.o))
                self.o += nbytes
            return ts

    def setup_consts():
        sc.op("pool", lambda e: e.memset(ones_f[:], 1.0), writes=["ones_f"])
        sc.op("pool", lambda e: e.memset(ident_f[:], 1.0), writes=["ident_f"])
        sc.op("pool", lambda e: e.affine_select(out=ident_f[:], in_=ident_f[:], pattern=[[-1, 128]],
                                                 compare_op=ALU.is_equal, fill=0.0, base=0,
                                                 channel_multiplier=1),
              reads=["ident_f"], writes=["ident_f"])
        sc.op("dve", lambda e: e.tensor_copy(out=ident_b[:], in_=ident_f[:]), reads=["ident_f"], writes=["ident_b"])
        sc.op("dve", lambda e: e.tensor_copy(out=ones_b[:], in_=ones_f[:]), reads=["ones_f"], writes=["ones_b"])
        stg = sc.sb("c_stg", [128, 128], F32, PH_OFF)
        jobs = []
        for l in range(2):
            for k, src in enumerate((ln_mix_g, ln_mix_b, ln_ffn_g, ln_ffn_b)):
                jobs.append((src[l], 16, lnp[:, l * 4 + k, :]))
            for k in range(3):
                jobs.append((ffn_conv_w[l, k], 86, convp[:, l, k, :]))
            jobs.append((ffn_conv_b[l], 86, convp[:, l, 3, :]))
        for n, (src, rows, dst) in enumerate(jobs):
            sc.dma("sp", "c_stg", stg[0:rows, :], src, writes=["c_stg"])
            sc.op("pe", lambda e, r=rows: e.transpose(PS[0][:, 0:r], stg[0:r, :], ident_f[0:r, 0:r]),
                  reads=["c_stg", "ident_f"], writes=[psk(0)])
            sc.op("dve", lambda e, r=rows, d=dst: e.tensor_copy(out=d, in_=PS[0][:, 0:r]),
                  reads=[psk(0)], writes=["consts"])

    def phase_input(x_src, Rout):
        A = Arena(PH_OFF)
        xin = A.take("xin", [128, D], F32, 2)
        st32 = A.take("st32", [128, DC, T], F32)
        for tt in range(NT):
            for tb in range(4):
                g = tt * 4 + tb
                xi = xin[g % 2]
                xk = "xin%d" % (g % 2)
                sc.dma("sp", xk, xi[:], x_src[g * 128:(g + 1) * 128, :], writes=[xk])
                for q in range(4):
                    bank = q % 2
                    for c4 in range(4):
                        dc = q * 4 + c4
                        sc.op("pe", lambda e, b=bank, c4=c4, dc=dc, xi=xi: e.transpose(
                            PS[b][:, c4 * 128:(c4 + 1) * 128], xi[:, dc * 128:(dc + 1) * 128], ident_f[:]),
                            reads=[xk, "ident_f"], writes=[psk(bank)], inc=(c4 == 3))
                    pv = PS[bank][:].rearrange("p (c t) -> p c t", c=4)
                    if 'act' not in _SKIP:
                      sc.op("act", lambda e, pv=pv, q=q, tb=tb: e.copy(
                        out=st32[:, q * 4:(q + 1) * 4, tb * 128:(tb + 1) * 128], in_=pv),
                        reads=[psk(bank)], writes=["st32"])
                    if 'dve' not in _SKIP:
                      sc.op("dve", lambda e, pv=pv, q=q, g=g: e.tensor_copy(
                        out=xTb[:, q * 4:(q + 1) * 4, g * 128:(g + 1) * 128], in_=pv),
                        reads=[psk(bank)], writes=["xTb%d" % tt])
            conv_pump(4)
            if 'st' not in _SKIP:
              sc.dma("sp", "st32", Rout[:, :, tt * T:(tt + 1) * T].rearrange("c p t -> p c t"), st32[:],
                   reads=["st32"], writes=[("R", id(Rout), tt)])

    def layer_norm(r32, rkey, gi, bi, tt, lo, Rout=None, nbuf=2):
        A = Arena(lo)
        sq = A.take("ln_sq", [128, T], BF16, nbuf)
        rb = A.take("ln_rb", [128, T], BF16, nbuf)
        mean = A.take("ln_mean", [128, T], F32)
        rstd = A.take("ln_rstd", [128, T], F32)
        for c in range(DC):
            s = sq[c % nbuf]
            sk = "ln_sq%d" % (c % nbuf)
            rb_ = rb[c % nbuf]
            rbk = "ln_rb%d" % (c % nbuf)
            sc.op("act", lambda e, s=s, c=c: e.activation(out=s[:], in_=r32[:, c, :], func=AF.Square),
                  reads=[rkey], writes=[sk])
            sc.op("dve", lambda e, rb_=rb_, c=c: e.tensor_copy(out=rb_[:], in_=r32[:, c, :]), reads=[rkey], writes=[rbk])
            sc.op("pe", lambda e, c=c, rb_=rb_: e.matmul(PS[6][:], lhsT=ones_b[:], rhs=rb_[:], start=(c == 0), stop=(c == DC - 1)),
                  reads=[rbk, "ones_b"], writes=[psk(6)], inc=True)
            sc.op("pe", lambda e, s=s, c=c: e.matmul(PS[7][:], lhsT=ones_b[:], rhs=s[:], start=(c == 0), stop=(c == DC - 1)),
                  reads=[sk, "ones_b"], writes=[psk(7)], inc=True)
        sc.op("act", lambda e: e.mul(out=mean[:], in_=PS[6][:], mul=1.0 / D), reads=[psk(6)], writes=["ln_mean"])
        sc.op("dve", lambda e: e.tensor_tensor(out=rstd[:], in0=mean[:], in1=mean[:], op=ALU.mult),
              reads=["ln_mean"], writes=["ln_rstd"])
        sc.op("dve", lambda e: e.scalar_tensor_tensor(out=rstd[:], in0=PS[7][:], scalar=1.0 / D, in1=rstd[:],
                                                      op0=ALU.mult, op1=ALU.subtract),
              reads=[psk(7), "ln_rstd"], writes=["ln_rstd"])
        sc.op("dve", lambda e: e.tensor_scalar(out=rstd[:], in0=rstd[:], scalar1=LN_EPS, scalar2=None, op0=ALU.add),
              reads=["ln_rstd"], writes=["ln_rstd"])
        sc.op("act", lambda e: e.activation(out=rstd[:], in_=rstd[:], func=AF.Sqrt), reads=["ln_rstd"], writes=["ln_rstd"])
        sc.op("dve", lambda e: e.reciprocal(out=rstd[:], in_=rstd[:]), reads=["ln_rstd"], writes=["ln_rstd"])

        def norm_chunk(c):
            sc.op("dve", lambda e: e.tensor_tensor(out=r32[:, c, :], in0=r32[:, c, :], in1=mean[:], op=ALU.subtract),
                  reads=[rkey, "ln_mean"], writes=[rkey])
            sc.op("dve", lambda e: e.tensor_tensor(out=r32[:, c, :], in0=r32[:, c, :], in1=rstd[:], op=ALU.mult),
                  reads=[rkey, "ln_rstd"], writes=[rkey])
            sc.op("act", lambda e: e.activation(out=r32[:, c, :], in_=r32[:, c, :], func=AF.Identity,
                                                bias=lnp[:, bi, c:c + 1], scale=lnp[:, gi, c:c + 1]),
                  reads=[rkey, "consts"], writes=[rkey])
            sc.op("pool", lambda e: e.tensor_copy(out=xTb[:, c, tt * T:(tt + 1) * T], in_=r32[:, c, :]),
                  reads=[rkey], writes=["xTb%d" % tt])

        def store():
            sc.dma("sp", "ln_out_" + str(rkey), Rout[:, :, tt * T:(tt + 1) * T].rearrange("c p t -> p c t"), r32[:],
                   reads=[rkey], writes=[("R", id(Rout), tt)])

        return [(lambda c=c: norm_chunk(c)) for c in range(DC)], store

    def phase_ffn(l, Rin, Rout, final):
        conv_pump(max(0, (l + 1) * 118 - conv_done[0]))
        A = Arena(PH_OFF)
        aT = A.take("aT", [128, NJ, T], BF16)
        r32 = A.take("r32", [128, DC, T], F32)
        NW = 3
        wg = A.take("wg", [128, DC, 128], BF16, NW)
        wv = A.take("wv", [128, DC, 128], BF16, NW)
        ND = 2
        wd = A.take("wd", [128, NJ, 128], BF16, ND)
        carry = A.take("carry", [128, 2, NJ, 2], F32, 2)
        gb = A.take("gb", [128, T], F32, 2)
        vb = A.take("vb", [128, T], F32, 2)
        lo = A.o
        wup = ffn_w_up[l].rearrange("(c p) f -> p c f", p=128)
        wdn = ffn_w_down[l].rearrange("(j p) d -> p j d", p=128)

        def cw(k, j):
            return convp[:, l, k, j:j + 1]

        pending_store = [None]
        pending_norm = []
        up_jobs = [(tt, j) for tt in range(NT) for j in range(NJ)]
        dn_jobs = [(tt, dc) for tt in range(NT) for dc in range(DC)]
        up_issued = [0]
        dn_issued = [0]

        def issue_up(n):
            while up_issued[0] < min(n, len(up_jobs)):
                i = up_issued[0]
                _, j = up_jobs[i]
                ws = i % NW
                sc.dma("sp", "wg%d" % ws, wg[ws][:], wupb[l][j], reads=[("wupb", l, j)], writes=["wg%d" % ws])
                sc.dma("sp", "wv%d" % ws, wv[ws][:], wupb[l][NJ + j], reads=[("wupb", l, NJ + j)], writes=["wv%d" % ws])
                up_issued[0] += 1

        def issue_dn(n):
            while dn_issued[0] < min(n, len(dn_jobs)):
                i = dn_issued[0]
                _, dc = dn_jobs[i]
                ds = i % ND
                sc.dma("sp", "wd%d" % ds, wd[ds][:], wdnb[l][dc], reads=[("wdnb", l, dc)], writes=["wd%d" % ds])
                dn_issued[0] += 1

        for tt in range(NT):
            tsl = slice(tt * T, (tt + 1) * T)
            xk = "xTb%d" % tt
            cin = carry[(tt + 1) % 2]
            cout = carry[tt % 2]
            cink = "carry%d" % ((tt + 1) % 2)
            coutk = "carry%d" % (tt % 2)
            for j in range(NJ):
                ui = tt * NJ + j
                issue_up(ui + NW - 0 if ui == 0 else ui + NW)
                if j == 30:
                    if pending_store[0] is not None:
                        pending_store[0]()
                        pending_store[0] = None
                    sc.dma("sp", "r32", r32[:], Rin[:, :, tsl].rearrange("c p t -> p c t"),
                           reads=[("R", id(Rin), tt)], writes=["r32"])
                if j == 8:
                    issue_dn(tt * DC + 1)
                if j == 16:
                    issue_dn(tt * DC + 2)
                ws = ui % NW
                pg = (j % 2) * 2
                for half, (w_, wk, gv, cofs) in enumerate(((wg[ws], "wg%d" % ws, gb[j % 2], 0),
                                                           (wv[ws], "wv%d" % ws, vb[j % 2], NJ))):
                    bank = pg + half
                    P = PS[bank]
                    for c in range(DC):
                        rhs = xTb[:, c, tsl]
                        sc.op("pe", lambda e, P=P, w_=w_, c=c, rhs=rhs: e.matmul(P[:], lhsT=w_[:, c, :], rhs=rhs,
                                                                                  start=(c == 0), stop=(c == DC - 1)),
                              reads=[wk, xk], writes=[psk(bank)], inc=(c == DC - 1))
                    gk = ("gb%d" if half == 0 else "vb%d") % (j % 2)
                    jj = j + cofs
                    sc.op("act", lambda e, P=P, gv=gv, jj=jj: e.activation(out=gv[:], in_=P[:], func=AF.Identity,
                                                                           bias=cw(3, jj), scale=cw(2, jj)),
                          reads=[psk(bank), "consts"], writes=[gk])
                    sc.op("dve", lambda e, P=P, gv=gv, jj=jj: e.scalar_tensor_tensor(
                        out=gv[:, 1:T], in0=P[:, 0:T - 1], scalar=cw(1, jj), in1=gv[:, 1:T], op0=ALU.mult, op1=ALU.add),
                        reads=[psk(bank), gk, "consts"], writes=[gk])
                    sc.op("dve", lambda e, P=P, gv=gv, jj=jj: e.scalar_tensor_tensor(
                        out=gv[:, 2:T], in0=P[:, 0:T - 2], scalar=cw(0, jj), in1=gv[:, 2:T], op0=ALU.mult, op1=ALU.add),
                        reads=[psk(bank), gk, "consts"], writes=[gk])
                    if tt > 0:
                        sc.op("dve", lambda e, gv=gv, jj=jj, half=half, j=j, cin=cin: e.scalar_tensor_tensor(
                            out=gv[:, 0:1], in0=cin[:, half, j, 1:2], scalar=cw(1, jj), in1=gv[:, 0:1],
                            op0=ALU.mult, op1=ALU.add), reads=[cink, gk, "consts"], writes=[gk])
                        sc.op("dve", lambda e, gv=gv, jj=jj, half=half, j=j, cin=cin: e.scalar_tensor_tensor(
                            out=gv[:, 0:2], in0=cin[:, half, j, 0:2], scalar=cw(0, jj), in1=gv[:, 0:2],
                            op0=ALU.mult, op1=ALU.add), reads=[cink, gk, "consts"], writes=[gk])
                    if tt < NT - 1:
                        sc.op("act", lambda e, P=P, half=half, j=j, cout=cout: e.copy(out=cout[:, half, j, :], in_=P[:, T - 2:T]),
                              reads=[psk(bank)], writes=[coutk])
                g_ = gb[j % 2]
                v_ = vb[j % 2]
                sc.op("act", lambda e, g_=g_: e.activation(out=g_[:], in_=g_[:], func=AF.Silu),
                      reads=["gb%d" % (j % 2)], writes=["gb%d" % (j % 2)])
                sc.op("pool", lambda e, g_=g_, v_=v_, j=j: e.tensor_tensor(out=aT[:, j, :], in0=g_[:], in1=v_[:], op=ALU.mult),
                      reads=["gb%d" % (j % 2), "vb%d" % (j % 2)], writes=["aT"])
                if pending_norm and j >= 2:
                    pending_norm.pop(0)()
            for dc in range(DC):
                di = tt * DC + dc
                issue_dn(di + ND)
                ds = di % ND
                bank = 4 + dc % 2
                P = PS[bank]
                for j in range(NJ):
                    sc.op("pe", lambda e, P=P, j=j, ds=ds: e.matmul(P[:], lhsT=wd[ds][:, j, :], rhs=aT[:, j, :],
                                                                    start=(j == 0), stop=(j == NJ - 1)),
                          reads=["wd%d" % ds, "aT"], writes=[psk(bank)], inc=(j == NJ - 1))
                sc.op("dve", lambda e, P=P, dc=dc: e.scalar_tensor_tensor(
                    out=r32[:, dc, :], in0=r32[:, dc, :], scalar=ALPHA, in1=P[:], op0=ALU.mult, op1=ALU.add),
                    reads=[psk(bank), "r32"], writes=["r32"])
            pending_norm, pending_store[0] = layer_norm(r32, "r32", l * 4 + 2, l * 4 + 3, tt, lo, Rout=Rout)
        for fn in pending_norm:
            fn()
        pending_store[0]()

    class Prefetch:
        def __init__(self, n, nslots, issue):
            self.n, self.nslots, self.issue, self.issued = n, nslots, issue, 0

        def ensure(self, i):
            while self.issued < min(self.n, i + self.nslots):
                self.issue(self.issued, self.issued % self.nslots)
                self.issued += 1

    MIX_OFF = PH_OFF
    PH2_OFF = MIX_OFF + DC * S * 2

    _mix = []

    def get_mixT():
        if not _mix:
            _mix.append(nc.alloc_sbuf_tensor_at("mixT_all", [128, DC, S], BF16, offset=MIX_OFF))
        t = _mix[0]
        if "mixlo" not in sc.ranges:
            sc.reg("mixlo", MIX_OFF, MIX_OFF + 8 * S * 2)
            sc.reg("mixhi", MIX_OFF + 8 * S * 2, MIX_OFF + 16 * S * 2)
        return t

    def phase_outproj(l, w_out, Rin, Rout):
        mixT = get_mixT()
        A = Arena(PH2_OFF)
        conv_pump(max(0, l * 118 + 16 - conv_done[0]))
        r32s = A.take("r32_", [128, DC, T], F32, 2)
        NW = 3
        wo = A.take("wo", [128, DC, 128], BF16, NW)
        lo = A.o
        stores = []
        pend_n = []
        jobs = [(tt, dc) for tt in range(NT) for dc in range(DC)]
        pf = Prefetch(len(jobs), NW, lambda i, s_: sc.dma(
            "sp", "wo%d" % s_, wo[s_][:], woutb[l][jobs[i][1]], reads=[("woutb", l, jobs[i][1])], writes=["wo%d" % s_]))
        for tt in range(NT):
            tsl = slice(tt * T, (tt + 1) * T)
            r32 = r32s[tt % 2]
            rk = "r32_%d" % (tt % 2)
            if tt >= 2:
                stores[tt - 2]()
            sc.dma("sp", rk, r32[:], Rin[:, :, tsl].rearrange("c p t -> p c t"),
                   reads=[("R", id(Rin), tt)], writes=[rk])
            for dc in range(DC):
                i = tt * DC + dc
                pf.ensure(i)
                ws = i % NW
                bank = 4 + dc % 2
                P = PS[bank]
                for c in range(DC):
                    rhs = mixT[:, c, tsl]
                    sc.op("pe", lambda e, P=P, ws=ws, c=c, rhs=rhs: e.matmul(P[:], lhsT=wo[ws][:, c, :], rhs=rhs,
                                                                          start=(c == 0), stop=(c == DC - 1)),
                          reads=["wo%d" % ws, "mixlo" if c < 8 else "mixhi"], writes=[psk(bank)], inc=(c == DC - 1))
                sc.op("dve", lambda e, P=P, dc=dc, r32=r32: e.scalar_tensor_tensor(
                    out=r32[:, dc, :], in0=r32[:, dc, :], scalar=ALPHA, in1=P[:], op0=ALU.mult, op1=ALU.add),
                    reads=[psk(bank), rk], writes=[rk])
                if pend_n:
                    pend_n.pop(0)()
            nrm, st_ = layer_norm(r32, rk, l * 4 + 0, l * 4 + 1, tt, lo, Rout=Rout, nbuf=1)
            pend_n.extend(nrm)
            stores.append(st_)
        for fn in pend_n:
            fn()
        stores[NT - 2]()
        stores[NT - 1]()

    def phase_fox():
        mixT = get_mixT()
        A = Arena(PH2_OFF)
        qT = A.take("qT", [128, S], BF16, 2)
        kT = A.take("kT", [128, S], BF16, 2)
        Vt = A.take("Vt", [128, 16, 128], BF16, 2)
        wq = A.take("wq", [128, DC, 128], BF16, 2)
        wk = A.take("wk", [128, DC, 128], BF16, 2)
        wv = A.take("wv", [128, DC, 128], BF16, 2)
        PT = A.take("PT", [128, T], BF16, 2)
        negc = A.take("negc", [128, S], F32)
        cneg = A.take("cneg", [128, S], F32)
        negcT = A.take("negcT", [128, 16, 8], F32)
        recips = A.take("recip", [128, T], F32, 2)
        sel = A.take("sel", [128, 8, 128], F32)
        sel3 = A.take("sel3", [128, 8, 128], BF16)
        cneg3 = A.take("cneg3", [128, S], BF16)
        maskb = A.take("maskb", [128, 128], BF16)
        wf = A.take("wf", [128, DC, 8], BF16)
        bfc = A.take("bfc", [128, 2], F32)
        ones_row = sc.sb("ones_row", [128, S], BF16, sc.ranges["qT0"][0])
        c_hi = sc.sb("c_hi", [128, S], BF16, sc.ranges["qT1"][0])
        c_mid = sc.sb("c_mid", [128, S], BF16, sc.ranges["kT1"][0])
        c_lo = sc.sb("c_lo", [128, S], BF16, sc.ranges["kT0"][0])
        win = ev_w_in.rearrange("(c p) f -> p c f", p=128)
        SC = 1.0 / math.sqrt(128.0)

        sc.op("pool", lambda e: e.memset(sel[:], 1.0), writes=["sel"])
        sc.op("pool", lambda e: e.affine_select(out=sel[:], in_=sel[:], pattern=[[-1, 8], [0, 128]],
                                                 compare_op=ALU.is_equal, fill=0.0, base=0, channel_multiplier=1),
              reads=["sel"], writes=["sel"])
        sc.op("pool", lambda e: e.memset(maskb[:], 0.0), writes=["maskb"])
        sc.op("pool", lambda e: e.affine_select(out=maskb[:], in_=maskb[:], pattern=[[1, 128]],
                                                 compare_op=ALU.is_ge, fill=-30000.0, base=0, channel_multiplier=-1),
              reads=["maskb"], writes=["maskb"])
        sc.op("pool", lambda e: e.memset(cneg[:], 0.0), writes=["cneg"])
        sc.op("pool", lambda e: e.memset(ones_row[0:8, :], 1.0), writes=["ones_row"])
        sc.dma("pool", "wf", wf[:], win[:, :, 3072:3080], writes=["wf"])
        sc.dma("sp", "bfc", bfc[0:8, 0:1], ev_b_f, writes=["bfc"])
        sc.op("dve", lambda e: e.tensor_scalar(out=bfc[0:8, 1:2], in0=bfc[0:8, 0:1], scalar1=-1.0, scalar2=None, op0=ALU.mult),
              reads=["bfc"], writes=["bfc"])
        for tt in range(NT):
            tsl = slice(tt * T, (tt + 1) * T)
            for c in range(DC):
                rhs = xTb[:, c, tsl]
                sc.op("pe", lambda e, c=c, rhs=rhs: e.matmul(PS[6][0:8, :], lhsT=wf[:, c, :], rhs=rhs,
                                                              start=(c == 0), stop=(c == DC - 1)),
                      reads=["wf", "xTb%d" % tt], writes=[psk(6)], inc=(c == DC - 1))
            sc.op("act", lambda e, tsl=tsl: e.activation(out=negc[0:8, tsl], in_=PS[6][0:8, :], func=AF.Exp,
                                                          bias=bfc[0:8, 1:2], scale=-1.0),
                  reads=[psk(6), "bfc"], writes=["negc"])
            sc.op("act", lambda e, tsl=tsl: e.activation(out=negc[0:8, tsl], in_=negc[0:8, tsl], func=AF.Ln, bias=1.0, scale=1.0),
                  reads=["negc"], writes=["negc"])
        sc.op("dve", lambda e: e.tensor_tensor_scan(out=negc[0:8, :], data0=ones_row[0:8, :], data1=negc[0:8, :],
                                                    initial=0.0, op0=ALU.mult, op1=ALU.add),
              reads=["negc", "ones_row"], writes=["negc"])
        sc.op("act", lambda e: e.mul(out=cneg[0:8, :], in_=negc[0:8, :], mul=-1.0), reads=["negc"], writes=["cneg"])
        for tb in range(16):
            sc.op("pe", lambda e, tb=tb: e.transpose(PS[7][:, tb * 8:(tb + 1) * 8], negc[0:8, tb * 128:(tb + 1) * 128],
                                                     ident_f[0:8, 0:8]),
                  reads=["negc", "ident_f"], writes=[psk(7)], inc=(tb == 15))
        sc.op("dve", lambda e: e.tensor_copy(out=negcT[:].rearrange("p a b -> p (a b)"), in_=PS[7][:, 0:128]),
              reads=[psk(7)], writes=["negcT"])

        sc.op("pool", lambda e: e.memset(cneg3[:], 0.0), writes=["cneg3"])
        sc.op("dve", lambda e: e.tensor_copy(out=sel3[:], in_=sel[:]), reads=["sel"], writes=["sel3"])
        sc.op("act", lambda e: e.copy(out=c_hi[0:8, :], in_=cneg[0:8, :]), reads=["cneg"], writes=["c_hi"])
        sc.op("dve", lambda e: e.tensor_tensor(out=negc[0:8, :], in0=cneg[0:8, :], in1=c_hi[0:8, :], op=ALU.subtract),
              reads=["cneg", "c_hi", "negc"], writes=["negc"])
        sc.op("act", lambda e: e.copy(out=c_mid[0:8, :], in_=negc[0:8, :]), reads=["negc"], writes=["c_mid"])
        sc.op("dve", lambda e: e.tensor_tensor(out=negc[0:8, :], in0=negc[0:8, :], in1=c_mid[0:8, :], op=ALU.subtract),
              reads=["negc", "c_mid"], writes=["negc"])
        sc.op("act", lambda e: e.copy(out=c_lo[0:8, :], in_=negc[0:8, :]), reads=["negc"], writes=["c_lo"])
        for r_, (src_, sk_) in enumerate(((c_hi, "c_hi"), (c_mid, "c_mid"), (c_lo, "c_lo"))):
            sc.dma("sp", "c3ld", cneg3[8 * r_:8 * r_ + 8, :], src_[0:8, :], reads=[sk_], writes=["cneg3"])
        for r_ in (1, 2):
            sc.dma("sp", "c3ld", sel3[8 * r_:8 * r_ + 8], sel3[0:8], reads=["sel3"], writes=["sel3"])

        def load_head(h):
            s_ = h % 2
            sc.dma("pool", "wq%d" % s_, wq[s_][:], win[:, :, h * 128:(h + 1) * 128], writes=["wq%d" % s_])
            sc.dma("pool", "wk%d" % s_, wk[s_][:], win[:, :, 1024 + h * 128:1024 + (h + 1) * 128], writes=["wk%d" % s_])
            sc.dma("pool", "wv%d" % s_, wv[s_][:], win[:, :, 2048 + h * 128:2048 + (h + 1) * 128], writes=["wv%d" % s_])

        load_head(0)
        pcnt = [0]
        qcnt = [0]
        for h in range(8):
            s_ = h % 2
            if h + 1 < 8:
                load_head(h + 1)
            qk, kk, vk = "qT%d" % s_, "kT%d" % s_, "Vt%d" % s_
            for tt in range(NT):
                tsl = slice(tt * T, (tt + 1) * T)
                for which, (w_, wkey) in enumerate(((wq[s_], "wq%d" % s_), (wk[s_], "wk%d" % s_))):
                    bank = which
                    for c in range(DC):
                        rhs = xTb[:, c, tsl]
                        sc.op("pe", lambda e, bank=bank, w_=w_, c=c, rhs=rhs: e.matmul(
                            PS[bank][:], lhsT=w_[:, c, :], rhs=rhs, start=(c == 0), stop=(c == DC - 1)),
                            reads=[wkey, "xTb%d" % tt], writes=[psk(bank)], inc=(c == DC - 1))
                    if which == 0:
                        sc.op("act", lambda e, tsl=tsl, s_=s_: e.activation(out=qT[s_][:, tsl], in_=PS[0][:], func=AF.Copy, scale=SC),
                              reads=[psk(0)], writes=[qk])
                    else:
                        sc.op("dve", lambda e, tsl=tsl, s_=s_: e.tensor_copy(out=kT[s_][:, tsl], in_=PS[1][:]),
                              reads=[psk(1)], writes=[kk])
            for t4 in range(4):
                bank = t4 % 2
                for q4 in range(4):
                    tb = t4 * 4 + q4
                    for c in range(DC):
                        lhsT = xTb[:, c, tb * 128:(tb + 1) * 128]
                        sc.op("pe", lambda e, bank=bank, q4=q4, c=c, lhsT=lhsT, s_=s_: e.matmul(
                            PS[bank][:, q4 * 128:(q4 + 1) * 128], lhsT=lhsT, rhs=wv[s_][:, c, :],
                            start=(c == 0), stop=(c == DC - 1)),
                            reads=["wv%d" % s_, "xTb%d" % (tb // 4)], writes=[psk(bank)], inc=(c == DC - 1))
                pv = PS[bank][:].rearrange("p (a b) -> p a b", a=4)
                if t4 % 2 == 0:
                    sc.op("act", lambda e, pv=pv, t4=t4, s_=s_: e.copy(out=Vt[s_][:, t4 * 4:(t4 + 1) * 4, :], in_=pv),
                          reads=[psk(bank)], writes=[vk])
                else:
                    sc.op("dve", lambda e, pv=pv, t4=t4, s_=s_: e.tensor_copy(out=Vt[s_][:, t4 * 4:(t4 + 1) * 4, :], in_=pv),
                          reads=[psk(bank)], writes=[vk])
            for Qi in range(4):
                conv_pump(3)
                ob = 4 + 2 * (qcnt[0] % 2)
                lb = ob + 1
                recip = recips[qcnt[0] % 2]
                rck_ = "recip%d" % (qcnt[0] % 2)
                qcnt[0] += 1
                nblk = 4 * Qi + 4

                def fox_scores(j, Qi=Qi, ob=ob, lb=lb):
                    n0 = max(0, j * 128 - Qi * T)
                    diag = j * 128 >= Qi * T
                    q0 = Qi * T + n0
                    q1 = (Qi + 1) * T
                    bank = 2 + pcnt[0] % 2
                    ps_ = pcnt[0] % 2
                    pcnt[0] += 1
                    P = PS[bank]
                    sc.op("pe", lambda e, P=P, n0=n0, j=j, q0=q0, q1=q1, s_=s_: e.matmul(
                        P[:, n0:T], lhsT=kT[s_][:, j * 128:(j + 1) * 128], rhs=qT[s_][:, q0:q1], start=True, stop=False),
                        reads=[kk, qk], writes=[psk(bank)], inc=False)
                    sc.op("pe", lambda e, P=P, n0=n0, q0=q0, q1=q1, h=h, diag=diag: e.matmul(
                        P[:, n0:T], lhsT=sel3[:, h, :], rhs=cneg3[:, q0:q1], start=False, stop=(not diag)),
                        reads=["sel3", "cneg3"], writes=[psk(bank)], inc=(not diag))
                    if diag:
                        sc.op("pe", lambda e, P=P, n0=n0: e.matmul(
                            P[:, n0:n0 + 128], lhsT=ident_b[:], rhs=maskb[:], start=False, stop=True),
                            reads=["ident_b", "maskb"], writes=[psk(bank)], inc=True)
                    ptk = "PT%d" % ps_
                    sc.op("act", lambda e, P=P, n0=n0, j=j, h=h, ps_=ps_: e.activation(
                        out=PT[ps_][:, n0:T], in_=P[:, n0:T], func=AF.Exp, bias=negcT[:, j, h:h + 1], scale=1.0),
                        reads=[psk(bank), "negcT"], writes=[ptk])
                    return (n0, ps_, ptk)

                def fox_pv(j, st_, ob=ob, lb=lb, nblk=nblk):
                    n0, ps_, ptk = st_
                    last = (j == nblk - 1)
                    sc.op("pe", lambda e, n0=n0, j=j, ps_=ps_, s_=s_, last=last, ob=ob: e.matmul(
                        PS[ob][:, n0:T], lhsT=Vt[s_][:, j, :], rhs=PT[ps_][:, n0:T], start=(j == 0), stop=last),
                        reads=[vk, ptk], writes=[psk(ob)], inc=False)
                    sc.op("pe", lambda e, n0=n0, j=j, ps_=ps_, last=last, lb=lb: e.matmul(
                        PS[lb][:, n0:T], lhsT=ones_b[:], rhs=PT[ps_][:, n0:T], start=(j == 0), stop=last),
                        reads=["ones_b", ptk], writes=[psk(lb)], inc=True)

                st_prev = fox_scores(0)
                for j in range(nblk):
                    st_next = fox_scores(j + 1) if j + 1 < nblk else None
                    fox_pv(j, st_prev)
                    st_prev = st_next
                sc.op("dve", lambda e, lb=lb, recip=recip: e.reciprocal(out=recip[:], in_=PS[lb][:]), reads=[psk(lb)], writes=[rck_])
                sc.op("dve", lambda e, h=h, Qi=Qi, ob=ob, recip=recip: e.tensor_tensor(
                    out=mixT[:, h, Qi * T:(Qi + 1) * T], in0=PS[ob][:], in1=recip[:], op=ALU.mult),
                    reads=[psk(ob), rck_], writes=["mixlo"])
        wu = wq
        pf = Prefetch(8, 2, lambda i, s_: sc.dma("pool", "wq%d" % s_, wu[s_][:], win[:, :, 3080 + i * 128:3080 + (i + 1) * 128],
                                                 writes=["wq%d" % s_]))
        for ch in range(8):
            pf.ensure(ch)
            s_ = ch % 2
            for tt in range(NT):
                tsl = slice(tt * T, (tt + 1) * T)
                bank = tt % 2
                for c in range(DC):
                    rhs = xTb[:, c, tsl]
                    sc.op("pe", lambda e, bank=bank, c=c, rhs=rhs, s_=s_: e.matmul(
                        PS[bank][:], lhsT=wu[s_][:, c, :], rhs=rhs, start=(c == 0), stop=(c == DC - 1)),
                        reads=["wq%d" % s_, "xTb%d" % tt], writes=[psk(bank)], inc=(c == DC - 1))
                if tt % 2 == 0:
                    sc.op("act", lambda e, bank=bank, ch=ch, tsl=tsl: e.copy(out=mixT[:, 8 + ch, tsl], in_=PS[bank][:]),
                          reads=[psk(bank)], writes=["mixhi"])
                else:
                    sc.op("dve", lambda e, bank=bank, ch=ch, tsl=tsl: e.tensor_copy(out=mixT[:, 8 + ch, tsl], in_=PS[bank][:]),
                          reads=[psk(bank)], writes=["mixhi"])

    Wd = nc.dram_tensor("s5w", [8, 128, 16, 128], BF16, kind="Internal").ap()
    TWO_PI = 2.0 * math.pi

    def bk_level(k, l, first, xr_, xi_, xrk, xik, i, AQ_re, AQ_im, AQ_in, x2=None):
        if first >= S:
            return []
        src = slice(first - k, S - k, 2 * k)
        dst = slice(first, S, 2 * k)
        ar = AQ_re[:, i, l:l + 1]
        ai = AQ_im[:, i, l:l + 1]
        an = AQ_in[:, i, l:l + 1]
        return [
            (lambda e: e.scalar_tensor_tensor(out=xr_[:, dst], in0=xi_[:, src], scalar=an, in1=xr_[:, dst],
                                              op0=ALU.mult, op1=ALU.add), [xrk, xik, "AQ_in"], [xrk]),
            (lambda e: e.scalar_tensor_tensor(out=xi_[:, dst], in0=xr_[:, src], scalar=ai, in1=xi_[:, dst],
                                              op0=ALU.mult, op1=ALU.add), [xik, xrk, "AQ_im"], [xik]),
            (lambda e: e.scalar_tensor_tensor(out=x2[:, :, dst], in0=x2[:, :, src], scalar=ar, in1=x2[:, :, dst],
                                              op0=ALU.mult, op1=ALU.add), [xrk, xik, "AQ_re"], [xrk, xik]),
        ]

    def phase_s5():
        mixT = get_mixT()
        AP_ = Arena(PH2_OFF)
        yg = AP_.take("yg", [128, 8, S], BF16)
        wslot = AP_.take("wslot", [128, 16, 128], BF16, 2)
        AQ_re = AP_.take("AQ_re", [128, 32, 11], F32)
        AQ_im = AP_.take("AQ_im", [128, 32, 11], F32)
        AQ_in = AP_.take("AQ_in", [128, 32, 11], F32)
        Dcol = AP_.take("Dcol", [128, 8], F32)
        y32 = AP_.take("y32", [128, T], F32, 2)
        gt = AP_.take("gt", [128, T], F32, 2)
        wz1 = AP_.take("wz1", [128, 8, 128], BF16, 2)
        wz2 = AP_.take("wz2", [128, 8, 128], BF16, 2)
        sig = AP_.take("sig", [128, T], F32, 2)
        AX = Arena(XTB_OFF)
        lamraw = AX.take("lamraw", [128, 2, 128], F32)
        names = ["lr", "li", "dt", "lrd", "lid", "mag", "kf", "rr", "rc", "m1", "sn", "cs", "are", "aim",
                 "den", "xr", "gre", "gim", "t1", "t2"]
        sm = {n: AX.take("s5_" + n, [128, 64], F32) for n in names}
        ki = AX.take("s5_ki", [128, 64], I32)
        P_re = AX.take("P_re", [128, 64, 11], F32)
        P_im = AX.take("P_im", [128, 64, 11], F32)
        b_re = AX.take("b_re", [128, 64, 16], F32)
        b_im = AX.take("b_im", [128, 64, 16], F32)
        bb_re = AX.take("bb_re", [128, 64, 16], F32)
        bb_im = AX.take("bb_im", [128, 64, 16], F32)
        btmp = AX.take("btmp", [128, 64, 16], F32)
        CT_re = AX.take("CT_re", [128, 8, 128], F32)
        CT_im = AX.take("CT_im", [128, 8, 128], F32)
        MC = AX.take("MC", [128, 4, 128], F32)
        MB = AX.take("MB", [128, 4, 128], F32)
        dstg = AX.take("dstg", [128, 128], F32)
        stage = AX.take("wstage", [128, 16, 128], BF16, 2)

        def dve(fn, reads, writes):
            sc.op("dve", fn, reads=reads, writes=writes)

        def tt_(out, a, b, op, reads, writes):
            dve(lambda e: e.tensor_tensor(out=out, in0=a, in1=b, op=op), reads, writes)

        K = lambda n: "s5_" + n
        for half in range(2):
            sc.dma("sp", "s5ld", lamraw[0:64, 0, half * 64:(half + 1) * 64], ev_lre, writes=["lamraw"])
            sc.dma("sp", "s5ld", lamraw[0:64, 1, half * 64:(half + 1) * 64], ev_lim, writes=["lamraw"])
            sc.dma("sp", "s5ld", b_re[half * 64:(half + 1) * 64], ev_bre.rearrange("g p c -> p g c"), writes=["b_re"])
            sc.dma("sp", "s5ld", b_im[half * 64:(half + 1) * 64], ev_bim.rearrange("g p c -> p g c"), writes=["b_im"])
            sc.dma("sp", "s5ld", CT_re[:, :, half * 64:(half + 1) * 64], ev_cre.rearrange("(j a) c p -> (a c) j p", a=8),
                   writes=["CT_re"])
            sc.dma("sp", "s5ld", CT_im[:, :, half * 64:(half + 1) * 64], ev_cim.rearrange("(j a) c p -> (a c) j p", a=8),
                   writes=["CT_im"])
        sc.dma("sp", "s5ld", sm["dt"][:], ev_lstep.broadcast_to([128, 64]), writes=[K("dt")])
        sc.dma("sp", "s5ld", dstg[0:8, :], ev_d, writes=["dstg"])
        for w, n in ((0, "lr"), (1, "li")):
            sc.op("pe", lambda e, w=w: e.transpose(PS[0][:, 0:64], lamraw[0:64, w, :], ident_f[0:64, 0:64]),
                  reads=["lamraw", "ident_f"], writes=[psk(0)])
            dve(lambda e, n=n: e.tensor_copy(out=sm[n][:], in_=PS[0][:, 0:64]), [psk(0)], [K(n)])
        sc.op("pe", lambda e: e.transpose(PS[0][:, 0:8], dstg[0:8, :], ident_f[0:8, 0:8]),
              reads=["dstg", "ident_f"], writes=[psk(0)])
        dve(lambda e: e.tensor_copy(out=Dcol[:], in_=PS[0][:, 0:8]), [psk(0)], ["Dcol"])
        sc.op("act", lambda e: e.activation(out=sm["dt"][:], in_=sm["dt"][:], func=AF.Exp), reads=[K("dt")], writes=[K("dt")])
        tt_(sm["lrd"][:], sm["lr"][:], sm["dt"][:], ALU.mult, [K("lr"), K("dt")], [K("lrd")])
        tt_(sm["lid"][:], sm["li"][:], sm["dt"][:], ALU.mult, [K("li"), K("dt")], [K("lid")])
        sc.op("act", lambda e: e.activation(out=sm["mag"][:], in_=sm["lrd"][:], func=AF.Exp), reads=[K("lrd")], writes=[K("mag")])
        dve(lambda e: e.tensor_scalar(out=sm["kf"][:], in0=sm["lid"][:], scalar1=1.0 / TWO_PI, scalar2=0.5,
                                      op0=ALU.mult, op1=ALU.add), [K("lid")], [K("kf")])
        dve(lambda e: e.tensor_copy(out=ki[:], in_=sm["kf"][:]), [K("kf")], ["s5_ki"])
        dve(lambda e: e.tensor_copy(out=sm["kf"][:], in_=ki[:]), ["s5_ki"], [K("kf")])
        dve(lambda e: e.scalar_tensor_tensor(out=sm["rr"][:], in0=sm["kf"][:], scalar=-TWO_PI, in1=sm["lid"][:],
                                             op0=ALU.mult, op1=ALU.add), [K("kf"), K("lid")], [K("rr")])

        def wrap(n):
            dve(lambda e: e.tensor_scalar(out=sm["m1"][:], in0=sm[n][:], scalar1=math.pi, scalar2=None, op0=ALU.is_gt),
                [K(n)], [K("m1")])
            dve(lambda e: e.scalar_tensor_tensor(out=sm[n][:], in0=sm["m1"][:], scalar=-TWO_PI, in1=sm[n][:],
                                                 op0=ALU.mult, op1=ALU.add), [K("m1"), K(n)], [K(n)])
            dve(lambda e: e.tensor_scalar(out=sm["m1"][:], in0=sm[n][:], scalar1=-math.pi, scalar2=None, op0=ALU.is_lt),
                [K(n)], [K("m1")])
            dve(lambda e: e.scalar_tensor_tensor(out=sm[n][:], in0=sm["m1"][:], scalar=TWO_PI, in1=sm[n][:],
                                                 op0=ALU.mult, op1=ALU.add), [K("m1"), K(n)], [K(n)])

        wrap("rr")
        dve(lambda e: e.tensor_scalar(out=sm["rc"][:], in0=sm["rr"][:], scalar1=0.5 * math.pi, scalar2=None, op0=ALU.add),
            [K("rr")], [K("rc")])
        wrap("rc")
        sc.op("act", lambda e: e.activation(out=sm["sn"][:], in_=sm["rr"][:], func=AF.Sin), reads=[K("rr")], writes=[K("sn")])
        sc.op("act", lambda e: e.activation(out=sm["cs"][:], in_=sm["rc"][:], func=AF.Sin), reads=[K("rc")], writes=[K("cs")])
        tt_(sm["are"][:], sm["mag"][:], sm["cs"][:], ALU.mult, [K("mag"), K("cs")], [K("are")])
        tt_(sm["aim"][:], sm["mag"][:], sm["sn"][:], ALU.mult, [K("mag"), K("sn")], [K("aim")])
        tt_(sm["t1"][:], sm["lr"][:], sm["lr"][:], ALU.mult, [K("lr")], [K("t1")])
        tt_(sm["den"][:], sm["li"][:], sm["li"][:], ALU.mult, [K("li")], [K("den")])
        tt_(sm["den"][:], sm["den"][:], sm["t1"][:], ALU.add, [K("den"), K("t1")], [K("den")])
        dve(lambda e: e.reciprocal(out=sm["den"][:], in_=sm["den"][:]), [K("den")], [K("den")])
        dve(lambda e: e.tensor_scalar(out=sm["xr"][:], in0=sm["are"][:], scalar1=-1.0, scalar2=None, op0=ALU.add),
            [K("are")], [K("xr")])
        tt_(sm["t1"][:], sm["xr"][:], sm["lr"][:], ALU.mult, [K("xr"), K("lr")], [K("t1")])
        tt_(sm["t2"][:], sm["aim"][:], sm["li"][:], ALU.mult, [K("aim"), K("li")], [K("t2")])
        tt_(sm["t1"][:], sm["t1"][:], sm["t2"][:], ALU.add, [K("t1"), K("t2")], [K("t1")])
        tt_(sm["gre"][:], sm["t1"][:], sm["den"][:], ALU.mult, [K("t1"), K("den")], [K("gre")])
        tt_(sm["t1"][:], sm["aim"][:], sm["lr"][:], ALU.mult, [K("aim"), K("lr")], [K("t1")])
        tt_(sm["t2"][:], sm["xr"][:], sm["li"][:], ALU.mult, [K("xr"), K("li")], [K("t2")])
        tt_(sm["t1"][:], sm["t1"][:], sm["t2"][:], ALU.subtract, [K("t1"), K("t2")], [K("t1")])
        tt_(sm["gim"][:], sm["t1"][:], sm["den"][:], ALU.mult, [K("t1"), K("den")], [K("gim")])
        gre_b = sm["gre"][:].unsqueeze(2).broadcast_to([128, 64, 16])
        gim_b = sm["gim"][:].unsqueeze(2).broadcast_to([128, 64, 16])
        tt_(bb_re[:], b_re[:], gre_b, ALU.mult, ["b_re", K("gre")], ["bb_re"])
        tt_(btmp[:], b_im[:], gim_b, ALU.mult, ["b_im", K("gim")], ["btmp"])
        tt_(bb_re[:], bb_re[:], btmp[:], ALU.subtract, ["bb_re", "btmp"], ["bb_re"])
        tt_(bb_im[:], b_im[:], gre_b, ALU.mult, ["b_im", K("gre")], ["bb_im"])
        tt_(btmp[:], b_re[:], gim_b, ALU.mult, ["b_re", K("gim")], ["btmp"])
        tt_(bb_im[:], bb_im[:], btmp[:], ALU.add, ["bb_im", "btmp"], ["bb_im"])
        dve(lambda e: e.tensor_copy(out=P_re[:, :, 0], in_=sm["are"][:]), [K("are")], ["P_re"])
        dve(lambda e: e.tensor_copy(out=P_im[:, :, 0], in_=sm["aim"][:]), [K("aim")], ["P_im"])
        for l in range(1, 11):
            tt_(sm["t1"][:], P_re[:, :, l - 1], P_re[:, :, l - 1], ALU.mult, ["P_re"], [K("t1")])
            tt_(sm["t2"][:], P_im[:, :, l - 1], P_im[:, :, l - 1], ALU.mult, ["P_im"], [K("t2")])
            tt_(P_re[:, :, l], sm["t1"][:], sm["t2"][:], ALU.subtract, [K("t1"), K("t2")], ["P_re"])
            tt_(sm["t1"][:], P_re[:, :, l - 1], P_im[:, :, l - 1], ALU.mult, ["P_re", "P_im"], [K("t1")])
            dve(lambda e, l=l: e.tensor_scalar(out=P_im[:, :, l], in0=sm["t1"][:], scalar1=2.0, scalar2=None, op0=ALU.mult),
                [K("t1")], ["P_im"])
        for (src, dst, dk) in ((P_re, AQ_re, "AQ_re"), (P_im, AQ_im, "AQ_im")):
            sv = src[:].rearrange("p (i two) l -> p i two l", two=2)
            sk = "P_re" if src is P_re else "P_im"
            dve(lambda e, sv=sv, dst=dst: e.tensor_copy(out=dst[0:64], in_=sv[0:64, :, 0, :]), [sk], [dk])
            dve(lambda e, sv=sv, dst=dst: e.tensor_copy(out=dst[64:128], in_=sv[64:128, :, 1, :]), [sk], [dk])
        dve(lambda e: e.tensor_scalar(out=AQ_in[:], in0=AQ_im[:], scalar1=-1.0, scalar2=None, op0=ALU.mult), ["AQ_im"], ["AQ_in"])
        sc.op("pool", lambda e: e.memset(MC[:], 0.0), writes=["MC"])
        for m in range(4):
            sc.op("pool", lambda e, m=m: e.memset(MC[0:64, m, 32 * m:32 * m + 16], 1.0), reads=["MC"], writes=["MC"])
            sc.op("pool", lambda e, m=m: e.memset(MC[64:128, m, 32 * m + 16:32 * m + 32], 1.0), reads=["MC"], writes=["MC"])
        for m in range(4):
            sc.op("pe", lambda e, m=m: e.transpose(PS[1][:, m * 128:(m + 1) * 128], MC[:, m, :], ident_f[:]),
                  reads=["MC", "ident_f"], writes=[psk(1)], inc=(m == 3))
        dve(lambda e: e.tensor_copy(out=MB[:].rearrange("p a b -> p (a b)"), in_=PS[1][:]), [psk(1)], ["MB"])
        for j in range(8):
            st_ = stage[j % 2]
            stk = "wstage%d" % (j % 2)
            srcs = ((bb_re[:, 8 * j:8 * j + 8, :].rearrange("p a b -> p (a b)"), "bb_re", 2, 0),
                    (bb_im[:, 8 * j:8 * j + 8, :].rearrange("p a b -> p (a b)"), "bb_im", 2, 1),
                    (CT_re[:, j, :], "CT_re", 3, 0), (CT_im[:, j, :], "CT_im", 3, 1))
            for (src, sk, bank, half) in srcs:
                sc.op("pe", lambda e, src=src, bank=bank, half=half: e.transpose(
                    PS[bank][:, half * 128:(half + 1) * 128], src, ident_f[:]),
                    reads=[sk, "ident_f"], writes=[psk(bank)], inc=True)
            for m in range(4):
                dve(lambda e, st_=st_, m=m: e.tensor_tensor(out=st_[:, 0 + m, :], in0=PS[2][:, 0:128], in1=MB[:, m, :], op=ALU.mult),
                    [psk(2), "MB"], [stk])
                dve(lambda e, st_=st_, m=m: e.tensor_tensor(out=st_[:, 4 + m, :], in0=PS[2][:, 128:256], in1=MB[:, m, :], op=ALU.mult),
                    [psk(2), "MB"], [stk])
                dve(lambda e, st_=st_, m=m: e.tensor_tensor(out=st_[:, 8 + m, :], in0=PS[3][:, 0:128], in1=MC[:, m, :], op=ALU.mult),
                    [psk(3), "MC"], [stk])
                dve(lambda e, st_=st_, m=m: e.scalar_tensor_tensor(out=st_[:, 12 + m, :], in0=PS[3][:, 128:256], scalar=-1.0,
                                                                   in1=MC[:, m, :], op0=ALU.mult, op1=ALU.mult),
                    [psk(3), "MC"], [stk])
            sc.dma("sp", stk, Wd[j], st_[:], reads=[stk], writes=[("Wd", j)])
        mixT2 = mixT
        AX2 = Arena(XTB_OFF)
        X2 = AX2.take("X2_", [128, 2, S], F32, 2)
        for i_ in range(2):
            lo_, hi_ = sc.ranges["X2_%d" % i_]
            sc.reg("X_re%d" % i_, lo_, lo_ + S * 4)
            sc.reg("X_im%d" % i_, lo_ + S * 4, hi_)
        X_re = [X2[i_][:, 0, :] for i_ in range(2)]
        X_im = [X2[i_][:, 1, :] for i_ in range(2)]
        H_re = AX2.take("H_re", [128, S], BF16, 4)
        H_im = AX2.take("H_im", [128, S], BF16, 4)
        C1 = 2.0 * math.sqrt(2.0 / math.pi)
        C2 = C1 * 0.044715
        bcnt = [0]
        for j in range(8):
            wsl = wslot[j % 2]
            wk_ = "wslot%d" % (j % 2)
            sc.dma("sp", wk_, wsl[:], Wd[j], reads=[("Wd", j)], writes=[wk_])
            for mp in (0, 2):
                conv_pump(8)
                chains = []
                for m in (mp, mp + 1):
                    i = 4 * j + m
                    xs = i % 2
                    xr_, xi_ = X_re[xs], X_im[xs]
                    xrk, xik = "X_re%d" % xs, "X_im%d" % xs
                    for tt in range(NT):
                        tsl = slice(tt * T, (tt + 1) * T)
                        for half, (dst, dk) in enumerate(((xr_, xrk), (xi_, xik))):
                            bank = bcnt[0] % 4
                            bcnt[0] += 1
                            sc.op("pe", lambda e, bank=bank, half=half, m=m, wsl=wsl, j=j, tsl=tsl: e.matmul(
                                PS[bank][:], lhsT=wsl[:, 4 * half + m, :], rhs=mixT2[:, 8 + j, tsl], start=True, stop=True),
                                reads=[wk_, "mixhi"], writes=[psk(bank)], inc=True)
                            sc.op("act", lambda e, bank=bank, dst=dst, tsl=tsl: e.copy(out=dst[:, tsl], in_=PS[bank][:]),
                                  reads=[psk(bank)], writes=[dk])
                    ops = []
                    for l in range(11):
                        ops += bk_level(1 << l, l, 2 * (1 << l) - 1, xr_, xi_, xrk, xik, i, AQ_re, AQ_im, AQ_in, x2=X2[xs])
                    for l in range(9, -1, -1):
                        ops += bk_level(1 << l, l, 3 * (1 << l) - 1, xr_, xi_, xrk, xik, i, AQ_re, AQ_im, AQ_in, x2=X2[xs])
                    chains.append((m, xr_, xi_, xrk, xik, ops))
                for idx in range(max(len(c[5]) for c in chains)):
                    for c_ in chains:
                        if idx < len(c_[5]):
                            fn, rd, wr = c_[5][idx]
                            sc.op("dve", fn, reads=rd, writes=wr)
                for (m, xr_, xi_, xrk, xik, _) in chains:
                    sc.op("act", lambda e, m=m, xr_=xr_: e.copy(out=H_re[m][:], in_=xr_[:]), reads=[xrk], writes=["H_re%d" % m])
                    sc.op("act", lambda e, m=m, xi_=xi_: e.copy(out=H_im[m][:], in_=xi_[:]), reads=[xik], writes=["H_im%d" % m])
            for tt in range(NT):
                tsl = slice(tt * T, (tt + 1) * T)
                bank = 4 + tt % 2
                for m in range(4):
                    sc.op("pe", lambda e, bank=bank, m=m, wsl=wsl, tsl=tsl: e.matmul(
                        PS[bank][:], lhsT=wsl[:, 8 + m, :], rhs=H_re[m][:, tsl], start=(m == 0), stop=False),
                        reads=[wk_, "H_re%d" % m], writes=[psk(bank)], inc=False)
                    sc.op("pe", lambda e, bank=bank, m=m, wsl=wsl, tsl=tsl: e.matmul(
                        PS[bank][:], lhsT=wsl[:, 12 + m, :], rhs=H_im[m][:, tsl], start=False, stop=(m == 3)),
                        reads=[wk_, "H_im%d" % m], writes=[psk(bank)], inc=(m == 3))
                y_ = y32[tt % 2]
                g_ = gt[tt % 2]
                yk, gk = "y32%d" % (tt % 2), "gt%d" % (tt % 2)
                sc.op("dve", lambda e, bank=bank, y_=y_, j=j, tsl=tsl: e.scalar_tensor_tensor(
                    out=y_[:], in0=mixT2[:, 8 + j, tsl], scalar=Dcol[:, j:j + 1], in1=PS[bank][:], op0=ALU.mult, op1=ALU.add),
                    reads=[psk(bank), "mixhi", "Dcol"], writes=[yk])
                sc.op("pool", lambda e, y_=y_, g_=g_: e.tensor_tensor(out=g_[:], in0=y_[:], in1=y_[:], op=ALU.mult),
                      reads=[yk], writes=[gk])
                sc.op("pool", lambda e, g_=g_: e.tensor_scalar(out=g_[:], in0=g_[:], scalar1=C2, scalar2=C1, op0=ALU.mult, op1=ALU.add),
                      reads=[gk], writes=[gk])
                sc.op("pool", lambda e, y_=y_, g_=g_: e.tensor_tensor(out=g_[:], in0=g_[:], in1=y_[:], op=ALU.mult),
                      reads=[gk, yk], writes=[gk])
                sc.op("act", lambda e, g_=g_: e.activation(out=g_[:], in_=g_[:], func=AF.Sigmoid), reads=[gk], writes=[gk])
                sc.op("pool", lambda e, y_=y_, g_=g_, j=j, tsl=tsl: e.tensor_tensor(out=yg[:, j, tsl], in0=g_[:], in1=y_[:], op=ALU.mult),
                      reads=[gk, yk], writes=["yg"])
        wgl = ev_w_glu.rearrange("(c p) f -> p c f", p=128)

        def issue_glu(i, s_):
            sc.dma("pool", "wz1%d" % s_, wz1[s_][:], wgl[:, :, i * 128:(i + 1) * 128], writes=["wz1%d" % s_])
            sc.dma("pool", "wz2%d" % s_, wz2[s_][:], wgl[:, :, 1024 + i * 128:1024 + (i + 1) * 128], writes=["wz2%d" % s_])

        pf = Prefetch(8, 2, issue_glu)
        for e_ in range(8):
            pf.ensure(e_)
            s_ = e_ % 2
            for tt in range(NT):
                tsl = slice(tt * T, (tt + 1) * T)
                ba, bb_ = (tt % 2) * 2, (tt % 2) * 2 + 1
                for (bank, w_, wkey) in ((ba, wz1[s_], "wz1%d" % s_), (bb_, wz2[s_], "wz2%d" % s_)):
                    for c in range(8):
                        sc.op("pe", lambda e, bank=bank, w_=w_, c=c, tsl=tsl: e.matmul(
                            PS[bank][:], lhsT=w_[:, c, :], rhs=yg[:, c, tsl], start=(c == 0), stop=(c == 7)),
                            reads=[wkey, "yg"], writes=[psk(bank)], inc=(c == 7))
                sg_ = sig[tt % 2]
                sgk = "sig%d" % (tt % 2)
                sc.op("act", lambda e, bb_=bb_, sg_=sg_: e.activation(out=sg_[:], in_=PS[bb_][:], func=AF.Sigmoid),
                      reads=[psk(bb_)], writes=[sgk])
                sc.op("dve", lambda e, ba=ba, sg_=sg_, e_=e_, tsl=tsl: e.tensor_tensor(
                    out=mixT2[:, 8 + e_, tsl], in0=PS[ba][:], in1=sg_[:], op=ALU.mult),
                    reads=[psk(ba), sgk], writes=["mixhi"])

    INVF = [float(v) for v in (np.float32(500000.0) ** (-(np.arange(8, dtype=np.float32) / np.float32(8.0))))]

    def psb(i):
        return PS[i][:].bitcast(BF16)

    def phase_swa():
        mixT = get_mixT()
        A = Arena(PH2_OFF)
        wcol = A.take("wcol", [128, DC, 256], BF16, 2)
        qb = A.take("qb", [128, 4, 64], BF16, 2)
        kd = A.take("kd", [128, 4, 2, 64], BF16, 2)
        qTp = A.take("qTp", [128, S], BF16, 4)
        kTd = A.take("kTd", [128, S], BF16, 4)
        Vs = A.take("Vs", [128, 16, 256], BF16)
        PTa = A.take("PTa", [128, T], BF16, 2)
        PTb = A.take("PTb", [128, T], BF16, 2)
        mask2 = A.take("mask2", [128, 512], BF16)
        maskA = mask2[:, 0:256].rearrange("p (a b) -> p a b", a=2)
        maskB = mask2[:, 256:512].rearrange("p (a b) -> p a b", a=2)
        cosk = A.take("cosk", [128, 16, 8], F32)
        sink = A.take("sink", [128, 16, 8], F32)
        cosq = A.take("cosq", [128, 16, 8], F32)
        sinq = A.take("sinq", [128, 16, 8], F32)
        ang = A.take("ang", [128, 16, 8], F32)
        angc = A.take("angc", [128, 16, 8], F32)
        rtmp = A.take("rtmp", [128, 16, 8], F32)
        rki = A.take("rki", [128, 16, 8], I32)
        posi = A.take("posi", [128, 128], I32)
        posf = A.take("posf", [128, 128], F32)
        posT = A.take("posT", [128, 16], F32)
        invf = A.take("invf", [128, 8], F32)
        es = A.take("es", [128, 32], F32)
        esP = A.take("esP", [128, 16], F32)
        rt = A.take("rt", [128, 6, 4, 8], F32, 2)
        rot = A.take("rot", [128, 4, 16], F32, 2)
        rc = A.take("rc", [128, 256], F32, 2)
        win = od_w_in.rearrange("(c p) f -> p c f", p=128)

        def dve(fn, reads, writes):
            sc.op("dve", fn, reads=reads, writes=writes)

        sc.op("pool", lambda e: e.memset(mask2[:], 0.0), writes=["mask2"])
        sc.op("pool", lambda e: e.affine_select(out=maskA, in_=maskA, pattern=[[0, 2], [-1, 128]],
                                                 compare_op=ALU.is_ge, fill=-30000.0, base=-1, channel_multiplier=1),
              reads=["mask2"], writes=["mask2"])
        sc.op("pool", lambda e: e.affine_select(out=maskB, in_=maskB, pattern=[[0, 2], [1, 128]],
                                                 compare_op=ALU.is_ge, fill=-30000.0, base=0, channel_multiplier=-1),
              reads=["mask2"], writes=["mask2"])
        sc.dma("sp", "swld", es[:], od_sinks.broadcast_to([128, 32]), writes=["es"])
        sc.op("act", lambda e: e.activation(out=es[:], in_=es[:], func=AF.Exp), reads=["es"], writes=["es"])
        esv = es[:].rearrange("p (a two) -> p a two", two=2)
        dve(lambda e: e.tensor_copy(out=esP[0:64, :], in_=esv[0:64, :, 0]), ["es"], ["esP"])
        dve(lambda e: e.tensor_copy(out=esP[64:128, :], in_=esv[64:128, :, 1]), ["es"], ["esP"])
        sc.dma("sp", "swld", posi[0:16, :], pos_in, writes=["posi"])
        dve(lambda e: e.tensor_copy(out=posf[0:16, :], in_=posi[0:16, :]), ["posi"], ["posf"])
        sc.op("pe", lambda e: e.transpose(PS[7][:, 0:16], posf[0:16, :], ident_f[0:16, 0:16]),
              reads=["posf", "ident_f"], writes=[psk(7)])
        dve(lambda e: e.tensor_copy(out=posT[:], in_=PS[7][:, 0:16]), [psk(7)], ["posT"])
        for f in range(8):
            sc.op("pool", lambda e, f=f: e.memset(invf[:, f:f + 1], INVF[f]), writes=["invf"])
        dve(lambda e: e.tensor_tensor(out=ang[:], in0=posT[:].unsqueeze(2).broadcast_to([128, 16, 8]),
                                      in1=invf[:].unsqueeze(1).broadcast_to([128, 16, 8]), op=ALU.mult),
            ["posT", "invf"], ["ang"])

        def reduce_(x, xk):
            dve(lambda e: e.tensor_scalar(out=rtmp[:], in0=x[:], scalar1=1.0 / TWO_PI, scalar2=0.5, op0=ALU.mult, op1=ALU.add),
                [xk], ["rtmp"])
            dve(lambda e: e.tensor_copy(out=rki[:], in_=rtmp[:]), ["rtmp"], ["rki"])
            dve(lambda e: e.tensor_copy(out=rtmp[:], in_=rki[:]), ["rki"], ["rtmp"])
            dve(lambda e: e.scalar_tensor_tensor(out=x[:], in0=rtmp[:], scalar=-TWO_PI, in1=x[:], op0=ALU.mult, op1=ALU.add),
                ["rtmp", xk], [xk])
            for (thr, op_, add) in ((math.pi, ALU.is_gt, -TWO_PI), (-math.pi, ALU.is_lt, TWO_PI)):
                dve(lambda e, thr=thr, op_=op_: e.tensor_scalar(out=rtmp[:], in0=x[:], scalar1=thr, scalar2=None, op0=op_),
                    [xk], ["rtmp"])
                dve(lambda e, add=add: e.scalar_tensor_tensor(out=x[:], in0=rtmp[:], scalar=add, in1=x[:], op0=ALU.mult, op1=ALU.add),
                    ["rtmp", xk], [xk])

        dve(lambda e: e.tensor_scalar(out=angc[:], in0=ang[:], scalar1=0.5 * math.pi, scalar2=None, op0=ALU.add), ["ang"], ["angc"])
        reduce_(ang, "ang")
        reduce_(angc, "angc")
        sc.op("act", lambda e: e.activation(out=sink[:], in_=ang[:], func=AF.Sin), reads=["ang"], writes=["sink"])
        sc.op("act", lambda e: e.activation(out=cosk[:], in_=angc[:], func=AF.Sin), reads=["angc"], writes=["cosk"])
        sc.op("act", lambda e: e.mul(out=sinq[:], in_=sink[:], mul=0.125), reads=["sink"], writes=["sinq"])
        sc.op("act", lambda e: e.mul(out=cosq[:], in_=cosk[:], mul=0.125), reads=["cosk"], writes=["cosq"])
        for hq in range(4):
            sc.op("pool", lambda e, hq=hq: e.memset(qTp[hq][:], 0.0), writes=["qTp%d" % hq])

        units = [("k", 2048), ("v", 2304)] + [("q", 256 * u) for u in range(8)]
        pf = Prefetch(len(units), 2, lambda i, s_: sc.dma("pool", "wcol%d" % s_, wcol[s_][:],
                                                          win[:, :, units[i][1]:units[i][1] + 256], writes=["wcol%d" % s_]))
        ecnt = [0]

        def rope(P4, cs, sn, tb, o1, o2, okeys, slot):
            r_ = rt[slot]
            rk = "rt%d" % slot
            cb = cs[:, tb, :].unsqueeze(1).broadcast_to([128, 4, 8])
            sb_ = sn[:, tb, :].unsqueeze(1).broadcast_to([128, 4, 8])
            x1 = P4[:, :, 0:8]
            x2 = P4[:, :, 8:16]
            ck = ["cosk", "sink", "cosq", "sinq"]
            dve(lambda e: e.tensor_tensor(out=r_[:, 0], in0=x1, in1=cb, op=ALU.mult), okeys["ps"] + ck, [rk])
            dve(lambda e: e.tensor_tensor(out=r_[:, 1], in0=x2, in1=sb_, op=ALU.mult), okeys["ps"] + ck, [rk])
            dve(lambda e: e.tensor_tensor(out=r_[:, 2], in0=x2, in1=cb, op=ALU.mult), okeys["ps"] + ck, [rk])
            dve(lambda e: e.tensor_tensor(out=r_[:, 3], in0=x1, in1=sb_, op=ALU.mult), okeys["ps"] + ck, [rk])
            dve(lambda e: e.tensor_tensor(out=o1, in0=r_[:, 0], in1=r_[:, 1], op=ALU.subtract), [rk], okeys["out"])
            dve(lambda e: e.tensor_tensor(out=o2, in0=r_[:, 2], in1=r_[:, 3], op=ALU.add), [rk], okeys["out"])

        for ui, (kind, col0) in enumerate(units):
            pf.ensure(ui)
            ws = ui % 2
            wkey = "wcol%d" % ws
            quad = ui - 2
            def inproj_mm(tb, kind=kind, ws=ws, wkey=wkey):
                bank = (tb // 2) % 2
                half = tb % 2
                Pfull = PS[bank][:, half * 256:(half + 1) * 256]
                for c in range(DC):
                    lhsT = xTb[:, c, tb * 128:(tb + 1) * 128]
                    sc.op("pe", lambda e, Pfull=Pfull, lhsT=lhsT, ws=ws, c=c: e.matmul(
                        Pfull, lhsT=lhsT, rhs=wcol[ws][:, c, :], start=(c == 0), stop=(c == DC - 1)),
                        reads=[wkey, "xTb%d" % (tb // 4)], writes=[psk(bank)], inc=(c == DC - 1))
            def inproj_post(tb, kind=kind, quad=quad):
                bank = (tb // 2) % 2
                half = tb % 2
                Pfull = PS[bank][:, half * 256:(half + 1) * 256]
                P4 = Pfull.rearrange("p (h d) -> p h d", h=4)
                slot = ecnt[0] % 2
                ecnt[0] += 1
                if kind == "v":
                    sc.op("act", lambda e, Pfull=Pfull, tb=tb: e.copy(out=Vs[:, tb, :], in_=Pfull), reads=[psk(bank)], writes=["Vs"])
                elif kind == "k":
                    kd_ = kd[slot]
                    kk = "kd%d" % slot
                    sc.op("act", lambda e, P4=P4, kd_=kd_: e.copy(out=kd_[:, :, 0, :], in_=P4), reads=[psk(bank)], writes=[kk])
                    ro = rot[slot]
                    rok = "rot%d" % slot
                    rope(P4, cosk, sink, tb, ro[:, :, 0:8], ro[:, :, 8:16], {"ps": [psk(bank)], "out": [rok]}, slot)
                    dve(lambda e, kd_=kd_, ro=ro: e.tensor_copy(out=kd_[:, :, 0, 0:16], in_=ro[:]), [rok, kk], [kk])
                    dve(lambda e, kd_=kd_: e.tensor_copy(out=kd_[:, :, 1, :], in_=kd_[:, :, 0, :]), [kk], [kk])
                    for kv in range(4):
                        sc.op("pe", lambda e, kv=kv, kd_=kd_: e.transpose(
                            psb(2)[:, kv * 128:(kv + 1) * 128], kd_[:, kv, :, :].rearrange("p a b -> p (a b)"), ident_b[:]),
                            reads=[kk, "ident_b"], writes=[psk(2)], inc=(kv == 3))
                    for kv in range(4):
                        eng = "act" if kv % 2 == 0 else "dve"
                        if eng == "act":
                            sc.op("act", lambda e, kv=kv, tb=tb: e.copy(out=kTd[kv][:, tb * 128:(tb + 1) * 128],
                                                                        in_=psb(2)[:, kv * 128:(kv + 1) * 128]),
                                  reads=[psk(2)], writes=["kTd%d" % kv])
                        else:
                            dve(lambda e, kv=kv, tb=tb: e.tensor_copy(out=kTd[kv][:, tb * 128:(tb + 1) * 128],
                                                                       in_=psb(2)[:, kv * 128:(kv + 1) * 128]),
                                [psk(2)], ["kTd%d" % kv])
                else:
                    qb_ = qb[slot]
                    qk_ = "qb%d" % slot
                    sc.op("act", lambda e, P4=P4, qb_=qb_: e.activation(out=qb_[:], in_=P4, func=AF.Copy, scale=0.125),
                          reads=[psk(bank)], writes=[qk_])
                    rope(P4, cosq, sinq, tb, qb_[:, :, 0:8], qb_[:, :, 8:16], {"ps": [psk(bank), qk_], "out": [qk_]}, slot)
                    for mc in range(2):
                        sc.op("pe", lambda e, mc=mc, qb_=qb_: e.transpose(
                            psb(2)[:, mc * 128:(mc + 1) * 128], qb_[:, 2 * mc:2 * mc + 2, :].rearrange("p a b -> p (a b)"),
                            ident_b[:]), reads=[qk_, "ident_b"], writes=[psk(2)], inc=(mc == 1))
                    for hq in range(4):
                        e_, mc = hq % 2, hq // 2
                        src = psb(2)[e_ * 64:(e_ + 1) * 64, mc * 128:(mc + 1) * 128]
                        dst = qTp[hq][e_ * 64:(e_ + 1) * 64, tb * 128:(tb + 1) * 128]
                        if hq % 2 == 0:
                            sc.op("act", lambda e, src=src, dst=dst: e.copy(out=dst, in_=src), reads=[psk(2)], writes=["qTp%d" % hq])
                        else:
                            dve(lambda e, src=src, dst=dst: e.tensor_copy(out=dst, in_=src), [psk(2)], ["qTp%d" % hq])
            inproj_mm(0)
            for tb in range(16):
                if tb + 1 < 16:
                    inproj_mm(tb + 1)
                inproj_post(tb)
            if kind != "q":
                continue
            hk = quad // 2
            units_ = [(i, mc) for i in range(16) for mc in range(2)]

            def emit_scores(u):
                i, mc = units_[u]
                bank = 3 + u % 2
                pt = PTa[u % 2]
                ptk = "PTa%d" % (u % 2)
                first = True
                for blk, kb in ((0, i - 1), (1, i)):
                    if kb < 0:
                        continue
                    for e_ in range(2):
                        hq = 2 * mc + e_
                        col = blk * 256 + e_ * 128
                        sc.op("pe", lambda e, bank=bank, kb=kb, hq=hq, i=i, hk=hk, col=col, first=first: e.matmul(
                            PS[bank][:, col:col + 128], lhsT=kTd[hk][:, kb * 128:(kb + 1) * 128],
                            rhs=qTp[hq][:, i * 128:(i + 1) * 128], start=first, stop=False),
                            reads=["kTd%d" % hk, "qTp%d" % hq], writes=[psk(bank)], inc=False)
                        first = False
                c0 = 0 if i > 0 else 256
                sc.op("pe", lambda e, bank=bank, c0=c0: e.matmul(
                    PS[bank][:, c0:512], lhsT=ident_b[:], rhs=mask2[:, c0:512], start=False, stop=True),
                    reads=["ident_b", "mask2"], writes=[psk(bank)], inc=True)
                sc.op("act", lambda e, bank=bank, pt=pt, c0=c0: e.activation(out=pt[:, c0:512], in_=PS[bank][:, c0:512], func=AF.Exp),
                      reads=[psk(bank)], writes=[ptk])

            def emit_pv(u):
                i, mc = units_[u]
                obank = 5 + u % 2
                pt = PTa[u % 2]
                ptk = "PTa%d" % (u % 2)
                blks = [(0, i - 1), (1, i)] if i > 0 else [(1, i)]
                for (c_off, use_v) in ((0, True), (128, False)):
                    for e_ in range(2):
                        out = PS[obank][e_ * 64:(e_ + 1) * 64, c_off:c_off + 128]
                        for bi, (blk, kb) in enumerate(blks):
                            lhsT = Vs[:, kb, hk * 64:(hk + 1) * 64] if use_v else ones_b[:, 0:64]
                            rhs = pt[:, blk * 256 + e_ * 128:blk * 256 + (e_ + 1) * 128]
                            lastmm = (not use_v) and e_ == 1 and bi == len(blks) - 1
                            sc.op("pe", lambda e, out=out, lhsT=lhsT, rhs=rhs, bi=bi, nb=len(blks), e_=e_: e.matmul(
                                out, lhsT=lhsT, rhs=rhs, start=(bi == 0), stop=(bi == nb - 1), tile_position=(0, e_ * 64)),
                                reads=[ptk, "Vs" if use_v else "ones_b"], writes=[psk(obank)], inc=lastmm)
                rc_ = rc[u % 2]
                rck = "rc%d" % (u % 2)
                ch = quad * 2 + mc
                dve(lambda e, rc_=rc_, obank=obank, ch=ch: e.tensor_scalar(
                    out=rc_[:, 0:128], in0=PS[obank][:, 128:256], scalar1=esP[:, ch:ch + 1], scalar2=None, op0=ALU.add),
                    [psk(obank), "esP"], [rck])
                dve(lambda e, rc_=rc_: e.reciprocal(out=rc_[:, 0:128], in_=rc_[:, 0:128]), [rck], [rck])
                dve(lambda e, rc_=rc_, obank=obank, ch=ch, i=i: e.tensor_tensor(
                    out=mixT[:, ch, i * 128:(i + 1) * 128], in0=PS[obank][:, 0:128], in1=rc_[:, 0:128], op=ALU.mult),
                    [psk(obank), rck], ["mixlo" if ch < 8 else "mixhi"])

            emit_scores(0)
            for u in range(len(units_)):
                if u + 1 < len(units_):
                    emit_scores(u + 1)
                emit_pv(u)

    def dbg_dump_mix():
        mixT = get_mixT()
        ov = out_d.rearrange("(c p) t -> p c t", p=128)
        for c in range(DC):
            sc.dma("pool", "dbgm", ov[:, c, :], mixT[:, c, :], reads=["mixlo", "mixhi"])

    def phase_output(R):
        A = Arena(PH_OFF)
        rt_ = A.take("o_r", [128, DC, T], F32, 2)
        ost = A.take("o_st", [128, D], F32, 3)
        for tt in range(NT):
            r_ = rt_[tt % 2]
            rk = "o_r%d" % (tt % 2)
            sc.dma("sp", rk, r_[:], R[:, :, tt * T:(tt + 1) * T].rearrange("c p t -> p c t"),
                   reads=[("R", id(R), tt)], writes=[rk])
            for tb in range(4):
                g = tt * 4 + tb
                os_ = ost[g % 3]
                ok = "o_st%d" % (g % 3)
                for q in range(4):
                    bank = q % 2 + 2 * (g % 2)
                    for c4 in range(4):
                        dc = q * 4 + c4
                        src = r_[:, dc, tb * 128:(tb + 1) * 128]
                        sc.op("pe", lambda e, b=bank, c4=c4, src=src: e.transpose(
                            PS[b][:, c4 * 128:(c4 + 1) * 128], src, ident_f[:]),
                            reads=[rk, "ident_f"], writes=[psk(bank)], inc=(c4 == 3))
                    dst = os_[:, q * 512:(q + 1) * 512]
                    if q % 2 == 0:
                        sc.op("act", lambda e, b=bank, dst=dst: e.copy(out=dst, in_=PS[b][:]), reads=[psk(bank)], writes=[ok])
                    else:
                        sc.op("dve", lambda e, b=bank, dst=dst: e.tensor_copy(out=dst, in_=PS[b][:]), reads=[psk(bank)], writes=[ok])
                sc.dma("sp", ok, out_d[g * 128:(g + 1) * 128, :], os_[:], reads=[ok], writes=[("out", g)])

    def dbg_copy_R(R):
        A = Arena(PH_OFF)
        bufs = A.take("dbgb", [128, S], F32, 2)
        for c in range(DC):
            k = "dbgb%d" % (c % 2)
            sc.dma("sp", k, bufs[c % 2][:], R[c], reads=[("R", id(R), tt) for tt in range(NT)], writes=[k])
            sc.dma("sp", k, out_d[c * 128:(c + 1) * 128, :], bufs[c % 2][:], reads=[k])

    setup_consts()
    sc.phase_reset()
    if mode == "full":
        phase_input(x_in, RA)
        sc.phase_reset()
        phase_fox()
        sc.phase_reset()
        phase_s5()
        sc.phase_reset()
        phase_outproj(0, ev_w_out, RA, RB)
        sc.phase_reset()
        phase_ffn(0, RB, RA, final=False)
        sc.phase_reset()
        phase_swa()
        sc.phase_reset()
        phase_outproj(1, od_w_out, RA, RB)
        sc.phase_reset()
        phase_ffn(1, RB, RA, final=False)
        sc.phase_reset()
        phase_output(RA)
    elif mode == "consts":
        sc.dma("sp", "dbg", out_d[0:128, 0:128], ident_f[:], reads=["ident_f"])
        sc.dma("sp", "dbg", out_d[128:256, 0:86 * 4].rearrange("p (a b) -> p a b", a=4), convp[:, 0, :, :], reads=["consts"])
        sc.dma("sp", "dbg", out_d[256:384, 0:128].rearrange("p (a b) -> p a b", a=8), lnp[:], reads=["consts"])
    elif mode == "p0":
        phase_input(x_in, RA)
        sc.phase_reset()
        if 'dbg' not in _SKIP:
            dbg_copy_R(RA)
    elif mode == "fox":
        phase_input(x_in, RA)
        sc.phase_reset()
        phase_fox()
        sc.phase_reset()
        dbg_dump_mix()
    elif mode == "s5":
        phase_input(x_in, RA)
        sc.phase_reset()
        phase_fox()
        sc.phase_reset()
        phase_s5()
        sc.phase_reset()
        dbg_dump_mix()
    elif mode == "swa":
        phase_input(x_in, RA)
        sc.phase_reset()
        phase_swa()
        sc.phase_reset()
        phase_outproj(1, od_w_out, RA, RB)
        sc.phase_reset()
        dbg_copy_R(RB)
    elif mode == "outproj0":
        phase_input(x_in, RA)
        sc.phase_reset()
        phase_fox()
        sc.phase_reset()
        phase_outproj(0, ev_w_out, RA, RB)
        sc.phase_reset()
        dbg_copy_R(RB)
    elif mode == "ffn0":
        phase_input(x_in, RA)
        sc.phase_reset()
        phase_ffn(0, RA, RB, final=False)
        sc.phase_reset()
        phase_output(RB)
    else:
        raise NotImplementedError(mode)
    sc.barrier()
    sc.emit()
    return nc, es


_CACHE = {}


def _prep_inputs(inp, b):
    f = np.ascontiguousarray
    m = {
        "x": f(inp["x"][b]),
        "pos": f(inp["positions"][b].reshape(16, 128).astype(np.int32)),
        "ev_w_in": f(inp["ev_w_in"][0]),
        "ev_b_f": f(inp["ev_b_f"][0].reshape(8, 1)),
        "ev_lre": f(inp["ev_lambda_re"][0]),
        "ev_lim": f(inp["ev_lambda_im"][0]),
        "ev_lstep": f(inp["ev_log_step"][0].reshape(1, 64)),
        "ev_bre": f(inp["ev_ssm_b_re"][0]),
        "ev_bim": f(inp["ev_ssm_b_im"][0]),
        "ev_cre": f(inp["ev_ssm_c_re"][0]),
        "ev_cim": f(inp["ev_ssm_c_im"][0]),
        "ev_d": f(inp["ev_ssm_d"][0].reshape(8, 128)),
        "ev_w_glu": f(inp["ev_w_glu"][0]),
        "ev_w_out": f(inp["ev_w_out"][0]),
        "od_w_in": f(inp["od_w_in"][0]),
        "od_sinks": f(inp["od_sinks"][0].reshape(1, 32)),
        "od_w_out": f(inp["od_w_out"][0]),
        "ln_mix_g": f(inp["ln_mix_g"].reshape(2, 16, 128)),
        "ln_mix_b": f(inp["ln_mix_b"].reshape(2, 16, 128)),
        "ffn_w_up": f(inp["ffn_w_up"]),
        "ffn_conv_w": f(inp["ffn_conv_w"].reshape(2, 3, 86, 128)),
        "ffn_conv_b": f(inp["ffn_conv_b"].reshape(2, 86, 128)),
        "ffn_w_down": f(inp["ffn_w_down"]),
        "ln_ffn_g": f(inp["ln_ffn_g"].reshape(2, 16, 128)),
        "ln_ffn_b": f(inp["ln_ffn_b"].reshape(2, 16, 128)),
    }
    return m


def run(inputs, mode="full", cores=8, trace=False):
    nc, es = build(mode)
    in_maps = [_prep_inputs(inputs, b) for b in range(cores)]
    res = run_bass_kernel_spmd(nc, in_maps, core_ids=list(range(cores)), trace=trace)
    es.close()
    return res


def kernel(**inputs):
    res = run(inputs, "full", 8)
    out = np.stack([np.asarray(r["out"]) for r in res.results], axis=0)
    return out.astype(np.float32)
```
